# Optimizing a Trainium2 kernel written in Bass

```python
import math
import jax
import jax.numpy as jnp
from jax import lax
import numpy as np

D_MODEL = 1024
BATCH = 8
SEQ = 2048
DEPTH = 2
DEC_BATCH = 128
DEC_SEQ = 1
PAST_LEN = 16384
PAGE_SIZE = 128

N_META = 16
CHUNK = 64
A_HEADS = 4
A_DK = 128
A_DV = 128
A_QK = A_HEADS * A_DK
A_V = A_HEADS * A_DV
CONV_W = 4
CONV_CH = 2 * A_QK + A_V
B_HEADS = 4
B_DK = 64
B_DV = 128
B_QK = B_HEADS * B_DK
B_V = B_HEADS * B_DV
B_GATE_RANK = 16
B_GATE_TAU = 16.0
D_FF = ((8 * D_MODEL + 3 * 256 - 1) // (3 * 256)) * 256
IN_SIZES = (CONV_CH, A_V, A_HEADS, A_HEADS, B_QK, B_QK, B_V, B_V, B_GATE_RANK, D_MODEL, D_MODEL)
D_IN = sum(IN_SIZES)
DEEPNORM_ALPHA = (2.0 * DEPTH) ** 0.25
DEEPNORM_BETA = (8.0 * DEPTH) ** -0.25
F32 = jnp.float32

kernel_name = 'hybrid_gdn_gla_deepnorm_step'


def _layer_norm(x, g, b, eps=1e-5):
    xf = x.astype(F32)
    mu = jnp.mean(xf, axis=-1, keepdims=True)
    var = jnp.mean(jnp.square(xf - mu), axis=-1, keepdims=True)
    return ((xf - mu) * lax.rsqrt(var + eps) * g.astype(F32) + b.astype(F32)).astype(x.dtype)


def _rms_norm(x, w, eps=1e-6):
    xf = x.astype(F32)
    return xf * lax.rsqrt(jnp.mean(xf * xf, axis=-1, keepdims=True) + eps) * w.astype(F32)


def _l2norm(x, eps=1e-6):
    return x * lax.rsqrt(jnp.sum(x * x, axis=-1, keepdims=True) + eps)


def _heads(t, n):
    b, l = t.shape[:2]
    return t.reshape(b, l, n, -1).transpose(0, 2, 1, 3).astype(F32)


def _split_in(p):
    offsets = np.cumsum(IN_SIZES)[:-1].tolist()
    return jnp.split(p, offsets, axis=-1)


def _causal_conv(u, buf, w):
    l = u.shape[1]
    ext = jnp.concatenate([buf.astype(u.dtype), u], axis=1)
    out = sum(ext[:, i:i + l] * w[i] for i in range(CONV_W))
    return jax.nn.silu(out), ext[:, l:]


def _gdn_chunks(q, k, v, g, beta, s0, chunk):
    b, h, l, dk = q.shape
    dv = v.shape[-1]
    n = l // chunk
    r = lambda t: t.reshape(t.shape[:2] + (n, chunk) + t.shape[3:])
    q, k, v, g, beta = r(q), r(k), r(v), r(g), r(beta)
    gam = jnp.cumsum(g, axis=-1)
    incl = jnp.tril(jnp.ones((chunk, chunk), dtype=bool))
    strict = jnp.tril(jnp.ones((chunk, chunk), dtype=bool), k=-1)
    diff = gam[..., :, None] - gam[..., None, :]
    dec_incl = jnp.exp(jnp.where(incl, diff, -jnp.inf))
    dec_strict = jnp.where(strict, dec_incl, 0.0)
    a_mat = beta[..., None] * dec_strict * jnp.einsum('bhnck,bhnjk->bhncj', k, k)
    eye = jnp.eye(chunk, dtype=F32)
    rhs = jnp.concatenate([beta[..., None] * v, (beta * jnp.exp(gam))[..., None] * k], axis=-1)
    sol = lax.linalg.triangular_solve(eye + a_mat, rhs, left_side=True, lower=True, unit_diagonal=True)
    u0, w = sol[..., :dv], sol[..., dv:]
    qk = jnp.einsum('bhnck,bhnjk->bhncj', q, k) * dec_incl
    q_dec = q * jnp.exp(gam)[..., None]
    k_dec = k * jnp.exp(gam[..., -1:] - gam)[..., None]
    g_tot = jnp.exp(gam[..., -1])
    xs = tuple(jnp.moveaxis(t, 2, 0) for t in (u0, w, qk, q_dec, k_dec, g_tot))

    def step(s, inp):
        u0_c, w_c, qk_c, qd_c, kd_c, gt_c = inp
        u = u0_c - jnp.einsum('bhck,bhkv->bhcv', w_c, s)
        o = jnp.einsum('bhck,bhkv->bhcv', qd_c, s) + jnp.einsum('bhcj,bhjv->bhcv', qk_c, u)
        s = gt_c[..., None, None] * s + jnp.einsum('bhck,bhcv->bhkv', kd_c, u)
        return s, o

    s, o = lax.scan(step, s0.astype(F32), xs)
    return jnp.moveaxis(o, 0, 2).reshape(b, h, l, dv), s


def _gla_chunks(q, k, v, lg, s0, chunk):
    b, h, l, dk = q.shape
    dv = v.shape[-1]
    n = l // chunk
    r = lambda t: t.reshape(t.shape[:2] + (n, chunk) + t.shape[3:])
    q, k, v, lg = r(q), r(k), r(v), r(lg)
    cb = jnp.cumsum(lg, axis=-2)
    q_dec = q * jnp.exp(cb)
    k_dec = k * jnp.exp(cb[..., -1:, :] - cb)
    a_tot = jnp.exp(cb[..., -1, :])
    incl = jnp.tril(jnp.ones((chunk, chunk), dtype=bool))[:, :, None]
    xs = tuple(jnp.moveaxis(t, 2, 0) for t in (q, k, v, cb, q_dec, k_dec, a_tot))

    def step(s, inp):
        q_c, k_c, v_c, cb_c, qd_c, kd_c, at_c = inp
        diff = cb_c[:, :, :, None, :] - cb_c[:, :, None, :, :]
        dec = jnp.exp(jnp.where(incl, diff, -jnp.inf))
        attn = jnp.einsum('bhtk,bhjk,bhtjk->bhtj', q_c, k_c, dec)
        o = jnp.einsum('bhck,bhkv->bhcv', qd_c, s) + jnp.einsum('bhtj,bhjv->bhtv', attn, v_c)
        s = at_c[..., None] * s + jnp.einsum('bhck,bhcv->bhkv', kd_c, v_c)
        return s, o

    s, o = lax.scan(step, s0.astype(F32), xs)
    return jnp.moveaxis(o, 0, 2).reshape(b, h, l, dv), s


def _token_mix(h, conv_buf, s_gdn, s_gla, segments, w_in, conv_w, a_log, dt_bias, gdn_norm_w,
               gla_gate_w2, gla_gate_b, gla_norm_w, w_branch_a, w_branch_b, w_out):
    b, l, _ = h.shape
    qkv_a, z_a, beta_a, a_a, q_b, k_b, v_b, r_b, glr_b, gate_a, gate_b = _split_in(h @ w_in)
    qkv_a, conv_buf = _causal_conv(qkv_a, conv_buf, conv_w)
    q_a, k_a, v_a = jnp.split(qkv_a, [A_QK, 2 * A_QK], axis=-1)
    q_a = _l2norm(_heads(q_a, A_HEADS)) * (A_DK ** -0.5)
    k_a = _l2norm(_heads(k_a, A_HEADS))
    v_a = _heads(v_a, A_HEADS)
    beta = jax.nn.sigmoid(beta_a.astype(F32)).transpose(0, 2, 1)
    g = -(jnp.exp(a_log.astype(F32)) * jax.nn.softplus(a_a.astype(F32) + dt_bias.astype(F32)))
    g = g.transpose(0, 2, 1)
    q_b = _heads(q_b, B_HEADS) * (B_DK ** -0.5)
    k_b = _heads(k_b, B_HEADS)
    v_b = _heads(v_b, B_HEADS)
    lg = jax.nn.log_sigmoid((glr_b @ gla_gate_w2 + gla_gate_b).astype(F32)) / B_GATE_TAU
    lg = _heads(lg, B_HEADS)
    o_a, o_b = [], []
    start = 0
    for length, chunk in segments:
        sl = slice(start, start + length)
        oa, s_gdn = _gdn_chunks(q_a[:, :, sl], k_a[:, :, sl], v_a[:, :, sl], g[:, :, sl], beta[:, :, sl], s_gdn, chunk)
        ob, s_gla = _gla_chunks(q_b[:, :, sl], k_b[:, :, sl], lg[:, :, sl], v_b[:, :, sl], s_gla, chunk) if False else _gla_chunks(q_b[:, :, sl], k_b[:, :, sl], v_b[:, :, sl], lg[:, :, sl], s_gla, chunk)
        o_a.append(oa)
        o_b.append(ob)
        start += length
    o_a = jnp.concatenate(o_a, axis=2).transpose(0, 2, 1, 3)
    o_b = jnp.concatenate(o_b, axis=2).transpose(0, 2, 1, 3)
    o_a = _rms_norm(o_a, gdn_norm_w).reshape(b, l, A_V) * jax.nn.silu(z_a.astype(F32))
    o_b = _rms_norm(o_b, gla_norm_w).reshape(b, l, B_V) * jax.nn.silu(r_b.astype(F32))
    y_a = o_a.astype(h.dtype) @ w_branch_a
    y_b = o_b.astype(h.dtype) @ w_branch_b
    merged = jax.nn.sigmoid(gate_a) * y_a + jax.nn.sigmoid(gate_b) * y_b
    return merged @ w_out, conv_buf, s_gdn, s_gla


def _swiglu(x, w_ffn_in, w_ffn_out):
    a, u = jnp.split(x @ w_ffn_in, 2, axis=-1)
    return (jax.nn.silu(a) * u) @ w_ffn_out


def _layer(x, conv_buf, s_gdn, s_gla, segments, w_in, conv_w, a_log, dt_bias, gdn_norm_w, gla_gate_w2,
           gla_gate_b, gla_norm_w, w_branch_a, w_branch_b, w_out, ln1_g, ln1_b, ln2_g, ln2_b, w_ffn_in, w_ffn_out):
    mix, conv_buf, s_gdn, s_gla = _token_mix(x, conv_buf, s_gdn, s_gla, segments, w_in, conv_w, a_log, dt_bias,
                                             gdn_norm_w, gla_gate_w2, gla_gate_b, gla_norm_w,
                                             w_branch_a, w_branch_b, w_out)
    x = _layer_norm(DEEPNORM_ALPHA * x + mix, ln1_g, ln1_b)
    x = _layer_norm(DEEPNORM_ALPHA * x + _swiglu(x, w_ffn_in, w_ffn_out), ln2_g, ln2_b)
    return x, s_gdn.astype(x.dtype), s_gla.astype(x.dtype), conv_buf


def setup_inputs(seed: int = 0) -> dict:
    key = jax.random.key(seed)
    ks = jax.random.split(key, 24)
    nrm = lambda k, shape, s: jax.random.normal(k, shape, F32) * s
    dt = jnp.exp(jax.random.uniform(ks[5], (DEPTH, A_HEADS), F32, math.log(1e-3), math.log(1e-1)))
    return {
        'x_prompt': nrm(ks[0], (BATCH, SEQ, D_MODEL), 1.0),
        'x_sample': nrm(ks[1], (DEC_BATCH, DEC_SEQ, D_MODEL), 1.0),
        'state_gdn': nrm(ks[2], (DEPTH, DEC_BATCH, A_HEADS, A_DK, A_DV), 0.3),
        'state_gla': nrm(ks[3], (DEPTH, DEC_BATCH, B_HEADS, B_DK, B_DV), 0.3),
        'state_conv': nrm(ks[4], (DEPTH, DEC_BATCH, CONV_W - 1, CONV_CH), 1.0),
        'meta_tokens': nrm(ks[6], (N_META, D_MODEL), 1.0),
        'w_in': nrm(ks[7], (DEPTH, D_MODEL, D_IN), D_MODEL ** -0.5),
        'conv_w': nrm(ks[8], (DEPTH, CONV_W, CONV_CH), CONV_W ** -0.5),
        'a_log': jnp.log(jax.random.uniform(ks[9], (DEPTH, A_HEADS), F32, 1.0, 16.0)),
        'dt_bias': dt + jnp.log(-jnp.expm1(-dt)),
        'gdn_norm_w': 1.0 + nrm(ks[10], (DEPTH, A_DV), 0.02),
        'gla_gate_w2': nrm(ks[11], (DEPTH, B_GATE_RANK, B_QK), B_GATE_RANK ** -0.5),
        'gla_gate_b': nrm(ks[12], (DEPTH, B_QK), 0.02),
        'gla_norm_w': 1.0 + nrm(ks[13], (DEPTH, B_DV), 0.02),
        'w_branch_a': nrm(ks[14], (DEPTH, A_V, D_MODEL), A_V ** -0.5 * DEEPNORM_BETA),
        'w_branch_b': nrm(ks[15], (DEPTH, B_V, D_MODEL), B_V ** -0.5 * DEEPNORM_BETA),
        'w_out': nrm(ks[16], (DEPTH, D_MODEL, D_MODEL), D_MODEL ** -0.5 * DEEPNORM_BETA),
        'ln1_g': 1.0 + nrm(ks[17], (DEPTH, D_MODEL), 0.02),
        'ln1_b': nrm(ks[18], (DEPTH, D_MODEL), 0.02),
        'ln2_g': 1.0 + nrm(ks[19], (DEPTH, D_MODEL), 0.02),
        'ln2_b': nrm(ks[20], (DEPTH, D_MODEL), 0.02),
        'w_ffn_in': nrm(ks[21], (DEPTH, D_MODEL, 2 * D_FF), D_MODEL ** -0.5),
        'w_ffn_out': nrm(ks[22], (DEPTH, D_FF, D_MODEL), D_FF ** -0.5 * DEEPNORM_BETA),
    }


def reference(x_prompt, x_sample, state_gdn, state_gla, state_conv, meta_tokens, w_in, conv_w, a_log, dt_bias,
              gdn_norm_w, gla_gate_w2, gla_gate_b, gla_norm_w, w_branch_a, w_branch_b, w_out,
              ln1_g, ln1_b, ln2_g, ln2_b, w_ffn_in, w_ffn_out):
    dtp = x_prompt.dtype
    bp = x_prompt.shape[0]
    seq = x_prompt.shape[1]
    dec_seq = x_sample.shape[1]
    meta = jnp.broadcast_to(meta_tokens.astype(dtp)[None], (bp, N_META, D_MODEL))
    hp = jnp.concatenate([meta, x_prompt], axis=1)
    hs = x_sample
    seg_p = ((N_META, N_META), (seq, math.gcd(seq, CHUNK)))
    seg_s = ((dec_seq, math.gcd(dec_seq, CHUNK)),)
    gdn_p, gla_p, conv_p, gdn_s, gla_s, conv_s = [], [], [], [], [], []
    for l in range(DEPTH):
        p = (w_in[l], conv_w[l], a_log[l], dt_bias[l], gdn_norm_w[l], gla_gate_w2[l], gla_gate_b[l], gla_norm_w[l],
             w_branch_a[l], w_branch_b[l], w_out[l], ln1_g[l], ln1_b[l], ln2_g[l], ln2_b[l], w_ffn_in[l], w_ffn_out[l])
        hp, sg, sl, cb = _layer(hp, jnp.zeros((bp, CONV_W - 1, CONV_CH), dtp),
                                jnp.zeros((bp, A_HEADS, A_DK, A_DV), F32),
                                jnp.zeros((bp, B_HEADS, B_DK, B_DV), F32), seg_p, *p)
        gdn_p.append(sg)
        gla_p.append(sl)
        conv_p.append(cb)
        hs, sg, sl, cb = _layer(hs, state_conv[l], state_gdn[l], state_gla[l], seg_s, *p)
        gdn_s.append(sg)
        gla_s.append(sl)
        conv_s.append(cb)
    y_prompt = hp[:, N_META:]
    return (y_prompt, hs, jnp.stack(gdn_p), jnp.stack(gla_p), jnp.stack(conv_p),
            jnp.stack(gdn_s), jnp.stack(gla_s), jnp.stack(conv_s))
```

```python
import numpy as np
from contextlib import ExitStack
import concourse.bass as bass
import concourse.mybir as mybir
from concourse.bass_utils import run_bass_kernel_spmd

F32 = mybir.dt.float32
BF16 = mybir.dt.bfloat16
AF = mybir.ActivationFunctionType
ALU = mybir.AluOpType

D = 1024
KC = 8
DEPTH = 2
NS = 16
NMETA = 16
D_IN = 5656
DFF = 2816
FC = DFF // 128
ALPHA = (2.0 * DEPTH) ** 0.25
NCORES = 8


class Buf:
    __slots__ = ("w", "r", "name", "excl")

    def __init__(self, name="", excl=False):
        self.w = None
        self.r = {}
        self.name = name
        self.excl = excl


class Sched:
    ENG = ("pe", "act", "dve", "pool", "sp")

    def __init__(self, nc, es, n_dma_slots=24):
        self.nc = nc
        self.e = {"pe": nc.tensor, "act": nc.scalar, "dve": nc.vector, "pool": nc.gpsimd, "sp": nc.sync}
        self.sem = {}
        self.cnt = {}
        for k in self.ENG:
            self.sem[k] = es.enter_context(nc.semaphore("s_" + k))
            self.cnt[k] = 0
        self.seen = {k: {} for k in self.ENG}
        self.pending = {k: False for k in self.ENG}
        self.nslots = n_dma_slots
        self.slot_sem = [es.enter_context(nc.semaphore("s_dma%d" % i)) for i in range(n_dma_slots)]
        self.slot_uses = [0] * n_dma_slots
        self.slot_next = 0
        self.semobj = dict(self.sem)
        for i in range(n_dma_slots):
            self.semobj[("dma", i)] = self.slot_sem[i]

    def _collect(self, eng, R, W, extra=()):
        need = {}
        def add(k, v):
            if v > need.get(k, 0):
                need[k] = v
        for b in R:
            if b.w is not None:
                add(*b.w)
        for b in W:
            if b.w is not None:
                add(*b.w)
            for k, v in b.r.items():
                add(k, v)
        for k, v in extra:
            add(k, v)
        out = []
        for k, v in need.items():
            if k == "pe" and eng == "pe":
                continue
            if k == eng and v > self.cnt[eng]:
                continue
            if v > self.seen[eng].get(k, 0):
                out.append((k, v))
        return out

    def op(self, eng, fn, R=(), W=(), inc=True, extra=()):
        if any(b.excl for b in R):
            W = list(W) + [b for b in R if b.excl]
            R = [b for b in R if not b.excl]
        waits = self._collect(eng, R, W, extra)
        e = self.e[eng]
        if eng == "pe":
            for k, v in waits:
                e.wait_ge(self.semobj[k], v)
                self.seen[eng][k] = v
            waits = []
        for k, v in waits[:-1]:
            e.wait_ge(self.semobj[k], v)
            self.seen[eng][k] = v
        inst = fn()
        if waits:
            k, v = waits[-1]
            inst.wait_op(self.semobj[k], v, "sem-ge")
            self.seen[eng][k] = v
        if inc:
            inst.then_inc(self.sem[eng], 1)
            self.cnt[eng] += 1
            c = self.cnt[eng]
            self.pending[eng] = False
        else:
            c = self.cnt[eng] + 1
            self.pending[eng] = True
        for b in R:
            if b.r.get(eng, 0) < c:
                b.r[eng] = c
        for b in W:
            b.w = (eng, c)
            b.r = {}
        return inst

    def dma(self, eng, out, in_, R=(), W=(), **kw):
        s = self.slot_next
        self.slot_next = (self.slot_next + 1) % self.nslots
        key = ("dma", s)
        prev = 16 * self.slot_uses[s]
        extra = [(key, prev)] if prev > 0 else []
        waits = self._collect(eng, R, W, extra)
        e = self.e[eng]
        for k, v in waits:
            e.wait_ge(self.semobj[k], v)
            self.seen[eng][k] = v
        inst = e.dma_start(out=out, in_=in_, **kw)
        self.slot_uses[s] += 1
        val = 16 * self.slot_uses[s]
        inst.then_inc(self.slot_sem[s], 16)
        for b in R:
            b.r[key] = val
        for b in W:
            b.w = (key, val)
            b.r = {}
        return inst

    def barrier(self):
        tgt = [(k, self.cnt[k]) for k in self.ENG if self.cnt[k] > 0]
        tgt += [(("dma", i), 16 * self.slot_uses[i]) for i in range(self.nslots) if self.slot_uses[i] > 0]
        for eng in self.ENG:
            assert not self.pending[eng]
            for k, v in tgt:
                if k == eng:
                    continue
                if v > self.seen[eng].get(k, 0):
                    self.e[eng].wait_ge(self.semobj[k], v)
                    self.seen[eng][k] = v

    def finish(self):
        self.barrier()


def make_consts():
    r = np.arange(128)
    same = (r[:, None] // 64) == (r[None, :] // 64)
    c = {}
    c["ident"] = np.eye(128, dtype=np.float32)
    c["ones"] = np.ones((128, 128), dtype=np.float32)
    c["mu_incl"] = (same & (r[None, :] >= r[:, None])).astype(np.float32)
    c["mu_strict"] = (same & (r[None, :] > r[:, None])).astype(np.float32)
    c["ml_strict"] = (same & (r[:, None] > r[None, :])).astype(np.float32)
    c["bd"] = same.astype(np.float32)
    c["blk0"] = np.repeat((r < 64).astype(np.float32)[:, None], 128, axis=1)
    c["blk1"] = np.repeat((r >= 64).astype(np.float32)[:, None], 128, axis=1)
    c["zero"] = np.zeros((128, 128), dtype=np.float32)
    idrep = np.tile(np.eye(16, dtype=np.float32).reshape(1, 256), (128, 1))
    c["idrep0"] = idrep[:, 0:128]
    c["idrep1"] = idrep[:, 128:256]
    names = ["ident", "ones", "mu_incl", "mu_strict", "ml_strict", "bd", "blk0", "blk1", "zero", "idrep0", "idrep1"]
    arr = np.stack([c[n] for n in names], axis=1)
    return names, np.ascontiguousarray(arr.astype(np.float32))


CONST_NAMES, CONST_ARR = make_consts()
NCONST = len(CONST_NAMES)
IDX = {n: i for i, n in enumerate(CONST_NAMES)}


class _Stop(Exception):
    pass


def build(TP=2048, stub_mixer=False, layers=DEPTH, dbg=False, stop=None, mix_sel="all"):
    nc = bass.Bass("TRN2", target_bir_lowering=False)
    NPOS = NMETA + TP
    T = 32 + TP
    NPT = max(TP // 512, 1)
    TILES = [(0, 32)] + [(32 + 512 * i, 512) for i in range(NPT)]

    def din(name, shape, dt=F32):
        return nc.dram_tensor(name, list(shape), dt, kind="ExternalInput").ap()

    def dout(name, shape, dt=F32):
        return nc.dram_tensor(name, list(shape), dt, kind="ExternalOutput").ap()

    x_prompt = din("x_prompt", [TP, D])
    x_sample = din("x_sample", [NS, D])
    meta = din("meta_tokens", [NMETA, D])
    state_gdn = din("state_gdn", [DEPTH, NS, 4, 128, 128])
    state_gla = din("state_gla", [DEPTH, NS, 4, 64, 128])
    state_conv = din("state_conv", [DEPTH, NS, 3, 1536])
    w_in = din("w_in", [DEPTH, D, D_IN])
    conv_w = din("conv_w", [DEPTH, 4, 1536])
    a_log = din("a_log", [DEPTH, 4])
    dt_bias = din("dt_bias", [DEPTH, 4])
    gdn_norm_w = din("gdn_norm_w", [DEPTH, 128])
    gla_gate_w2 = din("gla_gate_w2", [DEPTH, 16, 256])
    gla_gate_b = din("gla_gate_b", [DEPTH, 256])
    gla_norm_w = din("gla_norm_w", [DEPTH, 128])
    w_branch_a = din("w_branch_a", [DEPTH, 512, D])
    w_branch_b = din("w_branch_b", [DEPTH, 512, D])
    w_out = din("w_out", [DEPTH, D, D])
    ln1_g = din("ln1_g", [DEPTH, D])
    ln1_b = din("ln1_b", [DEPTH, D])
    ln2_g = din("ln2_g", [DEPTH, D])
    ln2_b = din("ln2_b", [DEPTH, D])
    w_ffn_in = din("w_ffn_in", [DEPTH, D, 2 * DFF])
    w_ffn_out = din("w_ffn_out", [DEPTH, DFF, D])
    consts_d = din("consts", [128, NCONST, 128])

    y_prompt = dout("y_prompt", [TP, D])
    y_sample = dout("y_sample", [NS, D])
    o_gdn_p = dout("new_gdn_prompt", [DEPTH, 4, 128, 128])
    o_gla_p = dout("new_gla_prompt", [DEPTH, 4, 64, 128])
    o_conv_p = dout("new_conv_prompt", [DEPTH, 3, 1536])
    o_gdn_s = dout("new_gdn_sample", [DEPTH, NS, 4, 128, 128])
    o_gla_s = dout("new_gla_sample", [DEPTH, NS, 4, 64, 128])
    o_conv_s = dout("new_conv_sample", [DEPTH, NS, 3, 1536])
    dbg_out = dout("dbg", [128, KC, T]) if dbg else None

    xres = nc.dram_tensor("xres_scratch", [128, KC, T], F32, kind="Internal").ap()

    es = ExitStack()
    with es:
      K = Sched(nc, es)

      def chk(name):
          if stop == name:
              raise _Stop()

      try:

        _uid = [0]

        def sbt(stack, name, shape, dt=F32):
            _uid[0] += 1
            return stack.enter_context(nc.sbuf_tensor("%s_%d" % (name, _uid[0]), list(shape), dt))

        cst = sbt(es, "cst", [128, NCONST, 128], F32)
        cstb = sbt(es, "cstb", [128, NCONST, 128], BF16)
        b_cst = Buf("cst")
        C = {n: cst[:, i, :] for i, n in enumerate(CONST_NAMES)}
        CB = {n: cstb[:, i, :] for i, n in enumerate(CONST_NAMES)}
        x_bf = sbt(es, "x_bf", [128, KC, T], BF16)
        b_xbf = Buf("x_bf")
        lnp = sbt(es, "lnp", [128, DEPTH, 4, KC], F32)
        b_lnp = Buf("lnp")
        psum = [es.enter_context(nc.psum_tensor("ps%d" % i, [128, 512], F32)) for i in range(8)]
        b_ps = [Buf("ps%d" % i, excl=True) for i in range(8)]
        b_xres = Buf("xres")

        epst = sbt(es, "epst", [128, 2], F32)
        b_eps = Buf("eps")
        K.op("dve", lambda: nc.vector.memset(epst[:, 0:1], 1e-6), W=[b_eps])
        K.op("dve", lambda: nc.vector.memset(epst[:, 1:2], 1.0), W=[b_eps])
        eps6 = epst[:, 0:1]
        one1 = epst[:, 1:2]
        nh_ = (len(TILES) + 1) // 2
        HW = max(sum(n for _, n in TILES[:nh_]), sum(n for _, n in TILES[nh_:]))
        BIGN = max(FC * HW, 8 * T)
        big = sbt(es, "big", [128, BIGN], BF16)
        K.dma("sp", cst[:], consts_d, W=[b_cst])
        K.op("act", lambda: nc.scalar.copy(out=cstb[:], in_=cst[:]), R=[b_cst], W=[b_cst])
        for l in range(DEPTH):
            for wi, src in enumerate((ln1_g, ln1_b, ln2_g, ln2_b)):
                K.dma("sp", lnp[:, l, wi, :], src[l].rearrange("(k p) -> p k", p=128), W=[b_lnp],
                      allow_slow_non_contiguous=True)

        chk("init")

        def run_threads(gens):
            active = list(gens)
            while active:
                for g in list(active):
                    try:
                        next(g)
                    except StopIteration:
                        active.remove(g)

        def run_pipeline(items, mkA, mkB, nA=2, nbuf=3):
            n_it = len(items)
            nextA, nextB = 0, 0
            activeA = {}
            doneA = set()
            curB = None
            while nextB < n_it:
                while nextA < n_it and len(activeA) < nA and (nextA - nextB) < nbuf:
                    activeA[nextA] = mkA(items[nextA], nextA)
                    nextA += 1
                if curB is None and nextB in doneA:
                    curB = mkB(items[nextB], nextB)
                for i, gen in list(activeA.items()):
                    try:
                        next(gen)
                    except StopIteration:
                        del activeA[i]
                        doneA.add(i)
                if curB is not None:
                    try:
                        next(curB)
                    except StopIteration:
                        curB = None
                        nextB += 1

        def mm_group(ps_ap, pairs, R, W, inc=True):
            n = len(pairs)
            for i, (lt, rh) in enumerate(pairs):
                last = (i == n - 1)
                K.op("pe", lambda lt=lt, rh=rh, i=i, last=last: nc.tensor.matmul(
                    ps_ap, lhsT=lt, rhs=rh, start=(i == 0), stop=last),
                    R=R, W=W, inc=(inc and last))

        with ExitStack() as p0:
            xin = [sbt(p0, "xin%d" % i, [128, D]) for i in range(2)]
            b_xin = [Buf("xin0"), Buf("xin1")]
            xst = [sbt(p0, "xst%d" % i, [128, KC, 128]) for i in range(2)]
            b_xst = [Buf("xst0"), Buf("xst1")]
            rows = [("ms", 0, 32)] + [("p", 128 * i, 128) for i in range(TP // 128)]
            import os as _os
            if _os.environ.get("SKIPMS"):
                rows = rows[1:]
            if _os.environ.get("ONLYMS"):
                rows = rows[:1]
            for ri, (kind, r0, n) in enumerate(rows):
                s = ri % 2
                if kind == "ms":
                    K.dma("sp", xin[s][0:16, :], x_sample, W=[b_xin[s]])
                    K.dma("sp", xin[s][16:32, :], meta, W=[b_xin[s]])
                    c0 = 0
                else:
                    K.dma("sp", xin[s][0:128, :], x_prompt[r0:r0 + 128, :], W=[b_xin[s]])
                    c0 = 32 + r0
                for g in range(2):
                    pb = (2 * ri + g) % 8
                    for kk in range(4):
                        k = 4 * g + kk
                        K.op("pe", lambda k=k, kk=kk, pb=pb, n=n, s=s: nc.tensor.transpose(
                            psum[pb][:, kk * 128:kk * 128 + n], xin[s][0:n, k * 128:(k + 1) * 128],
                            C["ident"][0:n, 0:n]), R=[b_xin[s], b_cst], W=[b_ps[pb]], inc=(kk == 3))
                    src = psum[pb][:, :].rearrange("p (a b) -> p a b", a=4)[:, :, 0:n]
                    K.op("act", lambda src=src, g=g, c0=c0, n=n: nc.scalar.copy(
                        out=x_bf[:, 4 * g:4 * g + 4, c0:c0 + n], in_=src), R=[b_ps[pb]], W=[b_xbf])
                    K.op("dve", lambda src=src, g=g, n=n, s=s: nc.vector.tensor_copy(
                        out=xst[s][:, 4 * g:4 * g + 4, 0:n], in_=src), R=[b_ps[pb]], W=[b_xst[s]])
                K.dma("sp", xres[:, :, c0:c0 + n], xst[s][:, :, 0:n], R=[b_xst[s]], W=[b_xres])
        K.barrier()
        chk("p0")

        def layer_norm(stack, v, b_v, l, which, write_bf=True):
            g_i, b_i = 2 * which, 2 * which + 1
            with ExitStack() as st:
                vb = sbt(st, "ln_vb", [128, KC, 512], BF16)
                sq = sbt(st, "ln_sq", [128, KC, 512], BF16)
                mt = sbt(st, "ln_m", [128, 512])
                m2 = sbt(st, "ln_m2", [128, 512])
                rs = sbt(st, "ln_rs", [128, 512])
                nm = sbt(st, "ln_nm", [128, 512])
                b_vb, b_sq, b_mt, b_m2, b_rs, b_nm = (Buf() for _ in range(6))
                for ti, (c0, n) in enumerate(TILES):
                    cols = slice(c0, c0 + n)
                    pm_, pq_ = (2 * ti) % 8, (2 * ti + 1) % 8
                    K.op("act", lambda cols=cols, n=n: nc.scalar.copy(out=vb[:, :, 0:n], in_=v[:, :, cols]),
                         R=[b_v], W=[b_vb])
                    K.op("act", lambda cols=cols, n=n: nc.scalar.activation(
                        out=sq[:, :, 0:n], in_=v[:, :, cols], func=AF.Square), R=[b_v], W=[b_sq])
                    mm_group(psum[pm_][:, 0:n], [(CB["ones"], vb[:, k, 0:n]) for k in range(KC)],
                             R=[b_vb, b_cst], W=[b_ps[pm_]])
                    mm_group(psum[pq_][:, 0:n], [(CB["ones"], sq[:, k, 0:n]) for k in range(KC)],
                             R=[b_sq, b_cst], W=[b_ps[pq_]])
                    K.op("dve", lambda n=n, pm_=pm_: nc.vector.tensor_scalar(
                        out=mt[:, 0:n], in0=psum[pm_][:, 0:n], scalar1=1.0 / D, scalar2=None, op0=ALU.mult),
                        R=[b_ps[pm_]], W=[b_mt])
                    K.op("dve", lambda n=n: nc.vector.tensor_tensor(
                        out=m2[:, 0:n], in0=mt[:, 0:n], in1=mt[:, 0:n], op=ALU.mult), R=[b_mt], W=[b_m2])
                    K.op("dve", lambda n=n, pq_=pq_: nc.vector.scalar_tensor_tensor(
                        out=m2[:, 0:n], in0=psum[pq_][:, 0:n], scalar=1.0 / D, in1=m2[:, 0:n],
                        op0=ALU.mult, op1=ALU.subtract), R=[b_ps[pq_], b_m2], W=[b_m2])
                    K.op("dve", lambda n=n: nc.vector.tensor_scalar(
                        out=m2[:, 0:n], in0=m2[:, 0:n], scalar1=1e-5, scalar2=None, op0=ALU.add),
                        R=[b_m2], W=[b_m2])
                    K.op("act", lambda n=n: nc.scalar.activation(
                        out=rs[:, 0:n], in_=m2[:, 0:n], func=AF.Ln), R=[b_m2], W=[b_rs])
                    K.op("act", lambda n=n: nc.scalar.activation(
                        out=rs[:, 0:n], in_=rs[:, 0:n], func=AF.Exp, scale=-0.5), R=[b_rs], W=[b_rs])
                    K.op("dve", lambda n=n: nc.vector.scalar_tensor_tensor(
                        out=nm[:, 0:n], in0=mt[:, 0:n], scalar=-1.0, in1=rs[:, 0:n],
                        op0=ALU.mult, op1=ALU.mult), R=[b_mt, b_rs], W=[b_nm])
                    K.op("dve", lambda n=n, cols=cols: nc.vector.tensor_tensor(
                        out=v[:, :, cols], in0=v[:, :, cols],
                        in1=rs[:, 0:n].unsqueeze(1).broadcast_to([128, KC, n]), op=ALU.mult),
                        R=[b_rs], W=[b_v])
                    K.op("dve", lambda n=n, cols=cols: nc.vector.tensor_tensor(
                        out=v[:, :, cols], in0=v[:, :, cols],
                        in1=nm[:, 0:n].unsqueeze(1).broadcast_to([128, KC, n]), op=ALU.add),
                        R=[b_nm], W=[b_v])
                    for k in range(KC):
                        K.op("act", lambda k=k, n=n, cols=cols: nc.scalar.activation(
                            out=v[:, k, cols], in_=v[:, k, cols], func=AF.Identity,
                            scale=lnp[:, l, g_i, k:k + 1], bias=lnp[:, l, b_i, k:k + 1]),
                            R=[b_lnp], W=[b_v])
                    if write_bf:
                        K.op("pool", lambda cols=cols: nc.gpsimd.tensor_copy(out=x_bf[:, :, cols], in_=v[:, :, cols]),
                             R=[b_v], W=[b_xbf])

        def subtiles(c0, n):
            if c0 == 0:
                return [("sample", 0, 16), ("meta", 16, 16)]
            return [("prompt", c0 + 128 * i, 128) for i in range(n // 128)]

        def gated_norm(st, oB, b_oB, gz, b_gz, nw, b_nw, og, b_og, hbase, c0, n, tagbufs):
            sq, b_sq, rsd, b_rsd, tn, b_tn = tagbufs
            K.op("act", lambda: nc.scalar.activation(out=sq[:, :, 0:n], in_=oB[:, :, 0:n], func=AF.Square),
                 R=[b_oB], W=[b_sq])
            for h in range(4):
                q = h % 2
                mm_group(psum[q][:, 0:n], [(CB["ones"], sq[:, h, 0:n])], R=[b_sq, b_cst], W=[b_ps[q]])
                K.op("act", lambda q=q: nc.scalar.activation(out=rsd[:, 0:n], in_=psum[q][:, 0:n], func=AF.Ln,
                                                             scale=1.0 / 128, bias=eps6[:, 0:1]),
                     R=[b_ps[q], b_eps], W=[b_rsd])
                K.op("act", lambda: nc.scalar.activation(out=rsd[:, 0:n], in_=rsd[:, 0:n], func=AF.Exp, scale=-0.5),
                     R=[b_rsd], W=[b_rsd])
                K.op("dve", lambda h=h: nc.vector.tensor_tensor(out=tn[:, 0:n], in0=oB[:, h, 0:n], in1=rsd[:, 0:n],
                                                                op=ALU.mult), R=[b_oB, b_rsd], W=[b_tn])
                K.op("dve", lambda h=h: nc.vector.scalar_tensor_tensor(
                    out=og[:, hbase + h, c0:c0 + n], in0=tn[:, 0:n], scalar=nw[:, 0:1], in1=gz[:, h, 0:n],
                    op0=ALU.mult, op1=ALU.mult), R=[b_tn, b_nw, b_gz], W=[b_og])

        def gla_phase(l, og, b_og, win_v):
            with ExitStack() as st:
                wB = sbt(st, "wB", [128, KC, 1552], BF16)
                b_wB = Buf("wB")
                for k in range(KC):
                    K.dma("pool", wB[:, k, :], win_v[:, k, 2056:3608], W=[b_wB])
                w2e = sbt(st, "w2e", [17, 256])
                b_w2e = Buf()
                K.dma("sp", w2e[0:16, :], gla_gate_w2[l], W=[b_w2e])
                K.dma("sp", w2e[16:17, :], gla_gate_b[l].rearrange("(o n) -> o n", o=1), W=[b_w2e])
                nw = sbt(st, "gla_nw", [128, 1])
                b_nw = Buf()
                K.dma("sp", nw[:, :], gla_norm_w[l].rearrange("(p o) -> p o", o=1), W=[b_nw],
                      allow_slow_non_contiguous=True)
                qk = sbt(st, "g_qk", [128, 4, 512])
                b_qk = Buf()
                sr = sbt(st, "g_sr", [128, 4, 512], BF16)
                b_sr = Buf()
                glr = sbt(st, "g_glr", [17, 512])
                b_glr = Buf()
                oB = sbt(st, "g_oB", [128, 4, 512])
                b_oB = Buf()
                sq = sbt(st, "g_sq", [128, 4, 512], BF16)
                rsd = sbt(st, "g_rsd", [128, 512])
                tn = sbt(st, "g_tn", [128, 512])
                nbufs = (sq, Buf(), rsd, Buf(), tn, Buf())
                Lt = [sbt(st, "g_L%d" % i, [128, 256]) for i in range(3)]
                E12 = [sbt(st, "g_E%d" % i, [128, 2, 2, 128]) for i in range(3)]
                atot = [sbt(st, "g_at%d" % i, [128, 2, 2]) for i in range(3)]
                qdk = [sbt(st, "g_qdk%d" % i, [128, 2, 2, 128], BF16) for i in range(3)]
                qdm = [sbt(st, "g_qdm%d" % i, [128, 4, 128], BF16) for i in range(3)]
                b_qdm = [Buf(), Buf(), Buf()]
                qm = sbt(st, "g_qm", [128, 4, 18])
                b_qm = Buf()
                E3 = [sbt(st, "g_E3%d" % i, [128, 256]) for i in range(3)]
                kdec = [sbt(st, "g_kd%d" % i, [128, 256], BF16) for i in range(3)]
                vtb = [sbt(st, "g_vt%d" % i, [128, 512], BF16) for i in range(3)]
                att = [sbt(st, "g_att%d" % i, [128, 4, 128], BF16) for i in range(3)]
                b_L, b_E12, b_at, b_qdk, b_E3, b_kd, b_vt, b_att = ([Buf(), Buf(), Buf()] for _ in range(8))
                S = sbt(st, "g_S", [128, 2, 128])
                Sb = sbt(st, "g_Sb", [128, 2, 128], BF16)
                b_S, b_Sb = Buf(), Buf()
                K.op("dve", lambda: nc.vector.memset(S[:], 0.0), W=[b_S])
                K.op("dve", lambda: nc.vector.memset(Sb[:], 0.0), W=[b_Sb])
                K.op("dve", lambda: nc.vector.memset(glr[:], 1.0), W=[b_glr])
                Ss = sbt(st, "g_Ss", [128, NS, 2, 128])
                b_Ss = Buf()
                sgl_v = state_gla[l].rearrange("s (pr hf) k v -> (hf k) s pr v", hf=2)
                for s4 in range(4):
                    K.dma("sp", Ss[:, 4 * s4:4 * s4 + 4], sgl_v[:, 4 * s4:4 * s4 + 4], W=[b_Ss])
                vbd = sbt(st, "g_vbd", [16, NS, 512], BF16)
                b_vbd = Buf()
                sub_ctr = [0]
                def glaA(kind, sc0, nt, u, bx, by, c0):
                    lc = sc0 - c0
                    if kind == "sample":
                        m_incl, m_strict_l = C["ident"], C["zero"]
                        mb_incl = CB["ident"]
                    else:
                        m_incl, m_strict_l = C["mu_incl"], C["ml_strict"]
                        mb_incl = CB["mu_incl"]
                    yield
                    mm_group(psum[bx][0:nt, 0:256], [(x_bf[:, k, sc0:sc0 + nt], wB[:, k, 256:512]) for k in range(KC)],
                             R=[b_wB, b_xbf], W=[b_ps[bx]])
                    yield
                    mm_group(psum[by][0:nt, 0:512], [(x_bf[:, k, sc0:sc0 + nt], wB[:, k, 512:1024]) for k in range(KC)],
                             R=[b_wB, b_xbf], W=[b_ps[by]])
                    yield
                    mm_group(psum[bx][0:nt, 256:512], [(glr[0:17, lc:lc + nt], w2e[0:17, :])],
                             R=[b_glr, b_w2e], W=[b_ps[bx]])
                    yield
                    K.op("act", lambda u=u: nc.scalar.activation(out=Lt[u][0:nt, :], in_=psum[bx][0:nt, 256:512],
                                                                 func=AF.Exp, scale=-1.0), R=[b_ps[bx]], W=[b_L[u]])
                    yield
                    K.op("act", lambda u=u: nc.scalar.activation(out=Lt[u][0:nt, :], in_=Lt[u][0:nt, :], func=AF.Ln,
                                                                 bias=one1[0:nt, 0:1]), R=[b_L[u], b_eps], W=[b_L[u]])
                    yield
                    K.op("act", lambda u=u: nc.scalar.copy(out=vtb[u][0:nt, :], in_=psum[by][0:nt, :]),
                         R=[b_ps[by]], W=[b_vt[u]])
                    yield
                    for pr_ in range(2):
                        mm_group(psum[by][:, pr_ * 128:pr_ * 128 + nt],
                                 [(Lt[u][0:nt, pr_ * 128:(pr_ + 1) * 128], m_incl[0:nt, 0:nt])],
                                 R=[b_L[u], b_cst], W=[b_ps[by]])
                    yield
                    mm_group(psum[by][0:nt, 256:512], [(m_strict_l[0:nt, 0:nt], Lt[u][0:nt, :])],
                             R=[b_L[u], b_cst], W=[b_ps[by]])
                    csT = psum[by][:, 0:256].rearrange("p (a b) -> p a b", a=2)[:, :, 0:nt]
                    yield
                    K.op("act", lambda u=u, csT=csT: nc.scalar.activation(out=E12[u][:, 0, :, 0:nt], in_=csT,
                                                                          func=AF.Exp, scale=-1.0 / 16),
                         R=[b_ps[by]], W=[b_E12[u]])
                    yield
                    K.op("act", lambda u=u, csT=csT: nc.scalar.activation(out=E12[u][:, 1, :, 0:nt], in_=csT,
                                                                          func=AF.Exp, scale=1.0 / 16),
                         R=[b_ps[by]], W=[b_E12[u]])
                    nch = 2 if nt == 128 else 1
                    cl = 64 if nt == 128 else nt
                    yield
                    for ch in range(nch):
                        lastc = ch * 64 + cl - 1
                        K.op("act", lambda u=u, ch=ch, lastc=lastc: nc.scalar.activation(
                            out=atot[u][:, :, ch:ch + 1],
                            in_=psum[by][:, 0:256].rearrange("p (a b) -> p a b", a=2)[:, :, lastc:lastc + 1],
                            func=AF.Exp, scale=-1.0 / 16), R=[b_ps[by]], W=[b_at[u]])
                    yield
                    K.op("act", lambda u=u: nc.scalar.activation(out=E3[u][0:nt, :], in_=psum[by][0:nt, 256:512],
                                                                 func=AF.Exp, scale=-1.0 / 16),
                         R=[b_ps[by]], W=[b_E3[u]])
                    yield
                    K.op("dve", lambda u=u, lc=lc: nc.vector.scalar_tensor_tensor(
                        out=qdk[u][:, 0, :, 0:nt], in0=qk[:, 0:2, lc:lc + nt], scalar=0.125, in1=E12[u][:, 0, :, 0:nt],
                        op0=ALU.mult, op1=ALU.mult), R=[b_qk, b_E12[u]], W=[b_qdk[u]])
                    yield
                    K.op("dve", lambda u=u, lc=lc: nc.vector.tensor_tensor(
                        out=qdk[u][:, 1, :, 0:nt], in0=qk[:, 2:4, lc:lc + nt], in1=E12[u][:, 1, :, 0:nt], op=ALU.mult),
                        R=[b_qk, b_E12[u]], W=[b_qdk[u]])
                    yield
                    K.op("dve", lambda u=u: nc.vector.tensor_tensor(
                        out=kdec[u][0:nt, :], in0=psum[bx][0:nt, 0:256], in1=E3[u][0:nt, :], op=ALU.mult),
                        R=[b_ps[bx], b_E3[u]], W=[b_kd[u]])
                    yield
                    if kind == "sample":
                        for h in range(4):
                            K.op("dve", lambda h=h: nc.vector.tensor_scalar(
                                out=qm[:, h, :], in0=qk[:, h // 2, 0:18], scalar1=C["blk%d" % (h % 2)][:, 0:1],
                                scalar2=None, op0=ALU.mult), R=[b_qk, b_cst], W=[b_qm])
                        gla_sample(l, u, qm, b_qm, E12, b_E12, kdec, b_kd, vtb, b_vt, Ss, b_Ss, vbd, b_vbd, oB, b_oB)
                        return
                    yield
                    for h in range(4):
                        K.op("dve", lambda h=h, u=u: nc.vector.tensor_scalar(
                            out=qdm[u][:, h, 0:nt], in0=qdk[u][:, 0, h // 2, 0:nt], scalar1=C["blk%d" % (h % 2)][:, 0:1],
                            scalar2=None, op0=ALU.mult), R=[b_qdk[u], b_cst], W=[b_qdm[u]])
                    yield
                    for h in range(4):
                        pr_, hf = h // 2, h % 2
                        rows = slice(hf * 64, hf * 64 + 64)
                        mm_group(psum[bx][0:nt, h * 128:h * 128 + nt],
                                 [(qdk[u][:, 1, pr_, 0:nt], qdm[u][:, h, 0:nt])],
                                 R=[b_qdk[u], b_qdm[u]], W=[b_ps[bx]], inc=(h == 3))
                    yield
                    K.op("dve", lambda u=u: nc.vector.tensor_tensor(
                        out=att[u][0:nt, :, 0:nt],
                        in0=psum[bx][0:nt, :].rearrange("p (a b) -> p a b", a=4)[:, :, 0:nt],
                        in1=m_incl[0:nt, 0:nt].unsqueeze(1).broadcast_to([nt, 4, nt]), op=ALU.mult),
                        R=[b_ps[bx], b_cst], W=[b_att[u]])

                def glaB(kind, sc0, nt, u, c0):
                    lc = sc0 - c0
                    nch = 2 if nt == 128 else 1
                    cl = 64 if nt == 128 else nt
                    for ch in range(nch):
                        trow = slice(ch * 64, ch * 64 + cl)
                        yield
                        for h in range(4):
                            pr_, hf = h // 2, h % 2
                            rows = slice(hf * 64, hf * 64 + 64)
                            K.op("pe", lambda h=h, pr_=pr_, trow=trow, u=u: nc.tensor.matmul(
                                psum[6][:, h * 64:h * 64 + cl], lhsT=Sb[:, pr_, :], rhs=qdm[u][:, h, trow],
                                start=True, stop=False), R=[b_Sb, b_qdm[u]], W=[b_ps[6]], inc=False)
                            K.op("pe", lambda h=h, trow=trow, u=u: nc.tensor.matmul(
                                psum[6][:, h * 64:h * 64 + cl], lhsT=vtb[u][0:nt, h * 128:(h + 1) * 128],
                                rhs=att[u][0:nt, h, trow], start=False, stop=True),
                                R=[b_vt[u], b_att[u]], W=[b_ps[6]], inc=(h == 3))
                        yield
                        for h in range(4):
                            pr_, hf = h // 2, h % 2
                            K.op("pe", lambda h=h, pr_=pr_, hf=hf, trow=trow, u=u: nc.tensor.matmul(
                                psum[7][hf * 64:hf * 64 + 64, pr_ * 128:(pr_ + 1) * 128],
                                lhsT=kdec[u][trow, h * 64:(h + 1) * 64], rhs=vtb[u][trow, h * 128:(h + 1) * 128],
                                start=True, stop=True), R=[b_kd[u], b_vt[u]], W=[b_ps[7]], inc=(h == 3))
                        yield
                        K.op("act", lambda lc=lc, trow=trow, ch=ch: nc.scalar.copy(
                            out=oB[:, :, lc + ch * 64:lc + ch * 64 + cl],
                            in_=psum[6][:, 0:256].rearrange("p (a b) -> p a b", a=4)[:, :, 0:cl]),
                            R=[b_ps[6]], W=[b_oB])
                        yield
                        for pr_ in range(2):
                            K.op("dve", lambda pr_=pr_, ch=ch, u=u: nc.vector.scalar_tensor_tensor(
                                out=S[:, pr_, :], in0=S[:, pr_, :], scalar=atot[u][:, pr_, ch:ch + 1],
                                in1=psum[7][:, pr_ * 128:(pr_ + 1) * 128], op0=ALU.mult, op1=ALU.add),
                                R=[b_at[u], b_ps[7]], W=[b_S])
                        yield
                        K.op("act", lambda: nc.scalar.copy(out=Sb[:], in_=S[:]), R=[b_S], W=[b_Sb])

                    yield

                for (c0, n) in TILES:
                    for bi in range(9):
                        q = bi % 2
                        if bi < 8:
                            woff = bi * 128 if bi < 4 else 1024 + (bi - 4) * 128
                            mw = 128
                        else:
                            woff, mw = 1536, 16
                        mm_group(psum[q][0:mw, 0:n], [(wB[:, k, woff:woff + mw], x_bf[:, k, c0:c0 + n]) for k in range(KC)],
                                 R=[b_wB, b_xbf], W=[b_ps[q]])
                        if bi < 4:
                            K.op("dve", lambda bi=bi, q=q: nc.vector.tensor_copy(out=qk[:, bi, 0:n], in_=psum[q][:, 0:n]),
                                 R=[b_ps[q]], W=[b_qk])
                        elif bi < 8:
                            K.op("act", lambda bi=bi, q=q: nc.scalar.activation(
                                out=sr[:, bi - 4, 0:n], in_=psum[q][:, 0:n], func=AF.Silu), R=[b_ps[q]], W=[b_sr])
                        else:
                            K.op("dve", lambda q=q: nc.vector.tensor_copy(out=glr[0:16, 0:n], in_=psum[q][0:16, 0:n]),
                                 R=[b_ps[q]], W=[b_glr])
                    subs = subtiles(c0, n)
                    def mkA(it, i, c0=c0):
                        gi = sub_ctr[0] + i
                        return glaA(it[0], it[1], it[2], gi % 3, 2 + 2 * (gi % 2), 3 + 2 * (gi % 2), c0)
                    def mkB(it, i, c0=c0):
                        if it[0] == "sample":
                            return iter(())
                        return glaB(it[0], it[1], it[2], (sub_ctr[0] + i) % 3, c0)
                    run_pipeline(subs, mkA, mkB, nA=2, nbuf=3)
                    sub_ctr[0] += len(subs)
                    gated_norm(st, oB, b_oB, sr, b_sr, nw, b_nw, og, b_og, 4, c0, n, nbufs)
                K.dma("sp", o_gla_p[l].rearrange("(pr hf) k v -> (hf k) pr v", hf=2), S[:], R=[b_S])

        def gla_sample(l, u, qm, b_qm, E12, b_E12, kdec, b_kd, vtb, b_vt, Ss, b_Ss, vbd, b_vbd, oB, b_oB):
            K.op("dve", lambda: nc.vector.tensor_tensor(
                out=vbd[:, :, :], in0=vtb[u][0:16, :].unsqueeze(1).broadcast_to([16, NS, 512]),
                in1=C["ident"][0:16, 0:16].unsqueeze(2).broadcast_to([16, NS, 512]), op=ALU.mult),
                R=[b_vt[u], b_cst], W=[b_vbd])
            for s4 in range(4):
                for si in range(4):
                    s = 4 * s4 + si
                    for h in range(4):
                        pr_, hf = h // 2, h % 2
                        q = 6 + si // 2
                        cb = (si % 2) * 256 + pr_ * 128
                        K.op("pe", lambda s=s, h=h, hf=hf, q=q, cb=cb: nc.tensor.matmul(
                            psum[q][hf * 64:hf * 64 + 64, cb:cb + 128],
                            lhsT=kdec[u][0:16, h * 64:(h + 1) * 64], rhs=vbd[0:16, s, h * 128:(h + 1) * 128],
                            start=True, stop=True), R=[b_kd[u], b_vbd], W=[b_ps[q]], inc=(h == 3))
                for si in range(4):
                    s = 4 * s4 + si
                    q = 6 + si // 2
                    for pr_ in range(2):
                        cb = (si % 2) * 256 + pr_ * 128
                        K.op("dve", lambda s=s, pr_=pr_, q=q, cb=cb: nc.vector.scalar_tensor_tensor(
                            out=Ss[:, s, pr_, :], in0=Ss[:, s, pr_, :], scalar=E12[u][:, 0, pr_, s:s + 1],
                            in1=psum[q][:, cb:cb + 128], op0=ALU.mult, op1=ALU.add),
                            R=[b_E12[u], b_ps[q]], W=[b_Ss])
            for s in range(NS):
                for h in range(4):
                    pr_, hf = h // 2, h % 2
                    K.op("pe", lambda s=s, h=h, pr_=pr_: nc.tensor.matmul(
                        psum[5][:, h * 32 + s:h * 32 + s + 2], lhsT=Ss[:, s, pr_, :], rhs=qm[:, h, s:s + 2],
                        start=True, stop=True), R=[b_Ss, b_qm], W=[b_ps[5]], inc=(s == NS - 1 and h == 3))
            K.op("act", lambda: nc.scalar.activation(
                out=oB[:, :, 0:16], in_=psum[5][:, 0:128].rearrange("p (a b) -> p a b", a=4)[:, :, 0:16],
                func=AF.Copy, scale=0.125), R=[b_ps[5]], W=[b_oB])
            sgl_o = o_gla_s[l].rearrange("s (pr hf) k v -> (hf k) s pr v", hf=2)
            for s4 in range(4):
                K.dma("sp", sgl_o[:, 4 * s4:4 * s4 + 4], Ss[:, 4 * s4:4 * s4 + 4], R=[b_Ss])

        def gdn_phase(l, og, b_og, win_v):
            keep = {}
            with ExitStack() as st0:
                s_wT = sbt(st0, "d_swT", [128, 4, 16], BF16)
                s_u0 = sbt(st0, "d_su0", [16, 4, 128])
                s_kd = sbt(st0, "d_skd", [16, 4, 128], BF16)
                s_qn = sbt(st0, "d_sqn", [128, 4, 16], BF16)
                s_G = sbt(st0, "d_sG", [128, NS, 4])
                s_gz = sbt(st0, "d_sgz", [128, 4, 16], BF16)
                b_skeep = Buf()
                nw = sbt(st0, "gdn_nw", [128, 1])
                b_nw = Buf()
                K.dma("sp", nw[:, :], gdn_norm_w[l].rearrange("(p o) -> p o", o=1), W=[b_nw],
                      allow_slow_non_contiguous=True)
                with ExitStack() as st:
                    wA = sbt(st, "wA", [128, KC, 2056], BF16)
                    b_wA = Buf("wA")
                    for k in range(KC):
                        K.dma("pool", wA[:, k, :], win_v[:, k, 0:2056], W=[b_wA])
                    cw = sbt(st, "d_cw", [128, 12, 4])
                    b_cw = Buf()
                    for i_ in range(4):
                        K.dma("sp", cw[:, :, i_], conv_w[l][i_].rearrange("(b p) -> p b", p=128), W=[b_cw],
                              allow_slow_non_contiguous=True)
                    abt = sbt(st, "d_abt", [128, 2, 4])
                    b_abt = Buf()
                    K.dma("sp", abt[:, 0, :], a_log[l].partition_broadcast(128), W=[b_abt])
                    K.dma("sp", abt[:, 1, :], dt_bias[l].partition_broadcast(128), W=[b_abt])
                    K.op("act", lambda: nc.scalar.activation(out=abt[:, 0, :], in_=abt[:, 0, :], func=AF.Exp),
                         R=[b_abt], W=[b_abt])
                    K.op("dve", lambda: nc.vector.tensor_scalar(out=abt[:, 0, :], in0=abt[:, 0, :], scalar1=-1.0,
                                                                scalar2=None, op0=ALU.mult), R=[b_abt], W=[b_abt])
                    cbT = sbt(st, "d_cbT", [128, 12, 48])
                    b_cbT = Buf()
                    with ExitStack() as stc:
                        cb48 = sbt(stc, "d_cb48", [48, 1536])
                        b_cb48 = Buf()
                        K.dma("sp", cb48[:], state_conv[l].rearrange("s i c -> (s i) c"), W=[b_cb48])
                        for half in range(2):
                            q = half
                            for b6 in range(6):
                                blk = 6 * half + b6
                                K.op("pe", lambda blk=blk, b6=b6, q=q: nc.tensor.transpose(
                                    psum[q][:, b6 * 48:(b6 + 1) * 48], cb48[0:48, blk * 128:(blk + 1) * 128],
                                    C["ident"][0:48, 0:48]), R=[b_cb48, b_cst], W=[b_ps[q]], inc=(b6 == 5))
                            K.op("act", lambda half=half, q=q: nc.scalar.copy(
                                out=cbT[:, 6 * half:6 * half + 6, :],
                                in_=psum[q][:, 0:288].rearrange("p (a b) -> p a b", a=6)), R=[b_ps[q]], W=[b_cbT])
                        K.barrier()
                    K.dma("sp", o_conv_s[l][:, 0:2, :], state_conv[l][:, 1:3, :])
                    halo = sbt(st, "d_halo", [128, 12, 4])
                    b_halo = Buf()
                    K.op("dve", lambda: nc.vector.memset(halo[:], 0.0), W=[b_halo])
                    Pe = [sbt(st, "d_Pe%d" % i, [128, 3 + 512]) for i in range(2)]
                    b_Pe = [Buf(), Buf()]
                    acc = [sbt(st, "d_acc%d" % i, [128, 512]) for i in range(2)]
                    b_acc = [Buf(), Buf()]
                    cq = [sbt(st, "d_cq%d" % i, [128, 512]) for i in range(2)]
                    b_cq = [Buf(), Buf()]
                    vn = sbt(st, "d_vn", [128, 4, 512], BF16)
                    b_vn = Buf()
                    qkn = sbt(st, "d_qkn", [128, 8, 512], BF16)
                    b_qkn = Buf()
                    gz = sbt(st, "d_gz", [128, 4, 512], BF16)
                    b_gz = Buf()
                    oA = big[:, 8 * T:8 * T + 4096].bitcast(F32).rearrange("p (a b) -> p a b", a=4)
                    b_oA = Buf()
                    sq = big[:, 8 * T + 4096:8 * T + 6144].rearrange("p (a b) -> p a b", a=4)
                    rsd = sbt(st, "d_rsd", [128, 512])
                    rsd_b = sbt(st, "d_rsdb", [128, 512])
                    tn = sbt(st, "d_tn", [128, 512])
                    b_sq, b_rsd, b_tn, b_rsdb = Buf(), Buf(), Buf(), Buf()
                    nbufs = (sq, b_sq, rsd, b_rsd, tn, b_tn)
                    pnew = big[0:16, 8 * T:8 * T + 3072].bitcast(F32)
                    b_pnew = b_oA
                    tb = [sbt(st, "d_tb%d" % i, [128, 6, 4]) for i in range(2)]
                    b_tb = [Buf(), Buf()]
                    r2 = sbt(st, "d_r2", [128, 8])
                    b_r2 = Buf()
                    Gbc = [sbt(st, "d_Gbc%d" % i, [128, 2, 4]) for i in range(2)]
                    b_Gbc = [Buf(), Buf()]
                    dg = sbt(st, "d_dg", [128, 8, 128])
                    b_dg = Buf()
                    gm = sbt(st, "d_gm", [128, 4, 128])
                    b_gm = Buf()
                    ET = sbt(st, "d_ET", [128, 4, 128])
                    ETi = sbt(st, "d_ETi", [128, 4, 128])
                    ETs = ET
                    b_ET, b_ETi = Buf(), Buf()
                    b_ETs = b_ET
                    QA = [sbt(st, "d_QA%d" % i, [128, 4, 128]) for i in range(2)]
                    QTA = [sbt(st, "d_QTA%d" % i, [128, 4, 128]) for i in range(2)]
                    b_QA = [Buf(), Buf()]
                    b_QTA = [Buf(), Buf()]
                    Qs = sbt(st, "d_Qs", [128, 4, 128])
                    QTs = sbt(st, "d_QTs", [128, 4, 128])
                    b_Qs, b_QTs = Buf(), Buf()
                    TT = [sbt(st, "d_TT%d" % i, [128, 4, 128]) for i in range(2)]
                    b_TT = [Buf(), Buf()]
                    TTb = sbt(st, "d_TTb", [128, 4, 128], BF16)
                    b_TTb = Buf()
                    bkq = [sbt(st, "d_bkq%d" % i, [128, 2, 4, 128], BF16) for i in range(2)]
                    b_bkq = [Buf(), Buf()]
                    qkT = [sbt(st, "d_qkT%d" % i, [128, 4, 128], BF16) for i in range(2)]
                    b_qkT = [Buf(), Buf()]
                    tok3 = sbt(st, "d_tok3", [128, 3, 4, 128], BF16)
                    b_tok3 = Buf()
                    u0 = sbt(st, "d_u0", [128, 4, 128])
                    wT = sbt(st, "d_wT", [128, 4, 128], BF16)
                    uu = sbt(st, "d_u", [128, 4, 128], BF16)
                    b_u0, b_wT, b_u = Buf(), Buf(), Buf()
                    S = sbt(st, "d_S", [128, 4, 128])
                    Sb = sbt(st, "d_Sb", [128, 4, 128], BF16)
                    b_S, b_Sb = Buf(), Buf()
                    K.op("dve", lambda: nc.vector.memset(S[:], 0.0), W=[b_S])
                    K.op("dve", lambda: nc.vector.memset(Sb[:], 0.0), W=[b_Sb])
                    K.op("dve", lambda: nc.vector.memset(uu[:], 0.0), W=[b_u])
                    sub_ctr = [0]
                    scan_done = [0]
                    def stageA(kind, sc0, nt, a, c0):
                        lc = sc0 - c0
                        smp = (kind == "sample")
                        m_incl = C["ident"] if smp else C["mu_incl"]
                        m_bd = C["ident"] if smp else C["bd"]
                        m_ustrict = C["zero"] if smp else C["mu_strict"]
                        m_lstrict = C["zero"] if smp else C["ml_strict"]
                        T_ = lambda i: tb[a][0:nt, i, :]
                        pv = lambda b: psum[b][:, :].rearrange("p (a c) -> p a c", a=4)[:, :, 0:nt]
                        pt = lambda b: psum[b][0:nt, :].rearrange("p (a c) -> p a c", a=4)[:, :, 0:nt]
                        p4 = lambda b: psum[b][0:nt, :].rearrange("p (a c) -> p a c", a=4)
                        bc = lambda i: tb[a][0:nt, i, :].unsqueeze(2).broadcast_to([nt, 4, 128])
                        mm_group(psum[2][0:nt, 0:8], [(x_bf[:, k, sc0:sc0 + nt], wA[:, k, 2048:2056]) for k in range(KC)],
                                 R=[b_wA, b_xbf], W=[b_ps[2]])
                        yield
                        K.op("act", lambda: nc.scalar.activation(out=T_(0), in_=psum[2][0:nt, 0:4], func=AF.Exp, scale=-1.0),
                             R=[b_ps[2]], W=[b_tb[a]])
                        yield
                        K.op("dve", lambda: nc.vector.tensor_tensor(out=T_(2), in0=psum[2][0:nt, 4:8], in1=abt[0:nt, 1, :],
                                                                    op=ALU.add), R=[b_ps[2], b_abt], W=[b_tb[a]])
                        yield
                        K.op("act", lambda: nc.scalar.activation(out=T_(0), in_=T_(0), func=AF.Ln, bias=one1[0:nt, 0:1]),
                             R=[b_tb[a], b_eps], W=[b_tb[a]])
                        yield
                        K.op("act", lambda: nc.scalar.activation(out=T_(0), in_=T_(0), func=AF.Exp, scale=-1.0),
                             R=[b_tb[a]], W=[b_tb[a]])
                        yield
                        K.op("act", lambda: nc.scalar.activation(out=T_(2), in_=T_(2), func=AF.Exp), R=[b_tb[a]], W=[b_tb[a]])
                        yield
                        K.op("act", lambda: nc.scalar.activation(out=T_(2), in_=T_(2), func=AF.Ln, bias=one1[0:nt, 0:1]),
                             R=[b_tb[a], b_eps], W=[b_tb[a]])
                        yield
                        K.op("dve", lambda: nc.vector.tensor_tensor(out=T_(2), in0=T_(2), in1=abt[0:nt, 0, :], op=ALU.mult),
                             R=[b_tb[a], b_abt], W=[b_tb[a]])
                        yield
                        mm_group(psum[2][0:nt, 8:12], [(m_incl[0:nt, 0:nt], T_(2))], R=[b_tb[a], b_cst], W=[b_ps[2]])
                        yield
                        mm_group(psum[2][0:nt, 12:16], [(m_bd[0:nt, 0:nt], T_(2))], R=[b_tb[a], b_cst], W=[b_ps[2]])
                        yield
                        K.op("act", lambda: nc.scalar.activation(out=T_(1), in_=psum[2][0:nt, 8:12], func=AF.Exp),
                             R=[b_ps[2]], W=[b_tb[a]])
                        yield
                        K.op("dve", lambda: nc.vector.tensor_copy(out=T_(5), in_=psum[2][0:nt, 8:12]), R=[b_ps[2]], W=[b_tb[a]])
                        yield
                        K.op("dve", lambda: nc.vector.tensor_tensor(out=T_(3), in0=psum[2][0:nt, 12:16], in1=T_(5),
                                                                    op=ALU.subtract), R=[b_ps[2], b_tb[a]], W=[b_tb[a]])
                        yield
                        K.op("act", lambda: nc.scalar.activation(out=T_(3), in_=T_(3), func=AF.Exp), R=[b_tb[a]], W=[b_tb[a]])
                        yield
                        K.op("dve", lambda: nc.vector.tensor_tensor(out=T_(4), in0=T_(0), in1=T_(1), op=ALU.mult),
                             R=[b_tb[a]], W=[b_tb[a]])
                        yield
                        if smp:
                            K.op("dve", lambda: nc.vector.tensor_tensor(
                                out=dg[0:16, 0:4, 0:16].rearrange("p h s -> p s h"),
                                in0=T_(2).unsqueeze(1).broadcast_to([16, 16, 4]),
                                in1=C["ident"][0:16, 0:16].unsqueeze(2).broadcast_to([16, 16, 4]), op=ALU.mult),
                                R=[b_tb[a], b_cst], W=[b_dg])
                            for h in range(4):
                                K.op("pe", lambda h=h: nc.tensor.matmul(psum[3][:, h * 16:(h + 1) * 16],
                                                                        lhsT=C["ones"][0:16, :], rhs=dg[0:16, h, 0:16],
                                                                        start=True, stop=True),
                                     R=[b_dg, b_cst], W=[b_ps[3]], inc=(h == 3))
                            K.op("act", lambda: nc.scalar.activation(
                                out=s_G[:, :, :].rearrange("p s h -> p h s"),
                                in_=psum[3][:, 0:64].rearrange("p (h s) -> p h s", h=4), func=AF.Exp),
                                R=[b_ps[3]], W=[b_skeep])
                        else:
                            K.op("dve", lambda: nc.vector.tensor_tensor(
                                out=r2[0:nt, :].rearrange("p (c h) -> p c h", c=2),
                                in0=T_(2).unsqueeze(1).broadcast_to([nt, 2, 4]),
                                in1=cst[0:nt, IDX["blk0"]:IDX["blk0"] + 2, 0:1].broadcast_to([nt, 2, 4]), op=ALU.mult),
                                R=[b_tb[a], b_cst], W=[b_r2])
                            mm_group(psum[2][:, 16:24], [(C["ones"][0:nt, :], r2[0:nt, :])], R=[b_r2, b_cst], W=[b_ps[2]])
                            K.op("act", lambda: nc.scalar.activation(out=Gbc[a][:, :, :].rearrange("p c h -> p (c h)"),
                                                                     in_=psum[2][:, 16:24], func=AF.Exp),
                                 R=[b_ps[2]], W=[b_Gbc[a]])
                        yield
                        K.op("dve", lambda: nc.vector.tensor_tensor(
                            out=dg[0:nt, :, 0:nt], in0=C["ident"][0:nt, 0:nt].unsqueeze(1).broadcast_to([nt, 8, nt]),
                            in1=tb[a][0:nt, 0:2, :].rearrange("p a h -> p (a h)").unsqueeze(2).broadcast_to([nt, 8, nt]),
                            op=ALU.mult), R=[b_tb[a], b_cst, b_skeep], W=[b_dg])
                        yield
                        for wh in range(2):
                            for h in range(4):
                                K.op("pe", lambda wh=wh, h=h: nc.tensor.matmul(
                                    psum[3 + wh][:, h * 128:h * 128 + nt], lhsT=C["ones"][0:nt, :],
                                    rhs=dg[0:nt, 4 * wh + h, 0:nt], start=True, stop=True),
                                    R=[b_dg, b_cst], W=[b_ps[3 + wh]], inc=(h == 3))
                        yield
                        K.op("dve", lambda: nc.vector.tensor_tensor(out=bkq[a][:, 0, :, 0:nt], in0=qkn[:, 4:8, lc:lc + nt],
                                                                    in1=pv(3), op=ALU.mult),
                             R=[b_qkn, b_ps[3]], W=[b_bkq[a]])
                        yield
                        K.op("dve", lambda: nc.vector.tensor_tensor(out=bkq[a][:, 1, :, 0:nt], in0=qkn[:, 0:4, lc:lc + nt],
                                                                    in1=pv(4), op=ALU.mult),
                             R=[b_qkn, b_ps[4]], W=[b_bkq[a]])
                        yield
                        K.op("dve", lambda: nc.vector.tensor_tensor(
                            out=gm[0:nt, :, 0:nt], in0=m_incl[0:nt, 0:nt].unsqueeze(1).broadcast_to([nt, 4, nt]),
                            in1=T_(2).unsqueeze(2).broadcast_to([nt, 4, nt]), op=ALU.mult), R=[b_tb[a], b_cst], W=[b_gm])
                        yield
                        for h in range(4):
                            K.op("pe", lambda h=h: nc.tensor.matmul(psum[2][0:nt, h * 128:h * 128 + nt],
                                                                    lhsT=m_lstrict[0:nt, 0:nt], rhs=gm[0:nt, h, 0:nt],
                                                                    start=True, stop=True),
                                 R=[b_gm, b_cst], W=[b_ps[2]], inc=(h == 3))
                        yield
                        K.op("act", lambda: nc.scalar.activation(out=ET[0:nt, :, 0:nt], in_=pt(2), func=AF.Exp),
                             R=[b_ps[2]], W=[b_ET])
                        yield
                        K.op("dve", lambda: nc.vector.tensor_tensor(
                            out=ETi[0:nt, :, 0:nt], in0=ET[0:nt, :, 0:nt],
                            in1=m_incl[0:nt, 0:nt].unsqueeze(1).broadcast_to([nt, 4, nt]), op=ALU.mult),
                            R=[b_ET, b_cst], W=[b_ETi])
                        yield
                        K.op("dve", lambda: nc.vector.tensor_tensor(
                            out=ETs[0:nt, :, 0:nt], in0=ET[0:nt, :, 0:nt],
                            in1=m_ustrict[0:nt, 0:nt].unsqueeze(1).broadcast_to([nt, 4, nt]), op=ALU.mult),
                            R=[b_ET, b_cst, b_ETi], W=[b_ETs])
                        yield
                        for h in range(4):
                            K.op("pe", lambda h=h: nc.tensor.matmul(psum[3][0:nt, h * 128:h * 128 + nt],
                                                                    lhsT=qkn[:, 4 + h, lc:lc + nt], rhs=bkq[a][:, 0, h, 0:nt],
                                                                    start=True, stop=True),
                                 R=[b_qkn, b_bkq[a]], W=[b_ps[3]], inc=(h == 3))
                        yield
                        for h in range(4):
                            K.op("pe", lambda h=h: nc.tensor.matmul(psum[4][0:nt, h * 128:h * 128 + nt],
                                                                    lhsT=qkn[:, 4 + h, lc:lc + nt], rhs=qkn[:, h, lc:lc + nt],
                                                                    start=True, stop=True),
                                 R=[b_qkn], W=[b_ps[4]], inc=(h == 3))
                        AT, A_ = QTA[a], QA[a]
                        yield
                        K.op("dve", lambda: nc.vector.tensor_tensor(out=AT[0:nt, :, 0:nt], in0=pt(3), in1=ETs[0:nt, :, 0:nt],
                                                                    op=ALU.mult), R=[b_ps[3], b_ETs], W=[b_QTA[a]])
                        yield
                        K.op("dve", lambda: nc.vector.tensor_tensor(out=qkT[a][0:nt, :, 0:nt], in0=pt(4), in1=ETi[0:nt, :, 0:nt],
                                                                    op=ALU.mult), R=[b_ps[4], b_ETi], W=[b_qkT[a]])
                        yield
                        K.op("dve", lambda: nc.vector.scalar_tensor_tensor(
                            out=TT[a][0:nt, :, 0:nt], in0=AT[0:nt, :, 0:nt], scalar=-1.0,
                            in1=C["ident"][0:nt, 0:nt].unsqueeze(1).broadcast_to([nt, 4, nt]),
                            op0=ALU.mult, op1=ALU.add), R=[b_QTA[a], b_cst], W=[b_TT[a]])
                        yield
                        if not smp:
                            for h in range(4):
                                K.op("pe", lambda h=h: nc.tensor.transpose(psum[2][0:nt, h * 128:h * 128 + nt],
                                                                           AT[0:nt, h, 0:nt], C["ident"][0:nt, 0:nt]),
                                     R=[b_QTA[a], b_cst], W=[b_ps[2]], inc=(h == 3))
                            yield
                            K.op("act", lambda: nc.scalar.copy(out=A_[0:nt, :, 0:nt], in_=pt(2)), R=[b_ps[2]], W=[b_QA[a]])

                    def stageB(kind, sc0, nt, a, c0):
                        lc = sc0 - c0
                        smp = (kind == "sample")
                        m_incl = C["ident"] if smp else C["mu_incl"]
                        m_bd = C["ident"] if smp else C["bd"]
                        m_ustrict = C["zero"] if smp else C["mu_strict"]
                        m_lstrict = C["zero"] if smp else C["ml_strict"]
                        T_ = lambda i: tb[a][0:nt, i, :]
                        pv = lambda b: psum[b][:, :].rearrange("p (a c) -> p a c", a=4)[:, :, 0:nt]
                        pt = lambda b: psum[b][0:nt, :].rearrange("p (a c) -> p a c", a=4)[:, :, 0:nt]
                        p4 = lambda b: psum[b][0:nt, :].rearrange("p (a c) -> p a c", a=4)
                        bc = lambda i: tb[a][0:nt, i, :].unsqueeze(2).broadcast_to([nt, 4, 128])
                        yield
                        for h in range(4):
                            K.op("pe", lambda h=h: nc.tensor.matmul(psum[5][0:nt, h * 128:(h + 1) * 128],
                                                                    lhsT=qkn[:, 4 + h, lc:lc + nt], rhs=CB["ident"],
                                                                    start=True, stop=True),
                                 R=[b_qkn, b_cst], W=[b_ps[5]], inc=(h == 3))
                        yield
                        for h in range(4):
                            K.op("pe", lambda h=h: nc.tensor.matmul(psum[6][0:nt, h * 128:(h + 1) * 128],
                                                                    lhsT=vn[:, h, lc:lc + nt], rhs=CB["ident"],
                                                                    start=True, stop=True),
                                 R=[b_vn, b_cst], W=[b_ps[6]], inc=(h == 3))
                        yield
                        K.op("dve", lambda: nc.vector.tensor_tensor(out=tok3[0:nt, 0], in0=p4(5), in1=bc(4), op=ALU.mult),
                             R=[b_ps[5], b_tb[a]], W=[b_tok3])
                        yield
                        K.op("dve", lambda: nc.vector.tensor_tensor(out=tok3[0:nt, 1], in0=p4(5), in1=bc(3), op=ALU.mult),
                             R=[b_ps[5], b_tb[a]], W=[b_tok3])
                        yield
                        K.op("dve", lambda: nc.vector.tensor_tensor(out=tok3[0:nt, 2], in0=p4(6), in1=bc(0), op=ALU.mult),
                             R=[b_ps[6], b_tb[a]], W=[b_tok3])
                        if not smp:
                            n_inc = 5 if nt == 128 else 3
                            cq_, cqt_, b_cq_, b_cqt_ = QA[a], QTA[a], b_QA[a], b_QTA[a]
                            nq_, nqt_, b_nq_, b_nqt_ = Qs, QTs, b_Qs, b_QTs
                            for lv in range(0, n_inc + 1):
                                need_q = lv < n_inc
                                need_qt = lv < n_inc - 1
                                if lv >= 1:
                                    for h in range(4):
                                        K.op("pe", lambda h=h: nc.tensor.matmul(
                                            psum[7][0:nt, h * 128:h * 128 + nt], lhsT=cq_[0:nt, h, 0:nt],
                                            rhs=TT[a][0:nt, h, 0:nt], start=True, stop=True),
                                            R=[b_cq_, b_TT[a]], W=[b_ps[7]], inc=(h == 3))
                                if need_q:
                                    for h in range(4):
                                        K.op("pe", lambda h=h: nc.tensor.matmul(
                                            psum[5][0:nt, h * 128:h * 128 + nt], lhsT=cqt_[0:nt, h, 0:nt],
                                            rhs=cq_[0:nt, h, 0:nt], start=True, stop=True),
                                            R=[b_cq_, b_cqt_], W=[b_ps[5]], inc=(h == 3))
                                if need_qt:
                                    for h in range(4):
                                        K.op("pe", lambda h=h: nc.tensor.matmul(
                                            psum[6][0:nt, h * 128:h * 128 + nt], lhsT=cq_[0:nt, h, 0:nt],
                                            rhs=cqt_[0:nt, h, 0:nt], start=True, stop=True),
                                            R=[b_cq_, b_cqt_], W=[b_ps[6]], inc=(h == 3))
                                yield
                                if need_q:
                                    K.op("act", lambda: nc.scalar.copy(out=nq_[0:nt, :, 0:nt], in_=pt(5)),
                                         R=[b_ps[5]], W=[b_nq_])
                                if lv >= 1:
                                    K.op("dve", lambda: nc.vector.tensor_tensor(out=TT[a][0:nt, :, 0:nt], in0=TT[a][0:nt, :, 0:nt],
                                                                                in1=pt(7), op=ALU.add),
                                         R=[b_ps[7], b_TT[a]], W=[b_TT[a]])
                                if need_qt:
                                    K.op("act" if lv >= 1 else "dve", (lambda: nc.scalar.copy(out=nqt_[0:nt, :, 0:nt], in_=pt(6))) if lv >= 1
                                         else (lambda: nc.vector.tensor_copy(out=nqt_[0:nt, :, 0:nt], in_=pt(6))),
                                         R=[b_ps[6]], W=[b_nqt_])
                                yield
                                cq_, cqt_, b_cq_, b_cqt_, nq_, nqt_, b_nq_, b_nqt_ = nq_, nqt_, b_nq_, b_nqt_, cq_, cqt_, b_cq_, b_cqt_
                        yield
                        K.op("act", lambda: nc.scalar.copy(out=TTb[0:nt, :, 0:nt], in_=TT[a][0:nt, :, 0:nt]), R=[b_TT[a]], W=[b_TTb])
                        yield
                        yield
                        for h in range(4):
                            K.op("pe", lambda h=h: nc.tensor.matmul(psum[7][0:nt, h * 128:(h + 1) * 128],
                                                                    lhsT=TTb[0:nt, h, 0:nt], rhs=tok3[0:nt, 2, h, :],
                                                                    start=True, stop=True),
                                 R=[b_TTb, b_tok3], W=[b_ps[7]], inc=(h == 3))
                        yield
                        for h in range(4):
                            K.op("pe", lambda h=h: nc.tensor.matmul(psum[5][:, h * 128:h * 128 + nt],
                                                                    lhsT=tok3[0:nt, 0, h, :], rhs=TTb[0:nt, h, 0:nt],
                                                                    start=True, stop=True),
                                 R=[b_TTb, b_tok3], W=[b_ps[5]], inc=(h == 3))
                        yield
                        if smp:
                            K.op("act", lambda: nc.scalar.copy(out=s_u0[:], in_=p4(7)), R=[b_ps[7]], W=[b_skeep])
                            K.op("dve", lambda: nc.vector.tensor_copy(out=s_wT[:], in_=pv(5)), R=[b_ps[5]], W=[b_skeep])
                            K.op("dve", lambda: nc.vector.tensor_copy(out=s_kd[:], in_=tok3[0:16, 1]), R=[b_tok3], W=[b_skeep])
                            K.op("dve", lambda: nc.vector.tensor_copy(out=s_qn[:], in_=qkn[:, 0:4, 0:16]), R=[b_qkn], W=[b_skeep])
                            K.op("dve", lambda: nc.vector.tensor_copy(out=s_gz[:], in_=gz[:, :, 0:16]), R=[b_gz], W=[b_skeep])
                            return
                        yield
                        K.op("act", lambda: nc.scalar.copy(out=u0[0:nt], in_=p4(7)), R=[b_ps[7]], W=[b_u0])
                        yield
                        K.op("dve", lambda: nc.vector.tensor_copy(out=wT[:, :, 0:nt], in_=pv(5)), R=[b_ps[5]], W=[b_wT])
                        nch = 2 if nt == 128 else 1
                        cl = 64 if nt == 128 else nt
                        yield
                        for ch in range(nch):
                            tr = slice(ch * 64, ch * 64 + cl)
                            yield
                            for h in range(4):
                                K.op("pe", lambda h=h, tr=tr: nc.tensor.matmul(psum[6][tr, h * 128:(h + 1) * 128],
                                                                              lhsT=wT[:, h, tr], rhs=Sb[:, h, :],
                                                                              start=True, stop=True),
                                     R=[b_wT, b_Sb], W=[b_ps[6]], inc=(h == 3))
                            yield
                            K.op("dve", lambda tr=tr: nc.vector.tensor_tensor(
                                out=uu[tr], in0=u0[tr], in1=psum[6][tr, :].rearrange("p (a c) -> p a c", a=4),
                                op=ALU.subtract), R=[b_u0, b_ps[6]], W=[b_u])
                            yield
                            for h in range(4):
                                K.op("pe", lambda h=h, tr=tr: nc.tensor.matmul(psum[7][:, h * 64:h * 64 + cl],
                                                                              lhsT=Sb[:, h, :], rhs=bkq[a][:, 1, h, tr],
                                                                              start=True, stop=False),
                                     R=[b_Sb, b_bkq[a]], W=[b_ps[7]], inc=False)
                                K.op("pe", lambda h=h, tr=tr: nc.tensor.matmul(psum[7][:, h * 64:h * 64 + cl],
                                                                              lhsT=uu[0:nt, h, :], rhs=qkT[a][0:nt, h, tr],
                                                                              start=False, stop=True),
                                     R=[b_u, b_qkT[a]], W=[b_ps[7]], inc=(h == 3))
                            yield
                            for h in range(4):
                                K.op("pe", lambda h=h, tr=tr: nc.tensor.matmul(psum[5][:, h * 128:(h + 1) * 128],
                                                                              lhsT=tok3[tr, 1, h, :], rhs=uu[tr, h, :],
                                                                              start=True, stop=True),
                                     R=[b_tok3, b_u], W=[b_ps[5]], inc=(h == 3))
                            yield
                            K.op("act", lambda ch=ch: nc.scalar.copy(
                                out=oA[:, :, lc + ch * 64:lc + ch * 64 + cl],
                                in_=psum[7][:, 0:256].rearrange("p (a b) -> p a b", a=4)[:, :, 0:cl]),
                                R=[b_ps[7]], W=[b_oA])
                            yield
                            for h in range(4):
                                K.op("dve", lambda h=h, ch=ch: nc.vector.scalar_tensor_tensor(
                                    out=S[:, h, :], in0=S[:, h, :], scalar=Gbc[a][:, ch, h:h + 1],
                                    in1=psum[5][:, h * 128:(h + 1) * 128], op0=ALU.mult, op1=ALU.add),
                                    R=[b_Gbc[a], b_ps[5]], W=[b_S])
                            yield
                            K.op("act", lambda: nc.scalar.copy(out=Sb[:], in_=S[:]), R=[b_S], W=[b_Sb])
                        scan_done[0] += 1
                        yield

                    blk_it = 0
                    for (c0, n) in TILES:
                        is_ms = (c0 == 0)
                        for blk in range(16):
                            q = blk % 2
                            mm_group(psum[q][:, 0:n], [(wA[:, k, blk * 128:(blk + 1) * 128], x_bf[:, k, c0:c0 + n])
                                                      for k in range(KC)], R=[b_wA, b_xbf], W=[b_ps[q]])
                            if blk >= 12:
                                K.op("act", lambda blk=blk, q=q: nc.scalar.activation(
                                    out=gz[:, blk - 12, 0:n], in_=psum[q][:, 0:n], func=AF.Silu), R=[b_ps[q]], W=[b_gz])
                                continue
                            pe_ = blk_it % 2
                            blk_it += 1
                            ncv = 16 if is_ms else n
                            K.op("pool", lambda pe_=pe_, blk=blk: nc.gpsimd.tensor_copy(out=Pe[pe_][:, 0:3], in_=halo[:, blk, 0:3]),
                                 R=[b_halo], W=[b_Pe[pe_]])
                            src0 = 16 if is_ms else 0
                            K.op("act", lambda pe_=pe_, q=q, src0=src0, ncv=ncv: nc.scalar.copy(
                                out=Pe[pe_][:, 3:3 + ncv], in_=psum[q][:, src0:src0 + ncv]), R=[b_ps[q]], W=[b_Pe[pe_]])
                            K.op("pool", lambda pe_=pe_, blk=blk, ncv=ncv: nc.gpsimd.tensor_copy(
                                out=halo[:, blk, 0:3], in_=Pe[pe_][:, ncv:ncv + 3]), R=[b_Pe[pe_]], W=[b_halo])
                            a_ = acc[pe_]
                            o0 = 16 if is_ms else 0
                            K.op("act", lambda pe_=pe_, blk=blk, ncv=ncv, o0=o0: nc.scalar.activation(
                                out=acc[pe_][:, o0:o0 + ncv], in_=Pe[pe_][:, 3:3 + ncv], func=AF.Identity, scale=cw[:, blk, 3:4]),
                                R=[b_Pe[pe_], b_cw], W=[b_acc[pe_]])
                            for i in (2, 1, 0):
                                K.op("dve", lambda pe_=pe_, blk=blk, ncv=ncv, o0=o0, i=i: nc.vector.scalar_tensor_tensor(
                                    out=acc[pe_][:, o0:o0 + ncv], in0=Pe[pe_][:, i:i + ncv], scalar=cw[:, blk, i:i + 1],
                                    in1=acc[pe_][:, o0:o0 + ncv], op0=ALU.mult, op1=ALU.add),
                                    R=[b_Pe[pe_], b_cw, b_acc[pe_]], W=[b_acc[pe_]])
                            if is_ms:
                                cbv = cbT[:, blk, :].rearrange("p (s i) -> p s i", i=3)
                                K.op("dve", lambda pe_=pe_, blk=blk, q=q: nc.vector.tensor_scalar(
                                    out=acc[pe_][:, 0:16], in0=psum[q][:, 0:16], scalar1=cw[:, blk, 3:4], scalar2=None,
                                    op0=ALU.mult), R=[b_ps[q], b_cw], W=[b_acc[pe_]])
                                for i in range(3):
                                    K.op("dve", lambda pe_=pe_, blk=blk, i=i, cbv=cbv: nc.vector.scalar_tensor_tensor(
                                        out=acc[pe_][:, 0:16], in0=cbv[:, :, i], scalar=cw[:, blk, i:i + 1],
                                        in1=acc[pe_][:, 0:16], op0=ALU.mult, op1=ALU.add),
                                        R=[b_cbT, b_cw, b_acc[pe_]], W=[b_acc[pe_]])
                            if blk < 8:
                                ci = blk % 2
                                rr = rsd if ci == 0 else rsd_b
                                b_rr = b_rsd if ci == 0 else b_rsdb
                                K.op("act", lambda pe_=pe_, ci=ci: nc.scalar.activation(
                                    out=cq[ci][:, 0:n], in_=acc[pe_][:, 0:n], func=AF.Silu), R=[b_acc[pe_]], W=[b_cq[ci]])
                                K.op("act", lambda ci=ci: nc.scalar.activation(out=sq[:, ci, 0:n], in_=cq[ci][:, 0:n],
                                                                              func=AF.Square), R=[b_cq[ci]], W=[b_sq])
                                mm_group(psum[q][:, 0:n], [(CB["ones"], sq[:, ci, 0:n])], R=[b_sq, b_cst], W=[b_ps[q]])
                                K.op("act", lambda q=q, rr=rr: nc.scalar.activation(out=rr[:, 0:n], in_=psum[q][:, 0:n], func=AF.Ln,
                                                                                 bias=eps6[:, 0:1]), R=[b_ps[q], b_eps], W=[b_rr])
                                K.op("act", lambda rr=rr: nc.scalar.activation(out=rr[:, 0:n], in_=rr[:, 0:n], func=AF.Exp,
                                                                             scale=-0.5), R=[b_rr], W=[b_rr])
                                scl = (128.0 ** -0.5) if blk < 4 else 1.0
                                K.op("dve", lambda blk=blk, scl=scl, ci=ci, rr=rr: nc.vector.scalar_tensor_tensor(
                                    out=qkn[:, blk, 0:n], in0=cq[ci][:, 0:n], scalar=scl, in1=rr[:, 0:n],
                                    op0=ALU.mult, op1=ALU.mult), R=[b_cq[ci], b_rr], W=[b_qkn])
                            else:
                                K.op("act", lambda pe_=pe_, blk=blk: nc.scalar.activation(
                                    out=vn[:, blk - 8, 0:n], in_=acc[pe_][:, 0:n], func=AF.Silu), R=[b_acc[pe_]], W=[b_vn])
                        if is_ms:
                            for g3 in range(3):
                                q = g3 % 2
                                mm_group(psum[q][0:16, :], [(x_bf[:, k, 0:16], wA[:, k, g3 * 512:(g3 + 1) * 512])
                                                            for k in range(KC)], R=[b_wA, b_xbf], W=[b_ps[q]])
                                K.op("act", lambda g3=g3, q=q: nc.scalar.copy(out=pnew[:, g3 * 512:(g3 + 1) * 512],
                                                                             in_=psum[q][0:16, :]), R=[b_ps[q]], W=[b_pnew])
                            K.dma("sp", o_conv_s[l][:, 2, :], pnew[:], R=[b_pnew])
                        subs = subtiles(c0, n)
                        prev = None
                        for si_ in range(len(subs) + 1):
                            gens = []
                            cur = None
                            if si_ < len(subs):
                                kind, sc0, nt = subs[si_]
                                a_ = sub_ctr[0] % 2
                                sub_ctr[0] += 1
                                gens.append(stageA(kind, sc0, nt, a_, c0))
                                cur = (kind, sc0, nt, a_)
                            if prev is not None:
                                gens.append(stageB(prev[0], prev[1], prev[2], prev[3], c0))
                            run_threads(gens)
                            prev = cur
                        if is_ms:
                            K.op("dve", lambda: nc.vector.memset(oA[:, :, 0:16], 0.0), W=[b_oA])
                        gated_norm(st, oA, b_oA, gz, b_gz, nw, b_nw, og, b_og, 0, c0, n, nbufs)
                    K.dma("sp", o_gdn_p[l].rearrange("h k v -> k h v"), S[:], R=[b_S])
                    for g3 in range(3):
                        for b4 in range(4):
                            blk = 4 * g3 + b4
                            K.op("pe", lambda blk=blk, b4=b4, g3=g3: nc.tensor.transpose(
                                psum[g3][0:4, b4 * 128:(b4 + 1) * 128], halo[:, blk, :], C["ident"]),
                                R=[b_halo, b_cst], W=[b_ps[g3]], inc=(b4 == 3))
                        K.op("act", lambda g3=g3: nc.scalar.copy(out=pnew[0:4, g3 * 512:(g3 + 1) * 512], in_=psum[g3][0:4, :]),
                             R=[b_ps[g3]], W=[b_pnew])
                    K.dma("sp", o_conv_p[l], pnew[0:3, :], R=[b_pnew])
                K.barrier()
                with ExitStack() as s2:
                    wTm = sbt(s2, "d_wTm", [128, 4, NS, 16], BF16)
                    b_wTm = Buf()
                    K.op("dve", lambda: nc.vector.tensor_tensor(
                        out=wTm[:], in0=s_wT[:, :, :].unsqueeze(2).broadcast_to([128, 4, NS, 16]),
                        in1=cst[:, IDX["idrep0"]:IDX["idrep0"] + 2, :].rearrange("p a (s t) -> p (a s) t", t=16)
                            .unsqueeze(1).broadcast_to([128, 4, NS, 16]), op=ALU.mult),
                        R=[b_skeep, b_cst], W=[b_wTm])
                    Sg = [sbt(s2, "d_Sg%d" % i, [128, 4, 4, 128]) for i in range(2)]
                    Sgb = [sbt(s2, "d_Sgb%d" % i, [128, 4, 4, 128], BF16) for i in range(2)]
                    b_Sg = [Buf(), Buf()]
                    b_Sgb = [Buf(), Buf()]
                    us = sbt(s2, "d_us", [16, 4, 128], BF16)
                    ubd = sbt(s2, "d_ubd", [16, 4, 512], BF16)
                    b_us, b_ubd = Buf(), Buf()
                    oS = sbt(s2, "d_oS", [128, 4, 16])
                    b_oS = Buf()
                    sq2 = sbt(s2, "d_sq2", [128, 4, 512], BF16)
                    rsd2 = sbt(s2, "d_rsd2", [128, 512])
                    tn2 = sbt(s2, "d_tn2", [128, 512])
                    sgd_v = state_gdn[l].rearrange("s h k v -> k s h v")
                    sgd_o = o_gdn_s[l].rearrange("s h k v -> k s h v")
                    for sg_ in range(4):
                        u = sg_ % 2
                        K.dma("sp", Sg[u][:], sgd_v[:, 4 * sg_:4 * sg_ + 4], W=[b_Sg[u]])
                        K.op("act", lambda u=u: nc.scalar.copy(out=Sgb[u][:], in_=Sg[u][:]), R=[b_Sg[u]], W=[b_Sgb[u]])
                        for h in range(4):
                            for si in range(4):
                                s = 4 * sg_ + si
                                K.op("pe", lambda h=h, si=si, s=s, u=u: nc.tensor.matmul(
                                    psum[0][0:16, h * 128:(h + 1) * 128], lhsT=wTm[:, h, s, :], rhs=Sgb[u][:, si, h, :],
                                    start=(si == 0), stop=(si == 3)), R=[b_wTm, b_Sgb[u]], W=[b_ps[0]],
                                    inc=(h == 3 and si == 3))
                        K.op("dve", lambda: nc.vector.tensor_tensor(
                            out=us[:], in0=s_u0[:], in1=psum[0][0:16, :].rearrange("p (a c) -> p a c", a=4), op=ALU.subtract),
                            R=[b_skeep, b_ps[0]], W=[b_us])
                        K.op("dve", lambda sg_=sg_: nc.vector.tensor_tensor(
                            out=ubd[:], in0=us[:, :, :].rearrange("p h v -> p (h v)").unsqueeze(1).broadcast_to([16, 4, 512]),
                            in1=C["ident"][0:16, 4 * sg_:4 * sg_ + 4].unsqueeze(2).broadcast_to([16, 4, 512]), op=ALU.mult),
                            R=[b_us, b_cst], W=[b_ubd])
                        for si in range(4):
                            for h in range(4):
                                K.op("pe", lambda h=h, si=si: nc.tensor.matmul(
                                    psum[1 + si][:, h * 128:(h + 1) * 128], lhsT=s_kd[0:16, h, :],
                                    rhs=ubd[0:16, si, h * 128:(h + 1) * 128], start=True, stop=True),
                                    R=[b_skeep, b_ubd], W=[b_ps[1 + si]], inc=(h == 3))
                        for si in range(4):
                            s = 4 * sg_ + si
                            for h in range(4):
                                K.op("dve", lambda h=h, si=si, s=s, u=u: nc.vector.scalar_tensor_tensor(
                                    out=Sg[u][:, si, h, :], in0=Sg[u][:, si, h, :], scalar=s_G[:, s, h:h + 1],
                                    in1=psum[1 + si][:, h * 128:(h + 1) * 128], op0=ALU.mult, op1=ALU.add),
                                    R=[b_skeep, b_ps[1 + si]], W=[b_Sg[u]])
                        K.op("act", lambda u=u: nc.scalar.copy(out=Sgb[u][:], in_=Sg[u][:]), R=[b_Sg[u]], W=[b_Sgb[u]])
                        for si in range(4):
                            s = 4 * sg_ + si
                            for h in range(4):
                                K.op("pe", lambda h=h, si=si, s=s, u=u: nc.tensor.matmul(
                                    psum[5][:, h * 16 + s:h * 16 + s + 1], lhsT=Sgb[u][:, si, h, :], rhs=s_qn[:, h, s:s + 1],
                                    start=True, stop=True), R=[b_Sgb[u], b_skeep], W=[b_ps[5]],
                                    inc=(h == 3 and si == 3))
                        K.dma("sp", sgd_o[:, 4 * sg_:4 * sg_ + 4], Sg[u][:], R=[b_Sg[u]])
                    K.op("act", lambda: nc.scalar.copy(out=oS[:], in_=psum[5][:, 0:64].rearrange("p (a b) -> p a b", a=4)),
                         R=[b_ps[5]], W=[b_oS])
                    gated_norm(s2, oS, b_oS, s_gz, b_skeep, nw, b_nw, og, b_og, 0, 0, 16,
                               (sq2, Buf(), rsd2, Buf(), tn2, Buf()))

        for l in range(layers):
            last_layer = (l == layers - 1)
            win_v = w_in[l].rearrange("(k p) n -> p k n", p=128)

            with ExitStack() as pm:
              og = big[:, 0:8 * T].rearrange("p (k t) -> p k t", k=8)
              b_og = Buf("og")
              if stub_mixer:
                  K.op("dve", lambda: nc.vector.tensor_copy(out=og, in_=x_bf[:]), R=[b_xbf], W=[b_og])
              else:
                  if mix_sel in ("all", "gdn"):
                      gdn_phase(l, og, b_og, win_v)
                      K.barrier()
                  if mix_sel in ("all", "gla"):
                      gla_phase(l, og, b_og, win_v)
              K.barrier()
              chk("mixer")
              v = sbt(pm, "v", [128, KC, T])
              b_v = Buf("v")
              with ExitStack() as pmg:
                mg = sbt(pmg, "mg", [128, KC, T], BF16)
                b_mg = Buf("mg")
                with ExitStack() as pb1:
                    NW = 2
                    wab = [sbt(pb1, "wab%d" % i, [128, 8, 128], BF16) for i in range(NW)]
                    wgg = [sbt(pb1, "wgg%d" % i, [128, 16, 128], BF16) for i in range(NW)]
                    b_wab = [Buf() for _ in range(NW)]
                    b_wgg = [Buf() for _ in range(NW)]
                    sg = [sbt(pb1, "sg%d" % i, [128, 2, 512]) for i in range(2)]
                    b_sg = [Buf(), Buf()]
                    wa_v = w_branch_a[l].rearrange("(k p) n -> p k n", p=128)
                    wb_v = w_branch_b[l].rearrange("(k p) n -> p k n", p=128)
                    it = 0
                    for jo in range(KC):
                        s = jo % NW
                        cs = slice(jo * 128, (jo + 1) * 128)
                        K.dma("pool", wab[s][:, 0:4, :], wa_v[:, :, cs], W=[b_wab[s]])
                        K.dma("pool", wab[s][:, 4:8, :], wb_v[:, :, cs], W=[b_wab[s]])
                        K.dma("pool", wgg[s][:, 0:8, :], win_v[:, :, 3608 + jo * 128:3608 + (jo + 1) * 128], W=[b_wgg[s]])
                        K.dma("pool", wgg[s][:, 8:16, :], win_v[:, :, 4632 + jo * 128:4632 + (jo + 1) * 128], W=[b_wgg[s]])
                        for (c0, n) in TILES:
                            q = 4 * (it % 2)
                            t2 = it % 2
                            it += 1
                            cols = slice(c0, c0 + n)
                            mm_group(psum[q + 0][:, 0:n], [(wab[s][:, k, :], og[:, k, cols]) for k in range(4)],
                                     R=[b_wab[s], b_og], W=[b_ps[q + 0]])
                            mm_group(psum[q + 1][:, 0:n], [(wab[s][:, 4 + k, :], og[:, 4 + k, cols]) for k in range(4)],
                                     R=[b_wab[s], b_og], W=[b_ps[q + 1]])
                            mm_group(psum[q + 2][:, 0:n], [(wgg[s][:, k, :], x_bf[:, k, cols]) for k in range(8)],
                                     R=[b_wgg[s], b_xbf], W=[b_ps[q + 2]])
                            mm_group(psum[q + 3][:, 0:n], [(wgg[s][:, 8 + k, :], x_bf[:, k, cols]) for k in range(8)],
                                     R=[b_wgg[s], b_xbf], W=[b_ps[q + 3]])
                            K.op("act", lambda q=q, t2=t2, n=n: nc.scalar.activation(
                                out=sg[t2][:, 0, 0:n], in_=psum[q + 2][:, 0:n], func=AF.Sigmoid),
                                R=[b_ps[q + 2]], W=[b_sg[t2]])
                            K.op("act", lambda q=q, t2=t2, n=n: nc.scalar.activation(
                                out=sg[t2][:, 1, 0:n], in_=psum[q + 3][:, 0:n], func=AF.Sigmoid),
                                R=[b_ps[q + 3]], W=[b_sg[t2]])
                            K.op("dve", lambda q=q, t2=t2, n=n: nc.vector.tensor_tensor(
                                out=sg[t2][:, 0, 0:n], in0=sg[t2][:, 0, 0:n], in1=psum[q + 0][:, 0:n], op=ALU.mult),
                                R=[b_sg[t2], b_ps[q + 0]], W=[b_sg[t2]])
                            K.op("dve", lambda q=q, t2=t2, n=n: nc.vector.tensor_tensor(
                                out=sg[t2][:, 1, 0:n], in0=sg[t2][:, 1, 0:n], in1=psum[q + 1][:, 0:n], op=ALU.mult),
                                R=[b_sg[t2], b_ps[q + 1]], W=[b_sg[t2]])
                            K.op("dve", lambda t2=t2, n=n, jo=jo, cols=cols: nc.vector.tensor_tensor(
                                out=mg[:, jo, cols], in0=sg[t2][:, 0, 0:n], in1=sg[t2][:, 1, 0:n], op=ALU.add),
                                R=[b_sg[t2]], W=[b_mg])
                K.barrier()
                chk("b1")
                K.dma("sp", v[:], xres, R=[b_xres], W=[b_v])
                with ExitStack() as pb2:
                    wo = [sbt(pb2, "wo%d" % i, [128, 8, 128], BF16) for i in range(2)]
                    b_wo = [Buf(), Buf()]
                    wo_v = w_out[l].rearrange("(k p) n -> p k n", p=128)
                    it = 0
                    for jo in range(KC):
                        s = jo % 2
                        K.dma("pool", wo[s][:], wo_v[:, :, jo * 128:(jo + 1) * 128], W=[b_wo[s]])
                        for (c0, n) in TILES:
                            q = it % 8
                            it += 1
                            cols = slice(c0, c0 + n)
                            mm_group(psum[q][:, 0:n], [(wo[s][:, k, :], mg[:, k, cols]) for k in range(8)],
                                     R=[b_wo[s], b_mg], W=[b_ps[q]])
                            K.op("dve", lambda q=q, n=n, jo=jo, cols=cols: nc.vector.scalar_tensor_tensor(
                                out=v[:, jo, cols], in0=v[:, jo, cols], scalar=ALPHA, in1=psum[q][:, 0:n],
                                op0=ALU.mult, op1=ALU.add), R=[b_ps[q]], W=[b_v])
                K.barrier()
                chk("b2")
              if True:
                layer_norm(pm, v, b_v, l, 0)
                K.barrier()
                chk("ln1")

                nh = (len(TILES) + 1) // 2
                halves = [TILES[:nh], TILES[nh:]]
                fin_v = w_ffn_in[l].rearrange("(k p) n -> p k n", p=128)
                fout_v = w_ffn_out[l].rearrange("(c p) n -> p c n", p=128)
                with ExitStack() as pf:
                    hid = big[:, 0:FC * HW].rearrange("p (c t) -> p c t", c=FC)
                    b_hid = Buf("hid")
                    wfi = [sbt(pf, "wfi%d" % i, [128, 2, KC, 256], BF16) for i in range(2)]
                    b_wfi = [Buf(), Buf()]
                    wfo = [sbt(pf, "wfo%d" % i, [128, FC, 128], BF16) for i in range(2)]
                    b_wfo = [Buf(), Buf()]
                    sil = [sbt(pf, "sil%d" % i, [128, 512]) for i in range(2)]
                    b_sil = [Buf(), Buf()]
                    it = 0
                    wi_it = 0
                    wo_it = 0
                    for half in halves:
                        if not half:
                            continue
                        h0 = half[0][0]
                        for g in range(FC // 2):
                            s = wi_it % 2
                            wi_it += 1
                            K.dma("pool", wfi[s][:, 0, :, :], fin_v[:, :, g * 256:(g + 1) * 256], W=[b_wfi[s]])
                            K.dma("pool", wfi[s][:, 1, :, :], fin_v[:, :, DFF + g * 256:DFF + (g + 1) * 256], W=[b_wfi[s]])
                            for jj in range(2):
                                j = 2 * g + jj
                                for (c0, n) in half:
                                    q = 2 * (it % 4)
                                    t2 = it % 2
                                    it += 1
                                    cols = slice(c0, c0 + n)
                                    hc = slice(c0 - h0, c0 - h0 + n)
                                    mm_group(psum[q][:, 0:n],
                                             [(wfi[s][:, 0, k, jj * 128:(jj + 1) * 128], x_bf[:, k, cols]) for k in range(8)],
                                             R=[b_wfi[s], b_xbf], W=[b_ps[q]])
                                    mm_group(psum[q + 1][:, 0:n],
                                             [(wfi[s][:, 1, k, jj * 128:(jj + 1) * 128], x_bf[:, k, cols]) for k in range(8)],
                                             R=[b_wfi[s], b_xbf], W=[b_ps[q + 1]])
                                    K.op("act", lambda q=q, t2=t2, n=n: nc.scalar.activation(
                                        out=sil[t2][:, 0:n], in_=psum[q][:, 0:n], func=AF.Silu),
                                        R=[b_ps[q]], W=[b_sil[t2]])
                                    K.op("dve", lambda q=q, t2=t2, n=n, j=j, hc=hc: nc.vector.tensor_tensor(
                                        out=hid[:, j, hc], in0=sil[t2][:, 0:n], in1=psum[q + 1][:, 0:n], op=ALU.mult),
                                        R=[b_sil[t2], b_ps[q + 1]], W=[b_hid])
                        for jo in range(KC):
                            s = wo_it % 2
                            wo_it += 1
                            K.dma("pool", wfo[s][:], fout_v[:, :, jo * 128:(jo + 1) * 128], W=[b_wfo[s]])
                            for (c0, n) in half:
                                q = it % 8
                                it += 1
                                cols = slice(c0, c0 + n)
                                hc = slice(c0 - h0, c0 - h0 + n)
                                mm_group(psum[q][:, 0:n], [(wfo[s][:, c, :], hid[:, c, hc]) for c in range(FC)],
                                         R=[b_wfo[s], b_hid], W=[b_ps[q]])
                                K.op("dve", lambda q=q, n=n, jo=jo, cols=cols: nc.vector.scalar_tensor_tensor(
                                    out=v[:, jo, cols], in0=v[:, jo, cols], scalar=ALPHA, in1=psum[q][:, 0:n],
                                    op0=ALU.mult, op1=ALU.add), R=[b_ps[q]], W=[b_v])
                K.barrier()
                chk("ffn")
                layer_norm(pm, v, b_v, l, 1, write_bf=not last_layer)
                K.barrier()
                chk("ln2")
                if not last_layer:
                    K.dma("sp", xres, v[:], R=[b_v], W=[b_xres])
                else:
                    with ExitStack() as po:
                        yst = [sbt(po, "yst%d" % i, [128, D]) for i in range(2)]
                        b_yst = [Buf(), Buf()]
                        rows = [("ms", 0, 32)] + [("p", 128 * i, 128) for i in range(TP // 128)]
                        for ri, (kind, r0, n) in enumerate(rows):
                            s = ri % 2
                            c0 = 0 if kind == "ms" else 32 + r0
                            for g in range(2):
                                pb = (2 * ri + g) % 8
                                for kk in range(4):
                                    k = 4 * g + kk
                                    K.op("pe", lambda k=k, kk=kk, pb=pb, n=n, c0=c0: nc.tensor.transpose(
                                        psum[pb][0:n, kk * 128:(kk + 1) * 128], v[:, k, c0:c0 + n], C["ident"]),
                                        R=[b_v, b_cst], W=[b_ps[pb]], inc=(kk == 3))
                                if g == 0:
                                    K.op("act", lambda pb=pb, n=n, s=s: nc.scalar.copy(
                                        out=yst[s][0:n, 0:512], in_=psum[pb][0:n, :]), R=[b_ps[pb]], W=[b_yst[s]])
                                else:
                                    K.op("dve", lambda pb=pb, n=n, s=s: nc.vector.tensor_copy(
                                        out=yst[s][0:n, 512:1024], in_=psum[pb][0:n, :]), R=[b_ps[pb]], W=[b_yst[s]])
                            if kind == "ms":
                                K.dma("sp", y_sample, yst[s][0:16, :], R=[b_yst[s]])
                            else:
                                K.dma("sp", y_prompt[r0:r0 + 128, :], yst[s][0:128, :], R=[b_yst[s]])
              K.barrier()
      except _Stop:
        pass
      K.finish()
    return nc


_NC_CACHE = {}


def kernel(x_prompt, x_sample, state_gdn, state_gla, state_conv, meta_tokens, w_in, conv_w, a_log, dt_bias,
           gdn_norm_w, gla_gate_w2, gla_gate_b, gla_norm_w, w_branch_a, w_branch_b, w_out,
           ln1_g, ln1_b, ln2_g, ln2_b, w_ffn_in, w_ffn_out, _build_kwargs=None):
    f = lambda a: np.ascontiguousarray(np.asarray(a), dtype=np.float32)
    x_prompt = f(x_prompt)
    TP = x_prompt.shape[1]
    bk = dict(_build_kwargs or {})
    key = (TP, tuple(sorted(bk.items())))
    if key not in _NC_CACHE:
        _NC_CACHE[key] = build(TP=TP, **bk)
    nc = _NC_CACHE[key]
    shared = dict(meta_tokens=f(meta_tokens), w_in=f(w_in), conv_w=f(conv_w), a_log=f(a_log), dt_bias=f(dt_bias),
                  gdn_norm_w=f(gdn_norm_w), gla_gate_w2=f(gla_gate_w2), gla_gate_b=f(gla_gate_b),
                  gla_norm_w=f(gla_norm_w), w_branch_a=f(w_branch_a), w_branch_b=f(w_branch_b), w_out=f(w_out),
                  ln1_g=f(ln1_g), ln1_b=f(ln1_b), ln2_g=f(ln2_g), ln2_b=f(ln2_b),
                  w_ffn_in=f(w_ffn_in), w_ffn_out=f(w_ffn_out), consts=CONST_ARR)
    x_sample = f(x_sample)
    state_gdn = f(state_gdn)
    state_gla = f(state_gla)
    state_conv = f(state_conv)
    in_maps = []
    for c in range(NCORES):
        sl = slice(NS * c, NS * (c + 1))
        m = dict(shared)
        m["x_prompt"] = x_prompt[c]
        m["x_sample"] = np.ascontiguousarray(x_sample[sl, 0, :])
        m["state_gdn"] = np.ascontiguousarray(state_gdn[:, sl])
        m["state_gla"] = np.ascontiguousarray(state_gla[:, sl])
        m["state_conv"] = np.ascontiguousarray(state_conv[:, sl])
        in_maps.append(m)
    res = run_bass_kernel_spmd(nc, in_maps, core_ids=list(range(NCORES)))
    R = res.results
    y_prompt = np.stack([R[c]["y_prompt"] for c in range(NCORES)], axis=0)
    y_sample = np.concatenate([R[c]["y_sample"] for c in range(NCORES)], axis=0)[:, None, :]
    gdn_p = np.stack([R[c]["new_gdn_prompt"] for c in range(NCORES)], axis=1)
    gla_p = np.stack([R[c]["new_gla_prompt"] for c in range(NCORES)], axis=1)
    conv_p = np.stack([R[c]["new_conv_prompt"] for c in range(NCORES)], axis=1)
    gdn_s = np.concatenate([R[c]["new_gdn_sample"] for c in range(NCORES)], axis=1)
    gla_s = np.concatenate([R[c]["new_gla_sample"] for c in range(NCORES)], axis=1)
    conv_s = np.concatenate([R[c]["new_conv_sample"] for c in range(NCORES)], axis=1)
    outs = (y_prompt, y_sample, gdn_p, gla_p, conv_p, gdn_s, gla_s, conv_s)
    return tuple(np.ascontiguousarray(o, dtype=np.float32) for o in outs)
```

```python
import threading
import numpy as np
from contextlib import ExitStack
import concourse.bass as bass
import concourse.mybir as mybir
from concourse.bass_utils import run_bass_kernel_spmd

F32 = mybir.dt.float32
BF16 = mybir.dt.bfloat16
AF = mybir.ActivationFunctionType
ALU = mybir.AluOpType

D = 1024
KC = 8
DEPTH = 2
NS = 16
NMETA = 16
D_IN = 5656
DFF = 2816
FC = DFF // 128
ALPHA = (2.0 * DEPTH) ** 0.25
NCORES = 8


class Buf:
    __slots__ = ("w", "r", "name", "excl")

    def __init__(self, name="", excl=False):
        self.w = None
        self.r = {}
        self.name = name
        self.excl = excl


_tls = threading.local()


class _Worker:
    def __init__(self, fn):
        self.fn = fn
        self.req = None
        self.done = False
        self.exc = None
        self.ev_req = threading.Event()
        self.ev_go = threading.Event()
        self.th = threading.Thread(target=self._run, daemon=True)

    def _run(self):
        _tls.worker = self
        try:
            self.ev_go.wait()
            self.ev_go.clear()
            r = self.fn()
            if r is not None and hasattr(r, "__next__"):
                for _ in r:
                    pass
        except BaseException as e:
            self.exc = e
        finally:
            self.done = True
            self.req = None
            self.ev_req.set()

    def post(self, req):
        self.req = req
        self.ev_req.set()
        self.ev_go.wait()
        self.ev_go.clear()


class Sched:
    ENG = ("pe", "act", "dve", "pool", "sp")
    COST = {"pe": 0.25, "act": 0.6, "dve": 0.7, "pool": 0.5, "sp": 0.1}

    def __init__(self, nc, es, n_dma_slots=24):
        self.nc = nc
        self.e = {"pe": nc.tensor, "act": nc.scalar, "dve": nc.vector, "pool": nc.gpsimd, "sp": nc.sync}
        self.sem = {}
        self.cnt = {}
        for k in self.ENG:
            self.sem[k] = es.enter_context(nc.semaphore("s_" + k))
            self.cnt[k] = 0
        self.seen = {k: {} for k in self.ENG}
        self.pending = {k: False for k in self.ENG}
        self.nslots = n_dma_slots
        self.slot_sem = [es.enter_context(nc.semaphore("s_dma%d" % i)) for i in range(n_dma_slots)]
        self.slot_uses = [0] * n_dma_slots
        self.slot_next = 0
        self.semobj = dict(self.sem)
        for i in range(n_dma_slots):
            self.semobj[("dma", i)] = self.slot_sem[i]
        self.tfree = {k: 0.0 for k in self.ENG}
        self.tdone = {}

    def _est(self, eng, R, W):
        t = self.tfree[eng]
        def dep(key):
            d = self.tdone.get(key)
            if d is None:
                d = self.tfree.get(key[0], 0.0) if not isinstance(key[0], tuple) else 0.0
            return d + (0.05 if key[0] == eng else 0.3)
        for b in R:
            if b.w is not None:
                t = max(t, dep(b.w))
            if b.excl:
                for k, v in b.r.items():
                    t = max(t, dep((k, v)))
        for b in W:
            if b.w is not None:
                t = max(t, dep(b.w))
            for k, v in b.r.items():
                t = max(t, dep((k, v)))
        return t

    def run_sched(self, threads):
        n = len(threads)
        workers = [None] * n
        started, finished = set(), set()

        def advance(w):
            w.ev_req.clear()
            w.ev_go.set()
            w.ev_req.wait()
            if w.exc is not None:
                raise w.exc

        while len(finished) < n:
            for i, (fn, after) in enumerate(threads):
                if i not in started and all(j in finished for j in after):
                    started.add(i)
                    workers[i] = _Worker(fn)
                    workers[i].th.start()
                    advance(workers[i])
                    if workers[i].done:
                        finished.add(i)
            cands = [i for i in started if i not in finished]
            if not cands:
                if len(finished) < n and len(started) == len(finished):
                    rem = [i for i in range(n) if i not in started]
                    assert any(all(j in finished for j in threads[i][1]) for i in rem), "scheduler deadlock"
                continue
            best = min(cands, key=lambda i: (self._est(*workers[i].req), i))
            advance(workers[best])
            if workers[best].done:
                finished.add(best)

    def _collect(self, eng, R, W, extra=()):
        need = {}
        def add(k, v):
            if v > need.get(k, 0):
                need[k] = v
        for b in R:
            if b.w is not None:
                add(*b.w)
        for b in W:
            if b.w is not None:
                add(*b.w)
            for k, v in b.r.items():
                add(k, v)
        for k, v in extra:
            add(k, v)
        out = []
        for k, v in need.items():
            if k == "pe" and eng == "pe":
                continue
            if k == eng and v > self.cnt[eng]:
                continue
            if v > self.seen[eng].get(k, 0):
                out.append((k, v))
        return out

    def op(self, eng, fn, R=(), W=(), inc=True, extra=(), c=None):
        w_ = getattr(_tls, "worker", None)
        if w_ is not None:
            w_.post((eng, tuple(R), tuple(W)))
        t0_ = self._est(eng, R, W)
        t1_ = t0_ + (c if c is not None else self.COST[eng])
        self.tfree[eng] = t1_
        if any(b.excl for b in R):
            W = list(W) + [b for b in R if b.excl]
            R = [b for b in R if not b.excl]
        waits = self._collect(eng, R, W, extra)
        e = self.e[eng]
        if eng == "pe":
            for k, v in waits:
                e.wait_ge(self.semobj[k], v)
                self.seen[eng][k] = v
            waits = []
        for k, v in waits[:-1]:
            e.wait_ge(self.semobj[k], v)
            self.seen[eng][k] = v
        inst = fn()
        if waits:
            k, v = waits[-1]
            inst.wait_op(self.semobj[k], v, "sem-ge")
            self.seen[eng][k] = v
        if inc:
            inst.then_inc(self.sem[eng], 1)
            self.cnt[eng] += 1
            cc = self.cnt[eng]
            self.pending[eng] = False
            self.tdone[(eng, cc)] = t1_
        else:
            cc = self.cnt[eng] + 1
            self.pending[eng] = True
        for b in R:
            if b.r.get(eng, 0) < cc:
                b.r[eng] = cc
        for b in W:
            b.w = (eng, cc)
            b.r = {}
        return inst

    def dma(self, eng, out, in_, R=(), W=(), **kw):
        w_ = getattr(_tls, "worker", None)
        if w_ is not None:
            w_.post((eng, tuple(R), tuple(W)))
        t0_ = self._est(eng, R, W)
        self.tfree[eng] = t0_ + 0.1
        s = self.slot_next
        self.slot_next = (self.slot_next + 1) % self.nslots
        key = ("dma", s)
        prev = 16 * self.slot_uses[s]
        extra = [(key, prev)] if prev > 0 else []
        waits = self._collect(eng, R, W, extra)
        e = self.e[eng]
        for k, v in waits:
            e.wait_ge(self.semobj[k], v)
            self.seen[eng][k] = v
        inst = e.dma_start(out=out, in_=in_, **kw)
        self.slot_uses[s] += 1
        val = 16 * self.slot_uses[s]
        inst.then_inc(self.slot_sem[s], 16)
        self.tdone[(key, val)] = t0_ + 3.0
        for b in R:
            b.r[key] = val
        for b in W:
            b.w = (key, val)
            b.r = {}
        return inst

    def barrier(self):
        tgt = [(k, self.cnt[k]) for k in self.ENG if self.cnt[k] > 0]
        tgt += [(("dma", i), 16 * self.slot_uses[i]) for i in range(self.nslots) if self.slot_uses[i] > 0]
        for eng in self.ENG:
            assert not self.pending[eng]
            for k, v in tgt:
                if k == eng:
                    continue
                if v > self.seen[eng].get(k, 0):
                    self.e[eng].wait_ge(self.semobj[k], v)
                    self.seen[eng][k] = v

    def finish(self):
        self.barrier()


def make_consts():
    r = np.arange(128)
    same = (r[:, None] // 64) == (r[None, :] // 64)
    c = {}
    c["ident"] = np.eye(128, dtype=np.float32)
    c["ones"] = np.ones((128, 128), dtype=np.float32)
    c["mu_incl"] = (same & (r[None, :] >= r[:, None])).astype(np.float32)
    c["mu_strict"] = (same & (r[None, :] > r[:, None])).astype(np.float32)
    c["ml_strict"] = (same & (r[:, None] > r[None, :])).astype(np.float32)
    c["bd"] = same.astype(np.float32)
    c["blk0"] = np.repeat((r < 64).astype(np.float32)[:, None], 128, axis=1)
    c["blk1"] = np.repeat((r >= 64).astype(np.float32)[:, None], 128, axis=1)
    c["zero"] = np.zeros((128, 128), dtype=np.float32)
    idrep = np.tile(np.eye(16, dtype=np.float32).reshape(1, 256), (128, 1))
    c["idrep0"] = idrep[:, 0:128]
    c["idrep1"] = idrep[:, 128:256]
    names = ["ident", "ones", "mu_incl", "mu_strict", "ml_strict", "bd", "blk0", "blk1", "zero", "idrep0", "idrep1"]
    arr = np.stack([c[n] for n in names], axis=1)
    return names, np.ascontiguousarray(arr.astype(np.float32))


CONST_NAMES, CONST_ARR = make_consts()
NCONST = len(CONST_NAMES)
IDX = {n: i for i, n in enumerate(CONST_NAMES)}


class _Stop(Exception):
    pass


def build(TP=2048, stub_mixer=False, layers=DEPTH, dbg=False, stop=None, mix_sel="all"):
    nc = bass.Bass("TRN2", target_bir_lowering=False)
    NPOS = NMETA + TP
    T = 32 + TP
    NPT = max(TP // 512, 1)
    TILES = [(0, 32)] + [(32 + 512 * i, 512) for i in range(NPT)]

    def din(name, shape, dt=F32):
        return nc.dram_tensor(name, list(shape), dt, kind="ExternalInput").ap()

    def dout(name, shape, dt=F32):
        return nc.dram_tensor(name, list(shape), dt, kind="ExternalOutput").ap()

    x_prompt = din("x_prompt", [TP, D])
    x_sample = din("x_sample", [NS, D])
    meta = din("meta_tokens", [NMETA, D])
    state_gdn = din("state_gdn", [DEPTH, NS, 4, 128, 128])
    state_gla = din("state_gla", [DEPTH, NS, 4, 64, 128])
    state_conv = din("state_conv", [DEPTH, NS, 3, 1536])
    w_in = din("w_in", [DEPTH, D, D_IN])
    conv_w = din("conv_w", [DEPTH, 4, 1536])
    a_log = din("a_log", [DEPTH, 4])
    dt_bias = din("dt_bias", [DEPTH, 4])
    gdn_norm_w = din("gdn_norm_w", [DEPTH, 128])
    gla_gate_w2 = din("gla_gate_w2", [DEPTH, 16, 256])
    gla_gate_b = din("gla_gate_b", [DEPTH, 256])
    gla_norm_w = din("gla_norm_w", [DEPTH, 128])
    w_branch_a = din("w_branch_a", [DEPTH, 512, D])
    w_branch_b = din("w_branch_b", [DEPTH, 512, D])
    w_out = din("w_out", [DEPTH, D, D])
    ln1_g = din("ln1_g", [DEPTH, D])
    ln1_b = din("ln1_b", [DEPTH, D])
    ln2_g = din("ln2_g", [DEPTH, D])
    ln2_b = din("ln2_b", [DEPTH, D])
    w_ffn_in = din("w_ffn_in", [DEPTH, D, 2 * DFF])
    w_ffn_out = din("w_ffn_out", [DEPTH, DFF, D])
    consts_d = din("consts", [128, NCONST, 128])

    y_prompt = dout("y_prompt", [TP, D])
    y_sample = dout("y_sample", [NS, D])
    o_gdn_p = dout("new_gdn_prompt", [DEPTH, 4, 128, 128])
    o_gla_p = dout("new_gla_prompt", [DEPTH, 4, 64, 128])
    o_conv_p = dout("new_conv_prompt", [DEPTH, 3, 1536])
    o_gdn_s = dout("new_gdn_sample", [DEPTH, NS, 4, 128, 128])
    o_gla_s = dout("new_gla_sample", [DEPTH, NS, 4, 64, 128])
    o_conv_s = dout("new_conv_sample", [DEPTH, NS, 3, 1536])
    dbg_out = dout("dbg", [128, KC, T]) if dbg else None

    xres = nc.dram_tensor("xres_scratch", [128, KC, T], F32, kind="Internal").ap()

    es = ExitStack()
    with es:
      K = Sched(nc, es)

      def chk(name):
          if stop == name:
              raise _Stop()

      try:

        _uid = [0]

        def sbt(stack, name, shape, dt=F32):
            _uid[0] += 1
            return stack.enter_context(nc.sbuf_tensor("%s_%d" % (name, _uid[0]), list(shape), dt))

        cst = sbt(es, "cst", [128, NCONST, 128], F32)
        cstb = sbt(es, "cstb", [128, NCONST, 128], BF16)
        b_cst = Buf("cst")
        C = {n: cst[:, i, :] for i, n in enumerate(CONST_NAMES)}
        CB = {n: cstb[:, i, :] for i, n in enumerate(CONST_NAMES)}
        x_bf = sbt(es, "x_bf", [128, KC, T], BF16)
        b_xbf = Buf("x_bf")
        lnp = sbt(es, "lnp", [128, DEPTH, 4, KC], F32)
        b_lnp = Buf("lnp")
        psum = [es.enter_context(nc.psum_tensor("ps%d" % i, [128, 512], F32)) for i in range(8)]
        b_ps = [Buf("ps%d" % i, excl=True) for i in range(8)]
        b_xres = Buf("xres")

        epst = sbt(es, "epst", [128, 2], F32)
        b_eps = Buf("eps")
        K.op("dve", lambda: nc.vector.memset(epst[:, 0:1], 1e-6), W=[b_eps])
        K.op("dve", lambda: nc.vector.memset(epst[:, 1:2], 1.0), W=[b_eps])
        eps6 = epst[:, 0:1]
        one1 = epst[:, 1:2]
        nh_ = (len(TILES) + 1) // 2
        HW = max(sum(n for _, n in TILES[:nh_]), sum(n for _, n in TILES[nh_:]))
        BIGN = max(FC * HW, 8 * T)
        big = sbt(es, "big", [128, BIGN], BF16)
        K.dma("sp", cst[:], consts_d, W=[b_cst])
        K.op("act", lambda: nc.scalar.copy(out=cstb[:], in_=cst[:]), R=[b_cst], W=[b_cst])
        for l in range(DEPTH):
            for wi, src in enumerate((ln1_g, ln1_b, ln2_g, ln2_b)):
                K.dma("sp", lnp[:, l, wi, :], src[l].rearrange("(k p) -> p k", p=128), W=[b_lnp],
                      allow_slow_non_contiguous=True)

        chk("init")

        def run_threads(gens):
            active = list(gens)
            while active:
                for g in list(active):
                    try:
                        next(g)
                    except StopIteration:
                        active.remove(g)

        def run_pipeline(items, mkA, mkB, nA=2, nbuf=3):
            n_it = len(items)
            nextA, nextB = 0, 0
            activeA = {}
            doneA = set()
            curB = None
            while nextB < n_it:
                while nextA < n_it and len(activeA) < nA and (nextA - nextB) < nbuf:
                    activeA[nextA] = mkA(items[nextA], nextA)
                    nextA += 1
                if curB is None and nextB in doneA:
                    curB = mkB(items[nextB], nextB)
                for i, gen in list(activeA.items()):
                    try:
                        next(gen)
                    except StopIteration:
                        del activeA[i]
                        doneA.add(i)
                if curB is not None:
                    try:
                        next(curB)
                    except StopIteration:
                        curB = None
                        nextB += 1

        def mm_group(ps_ap, pairs, R, W, inc=True):
            n = len(pairs)
            for i, (lt, rh) in enumerate(pairs):
                last = (i == n - 1)
                K.op("pe", lambda lt=lt, rh=rh, i=i, last=last: nc.tensor.matmul(
                    ps_ap, lhsT=lt, rhs=rh, start=(i == 0), stop=last),
                    R=R, W=W, inc=(inc and last))

        with ExitStack() as p0:
            NR = 3
            xin = [sbt(p0, "xin%d" % i, [128, D]) for i in range(NR)]
            b_xin = [Buf() for _ in range(NR)]
            xst = [sbt(p0, "xst%d" % i, [128, KC, 128]) for i in range(NR)]
            b_xst = [Buf() for _ in range(NR)]
            rows = [("ms", 0, 32)] + [("p", 128 * i, 128) for i in range(TP // 128)]

            def p0_tile(ri):
                kind, r0, n = rows[ri]
                s = ri % NR
                if kind == "ms":
                    K.dma("sp", xin[s][0:16, :], x_sample, W=[b_xin[s]])
                    K.dma("sp", xin[s][16:32, :], meta, W=[b_xin[s]])
                    c0 = 0
                else:
                    K.dma("sp", xin[s][0:128, :], x_prompt[r0:r0 + 128, :], W=[b_xin[s]])
                    c0 = 32 + r0
                for g in range(2):
                    pb = (2 * ri + g) % 8
                    for kk in range(4):
                        k = 4 * g + kk
                        K.op("pe", lambda k=k, kk=kk, pb=pb: nc.tensor.transpose(
                            psum[pb][:, kk * 128:kk * 128 + n], xin[s][0:n, k * 128:(k + 1) * 128],
                            C["ident"][0:n, 0:n]), R=[b_xin[s], b_cst], W=[b_ps[pb]], inc=(kk == 3), c=0.12)
                    src = psum[pb][:, :].rearrange("p (a b) -> p a b", a=4)[:, :, 0:n]
                    K.op("act", lambda src=src, g=g: nc.scalar.copy(
                        out=x_bf[:, 4 * g:4 * g + 4, c0:c0 + n], in_=src), R=[b_ps[pb]], W=[b_xbf])
                    K.op("dve", lambda src=src, g=g: nc.vector.tensor_copy(
                        out=xst[s][:, 4 * g:4 * g + 4, 0:n], in_=src), R=[b_ps[pb]], W=[b_xst[s]])
                K.dma("sp", xres[:, :, c0:c0 + n], xst[s][:, :, 0:n], R=[b_xst[s]])

            K.run_sched([(lambda ri=ri: p0_tile(ri), ([ri - NR] if ri >= NR else [])) for ri in range(len(rows))])
        K.barrier()
        chk("p0")

        def layer_norm(stack, v, b_v, l, which, write_bf=True):
            g_i, b_i = 2 * which, 2 * which + 1
            with ExitStack() as st:
                vb_ = [sbt(st, "ln_vb%d" % i, [128, KC, 512], BF16) for i in range(2)]
                sq_ = [sbt(st, "ln_sq%d" % i, [128, KC, 512], BF16) for i in range(2)]
                mt_ = [sbt(st, "ln_m%d" % i, [128, 512]) for i in range(2)]
                m2_ = [sbt(st, "ln_m2%d" % i, [128, 512]) for i in range(2)]
                rs_ = [sbt(st, "ln_rs%d" % i, [128, 512]) for i in range(2)]
                nm_ = [sbt(st, "ln_nm%d" % i, [128, 512]) for i in range(2)]
                bb = [[Buf() for _ in range(6)] for _ in range(2)]
                b_vts = []
                for _ in TILES:
                    t_ = Buf()
                    t_.w = b_v.w
                    t_.r = dict(b_v.r)
                    b_vts.append(t_)

                def ln_tile(ti):
                    c0, n = TILES[ti]
                    u_ = ti % 2
                    vb, sq, mt, m2, rs, nm = vb_[u_], sq_[u_], mt_[u_], m2_[u_], rs_[u_], nm_[u_]
                    b_vb, b_sq, b_mt, b_m2, b_rs, b_nm = bb[u_]
                    b_v = b_vts[ti]
                    cols = slice(c0, c0 + n)
                    pm_, pq_ = (2 * ti) % 8, (2 * ti + 1) % 8
                    K.op("act", lambda cols=cols, n=n: nc.scalar.copy(out=vb[:, :, 0:n], in_=v[:, :, cols]),
                         R=[b_v], W=[b_vb])
                    K.op("act", lambda cols=cols, n=n: nc.scalar.activation(
                        out=sq[:, :, 0:n], in_=v[:, :, cols], func=AF.Square), R=[b_v], W=[b_sq])
                    mm_group(psum[pm_][:, 0:n], [(CB["ones"], vb[:, k, 0:n]) for k in range(KC)],
                             R=[b_vb, b_cst], W=[b_ps[pm_]])
                    mm_group(psum[pq_][:, 0:n], [(CB["ones"], sq[:, k, 0:n]) for k in range(KC)],
                             R=[b_sq, b_cst], W=[b_ps[pq_]])
                    K.op("dve", lambda n=n, pm_=pm_: nc.vector.tensor_scalar(
                        out=mt[:, 0:n], in0=psum[pm_][:, 0:n], scalar1=1.0 / D, scalar2=None, op0=ALU.mult),
                        R=[b_ps[pm_]], W=[b_mt])
                    K.op("dve", lambda n=n: nc.vector.tensor_tensor(
                        out=m2[:, 0:n], in0=mt[:, 0:n], in1=mt[:, 0:n], op=ALU.mult), R=[b_mt], W=[b_m2])
                    K.op("dve", lambda n=n, pq_=pq_: nc.vector.scalar_tensor_tensor(
                        out=m2[:, 0:n], in0=psum[pq_][:, 0:n], scalar=1.0 / D, in1=m2[:, 0:n],
                        op0=ALU.mult, op1=ALU.subtract), R=[b_ps[pq_], b_m2], W=[b_m2])
                    K.op("dve", lambda n=n: nc.vector.tensor_scalar(
                        out=m2[:, 0:n], in0=m2[:, 0:n], scalar1=1e-5, scalar2=None, op0=ALU.add),
                        R=[b_m2], W=[b_m2])
                    K.op("act", lambda n=n: nc.scalar.activation(
                        out=rs[:, 0:n], in_=m2[:, 0:n], func=AF.Ln), R=[b_m2], W=[b_rs])
                    K.op("act", lambda n=n: nc.scalar.activation(
                        out=rs[:, 0:n], in_=rs[:, 0:n], func=AF.Exp, scale=-0.5), R=[b_rs], W=[b_rs])
                    K.op("dve", lambda n=n: nc.vector.scalar_tensor_tensor(
                        out=nm[:, 0:n], in0=mt[:, 0:n], scalar=-1.0, in1=rs[:, 0:n],
                        op0=ALU.mult, op1=ALU.mult), R=[b_mt, b_rs], W=[b_nm])
                    K.op("dve", lambda n=n, cols=cols: nc.vector.tensor_tensor(
                        out=v[:, :, cols], in0=v[:, :, cols],
                        in1=rs[:, 0:n].unsqueeze(1).broadcast_to([128, KC, n]), op=ALU.mult),
                        R=[b_rs], W=[b_v])
                    K.op("dve", lambda n=n, cols=cols: nc.vector.tensor_tensor(
                        out=v[:, :, cols], in0=v[:, :, cols],
                        in1=nm[:, 0:n].unsqueeze(1).broadcast_to([128, KC, n]), op=ALU.add),
                        R=[b_nm], W=[b_v])
                    for k in range(KC):
                        K.op("act", lambda k=k, n=n, cols=cols: nc.scalar.activation(
                            out=v[:, k, cols], in_=v[:, k, cols], func=AF.Identity,
                            scale=lnp[:, l, g_i, k:k + 1], bias=lnp[:, l, b_i, k:k + 1]),
                            R=[b_lnp], W=[b_v])
                    if write_bf:
                        K.op("pool", lambda cols=cols: nc.gpsimd.tensor_copy(out=x_bf[:, :, cols], in_=v[:, :, cols]),
                             R=[b_v], W=[b_xbf])

                K.run_sched([(lambda ti=ti: ln_tile(ti), ([ti - 2] if ti >= 2 else [])) for ti in range(len(TILES))])

        def subtiles(c0, n):
            if c0 == 0:
                return [("sample", 0, 16), ("meta", 16, 16)]
            return [("prompt", c0 + 128 * i, 128) for i in range(n // 128)]

        def gated_norm(st, oB, b_oB, gz, b_gz, nw, b_nw, og, b_og, hbase, c0, n, tagbufs):
            sq, b_sq, rsd, b_rsd, tn, b_tn = tagbufs
            K.op("act", lambda: nc.scalar.activation(out=sq[:, :, 0:n], in_=oB[:, :, 0:n], func=AF.Square),
                 R=[b_oB], W=[b_sq])
            for h in range(4):
                q = h % 2
                mm_group(psum[q][:, 0:n], [(CB["ones"], sq[:, h, 0:n])], R=[b_sq, b_cst], W=[b_ps[q]])
                K.op("act", lambda q=q: nc.scalar.activation(out=rsd[:, 0:n], in_=psum[q][:, 0:n], func=AF.Ln,
                                                             scale=1.0 / 128, bias=eps6[:, 0:1]),
                     R=[b_ps[q], b_eps], W=[b_rsd])
                K.op("act", lambda: nc.scalar.activation(out=rsd[:, 0:n], in_=rsd[:, 0:n], func=AF.Exp, scale=-0.5),
                     R=[b_rsd], W=[b_rsd])
                K.op("dve", lambda h=h: nc.vector.tensor_tensor(out=tn[:, 0:n], in0=oB[:, h, 0:n], in1=rsd[:, 0:n],
                                                                op=ALU.mult), R=[b_oB, b_rsd], W=[b_tn])
                K.op("dve", lambda h=h: nc.vector.scalar_tensor_tensor(
                    out=og[:, hbase + h, c0:c0 + n], in0=tn[:, 0:n], scalar=nw[:, 0:1], in1=gz[:, h, 0:n],
                    op0=ALU.mult, op1=ALU.mult), R=[b_tn, b_nw, b_gz], W=[b_og])

        def gla_phase(l, og, b_og, win_v):
            with ExitStack() as st:
                wB = sbt(st, "wB", [128, KC, 1552], BF16)
                b_wB = Buf("wB")
                for k in range(KC):
                    K.dma("pool", wB[:, k, :], win_v[:, k, 2056:3608], W=[b_wB])
                w2e = sbt(st, "w2e", [17, 256])
                b_w2e = Buf()
                K.dma("sp", w2e[0:16, :], gla_gate_w2[l], W=[b_w2e])
                K.dma("sp", w2e[16:17, :], gla_gate_b[l].rearrange("(o n) -> o n", o=1), W=[b_w2e])
                nw = sbt(st, "gla_nw", [128, 1])
                b_nw = Buf()
                K.dma("sp", nw[:, :], gla_norm_w[l].rearrange("(p o) -> p o", o=1), W=[b_nw],
                      allow_slow_non_contiguous=True)
                qk = sbt(st, "g_qk", [128, 4, 512])
                b_qk = Buf()
                sr = sbt(st, "g_sr", [128, 4, 512], BF16)
                b_sr = Buf()
                glr = sbt(st, "g_glr", [17, 512])
                b_glr = Buf()
                oB = sbt(st, "g_oB", [128, 4, 512])
                b_oB = Buf()
                sq = sbt(st, "g_sq", [128, 4, 512], BF16)
                rsd = sbt(st, "g_rsd", [128, 512])
                tn = sbt(st, "g_tn", [128, 512])
                nbufs = (sq, Buf(), rsd, Buf(), tn, Buf())
                Lt = [sbt(st, "g_L%d" % i, [128, 256]) for i in range(3)]
                E12 = [sbt(st, "g_E%d" % i, [128, 2, 2, 128]) for i in range(3)]
                atot = [sbt(st, "g_at%d" % i, [128, 2, 2]) for i in range(3)]
                qdk = [sbt(st, "g_qdk%d" % i, [128, 2, 2, 128], BF16) for i in range(3)]
                qdm = [sbt(st, "g_qdm%d" % i, [128, 4, 128], BF16) for i in range(3)]
                b_qdm = [Buf(), Buf(), Buf()]
                qm = sbt(st, "g_qm", [128, 4, 18])
                b_qm = Buf()
                E3 = [sbt(st, "g_E3%d" % i, [128, 256]) for i in range(3)]
                kdec = [sbt(st, "g_kd%d" % i, [128, 256], BF16) for i in range(3)]
                vtb = [sbt(st, "g_vt%d" % i, [128, 512], BF16) for i in range(3)]
                att = [sbt(st, "g_att%d" % i, [128, 4, 128], BF16) for i in range(3)]
                b_L, b_E12, b_at, b_qdk, b_E3, b_kd, b_vt, b_att = ([Buf(), Buf(), Buf()] for _ in range(8))
                S = sbt(st, "g_S", [128, 2, 128])
                Sb = sbt(st, "g_Sb", [128, 2, 128], BF16)
                b_S, b_Sb = Buf(), Buf()
                K.op("dve", lambda: nc.vector.memset(S[:], 0.0), W=[b_S])
                K.op("dve", lambda: nc.vector.memset(Sb[:], 0.0), W=[b_Sb])
                K.op("dve", lambda: nc.vector.memset(glr[:], 1.0), W=[b_glr])
                Ss = sbt(st, "g_Ss", [128, NS, 2, 128])
                b_Ss = Buf()
                sgl_v = state_gla[l].rearrange("s (pr hf) k v -> (hf k) s pr v", hf=2)
                for s4 in range(4):
                    K.dma("sp", Ss[:, 4 * s4:4 * s4 + 4], sgl_v[:, 4 * s4:4 * s4 + 4], W=[b_Ss])
                vbd = sbt(st, "g_vbd", [16, NS, 512], BF16)
                b_vbd = Buf()
                sub_ctr = [0]
                def glaA(kind, sc0, nt, u, bx, by, c0):
                    lc = sc0 - c0
                    if kind == "sample":
                        m_incl, m_strict_l = C["ident"], C["zero"]
                        mb_incl = CB["ident"]
                    else:
                        m_incl, m_strict_l = C["mu_incl"], C["ml_strict"]
                        mb_incl = CB["mu_incl"]
                    yield
                    mm_group(psum[bx][0:nt, 0:256], [(x_bf[:, k, sc0:sc0 + nt], wB[:, k, 256:512]) for k in range(KC)],
                             R=[b_wB, b_xbf], W=[b_ps[bx]])
                    yield
                    mm_group(psum[by][0:nt, 0:512], [(x_bf[:, k, sc0:sc0 + nt], wB[:, k, 512:1024]) for k in range(KC)],
                             R=[b_wB, b_xbf], W=[b_ps[by]])
                    yield
                    mm_group(psum[bx][0:nt, 256:512], [(glr[0:17, lc:lc + nt], w2e[0:17, :])],
                             R=[b_glr, b_w2e], W=[b_ps[bx]])
                    yield
                    K.op("act", lambda u=u: nc.scalar.activation(out=Lt[u][0:nt, :], in_=psum[bx][0:nt, 256:512],
                                                                 func=AF.Exp, scale=-1.0), R=[b_ps[bx]], W=[b_L[u]])
                    yield
                    K.op("act", lambda u=u: nc.scalar.activation(out=Lt[u][0:nt, :], in_=Lt[u][0:nt, :], func=AF.Ln,
                                                                 bias=one1[0:nt, 0:1]), R=[b_L[u], b_eps], W=[b_L[u]])
                    yield
                    K.op("act", lambda u=u: nc.scalar.copy(out=vtb[u][0:nt, :], in_=psum[by][0:nt, :]),
                         R=[b_ps[by]], W=[b_vt[u]])
                    yield
                    for pr_ in range(2):
                        mm_group(psum[by][:, pr_ * 128:pr_ * 128 + nt],
                                 [(Lt[u][0:nt, pr_ * 128:(pr_ + 1) * 128], m_incl[0:nt, 0:nt])],
                                 R=[b_L[u], b_cst], W=[b_ps[by]])
                    yield
                    mm_group(psum[by][0:nt, 256:512], [(m_strict_l[0:nt, 0:nt], Lt[u][0:nt, :])],
                             R=[b_L[u], b_cst], W=[b_ps[by]])
                    csT = psum[by][:, 0:256].rearrange("p (a b) -> p a b", a=2)[:, :, 0:nt]
                    yield
                    K.op("act", lambda u=u, csT=csT: nc.scalar.activation(out=E12[u][:, 0, :, 0:nt], in_=csT,
                                                                          func=AF.Exp, scale=-1.0 / 16),
                         R=[b_ps[by]], W=[b_E12[u]])
                    yield
                    K.op("act", lambda u=u, csT=csT: nc.scalar.activation(out=E12[u][:, 1, :, 0:nt], in_=csT,
                                                                          func=AF.Exp, scale=1.0 / 16),
                         R=[b_ps[by]], W=[b_E12[u]])
                    nch = 2 if nt == 128 else 1
                    cl = 64 if nt == 128 else nt
                    yield
                    for ch in range(nch):
                        lastc = ch * 64 + cl - 1
                        K.op("act", lambda u=u, ch=ch, lastc=lastc: nc.scalar.activation(
                            out=atot[u][:, :, ch:ch + 1],
                            in_=psum[by][:, 0:256].rearrange("p (a b) -> p a b", a=2)[:, :, lastc:lastc + 1],
                            func=AF.Exp, scale=-1.0 / 16), R=[b_ps[by]], W=[b_at[u]])
                    yield
                    K.op("act", lambda u=u: nc.scalar.activation(out=E3[u][0:nt, :], in_=psum[by][0:nt, 256:512],
                                                                 func=AF.Exp, scale=-1.0 / 16),
                         R=[b_ps[by]], W=[b_E3[u]])
                    yield
                    K.op("dve", lambda u=u, lc=lc: nc.vector.scalar_tensor_tensor(
                        out=qdk[u][:, 0, :, 0:nt], in0=qk[:, 0:2, lc:lc + nt], scalar=0.125, in1=E12[u][:, 0, :, 0:nt],
                        op0=ALU.mult, op1=ALU.mult), R=[b_qk, b_E12[u]], W=[b_qdk[u]])
                    yield
                    K.op("dve", lambda u=u, lc=lc: nc.vector.tensor_tensor(
                        out=qdk[u][:, 1, :, 0:nt], in0=qk[:, 2:4, lc:lc + nt], in1=E12[u][:, 1, :, 0:nt], op=ALU.mult),
                        R=[b_qk, b_E12[u]], W=[b_qdk[u]])
                    yield
                    K.op("dve", lambda u=u: nc.vector.tensor_tensor(
                        out=kdec[u][0:nt, :], in0=psum[bx][0:nt, 0:256], in1=E3[u][0:nt, :], op=ALU.mult),
                        R=[b_ps[bx], b_E3[u]], W=[b_kd[u]])
                    yield
                    if kind == "sample":
                        for h in range(4):
                            K.op("dve", lambda h=h: nc.vector.tensor_scalar(
                                out=qm[:, h, :], in0=qk[:, h // 2, 0:18], scalar1=C["blk%d" % (h % 2)][:, 0:1],
                                scalar2=None, op0=ALU.mult), R=[b_qk, b_cst], W=[b_qm])
                        gla_sample(l, u, qm, b_qm, E12, b_E12, kdec, b_kd, vtb, b_vt, Ss, b_Ss, vbd, b_vbd, oB, b_oB)
                        return
                    yield
                    for h in range(4):
                        K.op("dve", lambda h=h, u=u: nc.vector.tensor_scalar(
                            out=qdm[u][:, h, 0:nt], in0=qdk[u][:, 0, h // 2, 0:nt], scalar1=C["blk%d" % (h % 2)][:, 0:1],
                            scalar2=None, op0=ALU.mult), R=[b_qdk[u], b_cst], W=[b_qdm[u]])
                    yield
                    for h in range(4):
                        pr_, hf = h // 2, h % 2
                        rows = slice(hf * 64, hf * 64 + 64)
                        mm_group(psum[bx][0:nt, h * 128:h * 128 + nt],
                                 [(qdk[u][:, 1, pr_, 0:nt], qdm[u][:, h, 0:nt])],
                                 R=[b_qdk[u], b_qdm[u]], W=[b_ps[bx]], inc=(h == 3))
                    yield
                    K.op("dve", lambda u=u: nc.vector.tensor_tensor(
                        out=att[u][0:nt, :, 0:nt],
                        in0=psum[bx][0:nt, :].rearrange("p (a b) -> p a b", a=4)[:, :, 0:nt],
                        in1=m_incl[0:nt, 0:nt].unsqueeze(1).broadcast_to([nt, 4, nt]), op=ALU.mult),
                        R=[b_ps[bx], b_cst], W=[b_att[u]])

                def glaB(kind, sc0, nt, u, c0):
                    lc = sc0 - c0
                    nch = 2 if nt == 128 else 1
                    cl = 64 if nt == 128 else nt
                    for ch in range(nch):
                        trow = slice(ch * 64, ch * 64 + cl)
                        yield
                        for h in range(4):
                            pr_, hf = h // 2, h % 2
                            rows = slice(hf * 64, hf * 64 + 64)
                            K.op("pe", lambda h=h, pr_=pr_, trow=trow, u=u: nc.tensor.matmul(
                                psum[6][:, h * 64:h * 64 + cl], lhsT=Sb[:, pr_, :], rhs=qdm[u][:, h, trow],
                                start=True, stop=False), R=[b_Sb, b_qdm[u]], W=[b_ps[6]], inc=False)
                            K.op("pe", lambda h=h, trow=trow, u=u: nc.tensor.matmul(
                                psum[6][:, h * 64:h * 64 + cl], lhsT=vtb[u][0:nt, h * 128:(h + 1) * 128],
                                rhs=att[u][0:nt, h, trow], start=False, stop=True),
                                R=[b_vt[u], b_att[u]], W=[b_ps[6]], inc=(h == 3))
                        yield
                        for h in range(4):
                            pr_, hf = h // 2, h % 2
                            K.op("pe", lambda h=h, pr_=pr_, hf=hf, trow=trow, u=u: nc.tensor.matmul(
                                psum[7][hf * 64:hf * 64 + 64, pr_ * 128:(pr_ + 1) * 128],
                                lhsT=kdec[u][trow, h * 64:(h + 1) * 64], rhs=vtb[u][trow, h * 128:(h + 1) * 128],
                                start=True, stop=True), R=[b_kd[u], b_vt[u]], W=[b_ps[7]], inc=(h == 3))
                        yield
                        K.op("act", lambda lc=lc, trow=trow, ch=ch: nc.scalar.copy(
                            out=oB[:, :, lc + ch * 64:lc + ch * 64 + cl],
                            in_=psum[6][:, 0:256].rearrange("p (a b) -> p a b", a=4)[:, :, 0:cl]),
                            R=[b_ps[6]], W=[b_oB])
                        yield
                        for pr_ in range(2):
                            K.op("dve", lambda pr_=pr_, ch=ch, u=u: nc.vector.scalar_tensor_tensor(
                                out=S[:, pr_, :], in0=S[:, pr_, :], scalar=atot[u][:, pr_, ch:ch + 1],
                                in1=psum[7][:, pr_ * 128:(pr_ + 1) * 128], op0=ALU.mult, op1=ALU.add),
                                R=[b_at[u], b_ps[7]], W=[b_S])
                        yield
                        K.op("act", lambda: nc.scalar.copy(out=Sb[:], in_=S[:]), R=[b_S], W=[b_Sb])

                    yield

                for (c0, n) in TILES:
                    for bi in range(9):
                        q = bi % 2
                        if bi < 8:
                            woff = bi * 128 if bi < 4 else 1024 + (bi - 4) * 128
                            mw = 128
                        else:
                            woff, mw = 1536, 16
                        mm_group(psum[q][0:mw, 0:n], [(wB[:, k, woff:woff + mw], x_bf[:, k, c0:c0 + n]) for k in range(KC)],
                                 R=[b_wB, b_xbf], W=[b_ps[q]])
                        if bi < 4:
                            K.op("dve", lambda bi=bi, q=q: nc.vector.tensor_copy(out=qk[:, bi, 0:n], in_=psum[q][:, 0:n]),
                                 R=[b_ps[q]], W=[b_qk])
                        elif bi < 8:
                            K.op("act", lambda bi=bi, q=q: nc.scalar.activation(
                                out=sr[:, bi - 4, 0:n], in_=psum[q][:, 0:n], func=AF.Silu), R=[b_ps[q]], W=[b_sr])
                        else:
                            K.op("dve", lambda q=q: nc.vector.tensor_copy(out=glr[0:16, 0:n], in_=psum[q][0:16, 0:n]),
                                 R=[b_ps[q]], W=[b_glr])
                    subs = subtiles(c0, n)
                    ths = []
                    for i_, (kind, sc0, nt) in enumerate(subs):
                        gi = sub_ctr[0] + i_
                        afterA = ([2 * (i_ - 2)] if i_ >= 2 else []) + ([2 * (i_ - 3) + 1] if i_ >= 3 else [])
                        afterB = [2 * i_] + ([2 * (i_ - 1) + 1] if i_ >= 1 else [])
                        ths.append((lambda kind=kind, sc0=sc0, nt=nt, gi=gi, c0=c0: glaA(kind, sc0, nt, gi % 3, 2 + 2 * (gi % 2), 3 + 2 * (gi % 2), c0), afterA))
                        if kind == "sample":
                            ths.append((lambda: None, afterB))
                        else:
                            ths.append((lambda kind=kind, sc0=sc0, nt=nt, gi=gi, c0=c0: glaB(kind, sc0, nt, gi % 3, c0), afterB))
                    K.run_sched(ths)
                    sub_ctr[0] += len(subs)
                    gated_norm(st, oB, b_oB, sr, b_sr, nw, b_nw, og, b_og, 4, c0, n, nbufs)
                K.dma("sp", o_gla_p[l].rearrange("(pr hf) k v -> (hf k) pr v", hf=2), S[:], R=[b_S])

        def gla_sample(l, u, qm, b_qm, E12, b_E12, kdec, b_kd, vtb, b_vt, Ss, b_Ss, vbd, b_vbd, oB, b_oB):
            K.op("dve", lambda: nc.vector.tensor_tensor(
                out=vbd[:, :, :], in0=vtb[u][0:16, :].unsqueeze(1).broadcast_to([16, NS, 512]),
                in1=C["ident"][0:16, 0:16].unsqueeze(2).broadcast_to([16, NS, 512]), op=ALU.mult),
                R=[b_vt[u], b_cst], W=[b_vbd])
            for s4 in range(4):
                for si in range(4):
                    s = 4 * s4 + si
                    for h in range(4):
                        pr_, hf = h // 2, h % 2
                        q = 6 + si // 2
                        cb = (si % 2) * 256 + pr_ * 128
                        K.op("pe", lambda s=s, h=h, hf=hf, q=q, cb=cb: nc.tensor.matmul(
                            psum[q][hf * 64:hf * 64 + 64, cb:cb + 128],
                            lhsT=kdec[u][0:16, h * 64:(h + 1) * 64], rhs=vbd[0:16, s, h * 128:(h + 1) * 128],
                            start=True, stop=True), R=[b_kd[u], b_vbd], W=[b_ps[q]], inc=(h == 3))
                for si in range(4):
                    s = 4 * s4 + si
                    q = 6 + si // 2
                    for pr_ in range(2):
                        cb = (si % 2) * 256 + pr_ * 128
                        K.op("dve", lambda s=s, pr_=pr_, q=q, cb=cb: nc.vector.scalar_tensor_tensor(
                            out=Ss[:, s, pr_, :], in0=Ss[:, s, pr_, :], scalar=E12[u][:, 0, pr_, s:s + 1],
                            in1=psum[q][:, cb:cb + 128], op0=ALU.mult, op1=ALU.add),
                            R=[b_E12[u], b_ps[q]], W=[b_Ss])
            for s in range(NS):
                for h in range(4):
                    pr_, hf = h // 2, h % 2
                    K.op("pe", lambda s=s, h=h, pr_=pr_: nc.tensor.matmul(
                        psum[5][:, h * 32 + s:h * 32 + s + 2], lhsT=Ss[:, s, pr_, :], rhs=qm[:, h, s:s + 2],
                        start=True, stop=True), R=[b_Ss, b_qm], W=[b_ps[5]], inc=(s == NS - 1 and h == 3))
            K.op("act", lambda: nc.scalar.activation(
                out=oB[:, :, 0:16], in_=psum[5][:, 0:128].rearrange("p (a b) -> p a b", a=4)[:, :, 0:16],
                func=AF.Copy, scale=0.125), R=[b_ps[5]], W=[b_oB])
            sgl_o = o_gla_s[l].rearrange("s (pr hf) k v -> (hf k) s pr v", hf=2)
            for s4 in range(4):
                K.dma("sp", sgl_o[:, 4 * s4:4 * s4 + 4], Ss[:, 4 * s4:4 * s4 + 4], R=[b_Ss])

        def gdn_phase(l, og, b_og, win_v):
            keep = {}
            with ExitStack() as st0:
                s_wT = sbt(st0, "d_swT", [128, 4, 16], BF16)
                s_u0 = sbt(st0, "d_su0", [16, 4, 128])
                s_kd = sbt(st0, "d_skd", [16, 4, 128], BF16)
                s_qn = sbt(st0, "d_sqn", [128, 4, 16], BF16)
                s_G = sbt(st0, "d_sG", [128, NS, 4])
                s_gz = sbt(st0, "d_sgz", [128, 4, 16], BF16)
                b_skeep = Buf()
                nw = sbt(st0, "gdn_nw", [128, 1])
                b_nw = Buf()
                K.dma("sp", nw[:, :], gdn_norm_w[l].rearrange("(p o) -> p o", o=1), W=[b_nw],
                      allow_slow_non_contiguous=True)
                with ExitStack() as st:
                    wA = sbt(st, "wA", [128, KC, 2056], BF16)
                    b_wA = Buf("wA")
                    for k in range(KC):
                        K.dma("pool", wA[:, k, :], win_v[:, k, 0:2056], W=[b_wA])
                    cw = sbt(st, "d_cw", [128, 12, 4])
                    b_cw = Buf()
                    for i_ in range(4):
                        K.dma("sp", cw[:, :, i_], conv_w[l][i_].rearrange("(b p) -> p b", p=128), W=[b_cw],
                              allow_slow_non_contiguous=True)
                    abt = sbt(st, "d_abt", [128, 2, 4])
                    b_abt = Buf()
                    K.dma("sp", abt[:, 0, :], a_log[l].partition_broadcast(128), W=[b_abt])
                    K.dma("sp", abt[:, 1, :], dt_bias[l].partition_broadcast(128), W=[b_abt])
                    K.op("act", lambda: nc.scalar.activation(out=abt[:, 0, :], in_=abt[:, 0, :], func=AF.Exp),
                         R=[b_abt], W=[b_abt])
                    K.op("dve", lambda: nc.vector.tensor_scalar(out=abt[:, 0, :], in0=abt[:, 0, :], scalar1=-1.0,
                                                                scalar2=None, op0=ALU.mult), R=[b_abt], W=[b_abt])
                    cbT = sbt(st, "d_cbT", [128, 12, 48])
                    b_cbT = Buf()
                    with ExitStack() as stc:
                        cb48 = sbt(stc, "d_cb48", [48, 1536])
                        b_cb48 = Buf()
                        K.dma("sp", cb48[:], state_conv[l].rearrange("s i c -> (s i) c"), W=[b_cb48])
                        for half in range(2):
                            q = half
                            for b6 in range(6):
                                blk = 6 * half + b6
                                K.op("pe", lambda blk=blk, b6=b6, q=q: nc.tensor.transpose(
                                    psum[q][:, b6 * 48:(b6 + 1) * 48], cb48[0:48, blk * 128:(blk + 1) * 128],
                                    C["ident"][0:48, 0:48]), R=[b_cb48, b_cst], W=[b_ps[q]], inc=(b6 == 5))
                            K.op("act", lambda half=half, q=q: nc.scalar.copy(
                                out=cbT[:, 6 * half:6 * half + 6, :],
                                in_=psum[q][:, 0:288].rearrange("p (a b) -> p a b", a=6)), R=[b_ps[q]], W=[b_cbT])
                        K.barrier()
                    K.dma("sp", o_conv_s[l][:, 0:2, :], state_conv[l][:, 1:3, :])
                    halo = sbt(st, "d_halo", [128, 12, 4])
                    b_halo = Buf()
                    K.op("dve", lambda: nc.vector.memset(halo[:], 0.0), W=[b_halo])
                    Pe = [sbt(st, "d_Pe%d" % i, [128, 3 + 512]) for i in range(2)]
                    b_Pe = [Buf(), Buf()]
                    acc = [sbt(st, "d_acc%d" % i, [128, 512]) for i in range(2)]
                    b_acc = [Buf(), Buf()]
                    cq = [sbt(st, "d_cq%d" % i, [128, 512]) for i in range(2)]
                    b_cq = [Buf(), Buf()]
                    vn = sbt(st, "d_vn", [128, 4, 512], BF16)
                    b_vn = Buf()
                    qkn = sbt(st, "d_qkn", [128, 8, 512], BF16)
                    b_qkn = Buf()
                    gz = sbt(st, "d_gz", [128, 4, 512], BF16)
                    b_gz = Buf()
                    oA = big[:, 8 * T:8 * T + 4096].bitcast(F32).rearrange("p (a b) -> p a b", a=4)
                    b_oA = Buf()
                    sq = big[:, 8 * T + 4096:8 * T + 6144].rearrange("p (a b) -> p a b", a=4)
                    rsd = sbt(st, "d_rsd", [128, 512])
                    rsd_b = sbt(st, "d_rsdb", [128, 512])
                    tn = sbt(st, "d_tn", [128, 512])
                    b_sq, b_rsd, b_tn, b_rsdb = Buf(), Buf(), Buf(), Buf()
                    nbufs = (sq, b_sq, rsd, b_rsd, tn, b_tn)
                    pnew = big[0:16, 8 * T:8 * T + 3072].bitcast(F32)
                    b_pnew = b_oA
                    tb = [sbt(st, "d_tb%d" % i, [128, 6, 4]) for i in range(2)]
                    b_tb = [Buf(), Buf()]
                    r2 = sbt(st, "d_r2", [128, 8])
                    b_r2 = Buf()
                    Gbc = [sbt(st, "d_Gbc%d" % i, [128, 2, 4]) for i in range(2)]
                    b_Gbc = [Buf(), Buf()]
                    dg = sbt(st, "d_dg", [128, 8, 128])
                    b_dg = Buf()
                    gm = sbt(st, "d_gm", [128, 4, 128])
                    b_gm = Buf()
                    ET = sbt(st, "d_ET", [128, 4, 128])
                    ETi = sbt(st, "d_ETi", [128, 4, 128])
                    ETs = ET
                    b_ET, b_ETi = Buf(), Buf()
                    b_ETs = b_ET
                    QA = [sbt(st, "d_QA%d" % i, [128, 4, 128]) for i in range(2)]
                    QTA = [sbt(st, "d_QTA%d" % i, [128, 4, 128]) for i in range(2)]
                    b_QA = [Buf(), Buf()]
                    b_QTA = [Buf(), Buf()]
                    Qs = sbt(st, "d_Qs", [128, 4, 128])
                    QTs = sbt(st, "d_QTs", [128, 4, 128])
                    b_Qs, b_QTs = Buf(), Buf()
                    TT = [sbt(st, "d_TT%d" % i, [128, 4, 128]) for i in range(2)]
                    b_TT = [Buf(), Buf()]
                    TTb = sbt(st, "d_TTb", [128, 4, 128], BF16)
                    b_TTb = Buf()
                    bkq = [sbt(st, "d_bkq%d" % i, [128, 2, 4, 128], BF16) for i in range(2)]
                    b_bkq = [Buf(), Buf()]
                    qkT = [sbt(st, "d_qkT%d" % i, [128, 4, 128], BF16) for i in range(2)]
                    b_qkT = [Buf(), Buf()]
                    tok3 = sbt(st, "d_tok3", [128, 3, 4, 128], BF16)
                    b_tok3 = Buf()
                    u0 = sbt(st, "d_u0", [128, 4, 128])
                    wT = sbt(st, "d_wT", [128, 4, 128], BF16)
                    uu = sbt(st, "d_u", [128, 4, 128], BF16)
                    b_u0, b_wT, b_u = Buf(), Buf(), Buf()
                    S = sbt(st, "d_S", [128, 4, 128])
                    Sb = sbt(st, "d_Sb", [128, 4, 128], BF16)
                    b_S, b_Sb = Buf(), Buf()
                    K.op("dve", lambda: nc.vector.memset(S[:], 0.0), W=[b_S])
                    K.op("dve", lambda: nc.vector.memset(Sb[:], 0.0), W=[b_Sb])
                    K.op("dve", lambda: nc.vector.memset(uu[:], 0.0), W=[b_u])
                    sub_ctr = [0]
                    scan_done = [0]
                    def stageA(kind, sc0, nt, a, c0):
                        lc = sc0 - c0
                        smp = (kind == "sample")
                        m_incl = C["ident"] if smp else C["mu_incl"]
                        m_bd = C["ident"] if smp else C["bd"]
                        m_ustrict = C["zero"] if smp else C["mu_strict"]
                        m_lstrict = C["zero"] if smp else C["ml_strict"]
                        T_ = lambda i: tb[a][0:nt, i, :]
                        pv = lambda b: psum[b][:, :].rearrange("p (a c) -> p a c", a=4)[:, :, 0:nt]
                        pt = lambda b: psum[b][0:nt, :].rearrange("p (a c) -> p a c", a=4)[:, :, 0:nt]
                        p4 = lambda b: psum[b][0:nt, :].rearrange("p (a c) -> p a c", a=4)
                        bc = lambda i: tb[a][0:nt, i, :].unsqueeze(2).broadcast_to([nt, 4, 128])
                        mm_group(psum[2][0:nt, 0:8], [(x_bf[:, k, sc0:sc0 + nt], wA[:, k, 2048:2056]) for k in range(KC)],
                                 R=[b_wA, b_xbf], W=[b_ps[2]])
                        yield
                        K.op("act", lambda: nc.scalar.activation(out=T_(0), in_=psum[2][0:nt, 0:4], func=AF.Exp, scale=-1.0),
                             R=[b_ps[2]], W=[b_tb[a]])
                        yield
                        K.op("dve", lambda: nc.vector.tensor_tensor(out=T_(2), in0=psum[2][0:nt, 4:8], in1=abt[0:nt, 1, :],
                                                                    op=ALU.add), R=[b_ps[2], b_abt], W=[b_tb[a]])
                        yield
                        K.op("act", lambda: nc.scalar.activation(out=T_(0), in_=T_(0), func=AF.Ln, bias=one1[0:nt, 0:1]),
                             R=[b_tb[a], b_eps], W=[b_tb[a]])
                        yield
                        K.op("act", lambda: nc.scalar.activation(out=T_(0), in_=T_(0), func=AF.Exp, scale=-1.0),
                             R=[b_tb[a]], W=[b_tb[a]])
                        yield
                        K.op("act", lambda: nc.scalar.activation(out=T_(2), in_=T_(2), func=AF.Exp), R=[b_tb[a]], W=[b_tb[a]])
                        yield
                        K.op("act", lambda: nc.scalar.activation(out=T_(2), in_=T_(2), func=AF.Ln, bias=one1[0:nt, 0:1]),
                             R=[b_tb[a], b_eps], W=[b_tb[a]])
                        yield
                        K.op("dve", lambda: nc.vector.tensor_tensor(out=T_(2), in0=T_(2), in1=abt[0:nt, 0, :], op=ALU.mult),
                             R=[b_tb[a], b_abt], W=[b_tb[a]])
                        yield
                        mm_group(psum[2][0:nt, 8:12], [(m_incl[0:nt, 0:nt], T_(2))], R=[b_tb[a], b_cst], W=[b_ps[2]])
                        yield
                        mm_group(psum[2][0:nt, 12:16], [(m_bd[0:nt, 0:nt], T_(2))], R=[b_tb[a], b_cst], W=[b_ps[2]])
                        yield
                        K.op("act", lambda: nc.scalar.activation(out=T_(1), in_=psum[2][0:nt, 8:12], func=AF.Exp),
                             R=[b_ps[2]], W=[b_tb[a]])
                        yield
                        K.op("dve", lambda: nc.vector.tensor_copy(out=T_(5), in_=psum[2][0:nt, 8:12]), R=[b_ps[2]], W=[b_tb[a]])
                        yield
                        K.op("dve", lambda: nc.vector.tensor_tensor(out=T_(3), in0=psum[2][0:nt, 12:16], in1=T_(5),
                                                                    op=ALU.subtract), R=[b_ps[2], b_tb[a]], W=[b_tb[a]])
                        yield
                        K.op("act", lambda: nc.scalar.activation(out=T_(3), in_=T_(3), func=AF.Exp), R=[b_tb[a]], W=[b_tb[a]])
                        yield
                        K.op("dve", lambda: nc.vector.tensor_tensor(out=T_(4), in0=T_(0), in1=T_(1), op=ALU.mult),
                             R=[b_tb[a]], W=[b_tb[a]])
                        yield
                        if smp:
                            K.op("dve", lambda: nc.vector.tensor_tensor(
                                out=dg[0:16, 0:4, 0:16].rearrange("p h s -> p s h"),
                                in0=T_(2).unsqueeze(1).broadcast_to([16, 16, 4]),
                                in1=C["ident"][0:16, 0:16].unsqueeze(2).broadcast_to([16, 16, 4]), op=ALU.mult),
                                R=[b_tb[a], b_cst], W=[b_dg])
                            for h in range(4):
                                K.op("pe", lambda h=h: nc.tensor.matmul(psum[3][:, h * 16:(h + 1) * 16],
                                                                        lhsT=C["ones"][0:16, :], rhs=dg[0:16, h, 0:16],
                                                                        start=True, stop=True),
                                     R=[b_dg, b_cst], W=[b_ps[3]], inc=(h == 3))
                            K.op("act", lambda: nc.scalar.activation(
                                out=s_G[:, :, :].rearrange("p s h -> p h s"),
                                in_=psum[3][:, 0:64].rearrange("p (h s) -> p h s", h=4), func=AF.Exp),
                                R=[b_ps[3]], W=[b_skeep])
                        else:
                            K.op("dve", lambda: nc.vector.tensor_tensor(
                                out=r2[0:nt, :].rearrange("p (c h) -> p c h", c=2),
                                in0=T_(2).unsqueeze(1).broadcast_to([nt, 2, 4]),
                                in1=cst[0:nt, IDX["blk0"]:IDX["blk0"] + 2, 0:1].broadcast_to([nt, 2, 4]), op=ALU.mult),
                                R=[b_tb[a], b_cst], W=[b_r2])
                            mm_group(psum[2][:, 16:24], [(C["ones"][0:nt, :], r2[0:nt, :])], R=[b_r2, b_cst], W=[b_ps[2]])
                            K.op("act", lambda: nc.scalar.activation(out=Gbc[a][:, :, :].rearrange("p c h -> p (c h)"),
                                                                     in_=psum[2][:, 16:24], func=AF.Exp),
                                 R=[b_ps[2]], W=[b_Gbc[a]])
                        yield
                        K.op("dve", lambda: nc.vector.tensor_tensor(
                            out=dg[0:nt, :, 0:nt], in0=C["ident"][0:nt, 0:nt].unsqueeze(1).broadcast_to([nt, 8, nt]),
                            in1=tb[a][0:nt, 0:2, :].rearrange("p a h -> p (a h)").unsqueeze(2).broadcast_to([nt, 8, nt]),
                            op=ALU.mult), R=[b_tb[a], b_cst, b_skeep], W=[b_dg])
                        yield
                        for wh in range(2):
                            for h in range(4):
                                K.op("pe", lambda wh=wh, h=h: nc.tensor.matmul(
                                    psum[3 + wh][:, h * 128:h * 128 + nt], lhsT=C["ones"][0:nt, :],
                                    rhs=dg[0:nt, 4 * wh + h, 0:nt], start=True, stop=True),
                                    R=[b_dg, b_cst], W=[b_ps[3 + wh]], inc=(h == 3))
                        yield
                        K.op("dve", lambda: nc.vector.tensor_tensor(out=bkq[a][:, 0, :, 0:nt], in0=qkn[:, 4:8, lc:lc + nt],
                                                                    in1=pv(3), op=ALU.mult),
                             R=[b_qkn, b_ps[3]], W=[b_bkq[a]])
                        yield
                        K.op("dve", lambda: nc.vector.tensor_tensor(out=bkq[a][:, 1, :, 0:nt], in0=qkn[:, 0:4, lc:lc + nt],
                                                                    in1=pv(4), op=ALU.mult),
                             R=[b_qkn, b_ps[4]], W=[b_bkq[a]])
                        yield
                        K.op("dve", lambda: nc.vector.tensor_tensor(
                            out=gm[0:nt, :, 0:nt], in0=m_incl[0:nt, 0:nt].unsqueeze(1).broadcast_to([nt, 4, nt]),
                            in1=T_(2).unsqueeze(2).broadcast_to([nt, 4, nt]), op=ALU.mult), R=[b_tb[a], b_cst], W=[b_gm])
                        yield
                        for h in range(4):
                            K.op("pe", lambda h=h: nc.tensor.matmul(psum[2][0:nt, h * 128:h * 128 + nt],
                                                                    lhsT=m_lstrict[0:nt, 0:nt], rhs=gm[0:nt, h, 0:nt],
                                                                    start=True, stop=True),
                                 R=[b_gm, b_cst], W=[b_ps[2]], inc=(h == 3))
                        yield
                        K.op("act", lambda: nc.scalar.activation(out=ET[0:nt, :, 0:nt], in_=pt(2), func=AF.Exp),
                             R=[b_ps[2]], W=[b_ET])
                        yield
                        K.op("dve", lambda: nc.vector.tensor_tensor(
                            out=ETi[0:nt, :, 0:nt], in0=ET[0:nt, :, 0:nt],
                            in1=m_incl[0:nt, 0:nt].unsqueeze(1).broadcast_to([nt, 4, nt]), op=ALU.mult),
                            R=[b_ET, b_cst], W=[b_ETi])
                        yield
                        K.op("dve", lambda: nc.vector.tensor_tensor(
                            out=ETs[0:nt, :, 0:nt], in0=ET[0:nt, :, 0:nt],
                            in1=m_ustrict[0:nt, 0:nt].unsqueeze(1).broadcast_to([nt, 4, nt]), op=ALU.mult),
                            R=[b_ET, b_cst, b_ETi], W=[b_ETs])
                        yield
                        for h in range(4):
                            K.op("pe", lambda h=h: nc.tensor.matmul(psum[3][0:nt, h * 128:h * 128 + nt],
                                                                    lhsT=qkn[:, 4 + h, lc:lc + nt], rhs=bkq[a][:, 0, h, 0:nt],
                                                                    start=True, stop=True),
                                 R=[b_qkn, b_bkq[a]], W=[b_ps[3]], inc=(h == 3))
                        yield
                        for h in range(4):
                            K.op("pe", lambda h=h: nc.tensor.matmul(psum[4][0:nt, h * 128:h * 128 + nt],
                                                                    lhsT=qkn[:, 4 + h, lc:lc + nt], rhs=qkn[:, h, lc:lc + nt],
                                                                    start=True, stop=True),
                                 R=[b_qkn], W=[b_ps[4]], inc=(h == 3))
                        AT, A_ = QTA[a], QA[a]
                        yield
                        K.op("dve", lambda: nc.vector.tensor_tensor(out=AT[0:nt, :, 0:nt], in0=pt(3), in1=ETs[0:nt, :, 0:nt],
                                                                    op=ALU.mult), R=[b_ps[3], b_ETs], W=[b_QTA[a]])
                        yield
                        K.op("dve", lambda: nc.vector.tensor_tensor(out=qkT[a][0:nt, :, 0:nt], in0=pt(4), in1=ETi[0:nt, :, 0:nt],
                                                                    op=ALU.mult), R=[b_ps[4], b_ETi], W=[b_qkT[a]])
                        yield
                        K.op("dve", lambda: nc.vector.scalar_tensor_tensor(
                            out=TT[a][0:nt, :, 0:nt], in0=AT[0:nt, :, 0:nt], scalar=-1.0,
                            in1=C["ident"][0:nt, 0:nt].unsqueeze(1).broadcast_to([nt, 4, nt]),
                            op0=ALU.mult, op1=ALU.add), R=[b_QTA[a], b_cst], W=[b_TT[a]])
                        yield
                        if not smp:
                            for h in range(4):
                                K.op("pe", lambda h=h: nc.tensor.transpose(psum[2][0:nt, h * 128:h * 128 + nt],
                                                                           AT[0:nt, h, 0:nt], C["ident"][0:nt, 0:nt]),
                                     R=[b_QTA[a], b_cst], W=[b_ps[2]], inc=(h == 3))
                            yield
                            K.op("act", lambda: nc.scalar.copy(out=A_[0:nt, :, 0:nt], in_=pt(2)), R=[b_ps[2]], W=[b_QA[a]])

                    def stageB(kind, sc0, nt, a, c0):
                        lc = sc0 - c0
                        smp = (kind == "sample")
                        m_incl = C["ident"] if smp else C["mu_incl"]
                        m_bd = C["ident"] if smp else C["bd"]
                        m_ustrict = C["zero"] if smp else C["mu_strict"]
                        m_lstrict = C["zero"] if smp else C["ml_strict"]
                        T_ = lambda i: tb[a][0:nt, i, :]
                        pv = lambda b: psum[b][:, :].rearrange("p (a c) -> p a c", a=4)[:, :, 0:nt]
                        pt = lambda b: psum[b][0:nt, :].rearrange("p (a c) -> p a c", a=4)[:, :, 0:nt]
                        p4 = lambda b: psum[b][0:nt, :].rearrange("p (a c) -> p a c", a=4)
                        bc = lambda i: tb[a][0:nt, i, :].unsqueeze(2).broadcast_to([nt, 4, 128])
                        yield
                        for h in range(4):
                            K.op("pe", lambda h=h: nc.tensor.matmul(psum[5][0:nt, h * 128:(h + 1) * 128],
                                                                    lhsT=qkn[:, 4 + h, lc:lc + nt], rhs=CB["ident"],
                                                                    start=True, stop=True),
                                 R=[b_qkn, b_cst], W=[b_ps[5]], inc=(h == 3))
                        yield
                        for h in range(4):
                            K.op("pe", lambda h=h: nc.tensor.matmul(psum[6][0:nt, h * 128:(h + 1) * 128],
                                                                    lhsT=vn[:, h, lc:lc + nt], rhs=CB["ident"],
                                                                    start=True, stop=True),
                                 R=[b_vn, b_cst], W=[b_ps[6]], inc=(h == 3))
                        yield
                        K.op("dve", lambda: nc.vector.tensor_tensor(out=tok3[0:nt, 0], in0=p4(5), in1=bc(4), op=ALU.mult),
                             R=[b_ps[5], b_tb[a]], W=[b_tok3])
                        yield
                        K.op("dve", lambda: nc.vector.tensor_tensor(out=tok3[0:nt, 1], in0=p4(5), in1=bc(3), op=ALU.mult),
                             R=[b_ps[5], b_tb[a]], W=[b_tok3])
                        yield
                        K.op("dve", lambda: nc.vector.tensor_tensor(out=tok3[0:nt, 2], in0=p4(6), in1=bc(0), op=ALU.mult),
                             R=[b_ps[6], b_tb[a]], W=[b_tok3])
                        if not smp:
                            n_inc = 5 if nt == 128 else 3
                            cq_, cqt_, b_cq_, b_cqt_ = QA[a], QTA[a], b_QA[a], b_QTA[a]
                            nq_, nqt_, b_nq_, b_nqt_ = Qs, QTs, b_Qs, b_QTs
                            for lv in range(0, n_inc + 1):
                                need_q = lv < n_inc
                                need_qt = lv < n_inc - 1
                                if lv >= 1:
                                    for h in range(4):
                                        K.op("pe", lambda h=h: nc.tensor.matmul(
                                            psum[7][0:nt, h * 128:h * 128 + nt], lhsT=cq_[0:nt, h, 0:nt],
                                            rhs=TT[a][0:nt, h, 0:nt], start=True, stop=True),
                                            R=[b_cq_, b_TT[a]], W=[b_ps[7]], inc=(h == 3))
                                if need_q:
                                    for h in range(4):
                                        K.op("pe", lambda h=h: nc.tensor.matmul(
                                            psum[5][0:nt, h * 128:h * 128 + nt], lhsT=cqt_[0:nt, h, 0:nt],
                                            rhs=cq_[0:nt, h, 0:nt], start=True, stop=True),
                                            R=[b_cq_, b_cqt_], W=[b_ps[5]], inc=(h == 3))
                                if need_qt:
                                    for h in range(4):
                                        K.op("pe", lambda h=h: nc.tensor.matmul(
                                            psum[6][0:nt, h * 128:h * 128 + nt], lhsT=cq_[0:nt, h, 0:nt],
                                            rhs=cqt_[0:nt, h, 0:nt], start=True, stop=True),
                                            R=[b_cq_, b_cqt_], W=[b_ps[6]], inc=(h == 3))
                                yield
                                if need_q:
                                    K.op("act", lambda: nc.scalar.copy(out=nq_[0:nt, :, 0:nt], in_=pt(5)),
                                         R=[b_ps[5]], W=[b_nq_])
                                if lv >= 1:
                                    K.op("dve", lambda: nc.vector.tensor_tensor(out=TT[a][0:nt, :, 0:nt], in0=TT[a][0:nt, :, 0:nt],
                                                                                in1=pt(7), op=ALU.add),
                                         R=[b_ps[7], b_TT[a]], W=[b_TT[a]])
                                if need_qt:
                                    K.op("act" if lv >= 1 else "dve", (lambda: nc.scalar.copy(out=nqt_[0:nt, :, 0:nt], in_=pt(6))) if lv >= 1
                                         else (lambda: nc.vector.tensor_copy(out=nqt_[0:nt, :, 0:nt], in_=pt(6))),
                                         R=[b_ps[6]], W=[b_nqt_])
                                yield
                                cq_, cqt_, b_cq_, b_cqt_, nq_, nqt_, b_nq_, b_nqt_ = nq_, nqt_, b_nq_, b_nqt_, cq_, cqt_, b_cq_, b_cqt_
                        yield
                        K.op("act", lambda: nc.scalar.copy(out=TTb[0:nt, :, 0:nt], in_=TT[a][0:nt, :, 0:nt]), R=[b_TT[a]], W=[b_TTb])
                        yield
                        yield
                        for h in range(4):
                            K.op("pe", lambda h=h: nc.tensor.matmul(psum[7][0:nt, h * 128:(h + 1) * 128],
                                                                    lhsT=TTb[0:nt, h, 0:nt], rhs=tok3[0:nt, 2, h, :],
                                                                    start=True, stop=True),
                                 R=[b_TTb, b_tok3], W=[b_ps[7]], inc=(h == 3))
                        yield
                        for h in range(4):
                            K.op("pe", lambda h=h: nc.tensor.matmul(psum[5][:, h * 128:h * 128 + nt],
                                                                    lhsT=tok3[0:nt, 0, h, :], rhs=TTb[0:nt, h, 0:nt],
                                                                    start=True, stop=True),
                                 R=[b_TTb, b_tok3], W=[b_ps[5]], inc=(h == 3))
                        yield
                        if smp:
                            K.op("act", lambda: nc.scalar.copy(out=s_u0[:], in_=p4(7)), R=[b_ps[7]], W=[b_skeep])
                            K.op("dve", lambda: nc.vector.tensor_copy(out=s_wT[:], in_=pv(5)), R=[b_ps[5]], W=[b_skeep])
                            K.op("dve", lambda: nc.vector.tensor_copy(out=s_kd[:], in_=tok3[0:16, 1]), R=[b_tok3], W=[b_skeep])
                            K.op("dve", lambda: nc.vector.tensor_copy(out=s_qn[:], in_=qkn[:, 0:4, 0:16]), R=[b_qkn], W=[b_skeep])
                            K.op("dve", lambda: nc.vector.tensor_copy(out=s_gz[:], in_=gz[:, :, 0:16]), R=[b_gz], W=[b_skeep])
                            return
                        yield
                        K.op("act", lambda: nc.scalar.copy(out=u0[0:nt], in_=p4(7)), R=[b_ps[7]], W=[b_u0])
                        yield
                        K.op("dve", lambda: nc.vector.tensor_copy(out=wT[:, :, 0:nt], in_=pv(5)), R=[b_ps[5]], W=[b_wT])
                        nch = 2 if nt == 128 else 1
                        cl = 64 if nt == 128 else nt
                        yield
                        for ch in range(nch):
                            tr = slice(ch * 64, ch * 64 + cl)
                            yield
                            for h in range(4):
                                K.op("pe", lambda h=h, tr=tr: nc.tensor.matmul(psum[6][tr, h * 128:(h + 1) * 128],
                                                                              lhsT=wT[:, h, tr], rhs=Sb[:, h, :],
                                                                              start=True, stop=True),
                                     R=[b_wT, b_Sb], W=[b_ps[6]], inc=(h == 3))
                            yield
                            K.op("dve", lambda tr=tr: nc.vector.tensor_tensor(
                                out=uu[tr], in0=u0[tr], in1=psum[6][tr, :].rearrange("p (a c) -> p a c", a=4),
                                op=ALU.subtract), R=[b_u0, b_ps[6]], W=[b_u])
                            yield
                            for h in range(4):
                                K.op("pe", lambda h=h, tr=tr: nc.tensor.matmul(psum[7][:, h * 64:h * 64 + cl],
                                                                              lhsT=Sb[:, h, :], rhs=bkq[a][:, 1, h, tr],
                                                                              start=True, stop=False),
                                     R=[b_Sb, b_bkq[a]], W=[b_ps[7]], inc=False)
                                K.op("pe", lambda h=h, tr=tr: nc.tensor.matmul(psum[7][:, h * 64:h * 64 + cl],
                                                                              lhsT=uu[0:nt, h, :], rhs=qkT[a][0:nt, h, tr],
                                                                              start=False, stop=True),
                                     R=[b_u, b_qkT[a]], W=[b_ps[7]], inc=(h == 3))
                            yield
                            for h in range(4):
                                K.op("pe", lambda h=h, tr=tr: nc.tensor.matmul(psum[5][:, h * 128:(h + 1) * 128],
                                                                              lhsT=tok3[tr, 1, h, :], rhs=uu[tr, h, :],
                                                                              start=True, stop=True),
                                     R=[b_tok3, b_u], W=[b_ps[5]], inc=(h == 3))
                            yield
                            K.op("act", lambda ch=ch: nc.scalar.copy(
                                out=oA[:, :, lc + ch * 64:lc + ch * 64 + cl],
                                in_=psum[7][:, 0:256].rearrange("p (a b) -> p a b", a=4)[:, :, 0:cl]),
                                R=[b_ps[7]], W=[b_oA])
                            yield
                            for h in range(4):
                                K.op("dve", lambda h=h, ch=ch: nc.vector.scalar_tensor_tensor(
                                    out=S[:, h, :], in0=S[:, h, :], scalar=Gbc[a][:, ch, h:h + 1],
                                    in1=psum[5][:, h * 128:(h + 1) * 128], op0=ALU.mult, op1=ALU.add),
                                    R=[b_Gbc[a], b_ps[5]], W=[b_S])
                            yield
                            K.op("act", lambda: nc.scalar.copy(out=Sb[:], in_=S[:]), R=[b_S], W=[b_Sb])
                        scan_done[0] += 1
                        yield

                    blk_it = 0
                    for (c0, n) in TILES:
                        is_ms = (c0 == 0)
                        def blkfn(blk, c0=c0, n=n, is_ms=is_ms):
                            q = blk % 2
                            mm_group(psum[q][:, 0:n], [(wA[:, k, blk * 128:(blk + 1) * 128], x_bf[:, k, c0:c0 + n])
                                                      for k in range(KC)], R=[b_wA, b_xbf], W=[b_ps[q]])
                            if blk >= 12:
                                K.op("act", lambda blk=blk, q=q: nc.scalar.activation(
                                    out=gz[:, blk - 12, 0:n], in_=psum[q][:, 0:n], func=AF.Silu), R=[b_ps[q]], W=[b_gz])
                                return
                            pe_ = blk % 2
                            ncv = 16 if is_ms else n
                            K.op("pool", lambda pe_=pe_, blk=blk: nc.gpsimd.tensor_copy(out=Pe[pe_][:, 0:3], in_=halo[:, blk, 0:3]),
                                 R=[b_halo], W=[b_Pe[pe_]])
                            src0 = 16 if is_ms else 0
                            K.op("act", lambda pe_=pe_, q=q, src0=src0, ncv=ncv: nc.scalar.copy(
                                out=Pe[pe_][:, 3:3 + ncv], in_=psum[q][:, src0:src0 + ncv]), R=[b_ps[q]], W=[b_Pe[pe_]])
                            K.op("pool", lambda pe_=pe_, blk=blk, ncv=ncv: nc.gpsimd.tensor_copy(
                                out=halo[:, blk, 0:3], in_=Pe[pe_][:, ncv:ncv + 3]), R=[b_Pe[pe_]], W=[b_halo])
                            a_ = acc[pe_]
                            o0 = 16 if is_ms else 0
                            K.op("act", lambda pe_=pe_, blk=blk, ncv=ncv, o0=o0: nc.scalar.activation(
                                out=acc[pe_][:, o0:o0 + ncv], in_=Pe[pe_][:, 3:3 + ncv], func=AF.Identity, scale=cw[:, blk, 3:4]),
                                R=[b_Pe[pe_], b_cw], W=[b_acc[pe_]])
                            for i in (2, 1, 0):
                                K.op("dve", lambda pe_=pe_, blk=blk, ncv=ncv, o0=o0, i=i: nc.vector.scalar_tensor_tensor(
                                    out=acc[pe_][:, o0:o0 + ncv], in0=Pe[pe_][:, i:i + ncv], scalar=cw[:, blk, i:i + 1],
                                    in1=acc[pe_][:, o0:o0 + ncv], op0=ALU.mult, op1=ALU.add),
                                    R=[b_Pe[pe_], b_cw, b_acc[pe_]], W=[b_acc[pe_]])
                            if is_ms:
                                cbv = cbT[:, blk, :].rearrange("p (s i) -> p s i", i=3)
                                K.op("dve", lambda pe_=pe_, blk=blk, q=q: nc.vector.tensor_scalar(
                                    out=acc[pe_][:, 0:16], in0=psum[q][:, 0:16], scalar1=cw[:, blk, 3:4], scalar2=None,
                                    op0=ALU.mult), R=[b_ps[q], b_cw], W=[b_acc[pe_]])
                                for i in range(3):
                                    K.op("dve", lambda pe_=pe_, blk=blk, i=i, cbv=cbv: nc.vector.scalar_tensor_tensor(
                                        out=acc[pe_][:, 0:16], in0=cbv[:, :, i], scalar=cw[:, blk, i:i + 1],
                                        in1=acc[pe_][:, 0:16], op0=ALU.mult, op1=ALU.add),
                                        R=[b_cbT, b_cw, b_acc[pe_]], W=[b_acc[pe_]])
                            if blk < 8:
                                ci = blk % 2
                                rr = rsd if ci == 0 else rsd_b
                                b_rr = b_rsd if ci == 0 else b_rsdb
                                K.op("act", lambda pe_=pe_, ci=ci: nc.scalar.activation(
                                    out=cq[ci][:, 0:n], in_=acc[pe_][:, 0:n], func=AF.Silu), R=[b_acc[pe_]], W=[b_cq[ci]])
                                K.op("act", lambda ci=ci: nc.scalar.activation(out=sq[:, ci, 0:n], in_=cq[ci][:, 0:n],
                                                                              func=AF.Square), R=[b_cq[ci]], W=[b_sq])
                                mm_group(psum[q][:, 0:n], [(CB["ones"], sq[:, ci, 0:n])], R=[b_sq, b_cst], W=[b_ps[q]])
                                K.op("act", lambda q=q, rr=rr: nc.scalar.activation(out=rr[:, 0:n], in_=psum[q][:, 0:n], func=AF.Ln,
                                                                                 bias=eps6[:, 0:1]), R=[b_ps[q], b_eps], W=[b_rr])
                                K.op("act", lambda rr=rr: nc.scalar.activation(out=rr[:, 0:n], in_=rr[:, 0:n], func=AF.Exp,
                                                                             scale=-0.5), R=[b_rr], W=[b_rr])
                                scl = (128.0 ** -0.5) if blk < 4 else 1.0
                                K.op("dve", lambda blk=blk, scl=scl, ci=ci, rr=rr: nc.vector.scalar_tensor_tensor(
                                    out=qkn[:, blk, 0:n], in0=cq[ci][:, 0:n], scalar=scl, in1=rr[:, 0:n],
                                    op0=ALU.mult, op1=ALU.mult), R=[b_cq[ci], b_rr], W=[b_qkn])
                            else:
                                K.op("act", lambda pe_=pe_, blk=blk: nc.scalar.activation(
                                    out=vn[:, blk - 8, 0:n], in_=acc[pe_][:, 0:n], func=AF.Silu), R=[b_acc[pe_]], W=[b_vn])
                        K.run_sched([(lambda blk=blk: blkfn(blk), ([blk - 2] if blk >= 2 else [])) for blk in range(16)])
                        if is_ms:
                            for g3 in range(3):
                                q = g3 % 2
                                mm_group(psum[q][0:16, :], [(x_bf[:, k, 0:16], wA[:, k, g3 * 512:(g3 + 1) * 512])
                                                            for k in range(KC)], R=[b_wA, b_xbf], W=[b_ps[q]])
                                K.op("act", lambda g3=g3, q=q: nc.scalar.copy(out=pnew[:, g3 * 512:(g3 + 1) * 512],
                                                                             in_=psum[q][0:16, :]), R=[b_ps[q]], W=[b_pnew])
                            K.dma("sp", o_conv_s[l][:, 2, :], pnew[:], R=[b_pnew])
                        subs = subtiles(c0, n)
                        ths = []
                        for i_, (kind, sc0, nt) in enumerate(subs):
                            a_ = (sub_ctr[0] + i_) % 2
                            afterA = ([2 * (i_ - 1)] if i_ >= 1 else []) + ([2 * (i_ - 2) + 1] if i_ >= 2 else [])
                            afterB = [2 * i_] + ([2 * (i_ - 1) + 1] if i_ >= 1 else [])
                            ths.append((lambda kind=kind, sc0=sc0, nt=nt, a_=a_, c0=c0: stageA(kind, sc0, nt, a_, c0), afterA))
                            ths.append((lambda kind=kind, sc0=sc0, nt=nt, a_=a_, c0=c0: stageB(kind, sc0, nt, a_, c0), afterB))
                        K.run_sched(ths)
                        sub_ctr[0] += len(subs)
                        if is_ms:
                            K.op("dve", lambda: nc.vector.memset(oA[:, :, 0:16], 0.0), W=[b_oA])
                        gated_norm(st, oA, b_oA, gz, b_gz, nw, b_nw, og, b_og, 0, c0, n, nbufs)
                    K.dma("sp", o_gdn_p[l].rearrange("h k v -> k h v"), S[:], R=[b_S])
                    for g3 in range(3):
                        for b4 in range(4):
                            blk = 4 * g3 + b4
                            K.op("pe", lambda blk=blk, b4=b4, g3=g3: nc.tensor.transpose(
                                psum[g3][0:4, b4 * 128:(b4 + 1) * 128], halo[:, blk, :], C["ident"]),
                                R=[b_halo, b_cst], W=[b_ps[g3]], inc=(b4 == 3))
                        K.op("act", lambda g3=g3: nc.scalar.copy(out=pnew[0:4, g3 * 512:(g3 + 1) * 512], in_=psum[g3][0:4, :]),
                             R=[b_ps[g3]], W=[b_pnew])
                    K.dma("sp", o_conv_p[l], pnew[0:3, :], R=[b_pnew])
                K.barrier()
                with ExitStack() as s2:
                    wTm = sbt(s2, "d_wTm", [128, 4, NS, 16], BF16)
                    b_wTm = Buf()
                    K.op("dve", lambda: nc.vector.tensor_tensor(
                        out=wTm[:], in0=s_wT[:, :, :].unsqueeze(2).broadcast_to([128, 4, NS, 16]),
                        in1=cst[:, IDX["idrep0"]:IDX["idrep0"] + 2, :].rearrange("p a (s t) -> p (a s) t", t=16)
                            .unsqueeze(1).broadcast_to([128, 4, NS, 16]), op=ALU.mult),
                        R=[b_skeep, b_cst], W=[b_wTm])
                    Sg = [sbt(s2, "d_Sg%d" % i, [128, 4, 4, 128]) for i in range(2)]
                    Sgb = [sbt(s2, "d_Sgb%d" % i, [128, 4, 4, 128], BF16) for i in range(2)]
                    b_Sg = [Buf(), Buf()]
                    b_Sgb = [Buf(), Buf()]
                    us = sbt(s2, "d_us", [16, 4, 128], BF16)
                    ubd = sbt(s2, "d_ubd", [16, 4, 512], BF16)
                    b_us, b_ubd = Buf(), Buf()
                    oS = sbt(s2, "d_oS", [128, 4, 16])
                    b_oS = Buf()
                    sq2 = sbt(s2, "d_sq2", [128, 4, 512], BF16)
                    rsd2 = sbt(s2, "d_rsd2", [128, 512])
                    tn2 = sbt(s2, "d_tn2", [128, 512])
                    sgd_v = state_gdn[l].rearrange("s h k v -> k s h v")
                    sgd_o = o_gdn_s[l].rearrange("s h k v -> k s h v")
                    for sg_ in range(4):
                        u = sg_ % 2
                        K.dma("sp", Sg[u][:], sgd_v[:, 4 * sg_:4 * sg_ + 4], W=[b_Sg[u]])
                        K.op("act", lambda u=u: nc.scalar.copy(out=Sgb[u][:], in_=Sg[u][:]), R=[b_Sg[u]], W=[b_Sgb[u]])
                        for h in range(4):
                            for si in range(4):
                                s = 4 * sg_ + si
                                K.op("pe", lambda h=h, si=si, s=s, u=u: nc.tensor.matmul(
                                    psum[0][0:16, h * 128:(h + 1) * 128], lhsT=wTm[:, h, s, :], rhs=Sgb[u][:, si, h, :],
                                    start=(si == 0), stop=(si == 3)), R=[b_wTm, b_Sgb[u]], W=[b_ps[0]],
                                    inc=(h == 3 and si == 3))
                        K.op("dve", lambda: nc.vector.tensor_tensor(
                            out=us[:], in0=s_u0[:], in1=psum[0][0:16, :].rearrange("p (a c) -> p a c", a=4), op=ALU.subtract),
                            R=[b_skeep, b_ps[0]], W=[b_us])
                        K.op("dve", lambda sg_=sg_: nc.vector.tensor_tensor(
                            out=ubd[:], in0=us[:, :, :].rearrange("p h v -> p (h v)").unsqueeze(1).broadcast_to([16, 4, 512]),
                            in1=C["ident"][0:16, 4 * sg_:4 * sg_ + 4].unsqueeze(2).broadcast_to([16, 4, 512]), op=ALU.mult),
                            R=[b_us, b_cst], W=[b_ubd])
                        for si in range(4):
                            for h in range(4):
                                K.op("pe", lambda h=h, si=si: nc.tensor.matmul(
                                    psum[1 + si][:, h * 128:(h + 1) * 128], lhsT=s_kd[0:16, h, :],
                                    rhs=ubd[0:16, si, h * 128:(h + 1) * 128], start=True, stop=True),
                                    R=[b_skeep, b_ubd], W=[b_ps[1 + si]], inc=(h == 3))
                        for si in range(4):
                            s = 4 * sg_ + si
                            for h in range(4):
                                K.op("dve", lambda h=h, si=si, s=s, u=u: nc.vector.scalar_tensor_tensor(
                                    out=Sg[u][:, si, h, :], in0=Sg[u][:, si, h, :], scalar=s_G[:, s, h:h + 1],
                                    in1=psum[1 + si][:, h * 128:(h + 1) * 128], op0=ALU.mult, op1=ALU.add),
                                    R=[b_skeep, b_ps[1 + si]], W=[b_Sg[u]])
                        K.op("act", lambda u=u: nc.scalar.copy(out=Sgb[u][:], in_=Sg[u][:]), R=[b_Sg[u]], W=[b_Sgb[u]])
                        for si in range(4):
                            s = 4 * sg_ + si
                            for h in range(4):
                                K.op("pe", lambda h=h, si=si, s=s, u=u: nc.tensor.matmul(
                                    psum[5][:, h * 16 + s:h * 16 + s + 1], lhsT=Sgb[u][:, si, h, :], rhs=s_qn[:, h, s:s + 1],
                                    start=True, stop=True), R=[b_Sgb[u], b_skeep], W=[b_ps[5]],
                                    inc=(h == 3 and si == 3))
                        K.dma("sp", sgd_o[:, 4 * sg_:4 * sg_ + 4], Sg[u][:], R=[b_Sg[u]])
                    K.op("act", lambda: nc.scalar.copy(out=oS[:], in_=psum[5][:, 0:64].rearrange("p (a b) -> p a b", a=4)),
                         R=[b_ps[5]], W=[b_oS])
                    gated_norm(s2, oS, b_oS, s_gz, b_skeep, nw, b_nw, og, b_og, 0, 0, 16,
                               (sq2, Buf(), rsd2, Buf(), tn2, Buf()))

        for l in range(layers):
            last_layer = (l == layers - 1)
            win_v = w_in[l].rearrange("(k p) n -> p k n", p=128)

            with ExitStack() as pm:
              og = big[:, 0:8 * T].rearrange("p (k t) -> p k t", k=8)
              b_og = Buf("og")
              if stub_mixer:
                  K.op("dve", lambda: nc.vector.tensor_copy(out=og, in_=x_bf[:]), R=[b_xbf], W=[b_og])
              else:
                  if mix_sel in ("all", "gdn"):
                      gdn_phase(l, og, b_og, win_v)
                      K.barrier()
                  if mix_sel in ("all", "gla"):
                      gla_phase(l, og, b_og, win_v)
              K.barrier()
              chk("mixer")
              v = sbt(pm, "v", [128, KC, T])
              b_v = Buf("v")
              with ExitStack() as pmg:
                mg = sbt(pmg, "mg", [128, KC, T], BF16)
                b_mg = Buf("mg")
                with ExitStack() as pb1:
                    NW = 2
                    wab = [sbt(pb1, "wab%d" % i, [128, 8, 128], BF16) for i in range(NW)]
                    wgg = [sbt(pb1, "wgg%d" % i, [128, 16, 128], BF16) for i in range(NW)]
                    b_wab = [Buf() for _ in range(NW)]
                    b_wgg = [Buf() for _ in range(NW)]
                    sg = [sbt(pb1, "sg%d" % i, [128, 2, 512]) for i in range(2)]
                    b_sg = [Buf(), Buf()]
                    wa_v = w_branch_a[l].rearrange("(k p) n -> p k n", p=128)
                    wb_v = w_branch_b[l].rearrange("(k p) n -> p k n", p=128)
                    it = 0
                    for jo in range(KC):
                        s = jo % NW
                        cs = slice(jo * 128, (jo + 1) * 128)
                        K.dma("pool", wab[s][:, 0:4, :], wa_v[:, :, cs], W=[b_wab[s]])
                        K.dma("pool", wab[s][:, 4:8, :], wb_v[:, :, cs], W=[b_wab[s]])
                        K.dma("pool", wgg[s][:, 0:8, :], win_v[:, :, 3608 + jo * 128:3608 + (jo + 1) * 128], W=[b_wgg[s]])
                        K.dma("pool", wgg[s][:, 8:16, :], win_v[:, :, 4632 + jo * 128:4632 + (jo + 1) * 128], W=[b_wgg[s]])
                        for (c0, n) in TILES:
                            q = 4 * (it % 2)
                            t2 = it % 2
                            it += 1
                            cols = slice(c0, c0 + n)
                            mm_group(psum[q + 0][:, 0:n], [(wab[s][:, k, :], og[:, k, cols]) for k in range(4)],
                                     R=[b_wab[s], b_og], W=[b_ps[q + 0]])
                            mm_group(psum[q + 1][:, 0:n], [(wab[s][:, 4 + k, :], og[:, 4 + k, cols]) for k in range(4)],
                                     R=[b_wab[s], b_og], W=[b_ps[q + 1]])
                            mm_group(psum[q + 2][:, 0:n], [(wgg[s][:, k, :], x_bf[:, k, cols]) for k in range(8)],
                                     R=[b_wgg[s], b_xbf], W=[b_ps[q + 2]])
                            mm_group(psum[q + 3][:, 0:n], [(wgg[s][:, 8 + k, :], x_bf[:, k, cols]) for k in range(8)],
                                     R=[b_wgg[s], b_xbf], W=[b_ps[q + 3]])
                            K.op("act", lambda q=q, t2=t2, n=n: nc.scalar.activation(
                                out=sg[t2][:, 0, 0:n], in_=psum[q + 2][:, 0:n], func=AF.Sigmoid),
                                R=[b_ps[q + 2]], W=[b_sg[t2]])
                            K.op("act", lambda q=q, t2=t2, n=n: nc.scalar.activation(
                                out=sg[t2][:, 1, 0:n], in_=psum[q + 3][:, 0:n], func=AF.Sigmoid),
                                R=[b_ps[q + 3]], W=[b_sg[t2]])
                            K.op("dve", lambda q=q, t2=t2, n=n: nc.vector.tensor_tensor(
                                out=sg[t2][:, 0, 0:n], in0=sg[t2][:, 0, 0:n], in1=psum[q + 0][:, 0:n], op=ALU.mult),
                                R=[b_sg[t2], b_ps[q + 0]], W=[b_sg[t2]])
                            K.op("dve", lambda q=q, t2=t2, n=n: nc.vector.tensor_tensor(
                                out=sg[t2][:, 1, 0:n], in0=sg[t2][:, 1, 0:n], in1=psum[q + 1][:, 0:n], op=ALU.mult),
                                R=[b_sg[t2], b_ps[q + 1]], W=[b_sg[t2]])
                            K.op("dve", lambda t2=t2, n=n, jo=jo, cols=cols: nc.vector.tensor_tensor(
                                out=mg[:, jo, cols], in0=sg[t2][:, 0, 0:n], in1=sg[t2][:, 1, 0:n], op=ALU.add),
                                R=[b_sg[t2]], W=[b_mg])
                K.barrier()
                chk("b1")
                K.dma("sp", v[:], xres, R=[b_xres], W=[b_v])
                with ExitStack() as pb2:
                    wo = [sbt(pb2, "wo%d" % i, [128, 8, 128], BF16) for i in range(2)]
                    b_wo = [Buf(), Buf()]
                    wo_v = w_out[l].rearrange("(k p) n -> p k n", p=128)
                    it = 0
                    for jo in range(KC):
                        s = jo % 2
                        K.dma("pool", wo[s][:], wo_v[:, :, jo * 128:(jo + 1) * 128], W=[b_wo[s]])
                        for (c0, n) in TILES:
                            q = it % 8
                            it += 1
                            cols = slice(c0, c0 + n)
                            mm_group(psum[q][:, 0:n], [(wo[s][:, k, :], mg[:, k, cols]) for k in range(8)],
                                     R=[b_wo[s], b_mg], W=[b_ps[q]])
                            K.op("dve", lambda q=q, n=n, jo=jo, cols=cols: nc.vector.scalar_tensor_tensor(
                                out=v[:, jo, cols], in0=v[:, jo, cols], scalar=ALPHA, in1=psum[q][:, 0:n],
                                op0=ALU.mult, op1=ALU.add), R=[b_ps[q]], W=[b_v])
                K.barrier()
                chk("b2")
              if True:
                layer_norm(pm, v, b_v, l, 0)
                K.barrier()
                chk("ln1")

                nh = (len(TILES) + 1) // 2
                halves = [TILES[:nh], TILES[nh:]]
                fin_v = w_ffn_in[l].rearrange("(k p) n -> p k n", p=128)
                fout_v = w_ffn_out[l].rearrange("(c p) n -> p c n", p=128)
                with ExitStack() as pf:
                    hid = big[:, 0:FC * HW].rearrange("p (c t) -> p c t", c=FC)
                    b_hid = Buf("hid")
                    wfi = [sbt(pf, "wfi%d" % i, [128, 2, KC, 256], BF16) for i in range(2)]
                    b_wfi = [Buf(), Buf()]
                    wfo = [sbt(pf, "wfo%d" % i, [128, FC, 128], BF16) for i in range(2)]
                    b_wfo = [Buf(), Buf()]
                    sil = [sbt(pf, "sil%d" % i, [128, 512]) for i in range(2)]
                    b_sil = [Buf(), Buf()]
                    it = 0
                    wi_it = 0
                    wo_it = 0
                    for half in halves:
                        if not half:
                            continue
                        h0 = half[0][0]
                        for g in range(FC // 2):
                            s = wi_it % 2
                            wi_it += 1
                            K.dma("pool", wfi[s][:, 0, :, :], fin_v[:, :, g * 256:(g + 1) * 256], W=[b_wfi[s]])
                            K.dma("pool", wfi[s][:, 1, :, :], fin_v[:, :, DFF + g * 256:DFF + (g + 1) * 256], W=[b_wfi[s]])
                            for jj in range(2):
                                j = 2 * g + jj
                                for (c0, n) in half:
                                    q = 2 * (it % 4)
                                    t2 = it % 2
                                    it += 1
                                    cols = slice(c0, c0 + n)
                                    hc = slice(c0 - h0, c0 - h0 + n)
                                    mm_group(psum[q][:, 0:n],
                                             [(wfi[s][:, 0, k, jj * 128:(jj + 1) * 128], x_bf[:, k, cols]) for k in range(8)],
                                             R=[b_wfi[s], b_xbf], W=[b_ps[q]])
                                    mm_group(psum[q + 1][:, 0:n],
                                             [(wfi[s][:, 1, k, jj * 128:(jj + 1) * 128], x_bf[:, k, cols]) for k in range(8)],
                                             R=[b_wfi[s], b_xbf], W=[b_ps[q + 1]])
                                    K.op("act", lambda q=q, t2=t2, n=n: nc.scalar.activation(
                                        out=sil[t2][:, 0:n], in_=psum[q][:, 0:n], func=AF.Silu),
                                        R=[b_ps[q]], W=[b_sil[t2]])
                                    K.op("dve", lambda q=q, t2=t2, n=n, j=j, hc=hc: nc.vector.tensor_tensor(
                                        out=hid[:, j, hc], in0=sil[t2][:, 0:n], in1=psum[q + 1][:, 0:n], op=ALU.mult),
                                        R=[b_sil[t2], b_ps[q + 1]], W=[b_hid])
                        for jo in range(KC):
                            s = wo_it % 2
                            wo_it += 1
                            K.dma("pool", wfo[s][:], fout_v[:, :, jo * 128:(jo + 1) * 128], W=[b_wfo[s]])
                            for (c0, n) in half:
                                q = it % 8
                                it += 1
                                cols = slice(c0, c0 + n)
                                hc = slice(c0 - h0, c0 - h0 + n)
                                mm_group(psum[q][:, 0:n], [(wfo[s][:, c, :], hid[:, c, hc]) for c in range(FC)],
                                         R=[b_wfo[s], b_hid], W=[b_ps[q]])
                                K.op("dve", lambda q=q, n=n, jo=jo, cols=cols: nc.vector.scalar_tensor_tensor(
                                    out=v[:, jo, cols], in0=v[:, jo, cols], scalar=ALPHA, in1=psum[q][:, 0:n],
                                    op0=ALU.mult, op1=ALU.add), R=[b_ps[q]], W=[b_v])
                K.barrier()
                chk("ffn")
                layer_norm(pm, v, b_v, l, 1, write_bf=not last_layer)
                K.barrier()
                chk("ln2")
                if not last_layer:
                    K.dma("sp", xres, v[:], R=[b_v], W=[b_xres])
                else:
                    with ExitStack() as po:
                        NR = 3
                        yst = [sbt(po, "yst%d" % i, [128, D]) for i in range(NR)]
                        b_yst = [Buf() for _ in range(NR)]
                        rows = [("ms", 0, 32)] + [("p", 128 * i, 128) for i in range(TP // 128)]

                        def out_tile(ri):
                            kind, r0, n = rows[ri]
                            s = ri % NR
                            c0 = 0 if kind == "ms" else 32 + r0
                            for g in range(2):
                                pb = (2 * ri + g) % 8
                                for kk in range(4):
                                    k = 4 * g + kk
                                    K.op("pe", lambda k=k, kk=kk, pb=pb: nc.tensor.transpose(
                                        psum[pb][0:n, kk * 128:(kk + 1) * 128], v[:, k, c0:c0 + n], C["ident"]),
                                        R=[b_v, b_cst], W=[b_ps[pb]], inc=(kk == 3), c=0.12)
                                if g == 0:
                                    K.op("act", lambda pb=pb: nc.scalar.copy(
                                        out=yst[s][0:n, 0:512], in_=psum[pb][0:n, :]), R=[b_ps[pb]], W=[b_yst[s]])
                                else:
                                    K.op("dve", lambda pb=pb: nc.vector.tensor_copy(
                                        out=yst[s][0:n, 512:1024], in_=psum[pb][0:n, :]), R=[b_ps[pb]], W=[b_yst[s]])
                            if kind == "ms":
                                K.dma("sp", y_sample, yst[s][0:16, :], R=[b_yst[s]])
                            else:
                                K.dma("sp", y_prompt[r0:r0 + 128, :], yst[s][0:128, :], R=[b_yst[s]])

                        K.run_sched([(lambda ri=ri: out_tile(ri), ([ri - NR] if ri >= NR else [])) for ri in range(len(rows))])
              K.barrier()
      except _Stop:
        pass
      K.finish()
    return nc


_NC_CACHE = {}


def kernel(x_prompt, x_sample, state_gdn, state_gla, state_conv, meta_tokens, w_in, conv_w, a_log, dt_bias,
           gdn_norm_w, gla_gate_w2, gla_gate_b, gla_norm_w, w_branch_a, w_branch_b, w_out,
           ln1_g, ln1_b, ln2_g, ln2_b, w_ffn_in, w_ffn_out, _build_kwargs=None):
    f = lambda a: np.ascontiguousarray(np.asarray(a), dtype=np.float32)
    x_prompt = f(x_prompt)
    TP = x_prompt.shape[1]
    bk = dict(_build_kwargs or {})
    key = (TP, tuple(sorted(bk.items())))
    if key not in _NC_CACHE:
        _NC_CACHE[key] = build(TP=TP, **bk)
    nc = _NC_CACHE[key]
    shared = dict(meta_tokens=f(meta_tokens), w_in=f(w_in), conv_w=f(conv_w), a_log=f(a_log), dt_bias=f(dt_bias),
                  gdn_norm_w=f(gdn_norm_w), gla_gate_w2=f(gla_gate_w2), gla_gate_b=f(gla_gate_b),
                  gla_norm_w=f(gla_norm_w), w_branch_a=f(w_branch_a), w_branch_b=f(w_branch_b), w_out=f(w_out),
                  ln1_g=f(ln1_g), ln1_b=f(ln1_b), ln2_g=f(ln2_g), ln2_b=f(ln2_b),
                  w_ffn_in=f(w_ffn_in), w_ffn_out=f(w_ffn_out), consts=CONST_ARR)
    x_sample = f(x_sample)
    state_gdn = f(state_gdn)
    state_gla = f(state_gla)
    state_conv = f(state_conv)
    in_maps = []
    for c in range(NCORES):
        sl = slice(NS * c, NS * (c + 1))
        m = dict(shared)
        m["x_prompt"] = x_prompt[c]
        m["x_sample"] = np.ascontiguousarray(x_sample[sl, 0, :])
        m["state_gdn"] = np.ascontiguousarray(state_gdn[:, sl])
        m["state_gla"] = np.ascontiguousarray(state_gla[:, sl])
        m["state_conv"] = np.ascontiguousarray(state_conv[:, sl])
        in_maps.append(m)
    res = run_bass_kernel_spmd(nc, in_maps, core_ids=list(range(NCORES)))
    R = res.results
    y_prompt = np.stack([R[c]["y_prompt"] for c in range(NCORES)], axis=0)
    y_sample = np.concatenate([R[c]["y_sample"] for c in range(NCORES)], axis=0)[:, None, :]
    gdn_p = np.stack([R[c]["new_gdn_prompt"] for c in range(NCORES)], axis=1)
    gla_p = np.stack([R[c]["new_gla_prompt"] for c in range(NCORES)], axis=1)
    conv_p = np.stack([R[c]["new_conv_prompt"] for c in range(NCORES)], axis=1)
    gdn_s = np.concatenate([R[c]["new_gdn_sample"] for c in range(NCORES)], axis=1)
    gla_s = np.concatenate([R[c]["new_gla_sample"] for c in range(NCORES)], axis=1)
    conv_s = np.concatenate([R[c]["new_conv_sample"] for c in range(NCORES)], axis=1)
    outs = (y_prompt, y_sample, gdn_p, gla_p, conv_p, gdn_s, gla_s, conv_s)
    return tuple(np.ascontiguousarray(o, dtype=np.float32) for o in outs)
```

```python
import threading
import numpy as np
from contextlib import ExitStack
import concourse.bass as bass
import concourse.mybir as mybir
from concourse.bass_utils import run_bass_kernel_spmd

F32 = mybir.dt.float32
BF16 = mybir.dt.bfloat16
AF = mybir.ActivationFunctionType
ALU = mybir.AluOpType

D = 1024
KC = 8
DEPTH = 2
NS = 16
NMETA = 16
D_IN = 5656
DFF = 2816
FC = DFF // 128
ALPHA = (2.0 * DEPTH) ** 0.25
NCORES = 8


class Buf:
    __slots__ = ("w", "r", "name", "excl")

    def __init__(self, name="", excl=False):
        self.w = None
        self.r = {}
        self.name = name
        self.excl = excl


_tls = threading.local()


class _Worker:
    def __init__(self, fn):
        self.fn = fn
        self.req = None
        self.done = False
        self.exc = None
        self.ev_req = threading.Event()
        self.ev_go = threading.Event()
        self.th = threading.Thread(target=self._run, daemon=True)

    def _run(self):
        _tls.worker = self
        try:
            self.ev_go.wait()
            self.ev_go.clear()
            r = self.fn()
            if r is not None and hasattr(r, "__next__"):
                for _ in r:
                    pass
        except BaseException as e:
            self.exc = e
        finally:
            self.done = True
            self.req = None
            self.ev_req.set()

    def post(self, req):
        self.req = req
        self.ev_req.set()
        self.ev_go.wait()
        self.ev_go.clear()


class Sched:
    ENG = ("pe", "act", "dve", "pool", "sp")
    COST = {"pe": 0.25, "act": 0.6, "dve": 0.7, "pool": 0.5, "sp": 0.1}

    def __init__(self, nc, es, n_dma_slots=24):
        self.nc = nc
        self.e = {"pe": nc.tensor, "act": nc.scalar, "dve": nc.vector, "pool": nc.gpsimd, "sp": nc.sync}
        self.sem = {}
        self.cnt = {}
        for k in self.ENG:
            self.sem[k] = es.enter_context(nc.semaphore("s_" + k))
            self.cnt[k] = 0
        self.seen = {k: {} for k in self.ENG}
        self.pending = {k: False for k in self.ENG}
        self.nslots = n_dma_slots
        self.slot_sem = [es.enter_context(nc.semaphore("s_dma%d" % i)) for i in range(n_dma_slots)]
        self.slot_uses = [0] * n_dma_slots
        self.slot_next = 0
        self.semobj = dict(self.sem)
        for i in range(n_dma_slots):
            self.semobj[("dma", i)] = self.slot_sem[i]
        self.tfree = {k: 0.0 for k in self.ENG}
        self.tdone = {}

    def _est(self, eng, R, W):
        t = self.tfree[eng]
        def dep(key):
            d = self.tdone.get(key)
            if d is None:
                d = self.tfree.get(key[0], 0.0) if not isinstance(key[0], tuple) else 0.0
            return d + (0.05 if key[0] == eng else 0.3)
        for b in R:
            if b.w is not None:
                t = max(t, dep(b.w))
            if b.excl:
                for k, v in b.r.items():
                    t = max(t, dep((k, v)))
        for b in W:
            if b.w is not None:
                t = max(t, dep(b.w))
            for k, v in b.r.items():
                t = max(t, dep((k, v)))
        return t

    def run_sched(self, threads):
        n = len(threads)
        workers = [None] * n
        started, finished = set(), set()

        def advance(w):
            w.ev_req.clear()
            w.ev_go.set()
            w.ev_req.wait()
            if w.exc is not None:
                raise w.exc

        while len(finished) < n:
            for i, (fn, after) in enumerate(threads):
                if i not in started and all(j in finished for j in after):
                    started.add(i)
                    workers[i] = _Worker(fn)
                    workers[i].th.start()
                    advance(workers[i])
                    if workers[i].done:
                        finished.add(i)
            cands = [i for i in started if i not in finished]
            if not cands:
                if len(finished) < n and len(started) == len(finished):
                    rem = [i for i in range(n) if i not in started]
                    assert any(all(j in finished for j in threads[i][1]) for i in rem), "scheduler deadlock"
                continue
            best = min(cands, key=lambda i: (self._est(*workers[i].req), i))
            advance(workers[best])
            if workers[best].done:
                finished.add(best)

    def _collect(self, eng, R, W, extra=()):
        need = {}
        def add(k, v):
            if v > need.get(k, 0):
                need[k] = v
        for b in R:
            if b.w is not None:
                add(*b.w)
        for b in W:
            if b.w is not None:
                add(*b.w)
            for k, v in b.r.items():
                add(k, v)
        for k, v in extra:
            add(k, v)
        out = []
        for k, v in need.items():
            if k == "pe" and eng == "pe":
                continue
            if k == eng and v > self.cnt[eng]:
                continue
            if v > self.seen[eng].get(k, 0):
                out.append((k, v))
        return out

    def op(self, eng, fn, R=(), W=(), inc=True, extra=(), c=None):
        w_ = getattr(_tls, "worker", None)
        if w_ is not None:
            w_.post((eng, tuple(R), tuple(W)))
        t0_ = self._est(eng, R, W)
        t1_ = t0_ + (c if c is not None else self.COST[eng])
        self.tfree[eng] = t1_
        if any(b.excl for b in R):
            W = list(W) + [b for b in R if b.excl]
            R = [b for b in R if not b.excl]
        waits = self._collect(eng, R, W, extra)
        e = self.e[eng]
        if eng == "pe":
            for k, v in waits:
                e.wait_ge(self.semobj[k], v)
                self.seen[eng][k] = v
            waits = []
        for k, v in waits[:-1]:
            e.wait_ge(self.semobj[k], v)
            self.seen[eng][k] = v
        inst = fn()
        if waits:
            k, v = waits[-1]
            inst.wait_op(self.semobj[k], v, "sem-ge")
            self.seen[eng][k] = v
        if inc:
            inst.then_inc(self.sem[eng], 1)
            self.cnt[eng] += 1
            cc = self.cnt[eng]
            self.pending[eng] = False
            self.tdone[(eng, cc)] = t1_
        else:
            cc = self.cnt[eng] + 1
            self.pending[eng] = True
        for b in R:
            if b.r.get(eng, 0) < cc:
                b.r[eng] = cc
        for b in W:
            b.w = (eng, cc)
            b.r = {}
        return inst

    def dma(self, eng, out, in_, R=(), W=(), **kw):
        w_ = getattr(_tls, "worker", None)
        if w_ is not None:
            w_.post((eng, tuple(R), tuple(W)))
        t0_ = self._est(eng, R, W)
        self.tfree[eng] = t0_ + 0.1
        s = self.slot_next
        self.slot_next = (self.slot_next + 1) % self.nslots
        key = ("dma", s)
        prev = 16 * self.slot_uses[s]
        extra = [(key, prev)] if prev > 0 else []
        waits = self._collect(eng, R, W, extra)
        e = self.e[eng]
        for k, v in waits:
            e.wait_ge(self.semobj[k], v)
            self.seen[eng][k] = v
        inst = e.dma_start(out=out, in_=in_, **kw)
        self.slot_uses[s] += 1
        val = 16 * self.slot_uses[s]
        inst.then_inc(self.slot_sem[s], 16)
        self.tdone[(key, val)] = t0_ + 3.0
        for b in R:
            b.r[key] = val
        for b in W:
            b.w = (key, val)
            b.r = {}
        return inst

    def barrier(self):
        tgt = [(k, self.cnt[k]) for k in self.ENG if self.cnt[k] > 0]
        tgt += [(("dma", i), 16 * self.slot_uses[i]) for i in range(self.nslots) if self.slot_uses[i] > 0]
        for eng in self.ENG:
            assert not self.pending[eng]
            for k, v in tgt:
                if k == eng:
                    continue
                if v > self.seen[eng].get(k, 0):
                    self.e[eng].wait_ge(self.semobj[k], v)
                    self.seen[eng][k] = v

    def finish(self):
        self.barrier()


def make_consts():
    r = np.arange(128)
    same = (r[:, None] // 64) == (r[None, :] // 64)
    c = {}
    c["ident"] = np.eye(128, dtype=np.float32)
    c["ones"] = np.ones((128, 128), dtype=np.float32)
    c["mu_incl"] = (same & (r[None, :] >= r[:, None])).astype(np.float32)
    c["mu_strict"] = (same & (r[None, :] > r[:, None])).astype(np.float32)
    c["ml_strict"] = (same & (r[:, None] > r[None, :])).astype(np.float32)
    c["bd"] = same.astype(np.float32)
    c["blk0"] = np.repeat((r < 64).astype(np.float32)[:, None], 128, axis=1)
    c["blk1"] = np.repeat((r >= 64).astype(np.float32)[:, None], 128, axis=1)
    c["zero"] = np.zeros((128, 128), dtype=np.float32)
    idrep = np.tile(np.eye(16, dtype=np.float32).reshape(1, 256), (128, 1))
    c["idrep0"] = idrep[:, 0:128]
    c["idrep1"] = idrep[:, 128:256]
    names = ["ident", "ones", "mu_incl", "mu_strict", "ml_strict", "bd", "blk0", "blk1", "zero", "idrep0", "idrep1"]
    arr = np.stack([c[n] for n in names], axis=1)
    return names, np.ascontiguousarray(arr.astype(np.float32))


CONST_NAMES, CONST_ARR = make_consts()
NCONST = len(CONST_NAMES)
IDX = {n: i for i, n in enumerate(CONST_NAMES)}


class _Stop(Exception):
    pass


def build(TP=2048, stub_mixer=False, layers=DEPTH, dbg=False, stop=None, mix_sel="all"):
    nc = bass.Bass("TRN2", target_bir_lowering=False)
    NPOS = NMETA + TP
    T = 32 + TP
    NPT = max(TP // 512, 1)
    TILES = [(0, 32)] + [(32 + 512 * i, 512) for i in range(NPT)]

    def din(name, shape, dt=F32):
        return nc.dram_tensor(name, list(shape), dt, kind="ExternalInput").ap()

    def dout(name, shape, dt=F32):
        return nc.dram_tensor(name, list(shape), dt, kind="ExternalOutput").ap()

    x_prompt = din("x_prompt", [TP, D])
    x_sample = din("x_sample", [NS, D])
    meta = din("meta_tokens", [NMETA, D])
    state_gdn = din("state_gdn", [DEPTH, NS, 4, 128, 128])
    state_gla = din("state_gla", [DEPTH, NS, 4, 64, 128])
    state_conv = din("state_conv", [DEPTH, NS, 3, 1536])
    w_in = din("w_in", [DEPTH, D, D_IN])
    conv_w = din("conv_w", [DEPTH, 4, 1536])
    a_log = din("a_log", [DEPTH, 4])
    dt_bias = din("dt_bias", [DEPTH, 4])
    gdn_norm_w = din("gdn_norm_w", [DEPTH, 128])
    gla_gate_w2 = din("gla_gate_w2", [DEPTH, 16, 256])
    gla_gate_b = din("gla_gate_b", [DEPTH, 256])
    gla_norm_w = din("gla_norm_w", [DEPTH, 128])
    w_branch_a = din("w_branch_a", [DEPTH, 512, D])
    w_branch_b = din("w_branch_b", [DEPTH, 512, D])
    w_out = din("w_out", [DEPTH, D, D])
    ln1_g = din("ln1_g", [DEPTH, D])
    ln1_b = din("ln1_b", [DEPTH, D])
    ln2_g = din("ln2_g", [DEPTH, D])
    ln2_b = din("ln2_b", [DEPTH, D])
    w_ffn_in = din("w_ffn_in", [DEPTH, D, 2 * DFF])
    w_ffn_out = din("w_ffn_out", [DEPTH, DFF, D])
    consts_d = din("consts", [128, NCONST, 128])

    y_prompt = dout("y_prompt", [TP, D])
    y_sample = dout("y_sample", [NS, D])
    o_gdn_p = dout("new_gdn_prompt", [DEPTH, 4, 128, 128])
    o_gla_p = dout("new_gla_prompt", [DEPTH, 4, 64, 128])
    o_conv_p = dout("new_conv_prompt", [DEPTH, 3, 1536])
    o_gdn_s = dout("new_gdn_sample", [DEPTH, NS, 4, 128, 128])
    o_gla_s = dout("new_gla_sample", [DEPTH, NS, 4, 64, 128])
    o_conv_s = dout("new_conv_sample", [DEPTH, NS, 3, 1536])
    dbg_out = dout("dbg", [128, KC, T]) if dbg else None

    xres = nc.dram_tensor("xres_scratch", [128, KC, T], F32, kind="Internal").ap()

    es = ExitStack()
    with es:
      K = Sched(nc, es)

      def chk(name):
          if stop == name:
              raise _Stop()

      try:

        _uid = [0]

        def sbt(stack, name, shape, dt=F32):
            _uid[0] += 1
            return stack.enter_context(nc.sbuf_tensor("%s_%d" % (name, _uid[0]), list(shape), dt))

        cst = sbt(es, "cst", [128, NCONST, 128], F32)
        cstb = sbt(es, "cstb", [128, NCONST, 128], BF16)
        b_cst = Buf("cst")
        C = {n: cst[:, i, :] for i, n in enumerate(CONST_NAMES)}
        CB = {n: cstb[:, i, :] for i, n in enumerate(CONST_NAMES)}
        x_bf = sbt(es, "x_bf", [128, KC, T], BF16)
        b_xbf = Buf("x_bf")
        lnp = sbt(es, "lnp", [128, DEPTH, 4, KC], F32)
        b_lnp = Buf("lnp")
        psum = [es.enter_context(nc.psum_tensor("ps%d" % i, [128, 512], F32)) for i in range(8)]
        b_ps = [Buf("ps%d" % i, excl=True) for i in range(8)]
        b_xres = Buf("xres")

        epst = sbt(es, "epst", [128, 2], F32)
        b_eps = Buf("eps")
        K.op("dve", lambda: nc.vector.memset(epst[:, 0:1], 1e-6), W=[b_eps])
        K.op("dve", lambda: nc.vector.memset(epst[:, 1:2], 1.0), W=[b_eps])
        eps6 = epst[:, 0:1]
        one1 = epst[:, 1:2]
        nh_ = (len(TILES) + 1) // 2
        HW = max(sum(n for _, n in TILES[:nh_]), sum(n for _, n in TILES[nh_:]))
        BIGN = max(FC * HW, 8 * T)
        big = sbt(es, "big", [128, BIGN], BF16)
        K.dma("sp", cst[:], consts_d, W=[b_cst])
        K.op("act", lambda: nc.scalar.copy(out=cstb[:], in_=cst[:]), R=[b_cst], W=[b_cst])
        for l in range(DEPTH):
            for wi, src in enumerate((ln1_g, ln1_b, ln2_g, ln2_b)):
                K.dma("sp", lnp[:, l, wi, :], src[l].rearrange("(k p) -> p k", p=128), W=[b_lnp],
                      allow_slow_non_contiguous=True)

        chk("init")

        def run_threads(gens):
            active = list(gens)
            while active:
                for g in list(active):
                    try:
                        next(g)
                    except StopIteration:
                        active.remove(g)

        def run_pipeline(items, mkA, mkB, nA=2, nbuf=3):
            n_it = len(items)
            nextA, nextB = 0, 0
            activeA = {}
            doneA = set()
            curB = None
            while nextB < n_it:
                while nextA < n_it and len(activeA) < nA and (nextA - nextB) < nbuf:
                    activeA[nextA] = mkA(items[nextA], nextA)
                    nextA += 1
                if curB is None and nextB in doneA:
                    curB = mkB(items[nextB], nextB)
                for i, gen in list(activeA.items()):
                    try:
                        next(gen)
                    except StopIteration:
                        del activeA[i]
                        doneA.add(i)
                if curB is not None:
                    try:
                        next(curB)
                    except StopIteration:
                        curB = None
                        nextB += 1

        def mm_group(ps_ap, pairs, R, W, inc=True):
            n = len(pairs)
            for i, (lt, rh) in enumerate(pairs):
                last = (i == n - 1)
                K.op("pe", lambda lt=lt, rh=rh, i=i, last=last: nc.tensor.matmul(
                    ps_ap, lhsT=lt, rhs=rh, start=(i == 0), stop=last),
                    R=R, W=W, inc=(inc and last))

        with ExitStack() as p0:
            NR = 3
            xin = [sbt(p0, "xin%d" % i, [128, D]) for i in range(NR)]
            b_xin = [Buf() for _ in range(NR)]
            xst = [sbt(p0, "xst%d" % i, [128, KC, 128]) for i in range(NR)]
            b_xst = [Buf() for _ in range(NR)]
            rows = [("ms", 0, 32)] + [("p", 128 * i, 128) for i in range(TP // 128)]

            def p0_tile(ri):
                kind, r0, n = rows[ri]
                s = ri % NR
                if kind == "ms":
                    K.dma("sp", xin[s][0:16, :], x_sample, W=[b_xin[s]])
                    K.dma("sp", xin[s][16:32, :], meta, W=[b_xin[s]])
                    c0 = 0
                else:
                    K.dma("sp", xin[s][0:128, :], x_prompt[r0:r0 + 128, :], W=[b_xin[s]])
                    c0 = 32 + r0
                for g in range(2):
                    pb = (2 * ri + g) % 8
                    for kk in range(4):
                        k = 4 * g + kk
                        K.op("pe", lambda k=k, kk=kk, pb=pb: nc.tensor.transpose(
                            psum[pb][:, kk * 128:kk * 128 + n], xin[s][0:n, k * 128:(k + 1) * 128],
                            C["ident"][0:n, 0:n]), R=[b_xin[s], b_cst], W=[b_ps[pb]], inc=(kk == 3), c=0.12)
                    src = psum[pb][:, :].rearrange("p (a b) -> p a b", a=4)[:, :, 0:n]
                    K.op("act", lambda src=src, g=g: nc.scalar.copy(
                        out=x_bf[:, 4 * g:4 * g + 4, c0:c0 + n], in_=src), R=[b_ps[pb]], W=[b_xbf])
                    K.op("dve", lambda src=src, g=g: nc.vector.tensor_copy(
                        out=xst[s][:, 4 * g:4 * g + 4, 0:n], in_=src), R=[b_ps[pb]], W=[b_xst[s]])
                K.dma("sp", xres[:, :, c0:c0 + n], xst[s][:, :, 0:n], R=[b_xst[s]])

            K.run_sched([(lambda ri=ri: p0_tile(ri), ([ri - NR] if ri >= NR else [])) for ri in range(len(rows))])
        K.barrier()
        chk("p0")

        def layer_norm(stack, v, b_v, l, which, write_bf=True):
            g_i, b_i = 2 * which, 2 * which + 1
            with ExitStack() as st:
                vb_ = [sbt(st, "ln_vb%d" % i, [128, KC, 512], BF16) for i in range(2)]
                sq_ = [sbt(st, "ln_sq%d" % i, [128, KC, 512], BF16) for i in range(2)]
                mt_ = [sbt(st, "ln_m%d" % i, [128, 512]) for i in range(2)]
                m2_ = [sbt(st, "ln_m2%d" % i, [128, 512]) for i in range(2)]
                rs_ = [sbt(st, "ln_rs%d" % i, [128, 512]) for i in range(2)]
                nm_ = [sbt(st, "ln_nm%d" % i, [128, 512]) for i in range(2)]
                bb = [[Buf() for _ in range(6)] for _ in range(2)]
                b_vts = []
                for _ in TILES:
                    t_ = Buf()
                    t_.w = b_v.w
                    t_.r = dict(b_v.r)
                    b_vts.append(t_)

                def ln_tile(ti):
                    c0, n = TILES[ti]
                    u_ = ti % 2
                    vb, sq, mt, m2, rs, nm = vb_[u_], sq_[u_], mt_[u_], m2_[u_], rs_[u_], nm_[u_]
                    b_vb, b_sq, b_mt, b_m2, b_rs, b_nm = bb[u_]
                    b_v = b_vts[ti]
                    cols = slice(c0, c0 + n)
                    pm_, pq_ = (2 * ti) % 8, (2 * ti + 1) % 8
                    K.op("act", lambda cols=cols, n=n: nc.scalar.copy(out=vb[:, :, 0:n], in_=v[:, :, cols]),
                         R=[b_v], W=[b_vb])
                    K.op("act", lambda cols=cols, n=n: nc.scalar.activation(
                        out=sq[:, :, 0:n], in_=v[:, :, cols], func=AF.Square), R=[b_v], W=[b_sq])
                    mm_group(psum[pm_][:, 0:n], [(CB["ones"], vb[:, k, 0:n]) for k in range(KC)],
                             R=[b_vb, b_cst], W=[b_ps[pm_]])
                    mm_group(psum[pq_][:, 0:n], [(CB["ones"], sq[:, k, 0:n]) for k in range(KC)],
                             R=[b_sq, b_cst], W=[b_ps[pq_]])
                    K.op("dve", lambda n=n, pm_=pm_: nc.vector.tensor_scalar(
                        out=mt[:, 0:n], in0=psum[pm_][:, 0:n], scalar1=1.0 / D, scalar2=None, op0=ALU.mult),
                        R=[b_ps[pm_]], W=[b_mt])
                    K.op("dve", lambda n=n: nc.vector.tensor_tensor(
                        out=m2[:, 0:n], in0=mt[:, 0:n], in1=mt[:, 0:n], op=ALU.mult), R=[b_mt], W=[b_m2])
                    K.op("dve", lambda n=n, pq_=pq_: nc.vector.scalar_tensor_tensor(
                        out=m2[:, 0:n], in0=psum[pq_][:, 0:n], scalar=1.0 / D, in1=m2[:, 0:n],
                        op0=ALU.mult, op1=ALU.subtract), R=[b_ps[pq_], b_m2], W=[b_m2])
                    K.op("dve", lambda n=n: nc.vector.tensor_scalar(
                        out=m2[:, 0:n], in0=m2[:, 0:n], scalar1=1e-5, scalar2=None, op0=ALU.add),
                        R=[b_m2], W=[b_m2])
                    K.op("act", lambda n=n: nc.scalar.activation(
                        out=rs[:, 0:n], in_=m2[:, 0:n], func=AF.Ln), R=[b_m2], W=[b_rs])
                    K.op("act", lambda n=n: nc.scalar.activation(
                        out=rs[:, 0:n], in_=rs[:, 0:n], func=AF.Exp, scale=-0.5), R=[b_rs], W=[b_rs])
                    K.op("dve", lambda n=n: nc.vector.scalar_tensor_tensor(
                        out=nm[:, 0:n], in0=mt[:, 0:n], scalar=-1.0, in1=rs[:, 0:n],
                        op0=ALU.mult, op1=ALU.mult), R=[b_mt, b_rs], W=[b_nm])
                    K.op("dve", lambda n=n, cols=cols: nc.vector.tensor_tensor(
                        out=v[:, :, cols], in0=v[:, :, cols],
                        in1=rs[:, 0:n].unsqueeze(1).broadcast_to([128, KC, n]), op=ALU.mult),
                        R=[b_rs], W=[b_v])
                    K.op("dve", lambda n=n, cols=cols: nc.vector.tensor_tensor(
                        out=v[:, :, cols], in0=v[:, :, cols],
                        in1=nm[:, 0:n].unsqueeze(1).broadcast_to([128, KC, n]), op=ALU.add),
                        R=[b_nm], W=[b_v])
                    for k in range(KC):
                        K.op("act", lambda k=k, n=n, cols=cols: nc.scalar.activation(
                            out=v[:, k, cols], in_=v[:, k, cols], func=AF.Identity,
                            scale=lnp[:, l, g_i, k:k + 1], bias=lnp[:, l, b_i, k:k + 1]),
                            R=[b_lnp], W=[b_v])
                    if write_bf:
                        K.op("pool", lambda cols=cols: nc.gpsimd.tensor_copy(out=x_bf[:, :, cols], in_=v[:, :, cols]),
                             R=[b_v], W=[b_xbf])

                K.run_sched([(lambda ti=ti: ln_tile(ti), ([ti - 2] if ti >= 2 else [])) for ti in range(len(TILES))])

        def subtiles(c0, n):
            if c0 == 0:
                return [("sample", 0, 16), ("meta", 16, 16)]
            return [("prompt", c0 + 128 * i, 128) for i in range(n // 128)]

        def gated_norm(st, oB, b_oB, gz, b_gz, nw, b_nw, og, b_og, hbase, c0, n, tagbufs):
            sq, b_sq, rsd, b_rsd, tn, b_tn = tagbufs
            K.op("act", lambda: nc.scalar.activation(out=sq[:, :, 0:n], in_=oB[:, :, 0:n], func=AF.Square),
                 R=[b_oB], W=[b_sq])
            for h in range(4):
                q = h % 2
                mm_group(psum[q][:, 0:n], [(CB["ones"], sq[:, h, 0:n])], R=[b_sq, b_cst], W=[b_ps[q]])
                K.op("act", lambda q=q: nc.scalar.activation(out=rsd[:, 0:n], in_=psum[q][:, 0:n], func=AF.Ln,
                                                             scale=1.0 / 128, bias=eps6[:, 0:1]),
                     R=[b_ps[q], b_eps], W=[b_rsd])
                K.op("act", lambda: nc.scalar.activation(out=rsd[:, 0:n], in_=rsd[:, 0:n], func=AF.Exp, scale=-0.5),
                     R=[b_rsd], W=[b_rsd])
                K.op("dve", lambda h=h: nc.vector.tensor_tensor(out=tn[:, 0:n], in0=oB[:, h, 0:n], in1=rsd[:, 0:n],
                                                                op=ALU.mult), R=[b_oB, b_rsd], W=[b_tn])
                K.op("dve", lambda h=h: nc.vector.scalar_tensor_tensor(
                    out=og[:, hbase + h, c0:c0 + n], in0=tn[:, 0:n], scalar=nw[:, 0:1], in1=gz[:, h, 0:n],
                    op0=ALU.mult, op1=ALU.mult), R=[b_tn, b_nw, b_gz], W=[b_og])

        def gla_phase(l, og, b_og, win_v):
            with ExitStack() as st:
                wB = sbt(st, "wB", [128, KC, 1552], BF16)
                b_wBb = [Buf() for _ in range(13)]
                for blk_ in range(12):
                    K.dma("pool", wB[:, :, blk_ * 128:(blk_ + 1) * 128],
                          win_v[:, :, 2056 + blk_ * 128:2056 + (blk_ + 1) * 128], W=[b_wBb[blk_]])
                K.dma("pool", wB[:, :, 1536:1552], win_v[:, :, 2056 + 1536:2056 + 1552], W=[b_wBb[12]])
                w2e = sbt(st, "w2e", [17, 256])
                b_w2e = Buf()
                K.dma("sp", w2e[0:16, :], gla_gate_w2[l], W=[b_w2e])
                K.dma("sp", w2e[16:17, :], gla_gate_b[l].rearrange("(o n) -> o n", o=1), W=[b_w2e])
                nw = sbt(st, "gla_nw", [128, 1])
                b_nw = Buf()
                K.dma("sp", nw[:, :], gla_norm_w[l].rearrange("(p o) -> p o", o=1), W=[b_nw],
                      allow_slow_non_contiguous=True)
                qk = sbt(st, "g_qk", [128, 4, 512])
                b_qk = Buf()
                sr = sbt(st, "g_sr", [128, 4, 512], BF16)
                b_sr = Buf()
                glr = sbt(st, "g_glr", [17, 512])
                b_glr = Buf()
                oB = sbt(st, "g_oB", [128, 4, 512])
                b_oB = Buf()
                sq = sbt(st, "g_sq", [128, 4, 512], BF16)
                rsd = sbt(st, "g_rsd", [128, 512])
                tn = sbt(st, "g_tn", [128, 512])
                nbufs = (sq, Buf(), rsd, Buf(), tn, Buf())
                Lt = [sbt(st, "g_L%d" % i, [128, 256]) for i in range(3)]
                E12 = [sbt(st, "g_E%d" % i, [128, 2, 2, 128]) for i in range(3)]
                atot = [sbt(st, "g_at%d" % i, [128, 2, 2]) for i in range(3)]
                qdk = [sbt(st, "g_qdk%d" % i, [128, 2, 2, 128], BF16) for i in range(3)]
                qdm = [sbt(st, "g_qdm%d" % i, [128, 4, 128], BF16) for i in range(3)]
                b_qdm = [Buf(), Buf(), Buf()]
                qm = sbt(st, "g_qm", [128, 4, 18])
                b_qm = Buf()
                E3 = [sbt(st, "g_E3%d" % i, [128, 256]) for i in range(3)]
                kdec = [sbt(st, "g_kd%d" % i, [128, 256], BF16) for i in range(3)]
                vtb = [sbt(st, "g_vt%d" % i, [128, 512], BF16) for i in range(3)]
                att = [sbt(st, "g_att%d" % i, [128, 4, 128], BF16) for i in range(3)]
                b_L, b_E12, b_at, b_qdk, b_E3, b_kd, b_vt, b_att = ([Buf(), Buf(), Buf()] for _ in range(8))
                S = sbt(st, "g_S", [128, 2, 128])
                Sb = sbt(st, "g_Sb", [128, 2, 128], BF16)
                b_S, b_Sb = Buf(), Buf()
                K.op("dve", lambda: nc.vector.memset(S[:], 0.0), W=[b_S])
                K.op("dve", lambda: nc.vector.memset(Sb[:], 0.0), W=[b_Sb])
                K.op("dve", lambda: nc.vector.memset(glr[:], 1.0), W=[b_glr])
                Ss = sbt(st, "g_Ss", [128, NS, 2, 128])
                b_Ss = Buf()
                sgl_v = state_gla[l].rearrange("s (pr hf) k v -> (hf k) s pr v", hf=2)
                for s4 in range(4):
                    K.dma("sp", Ss[:, 4 * s4:4 * s4 + 4], sgl_v[:, 4 * s4:4 * s4 + 4], W=[b_Ss])
                vbd = sbt(st, "g_vbd", [16, NS, 512], BF16)
                b_vbd = Buf()
                sub_ctr = [0]
                def glaA(kind, sc0, nt, u, bx, by, c0):
                    lc = sc0 - c0
                    if kind == "sample":
                        m_incl, m_strict_l = C["ident"], C["zero"]
                        mb_incl = CB["ident"]
                    else:
                        m_incl, m_strict_l = C["mu_incl"], C["ml_strict"]
                        mb_incl = CB["mu_incl"]
                    yield
                    mm_group(psum[bx][0:nt, 0:256], [(x_bf[:, k, sc0:sc0 + nt], wB[:, k, 256:512]) for k in range(KC)],
                             R=b_wBb[2:4] + [b_xbf], W=[b_ps[bx]])
                    yield
                    mm_group(psum[by][0:nt, 0:512], [(x_bf[:, k, sc0:sc0 + nt], wB[:, k, 512:1024]) for k in range(KC)],
                             R=b_wBb[4:8] + [b_xbf], W=[b_ps[by]])
                    yield
                    mm_group(psum[bx][0:nt, 256:512], [(glr[0:17, lc:lc + nt], w2e[0:17, :])],
                             R=[b_glr, b_w2e], W=[b_ps[bx]])
                    yield
                    K.op("act", lambda u=u: nc.scalar.activation(out=Lt[u][0:nt, :], in_=psum[bx][0:nt, 256:512],
                                                                 func=AF.Exp, scale=-1.0), R=[b_ps[bx]], W=[b_L[u]])
                    yield
                    K.op("act", lambda u=u: nc.scalar.activation(out=Lt[u][0:nt, :], in_=Lt[u][0:nt, :], func=AF.Ln,
                                                                 bias=one1[0:nt, 0:1]), R=[b_L[u], b_eps], W=[b_L[u]])
                    yield
                    K.op("act", lambda u=u: nc.scalar.copy(out=vtb[u][0:nt, :], in_=psum[by][0:nt, :]),
                         R=[b_ps[by]], W=[b_vt[u]])
                    yield
                    for pr_ in range(2):
                        mm_group(psum[by][:, pr_ * 128:pr_ * 128 + nt],
                                 [(Lt[u][0:nt, pr_ * 128:(pr_ + 1) * 128], m_incl[0:nt, 0:nt])],
                                 R=[b_L[u], b_cst], W=[b_ps[by]])
                    yield
                    mm_group(psum[by][0:nt, 256:512], [(m_strict_l[0:nt, 0:nt], Lt[u][0:nt, :])],
                             R=[b_L[u], b_cst], W=[b_ps[by]])
                    csT = psum[by][:, 0:256].rearrange("p (a b) -> p a b", a=2)[:, :, 0:nt]
                    yield
                    K.op("act", lambda u=u, csT=csT: nc.scalar.activation(out=E12[u][:, 0, :, 0:nt], in_=csT,
                                                                          func=AF.Exp, scale=-1.0 / 16),
                         R=[b_ps[by]], W=[b_E12[u]])
                    yield
                    K.op("act", lambda u=u, csT=csT: nc.scalar.activation(out=E12[u][:, 1, :, 0:nt], in_=csT,
                                                                          func=AF.Exp, scale=1.0 / 16),
                         R=[b_ps[by]], W=[b_E12[u]])
                    nch = 2 if nt == 128 else 1
                    cl = 64 if nt == 128 else nt
                    yield
                    for ch in range(nch):
                        lastc = ch * 64 + cl - 1
                        K.op("act", lambda u=u, ch=ch, lastc=lastc: nc.scalar.activation(
                            out=atot[u][:, :, ch:ch + 1],
                            in_=psum[by][:, 0:256].rearrange("p (a b) -> p a b", a=2)[:, :, lastc:lastc + 1],
                            func=AF.Exp, scale=-1.0 / 16), R=[b_ps[by]], W=[b_at[u]])
                    yield
                    K.op("act", lambda u=u: nc.scalar.activation(out=E3[u][0:nt, :], in_=psum[by][0:nt, 256:512],
                                                                 func=AF.Exp, scale=-1.0 / 16),
                         R=[b_ps[by]], W=[b_E3[u]])
                    yield
                    K.op("dve", lambda u=u, lc=lc: nc.vector.scalar_tensor_tensor(
                        out=qdk[u][:, 0, :, 0:nt], in0=qk[:, 0:2, lc:lc + nt], scalar=0.125, in1=E12[u][:, 0, :, 0:nt],
                        op0=ALU.mult, op1=ALU.mult), R=[b_qk, b_E12[u]], W=[b_qdk[u]])
                    yield
                    K.op("dve", lambda u=u, lc=lc: nc.vector.tensor_tensor(
                        out=qdk[u][:, 1, :, 0:nt], in0=qk[:, 2:4, lc:lc + nt], in1=E12[u][:, 1, :, 0:nt], op=ALU.mult),
                        R=[b_qk, b_E12[u]], W=[b_qdk[u]])
                    yield
                    K.op("dve", lambda u=u: nc.vector.tensor_tensor(
                        out=kdec[u][0:nt, :], in0=psum[bx][0:nt, 0:256], in1=E3[u][0:nt, :], op=ALU.mult),
                        R=[b_ps[bx], b_E3[u]], W=[b_kd[u]])
                    yield
                    if kind == "sample":
                        for h in range(4):
                            K.op("dve", lambda h=h: nc.vector.tensor_scalar(
                                out=qm[:, h, :], in0=qk[:, h // 2, 0:18], scalar1=C["blk%d" % (h % 2)][:, 0:1],
                                scalar2=None, op0=ALU.mult), R=[b_qk, b_cst], W=[b_qm])
                        gla_sample(l, u, qm, b_qm, E12, b_E12, kdec, b_kd, vtb, b_vt, Ss, b_Ss, vbd, b_vbd, oB, b_oB)
                        return
                    yield
                    for h in range(4):
                        K.op("dve", lambda h=h, u=u: nc.vector.tensor_scalar(
                            out=qdm[u][:, h, 0:nt], in0=qdk[u][:, 0, h // 2, 0:nt], scalar1=C["blk%d" % (h % 2)][:, 0:1],
                            scalar2=None, op0=ALU.mult), R=[b_qdk[u], b_cst], W=[b_qdm[u]])
                    yield
                    for h in range(4):
                        pr_, hf = h // 2, h % 2
                        rows = slice(hf * 64, hf * 64 + 64)
                        mm_group(psum[bx][0:nt, h * 128:h * 128 + nt],
                                 [(qdk[u][:, 1, pr_, 0:nt], qdm[u][:, h, 0:nt])],
                                 R=[b_qdk[u], b_qdm[u]], W=[b_ps[bx]], inc=(h == 3))
                    yield
                    K.op("dve", lambda u=u: nc.vector.tensor_tensor(
                        out=att[u][0:nt, :, 0:nt],
                        in0=psum[bx][0:nt, :].rearrange("p (a b) -> p a b", a=4)[:, :, 0:nt],
                        in1=m_incl[0:nt, 0:nt].unsqueeze(1).broadcast_to([nt, 4, nt]), op=ALU.mult),
                        R=[b_ps[bx], b_cst], W=[b_att[u]])

                def glaB(kind, sc0, nt, u, c0):
                    lc = sc0 - c0
                    nch = 2 if nt == 128 else 1
                    cl = 64 if nt == 128 else nt
                    for ch in range(nch):
                        trow = slice(ch * 64, ch * 64 + cl)
                        yield
                        for h in range(4):
                            pr_, hf = h // 2, h % 2
                            rows = slice(hf * 64, hf * 64 + 64)
                            K.op("pe", lambda h=h, pr_=pr_, trow=trow, u=u: nc.tensor.matmul(
                                psum[6][:, h * 64:h * 64 + cl], lhsT=Sb[:, pr_, :], rhs=qdm[u][:, h, trow],
                                start=True, stop=False), R=[b_Sb, b_qdm[u]], W=[b_ps[6]], inc=False)
                            K.op("pe", lambda h=h, trow=trow, u=u: nc.tensor.matmul(
                                psum[6][:, h * 64:h * 64 + cl], lhsT=vtb[u][0:nt, h * 128:(h + 1) * 128],
                                rhs=att[u][0:nt, h, trow], start=False, stop=True),
                                R=[b_vt[u], b_att[u]], W=[b_ps[6]], inc=(h == 3))
                        yield
                        for h in range(4):
                            pr_, hf = h // 2, h % 2
                            K.op("pe", lambda h=h, pr_=pr_, hf=hf, trow=trow, u=u: nc.tensor.matmul(
                                psum[7][hf * 64:hf * 64 + 64, pr_ * 128:(pr_ + 1) * 128],
                                lhsT=kdec[u][trow, h * 64:(h + 1) * 64], rhs=vtb[u][trow, h * 128:(h + 1) * 128],
                                start=True, stop=True), R=[b_kd[u], b_vt[u]], W=[b_ps[7]], inc=(h == 3))
                        yield
                        K.op("act", lambda lc=lc, trow=trow, ch=ch: nc.scalar.copy(
                            out=oB[:, :, lc + ch * 64:lc + ch * 64 + cl],
                            in_=psum[6][:, 0:256].rearrange("p (a b) -> p a b", a=4)[:, :, 0:cl]),
                            R=[b_ps[6]], W=[b_oB])
                        yield
                        for pr_ in range(2):
                            K.op("dve", lambda pr_=pr_, ch=ch, u=u: nc.vector.scalar_tensor_tensor(
                                out=S[:, pr_, :], in0=S[:, pr_, :], scalar=atot[u][:, pr_, ch:ch + 1],
                                in1=psum[7][:, pr_ * 128:(pr_ + 1) * 128], op0=ALU.mult, op1=ALU.add),
                                R=[b_at[u], b_ps[7]], W=[b_S])
                        yield
                        K.op("act", lambda: nc.scalar.copy(out=Sb[:], in_=S[:]), R=[b_S], W=[b_Sb])

                    yield

                for (c0, n) in TILES:
                    for bi in range(9):
                        q = bi % 2
                        if bi < 8:
                            woff = bi * 128 if bi < 4 else 1024 + (bi - 4) * 128
                            mw = 128
                        else:
                            woff, mw = 1536, 16
                        mm_group(psum[q][0:mw, 0:n], [(wB[:, k, woff:woff + mw], x_bf[:, k, c0:c0 + n]) for k in range(KC)],
                                 R=[b_wBb[woff // 128], b_xbf], W=[b_ps[q]])
                        if bi < 4:
                            K.op("dve", lambda bi=bi, q=q: nc.vector.tensor_copy(out=qk[:, bi, 0:n], in_=psum[q][:, 0:n]),
                                 R=[b_ps[q]], W=[b_qk])
                        elif bi < 8:
                            K.op("act", lambda bi=bi, q=q: nc.scalar.activation(
                                out=sr[:, bi - 4, 0:n], in_=psum[q][:, 0:n], func=AF.Silu), R=[b_ps[q]], W=[b_sr])
                        else:
                            K.op("dve", lambda q=q: nc.vector.tensor_copy(out=glr[0:16, 0:n], in_=psum[q][0:16, 0:n]),
                                 R=[b_ps[q]], W=[b_glr])
                    subs = subtiles(c0, n)
                    ths = []
                    for i_, (kind, sc0, nt) in enumerate(subs):
                        gi = sub_ctr[0] + i_
                        afterA = ([2 * (i_ - 2)] if i_ >= 2 else []) + ([2 * (i_ - 3) + 1] if i_ >= 3 else [])
                        afterB = [2 * i_] + ([2 * (i_ - 1) + 1] if i_ >= 1 else [])
                        ths.append((lambda kind=kind, sc0=sc0, nt=nt, gi=gi, c0=c0: glaA(kind, sc0, nt, gi % 3, 2 + 2 * (gi % 2), 3 + 2 * (gi % 2), c0), afterA))
                        if kind == "sample":
                            ths.append((lambda: None, afterB))
                        else:
                            ths.append((lambda kind=kind, sc0=sc0, nt=nt, gi=gi, c0=c0: glaB(kind, sc0, nt, gi % 3, c0), afterB))
                    K.run_sched(ths)
                    sub_ctr[0] += len(subs)
                    gated_norm(st, oB, b_oB, sr, b_sr, nw, b_nw, og, b_og, 4, c0, n, nbufs)
                K.dma("sp", o_gla_p[l].rearrange("(pr hf) k v -> (hf k) pr v", hf=2), S[:], R=[b_S])

        def gla_sample(l, u, qm, b_qm, E12, b_E12, kdec, b_kd, vtb, b_vt, Ss, b_Ss, vbd, b_vbd, oB, b_oB):
            K.op("dve", lambda: nc.vector.tensor_tensor(
                out=vbd[:, :, :], in0=vtb[u][0:16, :].unsqueeze(1).broadcast_to([16, NS, 512]),
                in1=C["ident"][0:16, 0:16].unsqueeze(2).broadcast_to([16, NS, 512]), op=ALU.mult),
                R=[b_vt[u], b_cst], W=[b_vbd])
            for s4 in range(4):
                for si in range(4):
                    s = 4 * s4 + si
                    for h in range(4):
                        pr_, hf = h // 2, h % 2
                        q = 6 + si // 2
                        cb = (si % 2) * 256 + pr_ * 128
                        K.op("pe", lambda s=s, h=h, hf=hf, q=q, cb=cb: nc.tensor.matmul(
                            psum[q][hf * 64:hf * 64 + 64, cb:cb + 128],
                            lhsT=kdec[u][0:16, h * 64:(h + 1) * 64], rhs=vbd[0:16, s, h * 128:(h + 1) * 128],
                            start=True, stop=True), R=[b_kd[u], b_vbd], W=[b_ps[q]], inc=(h == 3))
                for si in range(4):
                    s = 4 * s4 + si
                    q = 6 + si // 2
                    for pr_ in range(2):
                        cb = (si % 2) * 256 + pr_ * 128
                        K.op("dve", lambda s=s, pr_=pr_, q=q, cb=cb: nc.vector.scalar_tensor_tensor(
                            out=Ss[:, s, pr_, :], in0=Ss[:, s, pr_, :], scalar=E12[u][:, 0, pr_, s:s + 1],
                            in1=psum[q][:, cb:cb + 128], op0=ALU.mult, op1=ALU.add),
                            R=[b_E12[u], b_ps[q]], W=[b_Ss])
            for s in range(NS):
                for h in range(4):
                    pr_, hf = h // 2, h % 2
                    K.op("pe", lambda s=s, h=h, pr_=pr_: nc.tensor.matmul(
                        psum[5][:, h * 32 + s:h * 32 + s + 2], lhsT=Ss[:, s, pr_, :], rhs=qm[:, h, s:s + 2],
                        start=True, stop=True), R=[b_Ss, b_qm], W=[b_ps[5]], inc=(s == NS - 1 and h == 3))
            K.op("act", lambda: nc.scalar.activation(
                out=oB[:, :, 0:16], in_=psum[5][:, 0:128].rearrange("p (a b) -> p a b", a=4)[:, :, 0:16],
                func=AF.Copy, scale=0.125), R=[b_ps[5]], W=[b_oB])
            sgl_o = o_gla_s[l].rearrange("s (pr hf) k v -> (hf k) s pr v", hf=2)
            for s4 in range(4):
                K.dma("sp", sgl_o[:, 4 * s4:4 * s4 + 4], Ss[:, 4 * s4:4 * s4 + 4], R=[b_Ss])

        def gdn_phase(l, og, b_og, win_v):
            keep = {}
            with ExitStack() as st0:
                s_wT = sbt(st0, "d_swT", [128, 4, 16], BF16)
                s_u0 = sbt(st0, "d_su0", [16, 4, 128])
                s_kd = sbt(st0, "d_skd", [16, 4, 128], BF16)
                s_qn = sbt(st0, "d_sqn", [128, 4, 16], BF16)
                s_G = sbt(st0, "d_sG", [128, NS, 4])
                s_gz = sbt(st0, "d_sgz", [128, 4, 16], BF16)
                b_skeep = Buf()
                nw = sbt(st0, "gdn_nw", [128, 1])
                b_nw = Buf()
                K.dma("sp", nw[:, :], gdn_norm_w[l].rearrange("(p o) -> p o", o=1), W=[b_nw],
                      allow_slow_non_contiguous=True)
                with ExitStack() as st:
                    wA = sbt(st, "wA", [128, KC, 2056], BF16)
                    b_wAb = [Buf() for _ in range(17)]
                    for blk_ in range(16):
                        K.dma("pool", wA[:, :, blk_ * 128:(blk_ + 1) * 128], win_v[:, :, blk_ * 128:(blk_ + 1) * 128],
                              W=[b_wAb[blk_]])
                    K.dma("pool", wA[:, :, 2048:2056], win_v[:, :, 2048:2056], W=[b_wAb[16]])
                    cw = sbt(st, "d_cw", [128, 12, 4])
                    b_cw = Buf()
                    for i_ in range(4):
                        K.dma("sp", cw[:, :, i_], conv_w[l][i_].rearrange("(b p) -> p b", p=128), W=[b_cw],
                              allow_slow_non_contiguous=True)
                    abt = sbt(st, "d_abt", [128, 2, 4])
                    b_abt = Buf()
                    K.dma("sp", abt[:, 0, :], a_log[l].partition_broadcast(128), W=[b_abt])
                    K.dma("sp", abt[:, 1, :], dt_bias[l].partition_broadcast(128), W=[b_abt])
                    K.op("act", lambda: nc.scalar.activation(out=abt[:, 0, :], in_=abt[:, 0, :], func=AF.Exp),
                         R=[b_abt], W=[b_abt])
                    K.op("dve", lambda: nc.vector.tensor_scalar(out=abt[:, 0, :], in0=abt[:, 0, :], scalar1=-1.0,
                                                                scalar2=None, op0=ALU.mult), R=[b_abt], W=[b_abt])
                    cbT = sbt(st, "d_cbT", [128, 12, 48])
                    b_cbT = Buf()
                    with ExitStack() as stc:
                        cb48 = sbt(stc, "d_cb48", [48, 1536])
                        b_cb48 = Buf()
                        K.dma("sp", cb48[:], state_conv[l].rearrange("s i c -> (s i) c"), W=[b_cb48])
                        for half in range(2):
                            q = half
                            for b6 in range(6):
                                blk = 6 * half + b6
                                K.op("pe", lambda blk=blk, b6=b6, q=q: nc.tensor.transpose(
                                    psum[q][:, b6 * 48:(b6 + 1) * 48], cb48[0:48, blk * 128:(blk + 1) * 128],
                                    C["ident"][0:48, 0:48]), R=[b_cb48, b_cst], W=[b_ps[q]], inc=(b6 == 5))
                            K.op("act", lambda half=half, q=q: nc.scalar.copy(
                                out=cbT[:, 6 * half:6 * half + 6, :],
                                in_=psum[q][:, 0:288].rearrange("p (a b) -> p a b", a=6)), R=[b_ps[q]], W=[b_cbT])
                        K.barrier()
                    K.dma("sp", o_conv_s[l][:, 0:2, :], state_conv[l][:, 1:3, :])
                    halo = sbt(st, "d_halo", [128, 12, 4])
                    b_halo = Buf()
                    K.op("dve", lambda: nc.vector.memset(halo[:], 0.0), W=[b_halo])
                    Pe = [sbt(st, "d_Pe%d" % i, [128, 3 + 512]) for i in range(2)]
                    b_Pe = [Buf(), Buf()]
                    acc = [sbt(st, "d_acc%d" % i, [128, 512]) for i in range(2)]
                    b_acc = [Buf(), Buf()]
                    cq = [sbt(st, "d_cq%d" % i, [128, 512]) for i in range(2)]
                    b_cq = [Buf(), Buf()]
                    vn = sbt(st, "d_vn", [128, 4, 512], BF16)
                    b_vn = Buf()
                    qkn = sbt(st, "d_qkn", [128, 8, 512], BF16)
                    b_qkn = Buf()
                    gz = sbt(st, "d_gz", [128, 4, 512], BF16)
                    b_gz = Buf()
                    oA = big[:, 8 * T:8 * T + 4096].bitcast(F32).rearrange("p (a b) -> p a b", a=4)
                    b_oA = Buf()
                    sq = big[:, 8 * T + 4096:8 * T + 6144].rearrange("p (a b) -> p a b", a=4)
                    rsd = sbt(st, "d_rsd", [128, 512])
                    rsd_b = sbt(st, "d_rsdb", [128, 512])
                    tn = acc[0]
                    b_sq, b_rsd, b_tn, b_rsdb = Buf(), Buf(), b_acc[0], Buf()
                    nbufs = (sq, b_sq, rsd, b_rsd, tn, b_tn)
                    pnew = big[0:16, 8 * T:8 * T + 3072].bitcast(F32)
                    b_pnew = b_oA
                    tb = [sbt(st, "d_tb%d" % i, [128, 6, 4]) for i in range(2)]
                    b_tb = [Buf(), Buf()]
                    r2 = sbt(st, "d_r2", [128, 8])
                    b_r2 = Buf()
                    Gbc = [sbt(st, "d_Gbc%d" % i, [128, 2, 4]) for i in range(2)]
                    b_Gbc = [Buf(), Buf()]
                    dg = sbt(st, "d_dg", [128, 8, 128])
                    b_dg = Buf()
                    gm = sbt(st, "d_gm", [128, 4, 128])
                    b_gm = Buf()
                    ET = sbt(st, "d_ET", [128, 4, 128])
                    ETi = sbt(st, "d_ETi", [128, 4, 128])
                    ETs = ET
                    b_ET, b_ETi = Buf(), Buf()
                    b_ETs = b_ET
                    QA = [sbt(st, "d_QA%d" % i, [128, 4, 128]) for i in range(2)]
                    QTA = [sbt(st, "d_QTA%d" % i, [128, 4, 128]) for i in range(2)]
                    b_QA = [Buf(), Buf()]
                    b_QTA = [Buf(), Buf()]
                    Qs = sbt(st, "d_Qs", [128, 4, 128])
                    QTs = sbt(st, "d_QTs", [128, 4, 128])
                    b_Qs, b_QTs = Buf(), Buf()
                    TT = [sbt(st, "d_TT%d" % i, [128, 4, 128]) for i in range(2)]
                    b_TT = [Buf(), Buf()]
                    TTb = sbt(st, "d_TTb", [128, 4, 128], BF16)
                    b_TTb = Buf()
                    bkq = [sbt(st, "d_bkq%d" % i, [128, 2, 4, 128], BF16) for i in range(2)]
                    b_bkq = [Buf(), Buf()]
                    qkT = [sbt(st, "d_qkT%d" % i, [128, 4, 128], BF16) for i in range(2)]
                    b_qkT = [Buf(), Buf()]
                    tok3 = [sbt(st, "d_tok3%d" % i, [128, 3, 4, 128], BF16) for i in range(2)]
                    b_tok3 = [Buf(), Buf()]
                    u0 = [sbt(st, "d_u0%d" % i, [128, 4, 128]) for i in range(2)]
                    wT = [sbt(st, "d_wT%d" % i, [128, 4, 128], BF16) for i in range(2)]
                    uu = sbt(st, "d_u", [128, 4, 128], BF16)
                    b_u0, b_wT, b_u = [Buf(), Buf()], [Buf(), Buf()], Buf()
                    S = sbt(st, "d_S", [128, 4, 128])
                    Sb = sbt(st, "d_Sb", [128, 4, 128], BF16)
                    b_S, b_Sb = Buf(), Buf()
                    K.op("dve", lambda: nc.vector.memset(S[:], 0.0), W=[b_S])
                    K.op("dve", lambda: nc.vector.memset(Sb[:], 0.0), W=[b_Sb])
                    K.op("dve", lambda: nc.vector.memset(uu[:], 0.0), W=[b_u])
                    sub_ctr = [0]
                    scan_done = [0]
                    def stageA(kind, sc0, nt, a, c0):
                        lc = sc0 - c0
                        smp = (kind == "sample")
                        m_incl = C["ident"] if smp else C["mu_incl"]
                        m_bd = C["ident"] if smp else C["bd"]
                        m_ustrict = C["zero"] if smp else C["mu_strict"]
                        m_lstrict = C["zero"] if smp else C["ml_strict"]
                        T_ = lambda i: tb[a][0:nt, i, :]
                        pv = lambda b: psum[b][:, :].rearrange("p (a c) -> p a c", a=4)[:, :, 0:nt]
                        pt = lambda b: psum[b][0:nt, :].rearrange("p (a c) -> p a c", a=4)[:, :, 0:nt]
                        p4 = lambda b: psum[b][0:nt, :].rearrange("p (a c) -> p a c", a=4)
                        bc = lambda i: tb[a][0:nt, i, :].unsqueeze(2).broadcast_to([nt, 4, 128])
                        mm_group(psum[2][0:nt, 0:8], [(x_bf[:, k, sc0:sc0 + nt], wA[:, k, 2048:2056]) for k in range(KC)],
                                 R=[b_wAb[16], b_xbf], W=[b_ps[2]])
                        yield
                        K.op("act", lambda: nc.scalar.activation(out=T_(0), in_=psum[2][0:nt, 0:4], func=AF.Exp, scale=-1.0),
                             R=[b_ps[2]], W=[b_tb[a]])
                        yield
                        K.op("dve", lambda: nc.vector.tensor_tensor(out=T_(2), in0=psum[2][0:nt, 4:8], in1=abt[0:nt, 1, :],
                                                                    op=ALU.add), R=[b_ps[2], b_abt], W=[b_tb[a]])
                        yield
                        K.op("act", lambda: nc.scalar.activation(out=T_(0), in_=T_(0), func=AF.Ln, bias=one1[0:nt, 0:1]),
                             R=[b_tb[a], b_eps], W=[b_tb[a]])
                        yield
                        K.op("act", lambda: nc.scalar.activation(out=T_(0), in_=T_(0), func=AF.Exp, scale=-1.0),
                             R=[b_tb[a]], W=[b_tb[a]])
                        yield
                        K.op("act", lambda: nc.scalar.activation(out=T_(2), in_=T_(2), func=AF.Exp), R=[b_tb[a]], W=[b_tb[a]])
                        yield
                        K.op("act", lambda: nc.scalar.activation(out=T_(2), in_=T_(2), func=AF.Ln, bias=one1[0:nt, 0:1]),
                             R=[b_tb[a], b_eps], W=[b_tb[a]])
                        yield
                        K.op("dve", lambda: nc.vector.tensor_tensor(out=T_(2), in0=T_(2), in1=abt[0:nt, 0, :], op=ALU.mult),
                             R=[b_tb[a], b_abt], W=[b_tb[a]])
                        yield
                        mm_group(psum[2][0:nt, 8:12], [(m_incl[0:nt, 0:nt], T_(2))], R=[b_tb[a], b_cst], W=[b_ps[2]])
                        yield
                        mm_group(psum[2][0:nt, 12:16], [(m_bd[0:nt, 0:nt], T_(2))], R=[b_tb[a], b_cst], W=[b_ps[2]])
                        yield
                        K.op("act", lambda: nc.scalar.activation(out=T_(1), in_=psum[2][0:nt, 8:12], func=AF.Exp),
                             R=[b_ps[2]], W=[b_tb[a]])
                        yield
                        K.op("dve", lambda: nc.vector.tensor_copy(out=T_(5), in_=psum[2][0:nt, 8:12]), R=[b_ps[2]], W=[b_tb[a]])
                        yield
                        K.op("dve", lambda: nc.vector.tensor_tensor(out=T_(3), in0=psum[2][0:nt, 12:16], in1=T_(5),
                                                                    op=ALU.subtract), R=[b_ps[2], b_tb[a]], W=[b_tb[a]])
                        yield
                        K.op("act", lambda: nc.scalar.activation(out=T_(3), in_=T_(3), func=AF.Exp), R=[b_tb[a]], W=[b_tb[a]])
                        yield
                        K.op("dve", lambda: nc.vector.tensor_tensor(out=T_(4), in0=T_(0), in1=T_(1), op=ALU.mult),
                             R=[b_tb[a]], W=[b_tb[a]])
                        yield
                        if smp:
                            K.op("dve", lambda: nc.vector.tensor_tensor(
                                out=dg[0:16, 0:4, 0:16].rearrange("p h s -> p s h"),
                                in0=T_(2).unsqueeze(1).broadcast_to([16, 16, 4]),
                                in1=C["ident"][0:16, 0:16].unsqueeze(2).broadcast_to([16, 16, 4]), op=ALU.mult),
                                R=[b_tb[a], b_cst], W=[b_dg])
                            for h in range(4):
                                K.op("pe", lambda h=h: nc.tensor.matmul(psum[3][:, h * 16:(h + 1) * 16],
                                                                        lhsT=C["ones"][0:16, :], rhs=dg[0:16, h, 0:16],
                                                                        start=True, stop=True),
                                     R=[b_dg, b_cst], W=[b_ps[3]], inc=(h == 3))
                            K.op("act", lambda: nc.scalar.activation(
                                out=s_G[:, :, :].rearrange("p s h -> p h s"),
                                in_=psum[3][:, 0:64].rearrange("p (h s) -> p h s", h=4), func=AF.Exp),
                                R=[b_ps[3]], W=[b_skeep])
                        else:
                            K.op("dve", lambda: nc.vector.tensor_tensor(
                                out=r2[0:nt, :].rearrange("p (c h) -> p c h", c=2),
                                in0=T_(2).unsqueeze(1).broadcast_to([nt, 2, 4]),
                                in1=cst[0:nt, IDX["blk0"]:IDX["blk0"] + 2, 0:1].broadcast_to([nt, 2, 4]), op=ALU.mult),
                                R=[b_tb[a], b_cst], W=[b_r2])
                            mm_group(psum[2][:, 16:24], [(C["ones"][0:nt, :], r2[0:nt, :])], R=[b_r2, b_cst], W=[b_ps[2]])
                            K.op("act", lambda: nc.scalar.activation(out=Gbc[a][:, :, :].rearrange("p c h -> p (c h)"),
                                                                     in_=psum[2][:, 16:24], func=AF.Exp),
                                 R=[b_ps[2]], W=[b_Gbc[a]])
                        yield
                        K.op("dve", lambda: nc.vector.tensor_tensor(
                            out=dg[0:nt, :, 0:nt], in0=C["ident"][0:nt, 0:nt].unsqueeze(1).broadcast_to([nt, 8, nt]),
                            in1=tb[a][0:nt, 0:2, :].rearrange("p a h -> p (a h)").unsqueeze(2).broadcast_to([nt, 8, nt]),
                            op=ALU.mult), R=[b_tb[a], b_cst, b_skeep], W=[b_dg])
                        yield
                        for wh in range(2):
                            for h in range(4):
                                K.op("pe", lambda wh=wh, h=h: nc.tensor.matmul(
                                    psum[3 + wh][:, h * 128:h * 128 + nt], lhsT=C["ones"][0:nt, :],
                                    rhs=dg[0:nt, 4 * wh + h, 0:nt], start=True, stop=True),
                                    R=[b_dg, b_cst], W=[b_ps[3 + wh]], inc=(h == 3))
                        yield
                        K.op("dve", lambda: nc.vector.tensor_tensor(out=bkq[a][:, 0, :, 0:nt], in0=qkn[:, 4:8, lc:lc + nt],
                                                                    in1=pv(3), op=ALU.mult),
                             R=[b_qkn, b_ps[3]], W=[b_bkq[a]])
                        yield
                        K.op("dve", lambda: nc.vector.tensor_tensor(out=bkq[a][:, 1, :, 0:nt], in0=qkn[:, 0:4, lc:lc + nt],
                                                                    in1=pv(4), op=ALU.mult),
                             R=[b_qkn, b_ps[4]], W=[b_bkq[a]])
                        yield
                        K.op("dve", lambda: nc.vector.tensor_tensor(
                            out=gm[0:nt, :, 0:nt], in0=m_incl[0:nt, 0:nt].unsqueeze(1).broadcast_to([nt, 4, nt]),
                            in1=T_(2).unsqueeze(2).broadcast_to([nt, 4, nt]), op=ALU.mult), R=[b_tb[a], b_cst], W=[b_gm])
                        yield
                        for h in range(4):
                            K.op("pe", lambda h=h: nc.tensor.matmul(psum[2][0:nt, h * 128:h * 128 + nt],
                                                                    lhsT=m_lstrict[0:nt, 0:nt], rhs=gm[0:nt, h, 0:nt],
                                                                    start=True, stop=True),
                                 R=[b_gm, b_cst], W=[b_ps[2]], inc=(h == 3))
                        yield
                        K.op("act", lambda: nc.scalar.activation(out=ET[0:nt, :, 0:nt], in_=pt(2), func=AF.Exp),
                             R=[b_ps[2]], W=[b_ET])
                        yield
                        K.op("dve", lambda: nc.vector.tensor_tensor(
                            out=ETi[0:nt, :, 0:nt], in0=ET[0:nt, :, 0:nt],
                            in1=m_incl[0:nt, 0:nt].unsqueeze(1).broadcast_to([nt, 4, nt]), op=ALU.mult),
                            R=[b_ET, b_cst], W=[b_ETi])
                        yield
                        K.op("dve", lambda: nc.vector.tensor_tensor(
                            out=ETs[0:nt, :, 0:nt], in0=ET[0:nt, :, 0:nt],
                            in1=m_ustrict[0:nt, 0:nt].unsqueeze(1).broadcast_to([nt, 4, nt]), op=ALU.mult),
                            R=[b_ET, b_cst, b_ETi], W=[b_ETs])
                        yield
                        for h in range(4):
                            K.op("pe", lambda h=h: nc.tensor.matmul(psum[3][0:nt, h * 128:h * 128 + nt],
                                                                    lhsT=qkn[:, 4 + h, lc:lc + nt], rhs=bkq[a][:, 0, h, 0:nt],
                                                                    start=True, stop=True),
                                 R=[b_qkn, b_bkq[a]], W=[b_ps[3]], inc=(h == 3))
                        yield
                        for h in range(4):
                            K.op("pe", lambda h=h: nc.tensor.matmul(psum[4][0:nt, h * 128:h * 128 + nt],
                                                                    lhsT=qkn[:, 4 + h, lc:lc + nt], rhs=qkn[:, h, lc:lc + nt],
                                                                    start=True, stop=True),
                                 R=[b_qkn], W=[b_ps[4]], inc=(h == 3))
                        AT, A_ = QTA[a], QA[a]
                        yield
                        K.op("dve", lambda: nc.vector.tensor_tensor(out=AT[0:nt, :, 0:nt], in0=pt(3), in1=ETs[0:nt, :, 0:nt],
                                                                    op=ALU.mult), R=[b_ps[3], b_ETs], W=[b_QTA[a]])
                        yield
                        K.op("dve", lambda: nc.vector.tensor_tensor(out=qkT[a][0:nt, :, 0:nt], in0=pt(4), in1=ETi[0:nt, :, 0:nt],
                                                                    op=ALU.mult), R=[b_ps[4], b_ETi], W=[b_qkT[a]])
                        yield
                        K.op("dve", lambda: nc.vector.scalar_tensor_tensor(
                            out=TT[a][0:nt, :, 0:nt], in0=AT[0:nt, :, 0:nt], scalar=-1.0,
                            in1=C["ident"][0:nt, 0:nt].unsqueeze(1).broadcast_to([nt, 4, nt]),
                            op0=ALU.mult, op1=ALU.add), R=[b_QTA[a], b_cst], W=[b_TT[a]])
                        yield
                        if not smp:
                            for h in range(4):
                                K.op("pe", lambda h=h: nc.tensor.transpose(psum[2][0:nt, h * 128:h * 128 + nt],
                                                                           AT[0:nt, h, 0:nt], C["ident"][0:nt, 0:nt]),
                                     R=[b_QTA[a], b_cst], W=[b_ps[2]], inc=(h == 3))
                            yield
                            K.op("act", lambda: nc.scalar.copy(out=A_[0:nt, :, 0:nt], in_=pt(2)), R=[b_ps[2]], W=[b_QA[a]])

                    def stageB1(kind, sc0, nt, a, c0):
                        lc = sc0 - c0
                        smp = (kind == "sample")
                        m_incl = C["ident"] if smp else C["mu_incl"]
                        m_bd = C["ident"] if smp else C["bd"]
                        m_ustrict = C["zero"] if smp else C["mu_strict"]
                        m_lstrict = C["zero"] if smp else C["ml_strict"]
                        T_ = lambda i: tb[a][0:nt, i, :]
                        pv = lambda b: psum[b][:, :].rearrange("p (a c) -> p a c", a=4)[:, :, 0:nt]
                        pt = lambda b: psum[b][0:nt, :].rearrange("p (a c) -> p a c", a=4)[:, :, 0:nt]
                        p4 = lambda b: psum[b][0:nt, :].rearrange("p (a c) -> p a c", a=4)
                        bc = lambda i: tb[a][0:nt, i, :].unsqueeze(2).broadcast_to([nt, 4, 128])
                        yield
                        for h in range(4):
                            K.op("pe", lambda h=h: nc.tensor.matmul(psum[5][0:nt, h * 128:(h + 1) * 128],
                                                                    lhsT=qkn[:, 4 + h, lc:lc + nt], rhs=CB["ident"],
                                                                    start=True, stop=True),
                                 R=[b_qkn, b_cst], W=[b_ps[5]], inc=(h == 3))
                        yield
                        for h in range(4):
                            K.op("pe", lambda h=h: nc.tensor.matmul(psum[6][0:nt, h * 128:(h + 1) * 128],
                                                                    lhsT=vn[:, h, lc:lc + nt], rhs=CB["ident"],
                                                                    start=True, stop=True),
                                 R=[b_vn, b_cst], W=[b_ps[6]], inc=(h == 3))
                        yield
                        K.op("dve", lambda: nc.vector.tensor_tensor(out=tok3[a][0:nt, 0], in0=p4(5), in1=bc(4), op=ALU.mult),
                             R=[b_ps[5], b_tb[a]], W=[b_tok3[a]])
                        yield
                        K.op("dve", lambda: nc.vector.tensor_tensor(out=tok3[a][0:nt, 1], in0=p4(5), in1=bc(3), op=ALU.mult),
                             R=[b_ps[5], b_tb[a]], W=[b_tok3[a]])
                        yield
                        K.op("dve", lambda: nc.vector.tensor_tensor(out=tok3[a][0:nt, 2], in0=p4(6), in1=bc(0), op=ALU.mult),
                             R=[b_ps[6], b_tb[a]], W=[b_tok3[a]])
                        if not smp:
                            n_inc = 5 if nt == 128 else 3
                            cq_, cqt_, b_cq_, b_cqt_ = QA[a], QTA[a], b_QA[a], b_QTA[a]
                            nq_, nqt_, b_nq_, b_nqt_ = Qs, QTs, b_Qs, b_QTs
                            for lv in range(0, n_inc + 1):
                                need_q = lv < n_inc
                                need_qt = lv < n_inc - 1
                                if lv >= 1:
                                    for h in range(4):
                                        K.op("pe", lambda h=h: nc.tensor.matmul(
                                            psum[7][0:nt, h * 128:h * 128 + nt], lhsT=cq_[0:nt, h, 0:nt],
                                            rhs=TT[a][0:nt, h, 0:nt], start=True, stop=True),
                                            R=[b_cq_, b_TT[a]], W=[b_ps[7]], inc=(h == 3))
                                if need_q:
                                    for h in range(4):
                                        K.op("pe", lambda h=h: nc.tensor.matmul(
                                            psum[5][0:nt, h * 128:h * 128 + nt], lhsT=cqt_[0:nt, h, 0:nt],
                                            rhs=cq_[0:nt, h, 0:nt], start=True, stop=True),
                                            R=[b_cq_, b_cqt_], W=[b_ps[5]], inc=(h == 3))
                                if need_qt:
                                    for h in range(4):
                                        K.op("pe", lambda h=h: nc.tensor.matmul(
                                            psum[6][0:nt, h * 128:h * 128 + nt], lhsT=cq_[0:nt, h, 0:nt],
                                            rhs=cqt_[0:nt, h, 0:nt], start=True, stop=True),
                                            R=[b_cq_, b_cqt_], W=[b_ps[6]], inc=(h == 3))
                                yield
                                if need_q:
                                    K.op("act", lambda: nc.scalar.copy(out=nq_[0:nt, :, 0:nt], in_=pt(5)),
                                         R=[b_ps[5]], W=[b_nq_])
                                if lv >= 1:
                                    K.op("dve", lambda: nc.vector.tensor_tensor(out=TT[a][0:nt, :, 0:nt], in0=TT[a][0:nt, :, 0:nt],
                                                                                in1=pt(7), op=ALU.add),
                                         R=[b_ps[7], b_TT[a]], W=[b_TT[a]])
                                if need_qt:
                                    K.op("act" if lv >= 1 else "dve", (lambda: nc.scalar.copy(out=nqt_[0:nt, :, 0:nt], in_=pt(6))) if lv >= 1
                                         else (lambda: nc.vector.tensor_copy(out=nqt_[0:nt, :, 0:nt], in_=pt(6))),
                                         R=[b_ps[6]], W=[b_nqt_])
                                yield
                                cq_, cqt_, b_cq_, b_cqt_, nq_, nqt_, b_nq_, b_nqt_ = nq_, nqt_, b_nq_, b_nqt_, cq_, cqt_, b_cq_, b_cqt_
                        yield
                        K.op("act", lambda: nc.scalar.copy(out=TTb[0:nt, :, 0:nt], in_=TT[a][0:nt, :, 0:nt]), R=[b_TT[a]], W=[b_TTb])
                        yield
                        yield
                        for h in range(4):
                            K.op("pe", lambda h=h: nc.tensor.matmul(psum[7][0:nt, h * 128:(h + 1) * 128],
                                                                    lhsT=TTb[0:nt, h, 0:nt], rhs=tok3[a][0:nt, 2, h, :],
                                                                    start=True, stop=True),
                                 R=[b_TTb, b_tok3[a]], W=[b_ps[7]], inc=(h == 3))
                        yield
                        for h in range(4):
                            K.op("pe", lambda h=h: nc.tensor.matmul(psum[5][:, h * 128:h * 128 + nt],
                                                                    lhsT=tok3[a][0:nt, 0, h, :], rhs=TTb[0:nt, h, 0:nt],
                                                                    start=True, stop=True),
                                 R=[b_TTb, b_tok3[a]], W=[b_ps[5]], inc=(h == 3))
                        yield
                        if smp:
                            K.op("act", lambda: nc.scalar.copy(out=s_u0[:], in_=p4(7)), R=[b_ps[7]], W=[b_skeep])
                            K.op("dve", lambda: nc.vector.tensor_copy(out=s_wT[:], in_=pv(5)), R=[b_ps[5]], W=[b_skeep])
                            K.op("dve", lambda: nc.vector.tensor_copy(out=s_kd[:], in_=tok3[a][0:16, 1]), R=[b_tok3[a]], W=[b_skeep])
                            K.op("dve", lambda: nc.vector.tensor_copy(out=s_qn[:], in_=qkn[:, 0:4, 0:16]), R=[b_qkn], W=[b_skeep])
                            K.op("dve", lambda: nc.vector.tensor_copy(out=s_gz[:], in_=gz[:, :, 0:16]), R=[b_gz], W=[b_skeep])
                            return
                        yield
                        K.op("act", lambda: nc.scalar.copy(out=u0[a][0:nt], in_=p4(7)), R=[b_ps[7]], W=[b_u0[a]])
                        yield
                        K.op("dve", lambda: nc.vector.tensor_copy(out=wT[a][:, :, 0:nt], in_=pv(5)), R=[b_ps[5]], W=[b_wT[a]])

                    def stageB2(kind, sc0, nt, a, c0):
                        lc = sc0 - c0
                        smp = (kind == "sample")
                        m_incl = C["ident"] if smp else C["mu_incl"]
                        m_bd = C["ident"] if smp else C["bd"]
                        m_ustrict = C["zero"] if smp else C["mu_strict"]
                        m_lstrict = C["zero"] if smp else C["ml_strict"]
                        T_ = lambda i: tb[a][0:nt, i, :]
                        pv = lambda b: psum[b][:, :].rearrange("p (a c) -> p a c", a=4)[:, :, 0:nt]
                        pt = lambda b: psum[b][0:nt, :].rearrange("p (a c) -> p a c", a=4)[:, :, 0:nt]
                        p4 = lambda b: psum[b][0:nt, :].rearrange("p (a c) -> p a c", a=4)
                        bc = lambda i: tb[a][0:nt, i, :].unsqueeze(2).broadcast_to([nt, 4, 128])
                        if smp:
                            return
                        nch = 2 if nt == 128 else 1
                        cl = 64 if nt == 128 else nt
                        yield
                        for ch in range(nch):
                            tr = slice(ch * 64, ch * 64 + cl)
                            yield
                            for h in range(4):
                                K.op("pe", lambda h=h, tr=tr: nc.tensor.matmul(psum[0][tr, h * 128:(h + 1) * 128],
                                                                              lhsT=wT[a][:, h, tr], rhs=Sb[:, h, :],
                                                                              start=True, stop=True),
                                     R=[b_wT[a], b_Sb], W=[b_ps[0]], inc=(h == 3))
                            yield
                            K.op("dve", lambda tr=tr: nc.vector.tensor_tensor(
                                out=uu[tr], in0=u0[a][tr], in1=psum[0][tr, :].rearrange("p (a c) -> p a c", a=4),
                                op=ALU.subtract), R=[b_u0[a], b_ps[0]], W=[b_u])
                            yield
                            for h in range(4):
                                K.op("pe", lambda h=h, tr=tr: nc.tensor.matmul(psum[1][:, h * 64:h * 64 + cl],
                                                                              lhsT=Sb[:, h, :], rhs=bkq[a][:, 1, h, tr],
                                                                              start=True, stop=False),
                                     R=[b_Sb, b_bkq[a]], W=[b_ps[1]], inc=False)
                                K.op("pe", lambda h=h, tr=tr: nc.tensor.matmul(psum[1][:, h * 64:h * 64 + cl],
                                                                              lhsT=uu[0:nt, h, :], rhs=qkT[a][0:nt, h, tr],
                                                                              start=False, stop=True),
                                     R=[b_u, b_qkT[a]], W=[b_ps[1]], inc=(h == 3))
                            yield
                            for h in range(4):
                                K.op("pe", lambda h=h, tr=tr: nc.tensor.matmul(psum[0][:, h * 128:(h + 1) * 128],
                                                                              lhsT=tok3[a][tr, 1, h, :], rhs=uu[tr, h, :],
                                                                              start=True, stop=True),
                                     R=[b_tok3[a], b_u], W=[b_ps[0]], inc=(h == 3))
                            yield
                            K.op("act", lambda ch=ch: nc.scalar.copy(
                                out=oA[:, :, lc + ch * 64:lc + ch * 64 + cl],
                                in_=psum[1][:, 0:256].rearrange("p (a b) -> p a b", a=4)[:, :, 0:cl]),
                                R=[b_ps[1]], W=[b_oA])
                            yield
                            for h in range(4):
                                K.op("dve", lambda h=h, ch=ch: nc.vector.scalar_tensor_tensor(
                                    out=S[:, h, :], in0=S[:, h, :], scalar=Gbc[a][:, ch, h:h + 1],
                                    in1=psum[0][:, h * 128:(h + 1) * 128], op0=ALU.mult, op1=ALU.add),
                                    R=[b_Gbc[a], b_ps[0]], W=[b_S])
                            yield
                            K.op("act", lambda: nc.scalar.copy(out=Sb[:], in_=S[:]), R=[b_S], W=[b_Sb])
                        scan_done[0] += 1
                        yield

                    blk_it = 0
                    for (c0, n) in TILES:
                        is_ms = (c0 == 0)
                        def blkfn(blk, c0=c0, n=n, is_ms=is_ms):
                            q = blk % 2
                            mm_group(psum[q][:, 0:n], [(wA[:, k, blk * 128:(blk + 1) * 128], x_bf[:, k, c0:c0 + n])
                                                      for k in range(KC)], R=[b_wAb[blk], b_xbf], W=[b_ps[q]])
                            if blk >= 12:
                                K.op("act", lambda blk=blk, q=q: nc.scalar.activation(
                                    out=gz[:, blk - 12, 0:n], in_=psum[q][:, 0:n], func=AF.Silu), R=[b_ps[q]], W=[b_gz])
                                return
                            pe_ = blk % 2
                            ncv = 16 if is_ms else n
                            K.op("pool", lambda pe_=pe_, blk=blk: nc.gpsimd.tensor_copy(out=Pe[pe_][:, 0:3], in_=halo[:, blk, 0:3]),
                                 R=[b_halo], W=[b_Pe[pe_]])
                            src0 = 16 if is_ms else 0
                            K.op("act", lambda pe_=pe_, q=q, src0=src0, ncv=ncv: nc.scalar.copy(
                                out=Pe[pe_][:, 3:3 + ncv], in_=psum[q][:, src0:src0 + ncv]), R=[b_ps[q]], W=[b_Pe[pe_]])
                            K.op("pool", lambda pe_=pe_, blk=blk, ncv=ncv: nc.gpsimd.tensor_copy(
                                out=halo[:, blk, 0:3], in_=Pe[pe_][:, ncv:ncv + 3]), R=[b_Pe[pe_]], W=[b_halo])
                            a_ = acc[pe_]
                            o0 = 16 if is_ms else 0
                            K.op("act", lambda pe_=pe_, blk=blk, ncv=ncv, o0=o0: nc.scalar.activation(
                                out=acc[pe_][:, o0:o0 + ncv], in_=Pe[pe_][:, 3:3 + ncv], func=AF.Identity, scale=cw[:, blk, 3:4]),
                                R=[b_Pe[pe_], b_cw], W=[b_acc[pe_]])
                            for i in (2, 1, 0):
                                K.op("dve", lambda pe_=pe_, blk=blk, ncv=ncv, o0=o0, i=i: nc.vector.scalar_tensor_tensor(
                                    out=acc[pe_][:, o0:o0 + ncv], in0=Pe[pe_][:, i:i + ncv], scalar=cw[:, blk, i:i + 1],
                                    in1=acc[pe_][:, o0:o0 + ncv], op0=ALU.mult, op1=ALU.add),
                                    R=[b_Pe[pe_], b_cw, b_acc[pe_]], W=[b_acc[pe_]])
                            if is_ms:
                                cbv = cbT[:, blk, :].rearrange("p (s i) -> p s i", i=3)
                                K.op("dve", lambda pe_=pe_, blk=blk, q=q: nc.vector.tensor_scalar(
                                    out=acc[pe_][:, 0:16], in0=psum[q][:, 0:16], scalar1=cw[:, blk, 3:4], scalar2=None,
                                    op0=ALU.mult), R=[b_ps[q], b_cw], W=[b_acc[pe_]])
                                for i in range(3):
                                    K.op("dve", lambda pe_=pe_, blk=blk, i=i, cbv=cbv: nc.vector.scalar_tensor_tensor(
                                        out=acc[pe_][:, 0:16], in0=cbv[:, :, i], scalar=cw[:, blk, i:i + 1],
                                        in1=acc[pe_][:, 0:16], op0=ALU.mult, op1=ALU.add),
                                        R=[b_cbT, b_cw, b_acc[pe_]], W=[b_acc[pe_]])
                            if blk < 8:
                                ci = blk % 2
                                rr = rsd if ci == 0 else rsd_b
                                b_rr = b_rsd if ci == 0 else b_rsdb
                                K.op("act", lambda pe_=pe_, ci=ci: nc.scalar.activation(
                                    out=cq[ci][:, 0:n], in_=acc[pe_][:, 0:n], func=AF.Silu), R=[b_acc[pe_]], W=[b_cq[ci]])
                                K.op("act", lambda ci=ci: nc.scalar.activation(out=sq[:, ci, 0:n], in_=cq[ci][:, 0:n],
                                                                              func=AF.Square), R=[b_cq[ci]], W=[b_sq])
                                mm_group(psum[q][:, 0:n], [(CB["ones"], sq[:, ci, 0:n])], R=[b_sq, b_cst], W=[b_ps[q]])
                                K.op("act", lambda q=q, rr=rr: nc.scalar.activation(out=rr[:, 0:n], in_=psum[q][:, 0:n], func=AF.Ln,
                                                                                 bias=eps6[:, 0:1]), R=[b_ps[q], b_eps], W=[b_rr])
                                K.op("act", lambda rr=rr: nc.scalar.activation(out=rr[:, 0:n], in_=rr[:, 0:n], func=AF.Exp,
                                                                             scale=-0.5), R=[b_rr], W=[b_rr])
                                scl = (128.0 ** -0.5) if blk < 4 else 1.0
                                K.op("dve", lambda blk=blk, scl=scl, ci=ci, rr=rr: nc.vector.scalar_tensor_tensor(
                                    out=qkn[:, blk, 0:n], in0=cq[ci][:, 0:n], scalar=scl, in1=rr[:, 0:n],
                                    op0=ALU.mult, op1=ALU.mult), R=[b_cq[ci], b_rr], W=[b_qkn])
                            else:
                                K.op("act", lambda pe_=pe_, blk=blk: nc.scalar.activation(
                                    out=vn[:, blk - 8, 0:n], in_=acc[pe_][:, 0:n], func=AF.Silu), R=[b_acc[pe_]], W=[b_vn])
                        K.run_sched([(lambda blk=blk: blkfn(blk), ([blk - 2] if blk >= 2 else [])) for blk in range(16)])
                        if is_ms:
                            for g3 in range(3):
                                q = g3 % 2
                                mm_group(psum[q][0:16, :], [(x_bf[:, k, 0:16], wA[:, k, g3 * 512:(g3 + 1) * 512])
                                                            for k in range(KC)], R=[b_wAb[blk], b_xbf], W=[b_ps[q]])
                                K.op("act", lambda g3=g3, q=q: nc.scalar.copy(out=pnew[:, g3 * 512:(g3 + 1) * 512],
                                                                             in_=psum[q][0:16, :]), R=[b_ps[q]], W=[b_pnew])
                            K.dma("sp", o_conv_s[l][:, 2, :], pnew[:], R=[b_pnew])
                        subs = subtiles(c0, n)
                        ths = []
                        for i_, (kind, sc0, nt) in enumerate(subs):
                            a_ = (sub_ctr[0] + i_) % 2
                            iA, iB1, iB2 = 3 * i_, 3 * i_ + 1, 3 * i_ + 2
                            afterA = ([iA - 3] if i_ >= 1 else []) + ([iB2 - 6] if i_ >= 2 else [])
                            afterB1 = [iA] + ([iB1 - 3] if i_ >= 1 else []) + ([iB2 - 6] if i_ >= 2 else [])
                            afterB2 = [iB1] + ([iB2 - 3] if i_ >= 1 else [])
                            ths.append((lambda kind=kind, sc0=sc0, nt=nt, a_=a_, c0=c0: stageA(kind, sc0, nt, a_, c0), afterA))
                            ths.append((lambda kind=kind, sc0=sc0, nt=nt, a_=a_, c0=c0: stageB1(kind, sc0, nt, a_, c0), afterB1))
                            ths.append((lambda kind=kind, sc0=sc0, nt=nt, a_=a_, c0=c0: stageB2(kind, sc0, nt, a_, c0), afterB2))
                        K.run_sched(ths)
                        sub_ctr[0] += len(subs)
                        if is_ms:
                            K.op("dve", lambda: nc.vector.memset(oA[:, :, 0:16], 0.0), W=[b_oA])
                        gated_norm(st, oA, b_oA, gz, b_gz, nw, b_nw, og, b_og, 0, c0, n, nbufs)
                    K.dma("sp", o_gdn_p[l].rearrange("h k v -> k h v"), S[:], R=[b_S])
                    for g3 in range(3):
                        for b4 in range(4):
                            blk = 4 * g3 + b4
                            K.op("pe", lambda blk=blk, b4=b4, g3=g3: nc.tensor.transpose(
                                psum[g3][0:4, b4 * 128:(b4 + 1) * 128], halo[:, blk, :], C["ident"]),
                                R=[b_halo, b_cst], W=[b_ps[g3]], inc=(b4 == 3))
                        K.op("act", lambda g3=g3: nc.scalar.copy(out=pnew[0:4, g3 * 512:(g3 + 1) * 512], in_=psum[g3][0:4, :]),
                             R=[b_ps[g3]], W=[b_pnew])
                    K.dma("sp", o_conv_p[l], pnew[0:3, :], R=[b_pnew])
                K.barrier()
                with ExitStack() as s2:
                    wTm = sbt(s2, "d_wTm", [128, 4, NS, 16], BF16)
                    b_wTm = Buf()
                    K.op("dve", lambda: nc.vector.tensor_tensor(
                        out=wTm[:], in0=s_wT[:, :, :].unsqueeze(2).broadcast_to([128, 4, NS, 16]),
                        in1=cst[:, IDX["idrep0"]:IDX["idrep0"] + 2, :].rearrange("p a (s t) -> p (a s) t", t=16)
                            .unsqueeze(1).broadcast_to([128, 4, NS, 16]), op=ALU.mult),
                        R=[b_skeep, b_cst], W=[b_wTm])
                    Sg = [sbt(s2, "d_Sg%d" % i, [128, 4, 4, 128]) for i in range(2)]
                    Sgb = [sbt(s2, "d_Sgb%d" % i, [128, 4, 4, 128], BF16) for i in range(2)]
                    b_Sg = [Buf(), Buf()]
                    b_Sgb = [Buf(), Buf()]
                    us = sbt(s2, "d_us", [16, 4, 128], BF16)
                    ubd = sbt(s2, "d_ubd", [16, 4, 512], BF16)
                    b_us, b_ubd = Buf(), Buf()
                    oS = sbt(s2, "d_oS", [128, 4, 16])
                    b_oS = Buf()
                    sq2 = sbt(s2, "d_sq2", [128, 4, 512], BF16)
                    rsd2 = sbt(s2, "d_rsd2", [128, 512])
                    tn2 = sbt(s2, "d_tn2", [128, 512])
                    sgd_v = state_gdn[l].rearrange("s h k v -> k s h v")
                    sgd_o = o_gdn_s[l].rearrange("s h k v -> k s h v")
                    for sg_ in range(4):
                        u = sg_ % 2
                        K.dma("sp", Sg[u][:], sgd_v[:, 4 * sg_:4 * sg_ + 4], W=[b_Sg[u]])
                        K.op("act", lambda u=u: nc.scalar.copy(out=Sgb[u][:], in_=Sg[u][:]), R=[b_Sg[u]], W=[b_Sgb[u]])
                        for h in range(4):
                            for si in range(4):
                                s = 4 * sg_ + si
                                K.op("pe", lambda h=h, si=si, s=s, u=u: nc.tensor.matmul(
                                    psum[0][0:16, h * 128:(h + 1) * 128], lhsT=wTm[:, h, s, :], rhs=Sgb[u][:, si, h, :],
                                    start=(si == 0), stop=(si == 3)), R=[b_wTm, b_Sgb[u]], W=[b_ps[0]],
                                    inc=(h == 3 and si == 3))
                        K.op("dve", lambda: nc.vector.tensor_tensor(
                            out=us[:], in0=s_u0[:], in1=psum[0][0:16, :].rearrange("p (a c) -> p a c", a=4), op=ALU.subtract),
                            R=[b_skeep, b_ps[0]], W=[b_us])
                        K.op("dve", lambda sg_=sg_: nc.vector.tensor_tensor(
                            out=ubd[:], in0=us[:, :, :].rearrange("p h v -> p (h v)").unsqueeze(1).broadcast_to([16, 4, 512]),
                            in1=C["ident"][0:16, 4 * sg_:4 * sg_ + 4].unsqueeze(2).broadcast_to([16, 4, 512]), op=ALU.mult),
                            R=[b_us, b_cst], W=[b_ubd])
                        for si in range(4):
                            for h in range(4):
                                K.op("pe", lambda h=h, si=si: nc.tensor.matmul(
                                    psum[1 + si][:, h * 128:(h + 1) * 128], lhsT=s_kd[0:16, h, :],
                                    rhs=ubd[0:16, si, h * 128:(h + 1) * 128], start=True, stop=True),
                                    R=[b_skeep, b_ubd], W=[b_ps[1 + si]], inc=(h == 3))
                        for si in range(4):
                            s = 4 * sg_ + si
                            for h in range(4):
                                K.op("dve", lambda h=h, si=si, s=s, u=u: nc.vector.scalar_tensor_tensor(
                                    out=Sg[u][:, si, h, :], in0=Sg[u][:, si, h, :], scalar=s_G[:, s, h:h + 1],
                                    in1=psum[1 + si][:, h * 128:(h + 1) * 128], op0=ALU.mult, op1=ALU.add),
                                    R=[b_skeep, b_ps[1 + si]], W=[b_Sg[u]])
                        K.op("act", lambda u=u: nc.scalar.copy(out=Sgb[u][:], in_=Sg[u][:]), R=[b_Sg[u]], W=[b_Sgb[u]])
                        for si in range(4):
                            s = 4 * sg_ + si
                            for h in range(4):
                                K.op("pe", lambda h=h, si=si, s=s, u=u: nc.tensor.matmul(
                                    psum[5][:, h * 16 + s:h * 16 + s + 1], lhsT=Sgb[u][:, si, h, :], rhs=s_qn[:, h, s:s + 1],
                                    start=True, stop=True), R=[b_Sgb[u], b_skeep], W=[b_ps[5]],
                                    inc=(h == 3 and si == 3))
                        K.dma("sp", sgd_o[:, 4 * sg_:4 * sg_ + 4], Sg[u][:], R=[b_Sg[u]])
                    K.op("act", lambda: nc.scalar.copy(out=oS[:], in_=psum[5][:, 0:64].rearrange("p (a b) -> p a b", a=4)),
                         R=[b_ps[5]], W=[b_oS])
                    gated_norm(s2, oS, b_oS, s_gz, b_skeep, nw, b_nw, og, b_og, 0, 0, 16,
                               (sq2, Buf(), rsd2, Buf(), tn2, Buf()))

        for l in range(layers):
            last_layer = (l == layers - 1)
            win_v = w_in[l].rearrange("(k p) n -> p k n", p=128)

            with ExitStack() as pm:
              og = big[:, 0:8 * T].rearrange("p (k t) -> p k t", k=8)
              b_og = Buf("og")
              if stub_mixer:
                  K.op("dve", lambda: nc.vector.tensor_copy(out=og, in_=x_bf[:]), R=[b_xbf], W=[b_og])
              else:
                  if mix_sel in ("all", "gdn"):
                      gdn_phase(l, og, b_og, win_v)
                      K.barrier()
                  if mix_sel in ("all", "gla"):
                      gla_phase(l, og, b_og, win_v)
              K.barrier()
              chk("mixer")
              v = sbt(pm, "v", [128, KC, T])
              b_v = Buf("v")
              with ExitStack() as pmg:
                mg = sbt(pmg, "mg", [128, KC, T], BF16)
                b_mg = Buf("mg")
                with ExitStack() as pb1:
                    NW = 2
                    wab = [sbt(pb1, "wab%d" % i, [128, 8, 128], BF16) for i in range(NW)]
                    wgg = [sbt(pb1, "wgg%d" % i, [128, 16, 128], BF16) for i in range(NW)]
                    b_wab = [Buf() for _ in range(NW)]
                    b_wgg = [Buf() for _ in range(NW)]
                    sg = [sbt(pb1, "sg%d" % i, [128, 2, 512]) for i in range(2)]
                    b_sg = [Buf(), Buf()]
                    wa_v = w_branch_a[l].rearrange("(k p) n -> p k n", p=128)
                    wb_v = w_branch_b[l].rearrange("(k p) n -> p k n", p=128)
                    it = 0
                    for jo in range(KC):
                        s = jo % NW
                        cs = slice(jo * 128, (jo + 1) * 128)
                        K.dma("pool", wab[s][:, 0:4, :], wa_v[:, :, cs], W=[b_wab[s]])
                        K.dma("pool", wab[s][:, 4:8, :], wb_v[:, :, cs], W=[b_wab[s]])
                        K.dma("pool", wgg[s][:, 0:8, :], win_v[:, :, 3608 + jo * 128:3608 + (jo + 1) * 128], W=[b_wgg[s]])
                        K.dma("pool", wgg[s][:, 8:16, :], win_v[:, :, 4632 + jo * 128:4632 + (jo + 1) * 128], W=[b_wgg[s]])
                        for (c0, n) in TILES:
                            q = 4 * (it % 2)
                            t2 = it % 2
                            it += 1
                            cols = slice(c0, c0 + n)
                            mm_group(psum[q + 0][:, 0:n], [(wab[s][:, k, :], og[:, k, cols]) for k in range(4)],
                                     R=[b_wab[s], b_og], W=[b_ps[q + 0]])
                            mm_group(psum[q + 1][:, 0:n], [(wab[s][:, 4 + k, :], og[:, 4 + k, cols]) for k in range(4)],
                                     R=[b_wab[s], b_og], W=[b_ps[q + 1]])
                            mm_group(psum[q + 2][:, 0:n], [(wgg[s][:, k, :], x_bf[:, k, cols]) for k in range(8)],
                                     R=[b_wgg[s], b_xbf], W=[b_ps[q + 2]])
                            mm_group(psum[q + 3][:, 0:n], [(wgg[s][:, 8 + k, :], x_bf[:, k, cols]) for k in range(8)],
                                     R=[b_wgg[s], b_xbf], W=[b_ps[q + 3]])
                            K.op("act", lambda q=q, t2=t2, n=n: nc.scalar.activation(
                                out=sg[t2][:, 0, 0:n], in_=psum[q + 2][:, 0:n], func=AF.Sigmoid),
                                R=[b_ps[q + 2]], W=[b_sg[t2]])
                            K.op("act", lambda q=q, t2=t2, n=n: nc.scalar.activation(
                                out=sg[t2][:, 1, 0:n], in_=psum[q + 3][:, 0:n], func=AF.Sigmoid),
                                R=[b_ps[q + 3]], W=[b_sg[t2]])
                            K.op("dve", lambda q=q, t2=t2, n=n: nc.vector.tensor_tensor(
                                out=sg[t2][:, 0, 0:n], in0=sg[t2][:, 0, 0:n], in1=psum[q + 0][:, 0:n], op=ALU.mult),
                                R=[b_sg[t2], b_ps[q + 0]], W=[b_sg[t2]])
                            K.op("dve", lambda q=q, t2=t2, n=n: nc.vector.tensor_tensor(
                                out=sg[t2][:, 1, 0:n], in0=sg[t2][:, 1, 0:n], in1=psum[q + 1][:, 0:n], op=ALU.mult),
                                R=[b_sg[t2], b_ps[q + 1]], W=[b_sg[t2]])
                            K.op("dve", lambda t2=t2, n=n, jo=jo, cols=cols: nc.vector.tensor_tensor(
                                out=mg[:, jo, cols], in0=sg[t2][:, 0, 0:n], in1=sg[t2][:, 1, 0:n], op=ALU.add),
                                R=[b_sg[t2]], W=[b_mg])
                K.barrier()
                chk("b1")
                K.dma("sp", v[:], xres, R=[b_xres], W=[b_v])
                with ExitStack() as pb2:
                    wo = [sbt(pb2, "wo%d" % i, [128, 8, 128], BF16) for i in range(2)]
                    b_wo = [Buf(), Buf()]
                    wo_v = w_out[l].rearrange("(k p) n -> p k n", p=128)
                    it = 0
                    for jo in range(KC):
                        s = jo % 2
                        K.dma("pool", wo[s][:], wo_v[:, :, jo * 128:(jo + 1) * 128], W=[b_wo[s]])
                        for (c0, n) in TILES:
                            q = it % 8
                            it += 1
                            cols = slice(c0, c0 + n)
                            mm_group(psum[q][:, 0:n], [(wo[s][:, k, :], mg[:, k, cols]) for k in range(8)],
                                     R=[b_wo[s], b_mg], W=[b_ps[q]])
                            K.op("dve", lambda q=q, n=n, jo=jo, cols=cols: nc.vector.scalar_tensor_tensor(
                                out=v[:, jo, cols], in0=v[:, jo, cols], scalar=ALPHA, in1=psum[q][:, 0:n],
                                op0=ALU.mult, op1=ALU.add), R=[b_ps[q]], W=[b_v])
                K.barrier()
                chk("b2")
              if True:
                layer_norm(pm, v, b_v, l, 0)
                K.barrier()
                chk("ln1")

                nh = (len(TILES) + 1) // 2
                halves = [TILES[:nh], TILES[nh:]]
                fin_v = w_ffn_in[l].rearrange("(k p) n -> p k n", p=128)
                fout_v = w_ffn_out[l].rearrange("(c p) n -> p c n", p=128)
                with ExitStack() as pf:
                    hid = big[:, 0:FC * HW].rearrange("p (c t) -> p c t", c=FC)
                    b_hid = Buf("hid")
                    wfi = [sbt(pf, "wfi%d" % i, [128, 2, KC, 256], BF16) for i in range(2)]
                    b_wfi = [Buf(), Buf()]
                    wfo = [sbt(pf, "wfo%d" % i, [128, FC, 128], BF16) for i in range(2)]
                    b_wfo = [Buf(), Buf()]
                    sil = [sbt(pf, "sil%d" % i, [128, 512]) for i in range(2)]
                    b_sil = [Buf(), Buf()]
                    it = 0
                    wi_it = 0
                    wo_it = 0
                    for half in halves:
                        if not half:
                            continue
                        h0 = half[0][0]
                        for g in range(FC // 2):
                            s = wi_it % 2
                            wi_it += 1
                            K.dma("pool", wfi[s][:, 0, :, :], fin_v[:, :, g * 256:(g + 1) * 256], W=[b_wfi[s]])
                            K.dma("pool", wfi[s][:, 1, :, :], fin_v[:, :, DFF + g * 256:DFF + (g + 1) * 256], W=[b_wfi[s]])
                            for jj in range(2):
                                j = 2 * g + jj
                                for (c0, n) in half:
                                    q = 2 * (it % 4)
                                    t2 = it % 2
                                    it += 1
                                    cols = slice(c0, c0 + n)
                                    hc = slice(c0 - h0, c0 - h0 + n)
                                    mm_group(psum[q][:, 0:n],
                                             [(wfi[s][:, 0, k, jj * 128:(jj + 1) * 128], x_bf[:, k, cols]) for k in range(8)],
                                             R=[b_wfi[s], b_xbf], W=[b_ps[q]])
                                    mm_group(psum[q + 1][:, 0:n],
                                             [(wfi[s][:, 1, k, jj * 128:(jj + 1) * 128], x_bf[:, k, cols]) for k in range(8)],
                                             R=[b_wfi[s], b_xbf], W=[b_ps[q + 1]])
                                    K.op("act", lambda q=q, t2=t2, n=n: nc.scalar.activation(
                                        out=sil[t2][:, 0:n], in_=psum[q][:, 0:n], func=AF.Silu),
                                        R=[b_ps[q]], W=[b_sil[t2]])
                                    K.op("dve", lambda q=q, t2=t2, n=n, j=j, hc=hc: nc.vector.tensor_tensor(
                                        out=hid[:, j, hc], in0=sil[t2][:, 0:n], in1=psum[q + 1][:, 0:n], op=ALU.mult),
                                        R=[b_sil[t2], b_ps[q + 1]], W=[b_hid])
                        for jo in range(KC):
                            s = wo_it % 2
                            wo_it += 1
                            K.dma("pool", wfo[s][:], fout_v[:, :, jo * 128:(jo + 1) * 128], W=[b_wfo[s]])
                            for (c0, n) in half:
                                q = it % 8
                                it += 1
                                cols = slice(c0, c0 + n)
                                hc = slice(c0 - h0, c0 - h0 + n)
                                mm_group(psum[q][:, 0:n], [(wfo[s][:, c, :], hid[:, c, hc]) for c in range(FC)],
                                         R=[b_wfo[s], b_hid], W=[b_ps[q]])
                                K.op("dve", lambda q=q, n=n, jo=jo, cols=cols: nc.vector.scalar_tensor_tensor(
                                    out=v[:, jo, cols], in0=v[:, jo, cols], scalar=ALPHA, in1=psum[q][:, 0:n],
                                    op0=ALU.mult, op1=ALU.add), R=[b_ps[q]], W=[b_v])
                K.barrier()
                chk("ffn")
                layer_norm(pm, v, b_v, l, 1, write_bf=not last_layer)
                K.barrier()
                chk("ln2")
                if not last_layer:
                    K.dma("sp", xres, v[:], R=[b_v], W=[b_xres])
                else:
                    with ExitStack() as po:
                        NR = 3
                        yst = [sbt(po, "yst%d" % i, [128, D]) for i in range(NR)]
                        b_yst = [Buf() for _ in range(NR)]
                        rows = [("ms", 0, 32)] + [("p", 128 * i, 128) for i in range(TP // 128)]

                        def out_tile(ri):
                            kind, r0, n = rows[ri]
                            s = ri % NR
                            c0 = 0 if kind == "ms" else 32 + r0
                            for g in range(2):
                                pb = (2 * ri + g) % 8
                                for kk in range(4):
                                    k = 4 * g + kk
                                    K.op("pe", lambda k=k, kk=kk, pb=pb: nc.tensor.transpose(
                                        psum[pb][0:n, kk * 128:(kk + 1) * 128], v[:, k, c0:c0 + n], C["ident"]),
                                        R=[b_v, b_cst], W=[b_ps[pb]], inc=(kk == 3), c=0.12)
                                if g == 0:
                                    K.op("act", lambda pb=pb: nc.scalar.copy(
                                        out=yst[s][0:n, 0:512], in_=psum[pb][0:n, :]), R=[b_ps[pb]], W=[b_yst[s]])
                                else:
                                    K.op("dve", lambda pb=pb: nc.vector.tensor_copy(
                                        out=yst[s][0:n, 512:1024], in_=psum[pb][0:n, :]), R=[b_ps[pb]], W=[b_yst[s]])
                            if kind == "ms":
                                K.dma("sp", y_sample, yst[s][0:16, :], R=[b_yst[s]])
                            else:
                                K.dma("sp", y_prompt[r0:r0 + 128, :], yst[s][0:128, :], R=[b_yst[s]])

                        K.run_sched([(lambda ri=ri: out_tile(ri), ([ri - NR] if ri >= NR else [])) for ri in range(len(rows))])
              K.barrier()
      except _Stop:
        pass
      K.finish()
    return nc


_NC_CACHE = {}


def kernel(x_prompt, x_sample, state_gdn, state_gla, state_conv, meta_tokens, w_in, conv_w, a_log, dt_bias,
           gdn_norm_w, gla_gate_w2, gla_gate_b, gla_norm_w, w_branch_a, w_branch_b, w_out,
           ln1_g, ln1_b, ln2_g, ln2_b, w_ffn_in, w_ffn_out, _build_kwargs=None):
    f = lambda a: np.ascontiguousarray(np.asarray(a), dtype=np.float32)
    x_prompt = f(x_prompt)
    TP = x_prompt.shape[1]
    bk = dict(_build_kwargs or {})
    key = (TP, tuple(sorted(bk.items())))
    if key not in _NC_CACHE:
        _NC_CACHE[key] = build(TP=TP, **bk)
    nc = _NC_CACHE[key]
    shared = dict(meta_tokens=f(meta_tokens), w_in=f(w_in), conv_w=f(conv_w), a_log=f(a_log), dt_bias=f(dt_bias),
                  gdn_norm_w=f(gdn_norm_w), gla_gate_w2=f(gla_gate_w2), gla_gate_b=f(gla_gate_b),
                  gla_norm_w=f(gla_norm_w), w_branch_a=f(w_branch_a), w_branch_b=f(w_branch_b), w_out=f(w_out),
                  ln1_g=f(ln1_g), ln1_b=f(ln1_b), ln2_g=f(ln2_g), ln2_b=f(ln2_b),
                  w_ffn_in=f(w_ffn_in), w_ffn_out=f(w_ffn_out), consts=CONST_ARR)
    x_sample = f(x_sample)
    state_gdn = f(state_gdn)
    state_gla = f(state_gla)
    state_conv = f(state_conv)
    in_maps = []
    for c in range(NCORES):
        sl = slice(NS * c, NS * (c + 1))
        m = dict(shared)
        m["x_prompt"] = x_prompt[c]
        m["x_sample"] = np.ascontiguousarray(x_sample[sl, 0, :])
        m["state_gdn"] = np.ascontiguousarray(state_gdn[:, sl])
        m["state_gla"] = np.ascontiguousarray(state_gla[:, sl])
        m["state_conv"] = np.ascontiguousarray(state_conv[:, sl])
        in_maps.append(m)
    res = run_bass_kernel_spmd(nc, in_maps, core_ids=list(range(NCORES)))
    R = res.results
    y_prompt = np.stack([R[c]["y_prompt"] for c in range(NCORES)], axis=0)
    y_sample = np.concatenate([R[c]["y_sample"] for c in range(NCORES)], axis=0)[:, None, :]
    gdn_p = np.stack([R[c]["new_gdn_prompt"] for c in range(NCORES)], axis=1)
    gla_p = np.stack([R[c]["new_gla_prompt"] for c in range(NCORES)], axis=1)
    conv_p = np.stack([R[c]["new_conv_prompt"] for c in range(NCORES)], axis=1)
    gdn_s = np.concatenate([R[c]["new_gdn_sample"] for c in range(NCORES)], axis=1)
    gla_s = np.concatenate([R[c]["new_gla_sample"] for c in range(NCORES)], axis=1)
    conv_s = np.concatenate([R[c]["new_conv_sample"] for c in range(NCORES)], axis=1)
    outs = (y_prompt, y_sample, gdn_p, gla_p, conv_p, gdn_s, gla_s, conv_s)
    return tuple(np.ascontiguousarray(o, dtype=np.float32) for o in outs)
```

```python
import threading
import numpy as np
from contextlib import ExitStack
import concourse.bass as bass
import concourse.mybir as mybir
from concourse.bass_utils import run_bass_kernel_spmd

F32 = mybir.dt.float32
BF16 = mybir.dt.bfloat16
F32R = mybir.dt.float32r
AF = mybir.ActivationFunctionType
ALU = mybir.AluOpType

D = 1024
KC = 8
DEPTH = 2
NS = 16
NMETA = 16
D_IN = 5656
DFF = 2816
FC = DFF // 128
ALPHA = (2.0 * DEPTH) ** 0.25
NCORES = 8


class Buf:
    __slots__ = ("w", "r", "name", "excl")

    def __init__(self, name="", excl=False):
        self.w = None
        self.r = {}
        self.name = name
        self.excl = excl


_tls = threading.local()


class _Worker:
    def __init__(self, fn):
        self.fn = fn
        self.req = None
        self.done = False
        self.exc = None
        self.ev_req = threading.Event()
        self.ev_go = threading.Event()
        self.th = threading.Thread(target=self._run, daemon=True)

    def _run(self):
        _tls.worker = self
        try:
            self.ev_go.wait()
            self.ev_go.clear()
            r = self.fn()
            if r is not None and hasattr(r, "__next__"):
                for _ in r:
                    pass
        except BaseException as e:
            self.exc = e
        finally:
            self.done = True
            self.req = None
            self.ev_req.set()

    def post(self, req):
        self.req = req
        self.ev_req.set()
        self.ev_go.wait()
        self.ev_go.clear()


class Sched:
    ENG = ("pe", "act", "dve", "pool", "sp")
    COST = {"pe": 0.25, "act": 0.6, "dve": 0.7, "pool": 0.5, "sp": 0.1}

    def __init__(self, nc, es, n_dma_slots=24):
        self.nc = nc
        self.e = {"pe": nc.tensor, "act": nc.scalar, "dve": nc.vector, "pool": nc.gpsimd, "sp": nc.sync}
        self.sem = {}
        self.cnt = {}
        for k in self.ENG:
            self.sem[k] = es.enter_context(nc.semaphore("s_" + k))
            self.cnt[k] = 0
        self.seen = {k: {} for k in self.ENG}
        self.pending = {k: False for k in self.ENG}
        self.nslots = n_dma_slots
        self.slot_sem = [es.enter_context(nc.semaphore("s_dma%d" % i)) for i in range(n_dma_slots)]
        self.slot_uses = [0] * n_dma_slots
        self.slot_next = 0
        self.semobj = dict(self.sem)
        for i in range(n_dma_slots):
            self.semobj[("dma", i)] = self.slot_sem[i]
        self.tfree = {k: 0.0 for k in self.ENG}
        self.tdone = {}

    def _est(self, eng, R, W):
        t = self.tfree[eng]
        def dep(key):
            d = self.tdone.get(key)
            if d is None:
                d = self.tfree.get(key[0], 0.0) if not isinstance(key[0], tuple) else 0.0
            return d + (0.05 if key[0] == eng else 0.3)
        for b in R:
            if b.w is not None:
                t = max(t, dep(b.w))
            if b.excl:
                for k, v in b.r.items():
                    t = max(t, dep((k, v)))
        for b in W:
            if b.w is not None:
                t = max(t, dep(b.w))
            for k, v in b.r.items():
                t = max(t, dep((k, v)))
        return t

    def run_sched(self, threads):
        n = len(threads)
        workers = [None] * n
        started, finished = set(), set()

        def advance(w):
            w.ev_req.clear()
            w.ev_go.set()
            w.ev_req.wait()
            if w.exc is not None:
                raise w.exc

        while len(finished) < n:
            for i, (fn, after) in enumerate(threads):
                if i not in started and all(j in finished for j in after):
                    started.add(i)
                    workers[i] = _Worker(fn)
                    workers[i].th.start()
                    advance(workers[i])
                    if workers[i].done:
                        finished.add(i)
            cands = [i for i in started if i not in finished]
            if not cands:
                if len(finished) < n and len(started) == len(finished):
                    rem = [i for i in range(n) if i not in started]
                    assert any(all(j in finished for j in threads[i][1]) for i in rem), "scheduler deadlock"
                continue
            best = min(cands, key=lambda i: (self._est(*workers[i].req), i))
            advance(workers[best])
            if workers[best].done:
                finished.add(best)

    def _collect(self, eng, R, W, extra=()):
        need = {}
        def add(k, v):
            if v > need.get(k, 0):
                need[k] = v
        for b in R:
            if b.w is not None:
                add(*b.w)
        for b in W:
            if b.w is not None:
                add(*b.w)
            for k, v in b.r.items():
                add(k, v)
        for k, v in extra:
            add(k, v)
        out = []
        for k, v in need.items():
            if k == "pe" and eng == "pe":
                continue
            if k == eng and v > self.cnt[eng]:
                continue
            if v > self.seen[eng].get(k, 0):
                out.append((k, v))
        return out

    def op(self, eng, fn, R=(), W=(), inc=True, extra=(), c=None):
        w_ = getattr(_tls, "worker", None)
        if w_ is not None:
            w_.post((eng, tuple(R), tuple(W)))
        t0_ = self._est(eng, R, W)
        t1_ = t0_ + (c if c is not None else self.COST[eng])
        self.tfree[eng] = t1_
        if any(b.excl for b in R):
            W = list(W) + [b for b in R if b.excl]
            R = [b for b in R if not b.excl]
        waits = self._collect(eng, R, W, extra)
        e = self.e[eng]
        if eng == "pe":
            for k, v in waits:
                e.wait_ge(self.semobj[k], v)
                self.seen[eng][k] = v
            waits = []
        for k, v in waits[:-1]:
            e.wait_ge(self.semobj[k], v)
            self.seen[eng][k] = v
        inst = fn()
        if waits:
            k, v = waits[-1]
            inst.wait_op(self.semobj[k], v, "sem-ge")
            self.seen[eng][k] = v
        if inc:
            inst.then_inc(self.sem[eng], 1)
            self.cnt[eng] += 1
            cc = self.cnt[eng]
            self.pending[eng] = False
            self.tdone[(eng, cc)] = t1_
        else:
            cc = self.cnt[eng] + 1
            self.pending[eng] = True
        for b in R:
            if b.r.get(eng, 0) < cc:
                b.r[eng] = cc
        for b in W:
            b.w = (eng, cc)
            b.r = {}
        return inst

    def dma(self, eng, out, in_, R=(), W=(), **kw):
        w_ = getattr(_tls, "worker", None)
        if w_ is not None:
            w_.post((eng, tuple(R), tuple(W)))
        t0_ = self._est(eng, R, W)
        self.tfree[eng] = t0_ + 0.1
        s = self.slot_next
        self.slot_next = (self.slot_next + 1) % self.nslots
        key = ("dma", s)
        prev = 16 * self.slot_uses[s]
        extra = [(key, prev)] if prev > 0 else []
        waits = self._collect(eng, R, W, extra)
        e = self.e[eng]
        for k, v in waits:
            e.wait_ge(self.semobj[k], v)
            self.seen[eng][k] = v
        inst = e.dma_start(out=out, in_=in_, **kw)
        self.slot_uses[s] += 1
        val = 16 * self.slot_uses[s]
        inst.then_inc(self.slot_sem[s], 16)
        self.tdone[(key, val)] = t0_ + 3.0
        for b in R:
            b.r[key] = val
        for b in W:
            b.w = (key, val)
            b.r = {}
        return inst

    def barrier(self):
        tgt = [(k, self.cnt[k]) for k in self.ENG if self.cnt[k] > 0]
        tgt += [(("dma", i), 16 * self.slot_uses[i]) for i in range(self.nslots) if self.slot_uses[i] > 0]
        for eng in self.ENG:
            assert not self.pending[eng]
            for k, v in tgt:
                if k == eng:
                    continue
                if v > self.seen[eng].get(k, 0):
                    self.e[eng].wait_ge(self.semobj[k], v)
                    self.seen[eng][k] = v

    def finish(self):
        self.barrier()


def make_consts():
    r = np.arange(128)
    same = (r[:, None] // 64) == (r[None, :] // 64)
    c = {}
    c["ident"] = np.eye(128, dtype=np.float32)
    c["ones"] = np.ones((128, 128), dtype=np.float32)
    c["mu_incl"] = (same & (r[None, :] >= r[:, None])).astype(np.float32)
    c["mu_strict"] = (same & (r[None, :] > r[:, None])).astype(np.float32)
    c["ml_strict"] = (same & (r[:, None] > r[None, :])).astype(np.float32)
    c["bd"] = same.astype(np.float32)
    c["blk0"] = np.repeat((r < 64).astype(np.float32)[:, None], 128, axis=1)
    c["blk1"] = np.repeat((r >= 64).astype(np.float32)[:, None], 128, axis=1)
    c["zero"] = np.zeros((128, 128), dtype=np.float32)
    idrep = np.tile(np.eye(16, dtype=np.float32).reshape(1, 256), (128, 1))
    c["idrep0"] = idrep[:, 0:128]
    c["idrep1"] = idrep[:, 128:256]
    names = ["ident", "ones", "mu_incl", "mu_strict", "ml_strict", "bd", "blk0", "blk1", "zero", "idrep0", "idrep1"]
    arr = np.stack([c[n] for n in names], axis=1)
    return names, np.ascontiguousarray(arr.astype(np.float32))


CONST_NAMES, CONST_ARR = make_consts()
NCONST = len(CONST_NAMES)
IDX = {n: i for i, n in enumerate(CONST_NAMES)}


class _Stop(Exception):
    pass


def build(TP=2048, stub_mixer=False, layers=DEPTH, dbg=False, stop=None, mix_sel="all"):
    nc = bass.Bass("TRN2", target_bir_lowering=False)
    NPOS = NMETA + TP
    T = 32 + TP
    NPT = max(TP // 512, 1)
    TILES = [(0, 32)] + [(32 + 512 * i, 512) for i in range(NPT)]

    def din(name, shape, dt=F32):
        return nc.dram_tensor(name, list(shape), dt, kind="ExternalInput").ap()

    def dout(name, shape, dt=F32):
        return nc.dram_tensor(name, list(shape), dt, kind="ExternalOutput").ap()

    x_prompt = din("x_prompt", [TP, D])
    x_sample = din("x_sample", [NS, D])
    meta = din("meta_tokens", [NMETA, D])
    state_gdn = din("state_gdn", [DEPTH, NS, 4, 128, 128])
    state_gla = din("state_gla", [DEPTH, NS, 4, 64, 128])
    state_conv = din("state_conv", [DEPTH, NS, 3, 1536])
    w_in = din("w_in", [DEPTH, D, D_IN])
    conv_w = din("conv_w", [DEPTH, 4, 1536])
    a_log = din("a_log", [DEPTH, 4])
    dt_bias = din("dt_bias", [DEPTH, 4])
    gdn_norm_w = din("gdn_norm_w", [DEPTH, 128])
    gla_gate_w2 = din("gla_gate_w2", [DEPTH, 16, 256])
    gla_gate_b = din("gla_gate_b", [DEPTH, 256])
    gla_norm_w = din("gla_norm_w", [DEPTH, 128])
    w_branch_a = din("w_branch_a", [DEPTH, 512, D])
    w_branch_b = din("w_branch_b", [DEPTH, 512, D])
    w_out = din("w_out", [DEPTH, D, D])
    ln1_g = din("ln1_g", [DEPTH, D])
    ln1_b = din("ln1_b", [DEPTH, D])
    ln2_g = din("ln2_g", [DEPTH, D])
    ln2_b = din("ln2_b", [DEPTH, D])
    w_ffn_in = din("w_ffn_in", [DEPTH, D, 2 * DFF])
    w_ffn_out = din("w_ffn_out", [DEPTH, DFF, D])
    consts_d = din("consts", [128, NCONST, 128])

    y_prompt = dout("y_prompt", [TP, D])
    y_sample = dout("y_sample", [NS, D])
    o_gdn_p = dout("new_gdn_prompt", [DEPTH, 4, 128, 128])
    o_gla_p = dout("new_gla_prompt", [DEPTH, 4, 64, 128])
    o_conv_p = dout("new_conv_prompt", [DEPTH, 3, 1536])
    o_gdn_s = dout("new_gdn_sample", [DEPTH, NS, 4, 128, 128])
    o_gla_s = dout("new_gla_sample", [DEPTH, NS, 4, 64, 128])
    o_conv_s = dout("new_conv_sample", [DEPTH, NS, 3, 1536])
    dbg_out = dout("dbg", [128, KC, T]) if dbg else None

    xres = nc.dram_tensor("xres_scratch", [128, KC, T], F32, kind="Internal").ap()

    es = ExitStack()
    with es:
      K = Sched(nc, es)

      def chk(name):
          if stop == name:
              raise _Stop()

      try:

        _uid = [0]

        def sbt(stack, name, shape, dt=F32):
            _uid[0] += 1
            return stack.enter_context(nc.sbuf_tensor("%s_%d" % (name, _uid[0]), list(shape), dt))

        cst = sbt(es, "cst", [128, NCONST, 128], F32)
        cstb = sbt(es, "cstb", [128, NCONST, 128], BF16)
        b_cst = Buf("cst")
        C = {n: cst[:, i, :] for i, n in enumerate(CONST_NAMES)}
        CB = {n: cstb[:, i, :] for i, n in enumerate(CONST_NAMES)}
        x_bf = sbt(es, "x_bf", [128, KC, T], BF16)
        b_xbf = Buf("x_bf")
        lnp = sbt(es, "lnp", [128, DEPTH, 4, KC], F32)
        b_lnp = Buf("lnp")
        psp = [es.enter_context(nc.psum_tensor("psp%d" % i, [128, 1024], F32)) for i in range(4)]
        psum = [psp[i // 2][:, (i % 2) * 512:(i % 2) * 512 + 512] for i in range(8)]
        ps67 = psp[3][:, :]
        b_ps = [Buf("ps%d" % i, excl=True) for i in range(8)]
        b_xres = Buf("xres")

        epst = sbt(es, "epst", [128, 2], F32)
        b_eps = Buf("eps")
        K.op("dve", lambda: nc.vector.memset(epst[:, 0:1], 1e-6), W=[b_eps])
        K.op("dve", lambda: nc.vector.memset(epst[:, 1:2], 1.0), W=[b_eps])
        eps6 = epst[:, 0:1]
        one1 = epst[:, 1:2]
        nh_ = (len(TILES) + 1) // 2
        HW = max(sum(n for _, n in TILES[:nh_]), sum(n for _, n in TILES[nh_:]))
        BIGN = max(FC * HW, 8 * T)
        big = sbt(es, "big", [128, BIGN], BF16)
        K.dma("sp", cst[:], consts_d, W=[b_cst])
        K.op("act", lambda: nc.scalar.copy(out=cstb[:], in_=cst[:]), R=[b_cst], W=[b_cst])
        ones_r = sbt(es, "ones_r", [128, 128], F32)
        b_onesr = Buf("ones_r")
        K.op("act", lambda: nc.scalar.copy(out=ones_r[:].bitcast(F32R), in_=C["ones"]), R=[b_cst], W=[b_onesr])
        for l in range(DEPTH):
            for wi, src in enumerate((ln1_g, ln1_b, ln2_g, ln2_b)):
                K.dma("sp", lnp[:, l, wi, :], src[l].rearrange("(k p) -> p k", p=128), W=[b_lnp],
                      allow_slow_non_contiguous=True)

        chk("init")

        def run_threads(gens):
            active = list(gens)
            while active:
                for g in list(active):
                    try:
                        next(g)
                    except StopIteration:
                        active.remove(g)

        def run_pipeline(items, mkA, mkB, nA=2, nbuf=3):
            n_it = len(items)
            nextA, nextB = 0, 0
            activeA = {}
            doneA = set()
            curB = None
            while nextB < n_it:
                while nextA < n_it and len(activeA) < nA and (nextA - nextB) < nbuf:
                    activeA[nextA] = mkA(items[nextA], nextA)
                    nextA += 1
                if curB is None and nextB in doneA:
                    curB = mkB(items[nextB], nextB)
                for i, gen in list(activeA.items()):
                    try:
                        next(gen)
                    except StopIteration:
                        del activeA[i]
                        doneA.add(i)
                if curB is not None:
                    try:
                        next(curB)
                    except StopIteration:
                        curB = None
                        nextB += 1

        def mm_group(ps_ap, pairs, R, W, inc=True):
            n = len(pairs)
            for i, (lt, rh) in enumerate(pairs):
                last = (i == n - 1)
                K.op("pe", lambda lt=lt, rh=rh, i=i, last=last: nc.tensor.matmul(
                    ps_ap, lhsT=lt, rhs=rh, start=(i == 0), stop=last),
                    R=R, W=W, inc=(inc and last))

        with ExitStack() as p0:
            NR = 3
            xin = [sbt(p0, "xin%d" % i, [128, D]) for i in range(NR)]
            b_xin = [Buf() for _ in range(NR)]
            xst = [sbt(p0, "xst%d" % i, [128, KC, 128]) for i in range(NR)]
            b_xst = [Buf() for _ in range(NR)]
            rows = [("ms", 0, 32)] + [("p", 128 * i, 128) for i in range(TP // 128)]

            def p0_tile(ri):
                kind, r0, n = rows[ri]
                s = ri % NR
                if kind == "ms":
                    K.dma("sp", xin[s][0:16, :], x_sample, W=[b_xin[s]])
                    K.dma("sp", xin[s][16:32, :], meta, W=[b_xin[s]])
                    c0 = 0
                else:
                    K.dma("sp", xin[s][0:128, :], x_prompt[r0:r0 + 128, :], W=[b_xin[s]])
                    c0 = 32 + r0
                for g in range(2):
                    pb = (2 * ri + g) % 8
                    for kk in range(4):
                        k = 4 * g + kk
                        K.op("pe", lambda k=k, kk=kk, pb=pb: nc.tensor.transpose(
                            psum[pb][:, kk * 128:kk * 128 + n], xin[s][0:n, k * 128:(k + 1) * 128],
                            C["ident"][0:n, 0:n]), R=[b_xin[s], b_cst], W=[b_ps[pb]], inc=(kk == 3), c=0.12)
                    src = psum[pb][:, :].rearrange("p (a b) -> p a b", a=4)[:, :, 0:n]
                    K.op("act", lambda src=src, g=g: nc.scalar.copy(
                        out=x_bf[:, 4 * g:4 * g + 4, c0:c0 + n], in_=src), R=[b_ps[pb]], W=[b_xbf])
                    K.op("dve", lambda src=src, g=g: nc.vector.tensor_copy(
                        out=xst[s][:, 4 * g:4 * g + 4, 0:n], in_=src), R=[b_ps[pb]], W=[b_xst[s]])
                K.dma("sp", xres[:, :, c0:c0 + n], xst[s][:, :, 0:n], R=[b_xst[s]])

            K.run_sched([(lambda ri=ri: p0_tile(ri), ([ri - NR] if ri >= NR else [])) for ri in range(len(rows))])
        K.barrier()
        chk("p0")

        def layer_norm(stack, v, b_v, l, which, write_bf=True):
            g_i, b_i = 2 * which, 2 * which + 1
            with ExitStack() as st:
                vb_ = [sbt(st, "ln_vb%d" % i, [128, KC, 512], BF16) for i in range(2)]
                sq_ = [sbt(st, "ln_sq%d" % i, [128, KC, 512], BF16) for i in range(2)]
                mt_ = [sbt(st, "ln_m%d" % i, [128, 512]) for i in range(2)]
                m2_ = [sbt(st, "ln_m2%d" % i, [128, 512]) for i in range(2)]
                rs_ = [sbt(st, "ln_rs%d" % i, [128, 512]) for i in range(2)]
                nm_ = [sbt(st, "ln_nm%d" % i, [128, 512]) for i in range(2)]
                bb = [[Buf() for _ in range(6)] for _ in range(2)]
                b_vts = []
                for _ in TILES:
                    t_ = Buf()
                    t_.w = b_v.w
                    t_.r = dict(b_v.r)
                    b_vts.append(t_)

                def ln_tile(ti):
                    c0, n = TILES[ti]
                    u_ = ti % 2
                    vb, sq, mt, m2, rs, nm = vb_[u_], sq_[u_], mt_[u_], m2_[u_], rs_[u_], nm_[u_]
                    b_vb, b_sq, b_mt, b_m2, b_rs, b_nm = bb[u_]
                    b_v = b_vts[ti]
                    cols = slice(c0, c0 + n)
                    pm_, pq_ = (2 * ti) % 8, (2 * ti + 1) % 8
                    K.op("act", lambda cols=cols, n=n: nc.scalar.copy(out=vb[:, :, 0:n], in_=v[:, :, cols]),
                         R=[b_v], W=[b_vb])
                    K.op("act", lambda cols=cols, n=n: nc.scalar.activation(
                        out=sq[:, :, 0:n], in_=v[:, :, cols], func=AF.Square), R=[b_v], W=[b_sq])
                    mm_group(psum[pm_][:, 0:n], [(CB["ones"], vb[:, k, 0:n]) for k in range(KC)],
                             R=[b_vb, b_cst], W=[b_ps[pm_]])
                    mm_group(psum[pq_][:, 0:n], [(CB["ones"], sq[:, k, 0:n]) for k in range(KC)],
                             R=[b_sq, b_cst], W=[b_ps[pq_]])
                    K.op("dve", lambda n=n, pm_=pm_: nc.vector.tensor_scalar(
                        out=mt[:, 0:n], in0=psum[pm_][:, 0:n], scalar1=1.0 / D, scalar2=None, op0=ALU.mult),
                        R=[b_ps[pm_]], W=[b_mt])
                    K.op("dve", lambda n=n: nc.vector.tensor_tensor(
                        out=m2[:, 0:n], in0=mt[:, 0:n], in1=mt[:, 0:n], op=ALU.mult), R=[b_mt], W=[b_m2])
                    K.op("dve", lambda n=n, pq_=pq_: nc.vector.scalar_tensor_tensor(
                        out=m2[:, 0:n], in0=psum[pq_][:, 0:n], scalar=1.0 / D, in1=m2[:, 0:n],
                        op0=ALU.mult, op1=ALU.subtract), R=[b_ps[pq_], b_m2], W=[b_m2])
                    K.op("dve", lambda n=n: nc.vector.tensor_scalar(
                        out=m2[:, 0:n], in0=m2[:, 0:n], scalar1=1e-5, scalar2=None, op0=ALU.add),
                        R=[b_m2], W=[b_m2])
                    K.op("act", lambda n=n: nc.scalar.activation(
                        out=rs[:, 0:n], in_=m2[:, 0:n], func=AF.Ln), R=[b_m2], W=[b_rs])
                    K.op("act", lambda n=n: nc.scalar.activation(
                        out=rs[:, 0:n], in_=rs[:, 0:n], func=AF.Exp, scale=-0.5), R=[b_rs], W=[b_rs])
                    K.op("dve", lambda n=n: nc.vector.scalar_tensor_tensor(
                        out=nm[:, 0:n], in0=mt[:, 0:n], scalar=-1.0, in1=rs[:, 0:n],
                        op0=ALU.mult, op1=ALU.mult), R=[b_mt, b_rs], W=[b_nm])
                    K.op("dve", lambda n=n, cols=cols: nc.vector.tensor_tensor(
                        out=v[:, :, cols], in0=v[:, :, cols],
                        in1=rs[:, 0:n].unsqueeze(1).broadcast_to([128, KC, n]), op=ALU.mult),
                        R=[b_rs], W=[b_v])
                    K.op("dve", lambda n=n, cols=cols: nc.vector.tensor_tensor(
                        out=v[:, :, cols], in0=v[:, :, cols],
                        in1=nm[:, 0:n].unsqueeze(1).broadcast_to([128, KC, n]), op=ALU.add),
                        R=[b_nm], W=[b_v])
                    for k in range(KC):
                        K.op("act", lambda k=k, n=n, cols=cols: nc.scalar.activation(
                            out=v[:, k, cols], in_=v[:, k, cols], func=AF.Identity,
                            scale=lnp[:, l, g_i, k:k + 1], bias=lnp[:, l, b_i, k:k + 1]),
                            R=[b_lnp], W=[b_v])
                    if write_bf:
                        K.op("pool", lambda cols=cols: nc.gpsimd.tensor_copy(out=x_bf[:, :, cols], in_=v[:, :, cols]),
                             R=[b_v], W=[b_xbf])

                K.run_sched([(lambda ti=ti: ln_tile(ti), ([ti - 2] if ti >= 2 else [])) for ti in range(len(TILES))])

        def subtiles(c0, n):
            if c0 == 0:
                return [("sample", 0, 16), ("meta", 16, 16)]
            return [("prompt", c0 + 128 * i, 128) for i in range(n // 128)]

        def gated_norm(st, oB, b_oB, gz, b_gz, nw, b_nw, og, b_og, hbase, c0, n, tagbufs):
            sq, b_sq, rsd, b_rsd, tn, b_tn = tagbufs
            K.op("act", lambda: nc.scalar.activation(out=sq[:, :, 0:n], in_=oB[:, :, 0:n], func=AF.Square),
                 R=[b_oB], W=[b_sq])
            for h in range(4):
                q = h % 2
                mm_group(psum[q][:, 0:n], [(CB["ones"], sq[:, h, 0:n])], R=[b_sq, b_cst], W=[b_ps[q]])
                K.op("act", lambda q=q: nc.scalar.activation(out=rsd[:, 0:n], in_=psum[q][:, 0:n], func=AF.Ln,
                                                             scale=1.0 / 128, bias=eps6[:, 0:1]),
                     R=[b_ps[q], b_eps], W=[b_rsd])
                K.op("act", lambda: nc.scalar.activation(out=rsd[:, 0:n], in_=rsd[:, 0:n], func=AF.Exp, scale=-0.5),
                     R=[b_rsd], W=[b_rsd])
                K.op("dve", lambda h=h: nc.vector.tensor_tensor(out=tn[:, 0:n], in0=oB[:, h, 0:n], in1=rsd[:, 0:n],
                                                                op=ALU.mult), R=[b_oB, b_rsd], W=[b_tn])
                K.op("dve", lambda h=h: nc.vector.scalar_tensor_tensor(
                    out=og[:, hbase + h, c0:c0 + n], in0=tn[:, 0:n], scalar=nw[:, 0:1], in1=gz[:, h, 0:n],
                    op0=ALU.mult, op1=ALU.mult), R=[b_tn, b_nw, b_gz], W=[b_og])

        def gla_phase(l, og, b_og, win_v):
            with ExitStack() as st:
                wB = sbt(st, "wB", [128, KC, 1552], BF16)
                b_wBb = [Buf() for _ in range(13)]
                for blk_ in range(12):
                    K.dma("pool", wB[:, :, blk_ * 128:(blk_ + 1) * 128],
                          win_v[:, :, 2056 + blk_ * 128:2056 + (blk_ + 1) * 128], W=[b_wBb[blk_]])
                K.dma("pool", wB[:, :, 1536:1552], win_v[:, :, 2056 + 1536:2056 + 1552], W=[b_wBb[12]])
                w2e = sbt(st, "w2e", [17, 256])
                b_w2e = Buf()
                K.dma("sp", w2e[0:16, :], gla_gate_w2[l], W=[b_w2e])
                K.dma("sp", w2e[16:17, :], gla_gate_b[l].rearrange("(o n) -> o n", o=1), W=[b_w2e])
                nw = sbt(st, "gla_nw", [128, 1])
                b_nw = Buf()
                K.dma("sp", nw[:, :], gla_norm_w[l].rearrange("(p o) -> p o", o=1), W=[b_nw],
                      allow_slow_non_contiguous=True)
                qk = sbt(st, "g_qk", [128, 4, 512])
                b_qk = Buf()
                sr = sbt(st, "g_sr", [128, 4, 512], BF16)
                b_sr = Buf()
                glr = sbt(st, "g_glr", [17, 512])
                b_glr = Buf()
                oB = sbt(st, "g_oB", [128, 4, 512])
                b_oB = Buf()
                sq = sbt(st, "g_sq", [128, 4, 512], BF16)
                rsd = sbt(st, "g_rsd", [128, 512])
                tn = sbt(st, "g_tn", [128, 512])
                nbufs = (sq, Buf(), rsd, Buf(), tn, Buf())
                Lt = [sbt(st, "g_L%d" % i, [128, 256]) for i in range(3)]
                E12 = [sbt(st, "g_E%d" % i, [128, 2, 2, 128]) for i in range(3)]
                atot = [sbt(st, "g_at%d" % i, [128, 2, 2]) for i in range(3)]
                qdk = [sbt(st, "g_qdk%d" % i, [128, 2, 2, 128], BF16) for i in range(3)]
                qdm = [sbt(st, "g_qdm%d" % i, [128, 4, 128], BF16) for i in range(3)]
                b_qdm = [Buf(), Buf(), Buf()]
                qm = sbt(st, "g_qm", [128, 4, 18])
                b_qm = Buf()
                E3 = [sbt(st, "g_E3%d" % i, [128, 256]) for i in range(3)]
                kdec = [sbt(st, "g_kd%d" % i, [128, 256], BF16) for i in range(3)]
                vtb = [sbt(st, "g_vt%d" % i, [128, 512], BF16) for i in range(3)]
                att = [sbt(st, "g_att%d" % i, [128, 4, 128], BF16) for i in range(3)]
                b_L, b_E12, b_at, b_qdk, b_E3, b_kd, b_vt, b_att = ([Buf(), Buf(), Buf()] for _ in range(8))
                S = sbt(st, "g_S", [128, 2, 128])
                Sb = sbt(st, "g_Sb", [128, 2, 128], BF16)
                b_S, b_Sb = Buf(), Buf()
                K.op("dve", lambda: nc.vector.memset(S[:], 0.0), W=[b_S])
                K.op("dve", lambda: nc.vector.memset(Sb[:], 0.0), W=[b_Sb])
                K.op("dve", lambda: nc.vector.memset(glr[:], 1.0), W=[b_glr])
                Ss = sbt(st, "g_Ss", [128, NS, 2, 128])
                b_Ss = Buf()
                sgl_v = state_gla[l].rearrange("s (pr hf) k v -> (hf k) s pr v", hf=2)
                for s4 in range(4):
                    K.dma("sp", Ss[:, 4 * s4:4 * s4 + 4], sgl_v[:, 4 * s4:4 * s4 + 4], W=[b_Ss])
                vbd = sbt(st, "g_vbd", [16, NS, 512], BF16)
                b_vbd = Buf()
                sub_ctr = [0]
                def glaA(kind, sc0, nt, u, bx, by, c0):
                    lc = sc0 - c0
                    if kind == "sample":
                        m_incl, m_strict_l = C["ident"], C["zero"]
                        mb_incl = CB["ident"]
                    else:
                        m_incl, m_strict_l = C["mu_incl"], C["ml_strict"]
                        mb_incl = CB["mu_incl"]
                    yield
                    mm_group(psum[bx][0:nt, 0:256], [(x_bf[:, k, sc0:sc0 + nt], wB[:, k, 256:512]) for k in range(KC)],
                             R=b_wBb[2:4] + [b_xbf], W=[b_ps[bx]])
                    yield
                    mm_group(psum[by][0:nt, 0:512], [(x_bf[:, k, sc0:sc0 + nt], wB[:, k, 512:1024]) for k in range(KC)],
                             R=b_wBb[4:8] + [b_xbf], W=[b_ps[by]])
                    yield
                    mm_group(psum[bx][0:nt, 256:512], [(glr[0:17, lc:lc + nt], w2e[0:17, :])],
                             R=[b_glr, b_w2e], W=[b_ps[bx]])
                    yield
                    K.op("act", lambda u=u: nc.scalar.activation(out=Lt[u][0:nt, :], in_=psum[bx][0:nt, 256:512],
                                                                 func=AF.Exp, scale=-1.0), R=[b_ps[bx]], W=[b_L[u]])
                    yield
                    K.op("act", lambda u=u: nc.scalar.activation(out=Lt[u][0:nt, :], in_=Lt[u][0:nt, :], func=AF.Ln,
                                                                 bias=one1[0:nt, 0:1]), R=[b_L[u], b_eps], W=[b_L[u]])
                    yield
                    K.op("act", lambda u=u: nc.scalar.copy(out=vtb[u][0:nt, :], in_=psum[by][0:nt, :]),
                         R=[b_ps[by]], W=[b_vt[u]])
                    yield
                    for pr_ in range(2):
                        mm_group(psum[by][:, pr_ * 128:pr_ * 128 + nt],
                                 [(Lt[u][0:nt, pr_ * 128:(pr_ + 1) * 128], m_incl[0:nt, 0:nt])],
                                 R=[b_L[u], b_cst], W=[b_ps[by]])
                    yield
                    mm_group(psum[by][0:nt, 256:512], [(m_strict_l[0:nt, 0:nt], Lt[u][0:nt, :])],
                             R=[b_L[u], b_cst], W=[b_ps[by]])
                    csT = psum[by][:, 0:256].rearrange("p (a b) -> p a b", a=2)[:, :, 0:nt]
                    yield
                    K.op("act", lambda u=u, csT=csT: nc.scalar.activation(out=E12[u][:, 0, :, 0:nt], in_=csT,
                                                                          func=AF.Exp, scale=-1.0 / 16),
                         R=[b_ps[by]], W=[b_E12[u]])
                    yield
                    K.op("act", lambda u=u, csT=csT: nc.scalar.activation(out=E12[u][:, 1, :, 0:nt], in_=csT,
                                                                          func=AF.Exp, scale=1.0 / 16),
                         R=[b_ps[by]], W=[b_E12[u]])
                    nch = 2 if nt == 128 else 1
                    cl = 64 if nt == 128 else nt
                    yield
                    for ch in range(nch):
                        lastc = ch * 64 + cl - 1
                        K.op("act", lambda u=u, ch=ch, lastc=lastc: nc.scalar.activation(
                            out=atot[u][:, :, ch:ch + 1],
                            in_=psum[by][:, 0:256].rearrange("p (a b) -> p a b", a=2)[:, :, lastc:lastc + 1],
                            func=AF.Exp, scale=-1.0 / 16), R=[b_ps[by]], W=[b_at[u]])
                    yield
                    K.op("act", lambda u=u: nc.scalar.activation(out=E3[u][0:nt, :], in_=psum[by][0:nt, 256:512],
                                                                 func=AF.Exp, scale=-1.0 / 16),
                         R=[b_ps[by]], W=[b_E3[u]])
                    yield
                    K.op("dve", lambda u=u, lc=lc: nc.vector.scalar_tensor_tensor(
                        out=qdk[u][:, 0, :, 0:nt], in0=qk[:, 0:2, lc:lc + nt], scalar=0.125, in1=E12[u][:, 0, :, 0:nt],
                        op0=ALU.mult, op1=ALU.mult), R=[b_qk, b_E12[u]], W=[b_qdk[u]])
                    yield
                    K.op("dve", lambda u=u, lc=lc: nc.vector.tensor_tensor(
                        out=qdk[u][:, 1, :, 0:nt], in0=qk[:, 2:4, lc:lc + nt], in1=E12[u][:, 1, :, 0:nt], op=ALU.mult),
                        R=[b_qk, b_E12[u]], W=[b_qdk[u]])
                    yield
                    K.op("dve", lambda u=u: nc.vector.tensor_tensor(
                        out=kdec[u][0:nt, :], in0=psum[bx][0:nt, 0:256], in1=E3[u][0:nt, :], op=ALU.mult),
                        R=[b_ps[bx], b_E3[u]], W=[b_kd[u]])
                    yield
                    if kind == "sample":
                        for h in range(4):
                            K.op("dve", lambda h=h: nc.vector.tensor_scalar(
                                out=qm[:, h, :], in0=qk[:, h // 2, 0:18], scalar1=C["blk%d" % (h % 2)][:, 0:1],
                                scalar2=None, op0=ALU.mult), R=[b_qk, b_cst], W=[b_qm])
                        gla_sample(l, u, qm, b_qm, E12, b_E12, kdec, b_kd, vtb, b_vt, Ss, b_Ss, vbd, b_vbd, oB, b_oB)
                        return
                    yield
                    for h in range(4):
                        K.op("dve", lambda h=h, u=u: nc.vector.tensor_scalar(
                            out=qdm[u][:, h, 0:nt], in0=qdk[u][:, 0, h // 2, 0:nt], scalar1=C["blk%d" % (h % 2)][:, 0:1],
                            scalar2=None, op0=ALU.mult), R=[b_qdk[u], b_cst], W=[b_qdm[u]])
                    yield
                    for h in range(4):
                        pr_, hf = h // 2, h % 2
                        rows = slice(hf * 64, hf * 64 + 64)
                        mm_group(psum[bx][0:nt, h * 128:h * 128 + nt],
                                 [(qdk[u][:, 1, pr_, 0:nt], qdm[u][:, h, 0:nt])],
                                 R=[b_qdk[u], b_qdm[u]], W=[b_ps[bx]], inc=(h == 3))
                    yield
                    K.op("dve", lambda u=u: nc.vector.tensor_tensor(
                        out=att[u][0:nt, :, 0:nt],
                        in0=psum[bx][0:nt, :].rearrange("p (a b) -> p a b", a=4)[:, :, 0:nt],
                        in1=m_incl[0:nt, 0:nt].unsqueeze(1).broadcast_to([nt, 4, nt]), op=ALU.mult),
                        R=[b_ps[bx], b_cst], W=[b_att[u]])

                def glaB(kind, sc0, nt, u, c0):
                    lc = sc0 - c0
                    nch = 2 if nt == 128 else 1
                    cl = 64 if nt == 128 else nt
                    for ch in range(nch):
                        trow = slice(ch * 64, ch * 64 + cl)
                        yield
                        for h in range(4):
                            pr_, hf = h // 2, h % 2
                            rows = slice(hf * 64, hf * 64 + 64)
                            K.op("pe", lambda h=h, pr_=pr_, trow=trow, u=u: nc.tensor.matmul(
                                psum[6][:, h * 64:h * 64 + cl], lhsT=Sb[:, pr_, :], rhs=qdm[u][:, h, trow],
                                start=True, stop=False), R=[b_Sb, b_qdm[u]], W=[b_ps[6]], inc=False)
                            K.op("pe", lambda h=h, trow=trow, u=u: nc.tensor.matmul(
                                psum[6][:, h * 64:h * 64 + cl], lhsT=vtb[u][0:nt, h * 128:(h + 1) * 128],
                                rhs=att[u][0:nt, h, trow], start=False, stop=True),
                                R=[b_vt[u], b_att[u]], W=[b_ps[6]], inc=(h == 3))
                        yield
                        for h in range(4):
                            pr_, hf = h // 2, h % 2
                            K.op("pe", lambda h=h, pr_=pr_, hf=hf, trow=trow, u=u: nc.tensor.matmul(
                                psum[7][hf * 64:hf * 64 + 64, pr_ * 128:(pr_ + 1) * 128],
                                lhsT=kdec[u][trow, h * 64:(h + 1) * 64], rhs=vtb[u][trow, h * 128:(h + 1) * 128],
                                start=True, stop=True), R=[b_kd[u], b_vt[u]], W=[b_ps[7]], inc=(h == 3))
                        yield
                        K.op("act", lambda lc=lc, trow=trow, ch=ch: nc.scalar.copy(
                            out=oB[:, :, lc + ch * 64:lc + ch * 64 + cl],
                            in_=psum[6][:, 0:256].rearrange("p (a b) -> p a b", a=4)[:, :, 0:cl]),
                            R=[b_ps[6]], W=[b_oB])
                        yield
                        for pr_ in range(2):
                            K.op("dve", lambda pr_=pr_, ch=ch, u=u: nc.vector.scalar_tensor_tensor(
                                out=S[:, pr_, :], in0=S[:, pr_, :], scalar=atot[u][:, pr_, ch:ch + 1],
                                in1=psum[7][:, pr_ * 128:(pr_ + 1) * 128], op0=ALU.mult, op1=ALU.add),
                                R=[b_at[u], b_ps[7]], W=[b_S])
                        yield
                        K.op("act", lambda: nc.scalar.copy(out=Sb[:], in_=S[:]), R=[b_S], W=[b_Sb])

                    yield

                for (c0, n) in TILES:
                    for bi in range(9):
                        q = bi % 2
                        if bi < 8:
                            woff = bi * 128 if bi < 4 else 1024 + (bi - 4) * 128
                            mw = 128
                        else:
                            woff, mw = 1536, 16
                        mm_group(psum[q][0:mw, 0:n], [(wB[:, k, woff:woff + mw], x_bf[:, k, c0:c0 + n]) for k in range(KC)],
                                 R=[b_wBb[woff // 128], b_xbf], W=[b_ps[q]])
                        if bi < 4:
                            K.op("dve", lambda bi=bi, q=q: nc.vector.tensor_copy(out=qk[:, bi, 0:n], in_=psum[q][:, 0:n]),
                                 R=[b_ps[q]], W=[b_qk])
                        elif bi < 8:
                            K.op("act", lambda bi=bi, q=q: nc.scalar.activation(
                                out=sr[:, bi - 4, 0:n], in_=psum[q][:, 0:n], func=AF.Silu), R=[b_ps[q]], W=[b_sr])
                        else:
                            K.op("dve", lambda q=q: nc.vector.tensor_copy(out=glr[0:16, 0:n], in_=psum[q][0:16, 0:n]),
                                 R=[b_ps[q]], W=[b_glr])
                    subs = subtiles(c0, n)
                    ths = []
                    for i_, (kind, sc0, nt) in enumerate(subs):
                        gi = sub_ctr[0] + i_
                        afterA = ([2 * (i_ - 2)] if i_ >= 2 else []) + ([2 * (i_ - 3) + 1] if i_ >= 3 else [])
                        afterB = [2 * i_] + ([2 * (i_ - 1) + 1] if i_ >= 1 else [])
                        ths.append((lambda kind=kind, sc0=sc0, nt=nt, gi=gi, c0=c0: glaA(kind, sc0, nt, gi % 3, 2 + 2 * (gi % 2), 3 + 2 * (gi % 2), c0), afterA))
                        if kind == "sample":
                            ths.append((lambda: None, afterB))
                        else:
                            ths.append((lambda kind=kind, sc0=sc0, nt=nt, gi=gi, c0=c0: glaB(kind, sc0, nt, gi % 3, c0), afterB))
                    K.run_sched(ths)
                    sub_ctr[0] += len(subs)
                    gated_norm(st, oB, b_oB, sr, b_sr, nw, b_nw, og, b_og, 4, c0, n, nbufs)
                K.dma("sp", o_gla_p[l].rearrange("(pr hf) k v -> (hf k) pr v", hf=2), S[:], R=[b_S])

        def gla_sample(l, u, qm, b_qm, E12, b_E12, kdec, b_kd, vtb, b_vt, Ss, b_Ss, vbd, b_vbd, oB, b_oB):
            K.op("dve", lambda: nc.vector.tensor_tensor(
                out=vbd[:, :, :], in0=vtb[u][0:16, :].unsqueeze(1).broadcast_to([16, NS, 512]),
                in1=C["ident"][0:16, 0:16].unsqueeze(2).broadcast_to([16, NS, 512]), op=ALU.mult),
                R=[b_vt[u], b_cst], W=[b_vbd])
            for s4 in range(4):
                for si in range(4):
                    s = 4 * s4 + si
                    for h in range(4):
                        pr_, hf = h // 2, h % 2
                        q = 6 + si // 2
                        cb = (si % 2) * 256 + pr_ * 128
                        K.op("pe", lambda s=s, h=h, hf=hf, q=q, cb=cb: nc.tensor.matmul(
                            psum[q][hf * 64:hf * 64 + 64, cb:cb + 128],
                            lhsT=kdec[u][0:16, h * 64:(h + 1) * 64], rhs=vbd[0:16, s, h * 128:(h + 1) * 128],
                            start=True, stop=True), R=[b_kd[u], b_vbd], W=[b_ps[q]], inc=(h == 3))
                for si in range(4):
                    s = 4 * s4 + si
                    q = 6 + si // 2
                    for pr_ in range(2):
                        cb = (si % 2) * 256 + pr_ * 128
                        K.op("dve", lambda s=s, pr_=pr_, q=q, cb=cb: nc.vector.scalar_tensor_tensor(
                            out=Ss[:, s, pr_, :], in0=Ss[:, s, pr_, :], scalar=E12[u][:, 0, pr_, s:s + 1],
                            in1=psum[q][:, cb:cb + 128], op0=ALU.mult, op1=ALU.add),
                            R=[b_E12[u], b_ps[q]], W=[b_Ss])
            for s in range(NS):
                for h in range(4):
                    pr_, hf = h // 2, h % 2
                    K.op("pe", lambda s=s, h=h, pr_=pr_: nc.tensor.matmul(
                        psum[5][:, h * 32 + s:h * 32 + s + 2], lhsT=Ss[:, s, pr_, :], rhs=qm[:, h, s:s + 2],
                        start=True, stop=True), R=[b_Ss, b_qm], W=[b_ps[5]], inc=(s == NS - 1 and h == 3))
            K.op("act", lambda: nc.scalar.activation(
                out=oB[:, :, 0:16], in_=psum[5][:, 0:128].rearrange("p (a b) -> p a b", a=4)[:, :, 0:16],
                func=AF.Copy, scale=0.125), R=[b_ps[5]], W=[b_oB])
            sgl_o = o_gla_s[l].rearrange("s (pr hf) k v -> (hf k) s pr v", hf=2)
            for s4 in range(4):
                K.dma("sp", sgl_o[:, 4 * s4:4 * s4 + 4], Ss[:, 4 * s4:4 * s4 + 4], R=[b_Ss])

        def gdn_phase(l, og, b_og, win_v):
            keep = {}
            with ExitStack() as st0:
                s_wT = sbt(st0, "d_swT", [128, 4, 16], BF16)
                s_u0 = sbt(st0, "d_su0", [16, 4, 128])
                s_kd = sbt(st0, "d_skd", [16, 4, 128], BF16)
                s_qn = sbt(st0, "d_sqn", [128, 4, 16], BF16)
                s_G = sbt(st0, "d_sG", [128, NS, 4])
                s_gz = sbt(st0, "d_sgz", [128, 4, 16], BF16)
                b_skeep = Buf()
                nw = sbt(st0, "gdn_nw", [128, 1])
                b_nw = Buf()
                K.dma("sp", nw[:, :], gdn_norm_w[l].rearrange("(p o) -> p o", o=1), W=[b_nw],
                      allow_slow_non_contiguous=True)
                with ExitStack() as st:
                    wA = sbt(st, "wA", [128, KC, 2056], BF16)
                    b_wAb = [Buf() for _ in range(17)]
                    for blk_ in range(16):
                        K.dma("pool", wA[:, :, blk_ * 128:(blk_ + 1) * 128], win_v[:, :, blk_ * 128:(blk_ + 1) * 128],
                              W=[b_wAb[blk_]])
                    K.dma("pool", wA[:, :, 2048:2056], win_v[:, :, 2048:2056], W=[b_wAb[16]])
                    cw = sbt(st, "d_cw", [128, 12, 4])
                    b_cw = Buf()
                    for i_ in range(4):
                        K.dma("sp", cw[:, :, i_], conv_w[l][i_].rearrange("(b p) -> p b", p=128), W=[b_cw],
                              allow_slow_non_contiguous=True)
                    abt = sbt(st, "d_abt", [128, 2, 4])
                    b_abt = Buf()
                    K.dma("sp", abt[:, 0, :], a_log[l].partition_broadcast(128), W=[b_abt])
                    K.dma("sp", abt[:, 1, :], dt_bias[l].partition_broadcast(128), W=[b_abt])
                    K.op("act", lambda: nc.scalar.activation(out=abt[:, 0, :], in_=abt[:, 0, :], func=AF.Exp),
                         R=[b_abt], W=[b_abt])
                    K.op("dve", lambda: nc.vector.tensor_scalar(out=abt[:, 0, :], in0=abt[:, 0, :], scalar1=-1.0,
                                                                scalar2=None, op0=ALU.mult), R=[b_abt], W=[b_abt])
                    Xs = sbt(st, "d_Xs", [128, 4, 2, 128])
                    tok3 = [sbt(st, "d_tok3%d" % i, [128, 3, 4, 128], BF16) for i in range(2)]
                    cbT = tok3[0][:, :, :, :].rearrange("p a h c -> p (a h c)").bitcast(F32)[:, 0:576].rearrange("p (b s) -> p b s", b=12)
                    b_cbT = Buf()
                    with ExitStack() as stc:
                        cb48 = sbt(stc, "d_cb48", [48, 1536])
                        b_cb48 = Buf()
                        K.dma("sp", cb48[:], state_conv[l].rearrange("s i c -> (s i) c"), W=[b_cb48])
                        for half in range(2):
                            q = half
                            for b6 in range(6):
                                blk = 6 * half + b6
                                K.op("pe", lambda blk=blk, b6=b6, q=q: nc.tensor.transpose(
                                    psum[q][:, b6 * 48:(b6 + 1) * 48], cb48[0:48, blk * 128:(blk + 1) * 128],
                                    C["ident"][0:48, 0:48]), R=[b_cb48, b_cst], W=[b_ps[q]], inc=(b6 == 5))
                            K.op("act", lambda half=half, q=q: nc.scalar.copy(
                                out=cbT[:, 6 * half:6 * half + 6, :],
                                in_=psum[q][:, 0:288].rearrange("p (a b) -> p a b", a=6)), R=[b_ps[q]], W=[b_cbT])
                        K.barrier()
                    K.dma("sp", o_conv_s[l][:, 0:2, :], state_conv[l][:, 1:3, :])
                    halo = sbt(st, "d_halo", [128, 12, 4])
                    b_halo = Buf()
                    K.op("dve", lambda: nc.vector.memset(halo[:], 0.0), W=[b_halo])
                    Pe = [sbt(st, "d_Pe%d" % i, [128, 3 + 512]) for i in range(2)]
                    b_Pe = [Buf(), Buf()]
                    acc = [sbt(st, "d_acc%d" % i, [128, 512]) for i in range(2)]
                    b_acc = [Buf(), Buf()]
                    cq = [sbt(st, "d_cq%d" % i, [128, 512]) for i in range(2)]
                    b_cq = [Buf(), Buf()]
                    vn = sbt(st, "d_vn", [128, 4, 512], BF16)
                    b_vn = Buf()
                    qkn = sbt(st, "d_qkn", [128, 8, 512], BF16)
                    b_qkn = Buf()
                    gz = sbt(st, "d_gz", [128, 4, 512], BF16)
                    b_gz = Buf()
                    oA = big[:, 8 * T:8 * T + 4096].bitcast(F32).rearrange("p (a b) -> p a b", a=4)
                    b_oA = Buf()
                    sq = big[:, 8 * T + 4096:8 * T + 6144].rearrange("p (a b) -> p a b", a=4)
                    rsd = sbt(st, "d_rsd", [128, 512])
                    rsd_b = sbt(st, "d_rsdb", [128, 512])
                    tn = acc[0]
                    b_sq, b_rsd, b_tn, b_rsdb = Buf(), Buf(), b_acc[0], Buf()
                    nbufs = (sq, b_sq, rsd, b_rsd, tn, b_tn)
                    pnew = big[0:16, 8 * T:8 * T + 3072].bitcast(F32)
                    b_pnew = b_oA
                    tb = [sbt(st, "d_tb%d" % i, [128, 6, 4]) for i in range(2)]
                    b_tb = [Buf(), Buf()]
                    r2 = sbt(st, "d_r2", [128, 8])
                    b_r2 = Buf()
                    Gbc = [sbt(st, "d_Gbc%d" % i, [128, 2, 4]) for i in range(2)]
                    b_Gbc = [Buf(), Buf()]
                    dg = sbt(st, "d_dg", [128, 8, 128])
                    b_dg = Buf()
                    gm = sbt(st, "d_gm", [128, 4, 128])
                    b_gm = Buf()
                    ET = sbt(st, "d_ET", [128, 4, 128])
                    ETi = sbt(st, "d_ETi", [128, 4, 128])
                    ETs = ET
                    b_ET, b_ETi = Buf(), Buf()
                    b_ETs = b_ET
                    QA = [sbt(st, "d_QA%d" % i, [128, 4, 128]) for i in range(2)]
                    XA = [sbt(st, "d_XA%d" % i, [128, 4, 2, 128]) for i in range(2)]
                    QTA = [XA[i][:, :, 0, :] for i in range(2)]
                    b_QA = [Buf(), Buf()]
                    b_QTA = [Buf(), Buf()]
                    Qs = sbt(st, "d_Qs", [128, 4, 128])
                    QTs = Xs[:, :, 0, :]
                    b_Qs, b_QTs, b_Xst = Buf(), Buf(), Buf()
                    TT = [XA[i][:, :, 1, :] for i in range(2)]
                    b_TT = [Buf(), Buf()]
                    TTb = sbt(st, "d_TTb", [128, 4, 128], BF16)
                    b_TTb = Buf()
                    bkq = [sbt(st, "d_bkq%d" % i, [128, 2, 4, 128], BF16) for i in range(2)]
                    b_bkq = [Buf(), Buf()]
                    qkT = [sbt(st, "d_qkT%d" % i, [128, 4, 128], BF16) for i in range(2)]
                    b_qkT = [Buf(), Buf()]
                    b_tok3 = [Buf(), Buf()]
                    u0 = [sbt(st, "d_u0%d" % i, [128, 4, 128]) for i in range(2)]
                    wT = [sbt(st, "d_wT%d" % i, [128, 4, 128], BF16) for i in range(2)]
                    uu = sbt(st, "d_u", [128, 4, 128], BF16)
                    b_u0, b_wT, b_u = [Buf(), Buf()], [Buf(), Buf()], Buf()
                    S = sbt(st, "d_S", [128, 4, 128])
                    Sb = sbt(st, "d_Sb", [128, 4, 128], BF16)
                    b_S, b_Sb = Buf(), Buf()
                    K.op("dve", lambda: nc.vector.memset(S[:], 0.0), W=[b_S])
                    K.op("dve", lambda: nc.vector.memset(Sb[:], 0.0), W=[b_Sb])
                    K.op("dve", lambda: nc.vector.memset(uu[:], 0.0), W=[b_u])
                    sub_ctr = [0]
                    scan_done = [0]
                    def stageA(kind, sc0, nt, a, c0):
                        lc = sc0 - c0
                        smp = (kind == "sample")
                        m_incl = C["ident"] if smp else C["mu_incl"]
                        m_bd = C["ident"] if smp else C["bd"]
                        m_ustrict = C["zero"] if smp else C["mu_strict"]
                        m_lstrict = C["zero"] if smp else C["ml_strict"]
                        T_ = lambda i: tb[a][0:nt, i, :]
                        pv = lambda b: psum[b][:, :].rearrange("p (a c) -> p a c", a=4)[:, :, 0:nt]
                        pt = lambda b: psum[b][0:nt, :].rearrange("p (a c) -> p a c", a=4)[:, :, 0:nt]
                        p4 = lambda b: psum[b][0:nt, :].rearrange("p (a c) -> p a c", a=4)
                        bc = lambda i: tb[a][0:nt, i, :].unsqueeze(2).broadcast_to([nt, 4, 128])
                        mm_group(psum[2][0:nt, 0:8], [(x_bf[:, k, sc0:sc0 + nt], wA[:, k, 2048:2056]) for k in range(KC)],
                                 R=[b_wAb[16], b_xbf], W=[b_ps[2]])
                        yield
                        K.op("act", lambda: nc.scalar.activation(out=T_(0), in_=psum[2][0:nt, 0:4], func=AF.Exp, scale=-1.0),
                             R=[b_ps[2]], W=[b_tb[a]])
                        yield
                        K.op("dve", lambda: nc.vector.tensor_tensor(out=T_(2), in0=psum[2][0:nt, 4:8], in1=abt[0:nt, 1, :],
                                                                    op=ALU.add), R=[b_ps[2], b_abt], W=[b_tb[a]])
                        yield
                        K.op("act", lambda: nc.scalar.activation(out=T_(0), in_=T_(0), func=AF.Ln, bias=one1[0:nt, 0:1]),
                             R=[b_tb[a], b_eps], W=[b_tb[a]])
                        yield
                        K.op("act", lambda: nc.scalar.activation(out=T_(0), in_=T_(0), func=AF.Exp, scale=-1.0),
                             R=[b_tb[a]], W=[b_tb[a]])
                        yield
                        K.op("act", lambda: nc.scalar.activation(out=T_(2), in_=T_(2), func=AF.Exp), R=[b_tb[a]], W=[b_tb[a]])
                        yield
                        K.op("act", lambda: nc.scalar.activation(out=T_(2), in_=T_(2), func=AF.Ln, bias=one1[0:nt, 0:1]),
                             R=[b_tb[a], b_eps], W=[b_tb[a]])
                        yield
                        K.op("dve", lambda: nc.vector.tensor_tensor(out=T_(2), in0=T_(2), in1=abt[0:nt, 0, :], op=ALU.mult),
                             R=[b_tb[a], b_abt], W=[b_tb[a]])
                        yield
                        mm_group(psum[2][0:nt, 8:12], [(m_incl[0:nt, 0:nt], T_(2))], R=[b_tb[a], b_cst], W=[b_ps[2]])
                        yield
                        mm_group(psum[2][0:nt, 12:16], [(m_bd[0:nt, 0:nt], T_(2))], R=[b_tb[a], b_cst], W=[b_ps[2]])
                        yield
                        K.op("act", lambda: nc.scalar.activation(out=T_(1), in_=psum[2][0:nt, 8:12], func=AF.Exp),
                             R=[b_ps[2]], W=[b_tb[a]])
                        yield
                        K.op("dve", lambda: nc.vector.tensor_copy(out=T_(5), in_=psum[2][0:nt, 8:12]), R=[b_ps[2]], W=[b_tb[a]])
                        yield
                        K.op("dve", lambda: nc.vector.tensor_tensor(out=T_(3), in0=psum[2][0:nt, 12:16], in1=T_(5),
                                                                    op=ALU.subtract), R=[b_ps[2], b_tb[a]], W=[b_tb[a]])
                        yield
                        K.op("act", lambda: nc.scalar.activation(out=T_(3), in_=T_(3), func=AF.Exp), R=[b_tb[a]], W=[b_tb[a]])
                        yield
                        K.op("dve", lambda: nc.vector.tensor_tensor(out=T_(4), in0=T_(0), in1=T_(1), op=ALU.mult),
                             R=[b_tb[a]], W=[b_tb[a]])
                        yield
                        if smp:
                            K.op("dve", lambda: nc.vector.tensor_tensor(
                                out=dg[0:16, 0:4, 0:16].bitcast(F32R).rearrange("p h s -> p s h"),
                                in0=T_(2).unsqueeze(1).broadcast_to([16, 16, 4]),
                                in1=C["ident"][0:16, 0:16].unsqueeze(2).broadcast_to([16, 16, 4]), op=ALU.mult),
                                R=[b_tb[a], b_cst], W=[b_dg])
                            for h in range(4):
                                K.op("pe", lambda h=h: nc.tensor.matmul(psum[3][:, h * 16:(h + 1) * 16],
                                                                        lhsT=C["ones"][0:16, :], rhs=dg[0:16, h, 0:16],
                                                                        start=True, stop=True),
                                     R=[b_dg, b_cst], W=[b_ps[3]], inc=(h == 3))
                            K.op("act", lambda: nc.scalar.activation(
                                out=s_G[:, :, :].rearrange("p s h -> p h s"),
                                in_=psum[3][:, 0:64].rearrange("p (h s) -> p h s", h=4), func=AF.Exp),
                                R=[b_ps[3]], W=[b_skeep])
                        else:
                            K.op("dve", lambda: nc.vector.tensor_tensor(
                                out=r2[0:nt, :].rearrange("p (c h) -> p c h", c=2),
                                in0=T_(2).unsqueeze(1).broadcast_to([nt, 2, 4]),
                                in1=cst[0:nt, IDX["blk0"]:IDX["blk0"] + 2, 0:1].broadcast_to([nt, 2, 4]), op=ALU.mult),
                                R=[b_tb[a], b_cst], W=[b_r2])
                            mm_group(psum[2][:, 16:24], [(C["ones"][0:nt, :], r2[0:nt, :])], R=[b_r2, b_cst], W=[b_ps[2]])
                            K.op("act", lambda: nc.scalar.activation(out=Gbc[a][:, :, :].rearrange("p c h -> p (c h)"),
                                                                     in_=psum[2][:, 16:24], func=AF.Exp),
                                 R=[b_ps[2]], W=[b_Gbc[a]])
                        yield
                        K.op("dve", lambda: nc.vector.tensor_tensor(
                            out=dg[0:nt, :, 0:nt].bitcast(F32R), in0=C["ident"][0:nt, 0:nt].unsqueeze(1).broadcast_to([nt, 8, nt]),
                            in1=tb[a][0:nt, 0:2, :].rearrange("p a h -> p (a h)").unsqueeze(2).broadcast_to([nt, 8, nt]),
                            op=ALU.mult), R=[b_tb[a], b_cst, b_skeep], W=[b_dg])
                        yield
                        for wh in range(2):
                            K.op("pe", lambda wh=wh: nc.tensor.matmul(
                                psum[3 + wh][:, :].rearrange("p (a c) -> p a c", a=4)[:, :, 0:nt],
                                lhsT=ones_r[0:nt, :].bitcast(F32R), rhs=dg[0:nt, 4 * wh:4 * wh + 4, 0:nt].bitcast(F32R),
                                start=True, stop=True), R=[b_dg, b_onesr], W=[b_ps[3 + wh]], c=0.25)
                        yield
                        K.op("dve", lambda: nc.vector.tensor_tensor(out=bkq[a][:, 0, :, 0:nt], in0=qkn[:, 4:8, lc:lc + nt],
                                                                    in1=pv(3), op=ALU.mult),
                             R=[b_qkn, b_ps[3]], W=[b_bkq[a]])
                        yield
                        K.op("dve", lambda: nc.vector.tensor_tensor(out=bkq[a][:, 1, :, 0:nt], in0=qkn[:, 0:4, lc:lc + nt],
                                                                    in1=pv(4), op=ALU.mult),
                             R=[b_qkn, b_ps[4]], W=[b_bkq[a]])
                        yield
                        K.op("dve", lambda: nc.vector.tensor_tensor(
                            out=gm[0:nt, :, 0:nt], in0=m_incl[0:nt, 0:nt].unsqueeze(1).broadcast_to([nt, 4, nt]),
                            in1=T_(2).unsqueeze(2).broadcast_to([nt, 4, nt]), op=ALU.mult), R=[b_tb[a], b_cst], W=[b_gm])
                        yield
                        for h in range(4):
                            K.op("pe", lambda h=h: nc.tensor.matmul(psum[2][0:nt, h * 128:h * 128 + nt],
                                                                    lhsT=m_lstrict[0:nt, 0:nt], rhs=gm[0:nt, h, 0:nt],
                                                                    start=True, stop=True),
                                 R=[b_gm, b_cst], W=[b_ps[2]], inc=(h == 3))
                        yield
                        K.op("act", lambda: nc.scalar.activation(out=ET[0:nt, :, 0:nt], in_=pt(2), func=AF.Exp),
                             R=[b_ps[2]], W=[b_ET])
                        yield
                        K.op("dve", lambda: nc.vector.tensor_tensor(
                            out=ETi[0:nt, :, 0:nt], in0=ET[0:nt, :, 0:nt],
                            in1=m_incl[0:nt, 0:nt].unsqueeze(1).broadcast_to([nt, 4, nt]), op=ALU.mult),
                            R=[b_ET, b_cst], W=[b_ETi])
                        yield
                        K.op("dve", lambda: nc.vector.tensor_tensor(
                            out=ETs[0:nt, :, 0:nt], in0=ET[0:nt, :, 0:nt],
                            in1=m_ustrict[0:nt, 0:nt].unsqueeze(1).broadcast_to([nt, 4, nt]), op=ALU.mult),
                            R=[b_ET, b_cst, b_ETi], W=[b_ETs])
                        yield
                        for h in range(4):
                            K.op("pe", lambda h=h: nc.tensor.matmul(psum[3][0:nt, h * 128:h * 128 + nt],
                                                                    lhsT=qkn[:, 4 + h, lc:lc + nt], rhs=bkq[a][:, 0, h, 0:nt],
                                                                    start=True, stop=True),
                                 R=[b_qkn, b_bkq[a]], W=[b_ps[3]], inc=(h == 3))
                        yield
                        for h in range(4):
                            K.op("pe", lambda h=h: nc.tensor.matmul(psum[4][0:nt, h * 128:h * 128 + nt],
                                                                    lhsT=qkn[:, 4 + h, lc:lc + nt], rhs=qkn[:, h, lc:lc + nt],
                                                                    start=True, stop=True),
                                 R=[b_qkn], W=[b_ps[4]], inc=(h == 3))
                        AT, A_ = QTA[a], QA[a]
                        yield
                        K.op("dve", lambda: nc.vector.tensor_tensor(out=AT[0:nt, :, 0:nt].bitcast(F32R), in0=pt(3), in1=ETs[0:nt, :, 0:nt],
                                                                    op=ALU.mult), R=[b_ps[3], b_ETs], W=[b_QTA[a]])
                        yield
                        K.op("dve", lambda: nc.vector.tensor_tensor(out=qkT[a][0:nt, :, 0:nt], in0=pt(4), in1=ETi[0:nt, :, 0:nt],
                                                                    op=ALU.mult), R=[b_ps[4], b_ETi], W=[b_qkT[a]])
                        yield
                        K.op("dve", lambda: nc.vector.scalar_tensor_tensor(
                            out=TT[a][0:nt, :, 0:nt].bitcast(F32R), in0=AT[0:nt, :, 0:nt], scalar=-1.0,
                            in1=C["ident"][0:nt, 0:nt].unsqueeze(1).broadcast_to([nt, 4, nt]),
                            op0=ALU.mult, op1=ALU.add), R=[b_QTA[a], b_cst], W=[b_TT[a]])
                        yield
                        if not smp:
                            for h in range(4):
                                K.op("pe", lambda h=h: nc.tensor.transpose(psum[2][0:nt, h * 128:h * 128 + nt],
                                                                           AT[0:nt, h, 0:nt], C["ident"][0:nt, 0:nt]),
                                     R=[b_QTA[a], b_cst], W=[b_ps[2]], inc=(h == 3))
                            yield
                            K.op("act", lambda: nc.scalar.copy(out=A_[0:nt, :, 0:nt].bitcast(F32R), in_=pt(2)), R=[b_ps[2]], W=[b_QA[a]])

                    def stageB1(kind, sc0, nt, a, c0):
                        lc = sc0 - c0
                        smp = (kind == "sample")
                        m_incl = C["ident"] if smp else C["mu_incl"]
                        m_bd = C["ident"] if smp else C["bd"]
                        m_ustrict = C["zero"] if smp else C["mu_strict"]
                        m_lstrict = C["zero"] if smp else C["ml_strict"]
                        T_ = lambda i: tb[a][0:nt, i, :]
                        pv = lambda b: psum[b][:, :].rearrange("p (a c) -> p a c", a=4)[:, :, 0:nt]
                        pt = lambda b: psum[b][0:nt, :].rearrange("p (a c) -> p a c", a=4)[:, :, 0:nt]
                        p4 = lambda b: psum[b][0:nt, :].rearrange("p (a c) -> p a c", a=4)
                        bc = lambda i: tb[a][0:nt, i, :].unsqueeze(2).broadcast_to([nt, 4, 128])
                        yield
                        for h in range(4):
                            K.op("pe", lambda h=h: nc.tensor.matmul(psum[5][0:nt, h * 128:(h + 1) * 128],
                                                                    lhsT=qkn[:, 4 + h, lc:lc + nt], rhs=CB["ident"],
                                                                    start=True, stop=True),
                                 R=[b_qkn, b_cst], W=[b_ps[5]], inc=(h == 3))
                        yield
                        for h in range(4):
                            K.op("pe", lambda h=h: nc.tensor.matmul(psum[6][0:nt, h * 128:(h + 1) * 128],
                                                                    lhsT=vn[:, h, lc:lc + nt], rhs=CB["ident"],
                                                                    start=True, stop=True),
                                 R=[b_vn, b_cst], W=[b_ps[6]], inc=(h == 3))
                        yield
                        K.op("dve", lambda: nc.vector.tensor_tensor(out=tok3[a][0:nt, 0], in0=p4(5), in1=bc(4), op=ALU.mult),
                             R=[b_ps[5], b_tb[a]], W=[b_tok3[a]])
                        yield
                        K.op("dve", lambda: nc.vector.tensor_tensor(out=tok3[a][0:nt, 1], in0=p4(5), in1=bc(3), op=ALU.mult),
                             R=[b_ps[5], b_tb[a]], W=[b_tok3[a]])
                        yield
                        K.op("dve", lambda: nc.vector.tensor_tensor(out=tok3[a][0:nt, 2], in0=p4(6), in1=bc(0), op=ALU.mult),
                             R=[b_ps[6], b_tb[a]], W=[b_tok3[a]])
                        fin_X, b_fin = XA[a], b_TT[a]
                        if not smp:
                            n_inc = 5 if nt == 128 else 3
                            cq_, b_cq_ = QA[a], b_QA[a]
                            nq_, b_nq_ = Qs, b_Qs
                            cX, b_cXq, b_cXt = XA[a], b_QTA[a], b_TT[a]
                            nX, b_nXq, b_nXt = Xs, b_QTs, b_Xst
                            f32r = lambda ap: ap.bitcast(F32R)
                            for lv in range(0, n_inc + 1):
                                need_q = lv < n_inc
                                need_qt = lv < n_inc - 1
                                if lv >= 1:
                                    for h in range(4):
                                        K.op("pe", lambda h=h: nc.tensor.matmul(
                                            ps67[0:nt, h * 256:(h + 1) * 256].rearrange("p (j c) -> p j c", j=2)[:, :, 0:nt],
                                            lhsT=f32r(cq_[0:nt, h, 0:nt]), rhs=f32r(cX[0:nt, h, :, 0:nt]), start=True, stop=True),
                                            R=[b_cq_, b_cXq, b_cXt], W=[b_ps[6], b_ps[7]], inc=(h == 3), c=0.13)
                                else:
                                    for h in range(4):
                                        K.op("pe", lambda h=h: nc.tensor.matmul(
                                            ps67[0:nt, h * 256:h * 256 + nt], lhsT=cq_[0:nt, h, 0:nt],
                                            rhs=cX[0:nt, h, 0, 0:nt], start=True, stop=True),
                                            R=[b_cq_, b_cXq], W=[b_ps[6], b_ps[7]], inc=(h == 3), c=0.22)
                                if need_q:
                                    for h in range(4):
                                        K.op("pe", lambda h=h: nc.tensor.matmul(
                                            psum[5][0:nt, h * 128:h * 128 + nt], lhsT=cX[0:nt, h, 0, 0:nt],
                                            rhs=cq_[0:nt, h, 0:nt], start=True, stop=True),
                                            R=[b_cq_, b_cXq], W=[b_ps[5]], inc=(h == 3), c=0.22)
                                yield
                                p67 = ps67[0:nt, :].rearrange("p (h j c) -> p h j c", h=4, j=2)
                                if need_q:
                                    K.op("act", lambda: nc.scalar.copy(out=nq_[0:nt, :, 0:nt].bitcast(F32R), in_=pt(5)),
                                         R=[b_ps[5]], W=[b_nq_])
                                if lv >= 1:
                                    K.op("dve", lambda: nc.vector.tensor_tensor(out=nX[0:nt, :, 1, 0:nt].bitcast(F32R), in0=cX[0:nt, :, 1, 0:nt],
                                                                                in1=p67[:, :, 1, 0:nt], op=ALU.add),
                                         R=[b_ps[6], b_ps[7], b_cXt], W=[b_nXt])
                                else:
                                    K.op("dve", lambda: nc.vector.tensor_copy(out=nX[0:nt, :, 1, 0:nt].bitcast(F32R), in_=cX[0:nt, :, 1, 0:nt]),
                                         R=[b_cXt], W=[b_nXt])
                                if need_qt:
                                    K.op("act", lambda: nc.scalar.copy(out=nX[0:nt, :, 0, 0:nt].bitcast(F32R), in_=p67[:, :, 0, 0:nt]),
                                         R=[b_ps[6], b_ps[7]], W=[b_nXq])
                                yield
                                cq_, b_cq_, nq_, b_nq_ = nq_, b_nq_, cq_, b_cq_
                                cX, b_cXq, b_cXt, nX, b_nXq, b_nXt = nX, b_nXq, b_nXt, cX, b_cXq, b_cXt
                            fin_X, b_fin = cX, b_cXt
                        yield
                        K.op("act", lambda: nc.scalar.copy(out=TTb[0:nt, :, 0:nt], in_=fin_X[0:nt, :, 1, 0:nt]), R=[b_fin], W=[b_TTb])
                        yield
                        yield
                        for h in range(4):
                            K.op("pe", lambda h=h: nc.tensor.matmul(psum[7][0:nt, h * 128:(h + 1) * 128],
                                                                    lhsT=TTb[0:nt, h, 0:nt], rhs=tok3[a][0:nt, 2, h, :],
                                                                    start=True, stop=True),
                                 R=[b_TTb, b_tok3[a]], W=[b_ps[7]], inc=(h == 3))
                        yield
                        for h in range(4):
                            K.op("pe", lambda h=h: nc.tensor.matmul(psum[5][:, h * 128:h * 128 + nt],
                                                                    lhsT=tok3[a][0:nt, 0, h, :], rhs=TTb[0:nt, h, 0:nt],
                                                                    start=True, stop=True),
                                 R=[b_TTb, b_tok3[a]], W=[b_ps[5]], inc=(h == 3))
                        yield
                        if smp:
                            K.op("act", lambda: nc.scalar.copy(out=s_u0[:], in_=p4(7)), R=[b_ps[7]], W=[b_skeep])
                            K.op("dve", lambda: nc.vector.tensor_copy(out=s_wT[:], in_=pv(5)), R=[b_ps[5]], W=[b_skeep])
                            K.op("dve", lambda: nc.vector.tensor_copy(out=s_kd[:], in_=tok3[a][0:16, 1]), R=[b_tok3[a]], W=[b_skeep])
                            K.op("dve", lambda: nc.vector.tensor_copy(out=s_qn[:], in_=qkn[:, 0:4, 0:16]), R=[b_qkn], W=[b_skeep])
                            K.op("dve", lambda: nc.vector.tensor_copy(out=s_gz[:], in_=gz[:, :, 0:16]), R=[b_gz], W=[b_skeep])
                            return
                        yield
                        K.op("act", lambda: nc.scalar.copy(out=u0[a][0:nt], in_=p4(7)), R=[b_ps[7]], W=[b_u0[a]])
                        yield
                        K.op("dve", lambda: nc.vector.tensor_copy(out=wT[a][:, :, 0:nt], in_=pv(5)), R=[b_ps[5]], W=[b_wT[a]])

                    def stageB2(kind, sc0, nt, a, c0):
                        lc = sc0 - c0
                        smp = (kind == "sample")
                        m_incl = C["ident"] if smp else C["mu_incl"]
                        m_bd = C["ident"] if smp else C["bd"]
                        m_ustrict = C["zero"] if smp else C["mu_strict"]
                        m_lstrict = C["zero"] if smp else C["ml_strict"]
                        T_ = lambda i: tb[a][0:nt, i, :]
                        pv = lambda b: psum[b][:, :].rearrange("p (a c) -> p a c", a=4)[:, :, 0:nt]
                        pt = lambda b: psum[b][0:nt, :].rearrange("p (a c) -> p a c", a=4)[:, :, 0:nt]
                        p4 = lambda b: psum[b][0:nt, :].rearrange("p (a c) -> p a c", a=4)
                        bc = lambda i: tb[a][0:nt, i, :].unsqueeze(2).broadcast_to([nt, 4, 128])
                        if smp:
                            return
                        nch = 2 if nt == 128 else 1
                        cl = 64 if nt == 128 else nt
                        yield
                        for ch in range(nch):
                            tr = slice(ch * 64, ch * 64 + cl)
                            yield
                            for h in range(4):
                                K.op("pe", lambda h=h, tr=tr: nc.tensor.matmul(psum[0][tr, h * 128:(h + 1) * 128],
                                                                              lhsT=wT[a][:, h, tr], rhs=Sb[:, h, :],
                                                                              start=True, stop=True),
                                     R=[b_wT[a], b_Sb], W=[b_ps[0]], inc=(h == 3))
                            yield
                            K.op("dve", lambda tr=tr: nc.vector.tensor_tensor(
                                out=uu[tr], in0=u0[a][tr], in1=psum[0][tr, :].rearrange("p (a c) -> p a c", a=4),
                                op=ALU.subtract), R=[b_u0[a], b_ps[0]], W=[b_u])
                            yield
                            for h in range(4):
                                K.op("pe", lambda h=h, tr=tr: nc.tensor.matmul(psum[1][:, h * 64:h * 64 + cl],
                                                                              lhsT=Sb[:, h, :], rhs=bkq[a][:, 1, h, tr],
                                                                              start=True, stop=False),
                                     R=[b_Sb, b_bkq[a]], W=[b_ps[1]], inc=False)
                                K.op("pe", lambda h=h, tr=tr: nc.tensor.matmul(psum[1][:, h * 64:h * 64 + cl],
                                                                              lhsT=uu[0:nt, h, :], rhs=qkT[a][0:nt, h, tr],
                                                                              start=False, stop=True),
                                     R=[b_u, b_qkT[a]], W=[b_ps[1]], inc=(h == 3))
                            yield
                            for h in range(4):
                                K.op("pe", lambda h=h, tr=tr: nc.tensor.matmul(psum[0][:, h * 128:(h + 1) * 128],
                                                                              lhsT=tok3[a][tr, 1, h, :], rhs=uu[tr, h, :],
                                                                              start=True, stop=True),
                                     R=[b_tok3[a], b_u], W=[b_ps[0]], inc=(h == 3))
                            yield
                            K.op("act", lambda ch=ch: nc.scalar.copy(
                                out=oA[:, :, lc + ch * 64:lc + ch * 64 + cl],
                                in_=psum[1][:, 0:256].rearrange("p (a b) -> p a b", a=4)[:, :, 0:cl]),
                                R=[b_ps[1]], W=[b_oA])
                            yield
                            for h in range(4):
                                K.op("dve", lambda h=h, ch=ch: nc.vector.scalar_tensor_tensor(
                                    out=S[:, h, :], in0=S[:, h, :], scalar=Gbc[a][:, ch, h:h + 1],
                                    in1=psum[0][:, h * 128:(h + 1) * 128], op0=ALU.mult, op1=ALU.add),
                                    R=[b_Gbc[a], b_ps[0]], W=[b_S])
                            yield
                            K.op("act", lambda: nc.scalar.copy(out=Sb[:], in_=S[:]), R=[b_S], W=[b_Sb])
                        scan_done[0] += 1
                        yield

                    blk_it = 0
                    for (c0, n) in TILES:
                        is_ms = (c0 == 0)
                        def blkfn(blk, c0=c0, n=n, is_ms=is_ms):
                            q = blk % 2
                            mm_group(psum[q][:, 0:n], [(wA[:, k, blk * 128:(blk + 1) * 128], x_bf[:, k, c0:c0 + n])
                                                      for k in range(KC)], R=[b_wAb[blk], b_xbf], W=[b_ps[q]])
                            if blk >= 12:
                                K.op("act", lambda blk=blk, q=q: nc.scalar.activation(
                                    out=gz[:, blk - 12, 0:n], in_=psum[q][:, 0:n], func=AF.Silu), R=[b_ps[q]], W=[b_gz])
                                return
                            pe_ = blk % 2
                            ncv = 16 if is_ms else n
                            K.op("pool", lambda pe_=pe_, blk=blk: nc.gpsimd.tensor_copy(out=Pe[pe_][:, 0:3], in_=halo[:, blk, 0:3]),
                                 R=[b_halo], W=[b_Pe[pe_]])
                            src0 = 16 if is_ms else 0
                            K.op("act", lambda pe_=pe_, q=q, src0=src0, ncv=ncv: nc.scalar.copy(
                                out=Pe[pe_][:, 3:3 + ncv], in_=psum[q][:, src0:src0 + ncv]), R=[b_ps[q]], W=[b_Pe[pe_]])
                            K.op("pool", lambda pe_=pe_, blk=blk, ncv=ncv: nc.gpsimd.tensor_copy(
                                out=halo[:, blk, 0:3], in_=Pe[pe_][:, ncv:ncv + 3]), R=[b_Pe[pe_]], W=[b_halo])
                            a_ = acc[pe_]
                            o0 = 16 if is_ms else 0
                            K.op("act", lambda pe_=pe_, blk=blk, ncv=ncv, o0=o0: nc.scalar.activation(
                                out=acc[pe_][:, o0:o0 + ncv], in_=Pe[pe_][:, 3:3 + ncv], func=AF.Identity, scale=cw[:, blk, 3:4]),
                                R=[b_Pe[pe_], b_cw], W=[b_acc[pe_]])
                            for i in (2, 1, 0):
                                K.op("dve", lambda pe_=pe_, blk=blk, ncv=ncv, o0=o0, i=i: nc.vector.scalar_tensor_tensor(
                                    out=acc[pe_][:, o0:o0 + ncv], in0=Pe[pe_][:, i:i + ncv], scalar=cw[:, blk, i:i + 1],
                                    in1=acc[pe_][:, o0:o0 + ncv], op0=ALU.mult, op1=ALU.add),
                                    R=[b_Pe[pe_], b_cw, b_acc[pe_]], W=[b_acc[pe_]])
                            if is_ms:
                                cbv = cbT[:, blk, :].rearrange("p (s i) -> p s i", i=3)
                                K.op("dve", lambda pe_=pe_, blk=blk, q=q: nc.vector.tensor_scalar(
                                    out=acc[pe_][:, 0:16], in0=psum[q][:, 0:16], scalar1=cw[:, blk, 3:4], scalar2=None,
                                    op0=ALU.mult), R=[b_ps[q], b_cw], W=[b_acc[pe_]])
                                for i in range(3):
                                    K.op("dve", lambda pe_=pe_, blk=blk, i=i, cbv=cbv: nc.vector.scalar_tensor_tensor(
                                        out=acc[pe_][:, 0:16], in0=cbv[:, :, i], scalar=cw[:, blk, i:i + 1],
                                        in1=acc[pe_][:, 0:16], op0=ALU.mult, op1=ALU.add),
                                        R=[b_cbT, b_cw, b_acc[pe_]], W=[b_acc[pe_]])
                            if blk < 8:
                                ci = blk % 2
                                rr = rsd if ci == 0 else rsd_b
                                b_rr = b_rsd if ci == 0 else b_rsdb
                                K.op("act", lambda pe_=pe_, ci=ci: nc.scalar.activation(
                                    out=cq[ci][:, 0:n], in_=acc[pe_][:, 0:n], func=AF.Silu), R=[b_acc[pe_]], W=[b_cq[ci]])
                                K.op("act", lambda ci=ci: nc.scalar.activation(out=sq[:, ci, 0:n], in_=cq[ci][:, 0:n],
                                                                              func=AF.Square), R=[b_cq[ci]], W=[b_sq])
                                mm_group(psum[q][:, 0:n], [(CB["ones"], sq[:, ci, 0:n])], R=[b_sq, b_cst], W=[b_ps[q]])
                                K.op("act", lambda q=q, rr=rr: nc.scalar.activation(out=rr[:, 0:n], in_=psum[q][:, 0:n], func=AF.Ln,
                                                                                 bias=eps6[:, 0:1]), R=[b_ps[q], b_eps], W=[b_rr])
                                K.op("act", lambda rr=rr: nc.scalar.activation(out=rr[:, 0:n], in_=rr[:, 0:n], func=AF.Exp,
                                                                             scale=-0.5), R=[b_rr], W=[b_rr])
                                scl = (128.0 ** -0.5) if blk < 4 else 1.0
                                K.op("dve", lambda blk=blk, scl=scl, ci=ci, rr=rr: nc.vector.scalar_tensor_tensor(
                                    out=qkn[:, blk, 0:n], in0=cq[ci][:, 0:n], scalar=scl, in1=rr[:, 0:n],
                                    op0=ALU.mult, op1=ALU.mult), R=[b_cq[ci], b_rr], W=[b_qkn])
                            else:
                                K.op("act", lambda pe_=pe_, blk=blk: nc.scalar.activation(
                                    out=vn[:, blk - 8, 0:n], in_=acc[pe_][:, 0:n], func=AF.Silu), R=[b_acc[pe_]], W=[b_vn])
                        K.run_sched([(lambda blk=blk: blkfn(blk), ([blk - 2] if blk >= 2 else [])) for blk in range(16)])
                        if is_ms:
                            for g3 in range(3):
                                q = g3 % 2
                                mm_group(psum[q][0:16, :], [(x_bf[:, k, 0:16], wA[:, k, g3 * 512:(g3 + 1) * 512])
                                                            for k in range(KC)], R=[b_wAb[blk], b_xbf], W=[b_ps[q]])
                                K.op("act", lambda g3=g3, q=q: nc.scalar.copy(out=pnew[:, g3 * 512:(g3 + 1) * 512],
                                                                             in_=psum[q][0:16, :]), R=[b_ps[q]], W=[b_pnew])
                            K.dma("sp", o_conv_s[l][:, 2, :], pnew[:], R=[b_pnew])
                        subs = subtiles(c0, n)
                        ths = []
                        for i_, (kind, sc0, nt) in enumerate(subs):
                            a_ = (sub_ctr[0] + i_) % 2
                            iA, iB1, iB2 = 3 * i_, 3 * i_ + 1, 3 * i_ + 2
                            afterA = ([iA - 3] if i_ >= 1 else []) + ([iB2 - 6] if i_ >= 2 else [])
                            afterB1 = [iA] + ([iB1 - 3] if i_ >= 1 else []) + ([iB2 - 6] if i_ >= 2 else [])
                            afterB2 = [iB1] + ([iB2 - 3] if i_ >= 1 else [])
                            ths.append((lambda kind=kind, sc0=sc0, nt=nt, a_=a_, c0=c0: stageA(kind, sc0, nt, a_, c0), afterA))
                            ths.append((lambda kind=kind, sc0=sc0, nt=nt, a_=a_, c0=c0: stageB1(kind, sc0, nt, a_, c0), afterB1))
                            ths.append((lambda kind=kind, sc0=sc0, nt=nt, a_=a_, c0=c0: stageB2(kind, sc0, nt, a_, c0), afterB2))
                        K.run_sched(ths)
                        sub_ctr[0] += len(subs)
                        if is_ms:
                            K.op("dve", lambda: nc.vector.memset(oA[:, :, 0:16], 0.0), W=[b_oA])
                        gated_norm(st, oA, b_oA, gz, b_gz, nw, b_nw, og, b_og, 0, c0, n, nbufs)
                    K.dma("sp", o_gdn_p[l].rearrange("h k v -> k h v"), S[:], R=[b_S])
                    for g3 in range(3):
                        for b4 in range(4):
                            blk = 4 * g3 + b4
                            K.op("pe", lambda blk=blk, b4=b4, g3=g3: nc.tensor.transpose(
                                psum[g3][0:4, b4 * 128:(b4 + 1) * 128], halo[:, blk, :], C["ident"]),
                                R=[b_halo, b_cst], W=[b_ps[g3]], inc=(b4 == 3))
                        K.op("act", lambda g3=g3: nc.scalar.copy(out=pnew[0:4, g3 * 512:(g3 + 1) * 512], in_=psum[g3][0:4, :]),
                             R=[b_ps[g3]], W=[b_pnew])
                    K.dma("sp", o_conv_p[l], pnew[0:3, :], R=[b_pnew])
                K.barrier()
                with ExitStack() as s2:
                    wTm = sbt(s2, "d_wTm", [128, 4, NS, 16], BF16)
                    b_wTm = Buf()
                    K.op("dve", lambda: nc.vector.tensor_tensor(
                        out=wTm[:], in0=s_wT[:, :, :].unsqueeze(2).broadcast_to([128, 4, NS, 16]),
                        in1=cst[:, IDX["idrep0"]:IDX["idrep0"] + 2, :].rearrange("p a (s t) -> p (a s) t", t=16)
                            .unsqueeze(1).broadcast_to([128, 4, NS, 16]), op=ALU.mult),
                        R=[b_skeep, b_cst], W=[b_wTm])
                    Sg = [sbt(s2, "d_Sg%d" % i, [128, 4, 4, 128]) for i in range(2)]
                    Sgb = [sbt(s2, "d_Sgb%d" % i, [128, 4, 4, 128], BF16) for i in range(2)]
                    b_Sg = [Buf(), Buf()]
                    b_Sgb = [Buf(), Buf()]
                    us = sbt(s2, "d_us", [16, 4, 128], BF16)
                    ubd = sbt(s2, "d_ubd", [16, 4, 512], BF16)
                    b_us, b_ubd = Buf(), Buf()
                    oS = sbt(s2, "d_oS", [128, 4, 16])
                    b_oS = Buf()
                    sq2 = sbt(s2, "d_sq2", [128, 4, 512], BF16)
                    rsd2 = sbt(s2, "d_rsd2", [128, 512])
                    tn2 = sbt(s2, "d_tn2", [128, 512])
                    sgd_v = state_gdn[l].rearrange("s h k v -> k s h v")
                    sgd_o = o_gdn_s[l].rearrange("s h k v -> k s h v")
                    for sg_ in range(4):
                        u = sg_ % 2
                        K.dma("sp", Sg[u][:], sgd_v[:, 4 * sg_:4 * sg_ + 4], W=[b_Sg[u]])
                        K.op("act", lambda u=u: nc.scalar.copy(out=Sgb[u][:], in_=Sg[u][:]), R=[b_Sg[u]], W=[b_Sgb[u]])
                        for h in range(4):
                            for si in range(4):
                                s = 4 * sg_ + si
                                K.op("pe", lambda h=h, si=si, s=s, u=u: nc.tensor.matmul(
                                    psum[0][0:16, h * 128:(h + 1) * 128], lhsT=wTm[:, h, s, :], rhs=Sgb[u][:, si, h, :],
                                    start=(si == 0), stop=(si == 3)), R=[b_wTm, b_Sgb[u]], W=[b_ps[0]],
                                    inc=(h == 3 and si == 3))
                        K.op("dve", lambda: nc.vector.tensor_tensor(
                            out=us[:], in0=s_u0[:], in1=psum[0][0:16, :].rearrange("p (a c) -> p a c", a=4), op=ALU.subtract),
                            R=[b_skeep, b_ps[0]], W=[b_us])
                        K.op("dve", lambda sg_=sg_: nc.vector.tensor_tensor(
                            out=ubd[:], in0=us[:, :, :].rearrange("p h v -> p (h v)").unsqueeze(1).broadcast_to([16, 4, 512]),
                            in1=C["ident"][0:16, 4 * sg_:4 * sg_ + 4].unsqueeze(2).broadcast_to([16, 4, 512]), op=ALU.mult),
                            R=[b_us, b_cst], W=[b_ubd])
                        for si in range(4):
                            for h in range(4):
                                K.op("pe", lambda h=h, si=si: nc.tensor.matmul(
                                    psum[1 + si][:, h * 128:(h + 1) * 128], lhsT=s_kd[0:16, h, :],
                                    rhs=ubd[0:16, si, h * 128:(h + 1) * 128], start=True, stop=True),
                                    R=[b_skeep, b_ubd], W=[b_ps[1 + si]], inc=(h == 3))
                        for si in range(4):
                            s = 4 * sg_ + si
                            for h in range(4):
                                K.op("dve", lambda h=h, si=si, s=s, u=u: nc.vector.scalar_tensor_tensor(
                                    out=Sg[u][:, si, h, :], in0=Sg[u][:, si, h, :], scalar=s_G[:, s, h:h + 1],
                                    in1=psum[1 + si][:, h * 128:(h + 1) * 128], op0=ALU.mult, op1=ALU.add),
                                    R=[b_skeep, b_ps[1 + si]], W=[b_Sg[u]])
                        K.op("act", lambda u=u: nc.scalar.copy(out=Sgb[u][:], in_=Sg[u][:]), R=[b_Sg[u]], W=[b_Sgb[u]])
                        for si in range(4):
                            s = 4 * sg_ + si
                            for h in range(4):
                                K.op("pe", lambda h=h, si=si, s=s, u=u: nc.tensor.matmul(
                                    psum[5][:, h * 16 + s:h * 16 + s + 1], lhsT=Sgb[u][:, si, h, :], rhs=s_qn[:, h, s:s + 1],
                                    start=True, stop=True), R=[b_Sgb[u], b_skeep], W=[b_ps[5]],
                                    inc=(h == 3 and si == 3))
                        K.dma("sp", sgd_o[:, 4 * sg_:4 * sg_ + 4], Sg[u][:], R=[b_Sg[u]])
                    K.op("act", lambda: nc.scalar.copy(out=oS[:], in_=psum[5][:, 0:64].rearrange("p (a b) -> p a b", a=4)),
                         R=[b_ps[5]], W=[b_oS])
                    gated_norm(s2, oS, b_oS, s_gz, b_skeep, nw, b_nw, og, b_og, 0, 0, 16,
                               (sq2, Buf(), rsd2, Buf(), tn2, Buf()))

        for l in range(layers):
            last_layer = (l == layers - 1)
            win_v = w_in[l].rearrange("(k p) n -> p k n", p=128)

            with ExitStack() as pm:
              og = big[:, 0:8 * T].rearrange("p (k t) -> p k t", k=8)
              b_og = Buf("og")
              if stub_mixer:
                  K.op("dve", lambda: nc.vector.tensor_copy(out=og, in_=x_bf[:]), R=[b_xbf], W=[b_og])
              else:
                  if mix_sel in ("all", "gdn"):
                      gdn_phase(l, og, b_og, win_v)
                      K.barrier()
                  if mix_sel in ("all", "gla"):
                      gla_phase(l, og, b_og, win_v)
              K.barrier()
              chk("mixer")
              v = sbt(pm, "v", [128, KC, T])
              b_v = Buf("v")
              with ExitStack() as pmg:
                mg = sbt(pmg, "mg", [128, KC, T], BF16)
                b_mg = Buf("mg")
                with ExitStack() as pb1:
                    NW = 2
                    wab = [sbt(pb1, "wab%d" % i, [128, 8, 128], BF16) for i in range(NW)]
                    wgg = [sbt(pb1, "wgg%d" % i, [128, 16, 128], BF16) for i in range(NW)]
                    b_wab = [Buf() for _ in range(NW)]
                    b_wgg = [Buf() for _ in range(NW)]
                    sg = [sbt(pb1, "sg%d" % i, [128, 2, 512]) for i in range(2)]
                    b_sg = [Buf(), Buf()]
                    wa_v = w_branch_a[l].rearrange("(k p) n -> p k n", p=128)
                    wb_v = w_branch_b[l].rearrange("(k p) n -> p k n", p=128)
                    it = 0
                    for jo in range(KC):
                        s = jo % NW
                        cs = slice(jo * 128, (jo + 1) * 128)
                        K.dma("pool", wab[s][:, 0:4, :], wa_v[:, :, cs], W=[b_wab[s]])
                        K.dma("pool", wab[s][:, 4:8, :], wb_v[:, :, cs], W=[b_wab[s]])
                        K.dma("pool", wgg[s][:, 0:8, :], win_v[:, :, 3608 + jo * 128:3608 + (jo + 1) * 128], W=[b_wgg[s]])
                        K.dma("pool", wgg[s][:, 8:16, :], win_v[:, :, 4632 + jo * 128:4632 + (jo + 1) * 128], W=[b_wgg[s]])
                        for (c0, n) in TILES:
                            q = 4 * (it % 2)
                            t2 = it % 2
                            it += 1
                            cols = slice(c0, c0 + n)
                            mm_group(psum[q + 0][:, 0:n], [(wab[s][:, k, :], og[:, k, cols]) for k in range(4)],
                                     R=[b_wab[s], b_og], W=[b_ps[q + 0]])
                            mm_group(psum[q + 1][:, 0:n], [(wab[s][:, 4 + k, :], og[:, 4 + k, cols]) for k in range(4)],
                                     R=[b_wab[s], b_og], W=[b_ps[q + 1]])
                            mm_group(psum[q + 2][:, 0:n], [(wgg[s][:, k, :], x_bf[:, k, cols]) for k in range(8)],
                                     R=[b_wgg[s], b_xbf], W=[b_ps[q + 2]])
                            mm_group(psum[q + 3][:, 0:n], [(wgg[s][:, 8 + k, :], x_bf[:, k, cols]) for k in range(8)],
                                     R=[b_wgg[s], b_xbf], W=[b_ps[q + 3]])
                            K.op("act", lambda q=q, t2=t2, n=n: nc.scalar.activation(
                                out=sg[t2][:, 0, 0:n], in_=psum[q + 2][:, 0:n], func=AF.Sigmoid),
                                R=[b_ps[q + 2]], W=[b_sg[t2]])
                            K.op("act", lambda q=q, t2=t2, n=n: nc.scalar.activation(
                                out=sg[t2][:, 1, 0:n], in_=psum[q + 3][:, 0:n], func=AF.Sigmoid),
                                R=[b_ps[q + 3]], W=[b_sg[t2]])
                            K.op("dve", lambda q=q, t2=t2, n=n: nc.vector.tensor_tensor(
                                out=sg[t2][:, 0, 0:n], in0=sg[t2][:, 0, 0:n], in1=psum[q + 0][:, 0:n], op=ALU.mult),
                                R=[b_sg[t2], b_ps[q + 0]], W=[b_sg[t2]])
                            K.op("dve", lambda q=q, t2=t2, n=n: nc.vector.tensor_tensor(
                                out=sg[t2][:, 1, 0:n], in0=sg[t2][:, 1, 0:n], in1=psum[q + 1][:, 0:n], op=ALU.mult),
                                R=[b_sg[t2], b_ps[q + 1]], W=[b_sg[t2]])
                            K.op("dve", lambda t2=t2, n=n, jo=jo, cols=cols: nc.vector.tensor_tensor(
                                out=mg[:, jo, cols], in0=sg[t2][:, 0, 0:n], in1=sg[t2][:, 1, 0:n], op=ALU.add),
                                R=[b_sg[t2]], W=[b_mg])
                K.barrier()
                chk("b1")
                K.dma("sp", v[:], xres, R=[b_xres], W=[b_v])
                with ExitStack() as pb2:
                    wo = [sbt(pb2, "wo%d" % i, [128, 8, 128], BF16) for i in range(2)]
                    b_wo = [Buf(), Buf()]
                    wo_v = w_out[l].rearrange("(k p) n -> p k n", p=128)
                    it = 0
                    for jo in range(KC):
                        s = jo % 2
                        K.dma("pool", wo[s][:], wo_v[:, :, jo * 128:(jo + 1) * 128], W=[b_wo[s]])
                        for (c0, n) in TILES:
                            q = it % 8
                            it += 1
                            cols = slice(c0, c0 + n)
                            mm_group(psum[q][:, 0:n], [(wo[s][:, k, :], mg[:, k, cols]) for k in range(8)],
                                     R=[b_wo[s], b_mg], W=[b_ps[q]])
                            K.op("dve", lambda q=q, n=n, jo=jo, cols=cols: nc.vector.scalar_tensor_tensor(
                                out=v[:, jo, cols], in0=v[:, jo, cols], scalar=ALPHA, in1=psum[q][:, 0:n],
                                op0=ALU.mult, op1=ALU.add), R=[b_ps[q]], W=[b_v])
                K.barrier()
                chk("b2")
              if True:
                layer_norm(pm, v, b_v, l, 0)
                K.barrier()
                chk("ln1")

                nh = (len(TILES) + 1) // 2
                halves = [TILES[:nh], TILES[nh:]]
                fin_v = w_ffn_in[l].rearrange("(k p) n -> p k n", p=128)
                fout_v = w_ffn_out[l].rearrange("(c p) n -> p c n", p=128)
                with ExitStack() as pf:
                    hid = big[:, 0:FC * HW].rearrange("p (c t) -> p c t", c=FC)
                    b_hid = Buf("hid")
                    wfi = [sbt(pf, "wfi%d" % i, [128, 2, KC, 256], BF16) for i in range(2)]
                    b_wfi = [Buf(), Buf()]
                    wfo = [sbt(pf, "wfo%d" % i, [128, FC, 128], BF16) for i in range(2)]
                    b_wfo = [Buf(), Buf()]
                    sil = [sbt(pf, "sil%d" % i, [128, 512]) for i in range(2)]
                    b_sil = [Buf(), Buf()]
                    it = 0
                    wi_it = 0
                    wo_it = 0
                    for half in halves:
                        if not half:
                            continue
                        h0 = half[0][0]
                        for g in range(FC // 2):
                            s = wi_it % 2
                            wi_it += 1
                            K.dma("pool", wfi[s][:, 0, :, :], fin_v[:, :, g * 256:(g + 1) * 256], W=[b_wfi[s]])
                            K.dma("pool", wfi[s][:, 1, :, :], fin_v[:, :, DFF + g * 256:DFF + (g + 1) * 256], W=[b_wfi[s]])
                            for jj in range(2):
                                j = 2 * g + jj
                                for (c0, n) in half:
                                    q = 2 * (it % 4)
                                    t2 = it % 2
                                    it += 1
                                    cols = slice(c0, c0 + n)
                                    hc = slice(c0 - h0, c0 - h0 + n)
                                    mm_group(psum[q][:, 0:n],
                                             [(wfi[s][:, 0, k, jj * 128:(jj + 1) * 128], x_bf[:, k, cols]) for k in range(8)],
                                             R=[b_wfi[s], b_xbf], W=[b_ps[q]])
                                    mm_group(psum[q + 1][:, 0:n],
                                             [(wfi[s][:, 1, k, jj * 128:(jj + 1) * 128], x_bf[:, k, cols]) for k in range(8)],
                                             R=[b_wfi[s], b_xbf], W=[b_ps[q + 1]])
                                    K.op("act", lambda q=q, t2=t2, n=n: nc.scalar.activation(
                                        out=sil[t2][:, 0:n], in_=psum[q][:, 0:n], func=AF.Silu),
                                        R=[b_ps[q]], W=[b_sil[t2]])
                                    K.op("dve", lambda q=q, t2=t2, n=n, j=j, hc=hc: nc.vector.tensor_tensor(
                                        out=hid[:, j, hc], in0=sil[t2][:, 0:n], in1=psum[q + 1][:, 0:n], op=ALU.mult),
                                        R=[b_sil[t2], b_ps[q + 1]], W=[b_hid])
                        for jo in range(KC):
                            s = wo_it % 2
                            wo_it += 1
                            K.dma("pool", wfo[s][:], fout_v[:, :, jo * 128:(jo + 1) * 128], W=[b_wfo[s]])
                            for (c0, n) in half:
                                q = it % 8
                                it += 1
                                cols = slice(c0, c0 + n)
                                hc = slice(c0 - h0, c0 - h0 + n)
                                mm_group(psum[q][:, 0:n], [(wfo[s][:, c, :], hid[:, c, hc]) for c in range(FC)],
                                         R=[b_wfo[s], b_hid], W=[b_ps[q]])
                                K.op("dve", lambda q=q, n=n, jo=jo, cols=cols: nc.vector.scalar_tensor_tensor(
                                    out=v[:, jo, cols], in0=v[:, jo, cols], scalar=ALPHA, in1=psum[q][:, 0:n],
                                    op0=ALU.mult, op1=ALU.add), R=[b_ps[q]], W=[b_v])
                K.barrier()
                chk("ffn")
                layer_norm(pm, v, b_v, l, 1, write_bf=not last_layer)
                K.barrier()
                chk("ln2")
                if not last_layer:
                    K.dma("sp", xres, v[:], R=[b_v], W=[b_xres])
                else:
                    with ExitStack() as po:
                        NR = 3
                        yst = [sbt(po, "yst%d" % i, [128, D]) for i in range(NR)]
                        b_yst = [Buf() for _ in range(NR)]
                        rows = [("ms", 0, 32)] + [("p", 128 * i, 128) for i in range(TP // 128)]

                        def out_tile(ri):
                            kind, r0, n = rows[ri]
                            s = ri % NR
                            c0 = 0 if kind == "ms" else 32 + r0
                            for g in range(2):
                                pb = (2 * ri + g) % 8
                                for kk in range(4):
                                    k = 4 * g + kk
                                    K.op("pe", lambda k=k, kk=kk, pb=pb: nc.tensor.transpose(
                                        psum[pb][0:n, kk * 128:(kk + 1) * 128], v[:, k, c0:c0 + n], C["ident"]),
                                        R=[b_v, b_cst], W=[b_ps[pb]], inc=(kk == 3), c=0.12)
                                if g == 0:
                                    K.op("act", lambda pb=pb: nc.scalar.copy(
                                        out=yst[s][0:n, 0:512], in_=psum[pb][0:n, :]), R=[b_ps[pb]], W=[b_yst[s]])
                                else:
                                    K.op("dve", lambda pb=pb: nc.vector.tensor_copy(
                                        out=yst[s][0:n, 512:1024], in_=psum[pb][0:n, :]), R=[b_ps[pb]], W=[b_yst[s]])
                            if kind == "ms":
                                K.dma("sp", y_sample, yst[s][0:16, :], R=[b_yst[s]])
                            else:
                                K.dma("sp", y_prompt[r0:r0 + 128, :], yst[s][0:128, :], R=[b_yst[s]])

                        K.run_sched([(lambda ri=ri: out_tile(ri), ([ri - NR] if ri >= NR else [])) for ri in range(len(rows))])
              K.barrier()
      except _Stop:
        pass
      K.finish()
    return nc


_NC_CACHE = {}


def kernel(x_prompt, x_sample, state_gdn, state_gla, state_conv, meta_tokens, w_in, conv_w, a_log, dt_bias,
           gdn_norm_w, gla_gate_w2, gla_gate_b, gla_norm_w, w_branch_a, w_branch_b, w_out,
           ln1_g, ln1_b, ln2_g, ln2_b, w_ffn_in, w_ffn_out, _build_kwargs=None):
    f = lambda a: np.ascontiguousarray(np.asarray(a), dtype=np.float32)
    x_prompt = f(x_prompt)
    TP = x_prompt.shape[1]
    bk = dict(_build_kwargs or {})
    key = (TP, tuple(sorted(bk.items())))
    if key not in _NC_CACHE:
        _NC_CACHE[key] = build(TP=TP, **bk)
    nc = _NC_CACHE[key]
    shared = dict(meta_tokens=f(meta_tokens), w_in=f(w_in), conv_w=f(conv_w), a_log=f(a_log), dt_bias=f(dt_bias),
                  gdn_norm_w=f(gdn_norm_w), gla_gate_w2=f(gla_gate_w2), gla_gate_b=f(gla_gate_b),
                  gla_norm_w=f(gla_norm_w), w_branch_a=f(w_branch_a), w_branch_b=f(w_branch_b), w_out=f(w_out),
                  ln1_g=f(ln1_g), ln1_b=f(ln1_b), ln2_g=f(ln2_g), ln2_b=f(ln2_b),
                  w_ffn_in=f(w_ffn_in), w_ffn_out=f(w_ffn_out), consts=CONST_ARR)
    x_sample = f(x_sample)
    state_gdn = f(state_gdn)
    state_gla = f(state_gla)
    state_conv = f(state_conv)
    in_maps = []
    for c in range(NCORES):
        sl = slice(NS * c, NS * (c + 1))
        m = dict(shared)
        m["x_prompt"] = x_prompt[c]
        m["x_sample"] = np.ascontiguousarray(x_sample[sl, 0, :])
        m["state_gdn"] = np.ascontiguousarray(state_gdn[:, sl])
        m["state_gla"] = np.ascontiguousarray(state_gla[:, sl])
        m["state_conv"] = np.ascontiguousarray(state_conv[:, sl])
        in_maps.append(m)
    res = run_bass_kernel_spmd(nc, in_maps, core_ids=list(range(NCORES)))
    R = res.results
    y_prompt = np.stack([R[c]["y_prompt"] for c in range(NCORES)], axis=0)
    y_sample = np.concatenate([R[c]["y_sample"] for c in range(NCORES)], axis=0)[:, None, :]
    gdn_p = np.stack([R[c]["new_gdn_prompt"] for c in range(NCORES)], axis=1)
    gla_p = np.stack([R[c]["new_gla_prompt"] for c in range(NCORES)], axis=1)
    conv_p = np.stack([R[c]["new_conv_prompt"] for c in range(NCORES)], axis=1)
    gdn_s = np.concatenate([R[c]["new_gdn_sample"] for c in range(NCORES)], axis=1)
    gla_s = np.concatenate([R[c]["new_gla_sample"] for c in range(NCORES)], axis=1)
    conv_s = np.concatenate([R[c]["new_conv_sample"] for c in range(NCORES)], axis=1)
    outs = (y_prompt, y_sample, gdn_p, gla_p, conv_p, gdn_s, gla_s, conv_s)
    return tuple(np.ascontiguousarray(o, dtype=np.float32) for o in outs)
```

```python
import threading
import numpy as np
from contextlib import ExitStack
import concourse.bass as bass
import concourse.mybir as mybir
from concourse.bass_utils import run_bass_kernel_spmd

F32 = mybir.dt.float32
BF16 = mybir.dt.bfloat16
F32R = mybir.dt.float32r
AF = mybir.ActivationFunctionType
ALU = mybir.AluOpType

D = 1024
KC = 8
DEPTH = 2
NS = 16
NMETA = 16
D_IN = 5656
DFF = 2816
FC = DFF // 128
ALPHA = (2.0 * DEPTH) ** 0.25
NCORES = 8


class Buf:
    __slots__ = ("w", "r", "name", "excl")

    def __init__(self, name="", excl=False):
        self.w = None
        self.r = {}
        self.name = name
        self.excl = excl


_tls = threading.local()


class _Worker:
    def __init__(self, fn):
        self.fn = fn
        self.req = None
        self.done = False
        self.exc = None
        self.ev_req = threading.Event()
        self.ev_go = threading.Event()
        self.th = threading.Thread(target=self._run, daemon=True)

    def _run(self):
        _tls.worker = self
        try:
            self.ev_go.wait()
            self.ev_go.clear()
            r = self.fn()
            if r is not None and hasattr(r, "__next__"):
                for _ in r:
                    pass
        except BaseException as e:
            self.exc = e
        finally:
            self.done = True
            self.req = None
            self.ev_req.set()

    def post(self, req):
        self.req = req
        self.ev_req.set()
        self.ev_go.wait()
        self.ev_go.clear()


class Sched:
    ENG = ("pe", "act", "dve", "pool", "sp")
    COST = {"pe": 0.25, "act": 0.6, "dve": 0.7, "pool": 0.5, "sp": 0.1}

    def __init__(self, nc, es, n_dma_slots=24):
        self.nc = nc
        self.e = {"pe": nc.tensor, "act": nc.scalar, "dve": nc.vector, "pool": nc.gpsimd, "sp": nc.sync}
        self.sem = {}
        self.cnt = {}
        for k in self.ENG:
            self.sem[k] = es.enter_context(nc.semaphore("s_" + k))
            self.cnt[k] = 0
        self.seen = {k: {} for k in self.ENG}
        self.pending = {k: False for k in self.ENG}
        self.nslots = n_dma_slots
        self.slot_sem = [es.enter_context(nc.semaphore("s_dma%d" % i)) for i in range(n_dma_slots)]
        self.slot_uses = [0] * n_dma_slots
        self.slot_next = 0
        self.semobj = dict(self.sem)
        for i in range(n_dma_slots):
            self.semobj[("dma", i)] = self.slot_sem[i]
        self.tfree = {k: 0.0 for k in self.ENG}
        self.tdone = {}

    def _est(self, eng, R, W):
        t = self.tfree[eng]
        def dep(key):
            d = self.tdone.get(key)
            if d is None:
                d = self.tfree.get(key[0], 0.0) if not isinstance(key[0], tuple) else 0.0
            return d + (0.05 if key[0] == eng else 0.3)
        for b in R:
            if b.w is not None:
                t = max(t, dep(b.w))
            if b.excl:
                for k, v in b.r.items():
                    t = max(t, dep((k, v)))
        for b in W:
            if b.w is not None:
                t = max(t, dep(b.w))
            for k, v in b.r.items():
                t = max(t, dep((k, v)))
        return t

    def run_sched(self, threads):
        n = len(threads)
        workers = [None] * n
        started, finished = set(), set()

        def advance(w):
            w.ev_req.clear()
            w.ev_go.set()
            w.ev_req.wait()
            if w.exc is not None:
                raise w.exc

        while len(finished) < n:
            for i, th_ in enumerate(threads):
                fn, after = th_[0], th_[1]
                if i not in started and all(j in finished for j in after):
                    started.add(i)
                    workers[i] = _Worker(fn)
                    workers[i].th.start()
                    advance(workers[i])
                    if workers[i].done:
                        finished.add(i)
            cands = [i for i in started if i not in finished]
            if not cands:
                if len(finished) < n and len(started) == len(finished):
                    rem = [i for i in range(n) if i not in started]
                    assert any(all(j in finished for j in threads[i][1]) for i in rem), "scheduler deadlock"
                continue
            best = min(cands, key=lambda i: (self._est(*workers[i].req) - (threads[i][2] if len(threads[i]) > 2 else 0.0), i))
            advance(workers[best])
            if workers[best].done:
                finished.add(best)

    def _collect(self, eng, R, W, extra=()):
        need = {}
        def add(k, v):
            if v > need.get(k, 0):
                need[k] = v
        for b in R:
            if b.w is not None:
                add(*b.w)
        for b in W:
            if b.w is not None:
                add(*b.w)
            for k, v in b.r.items():
                add(k, v)
        for k, v in extra:
            add(k, v)
        out = []
        for k, v in need.items():
            if k == "pe" and eng == "pe":
                continue
            if k == eng and v > self.cnt[eng]:
                continue
            if v > self.seen[eng].get(k, 0):
                out.append((k, v))
        return out

    def op(self, eng, fn, R=(), W=(), inc=True, extra=(), c=None):
        w_ = getattr(_tls, "worker", None)
        if w_ is not None:
            w_.post((eng, tuple(R), tuple(W)))
        t0_ = self._est(eng, R, W)
        t1_ = t0_ + (c if c is not None else self.COST[eng])
        self.tfree[eng] = t1_
        if any(b.excl for b in R):
            W = list(W) + [b for b in R if b.excl]
            R = [b for b in R if not b.excl]
        waits = self._collect(eng, R, W, extra)
        e = self.e[eng]
        if eng == "pe":
            for k, v in waits:
                e.wait_ge(self.semobj[k], v)
                self.seen[eng][k] = v
            waits = []
        for k, v in waits[:-1]:
            e.wait_ge(self.semobj[k], v)
            self.seen[eng][k] = v
        inst = fn()
        if waits:
            k, v = waits[-1]
            inst.wait_op(self.semobj[k], v, "sem-ge")
            self.seen[eng][k] = v
        if inc:
            inst.then_inc(self.sem[eng], 1)
            self.cnt[eng] += 1
            cc = self.cnt[eng]
            self.pending[eng] = False
            self.tdone[(eng, cc)] = t1_
        else:
            cc = self.cnt[eng] + 1
            self.pending[eng] = True
        for b in R:
            if b.r.get(eng, 0) < cc:
                b.r[eng] = cc
        for b in W:
            b.w = (eng, cc)
            b.r = {}
        return inst

    def dma(self, eng, out, in_, R=(), W=(), **kw):
        w_ = getattr(_tls, "worker", None)
        if w_ is not None:
            w_.post((eng, tuple(R), tuple(W)))
        t0_ = self._est(eng, R, W)
        self.tfree[eng] = t0_ + 0.1
        s = self.slot_next
        self.slot_next = (self.slot_next + 1) % self.nslots
        key = ("dma", s)
        prev = 16 * self.slot_uses[s]
        extra = [(key, prev)] if prev > 0 else []
        waits = self._collect(eng, R, W, extra)
        e = self.e[eng]
        for k, v in waits:
            e.wait_ge(self.semobj[k], v)
            self.seen[eng][k] = v
        inst = e.dma_start(out=out, in_=in_, **kw)
        self.slot_uses[s] += 1
        val = 16 * self.slot_uses[s]
        inst.then_inc(self.slot_sem[s], 16)
        self.tdone[(key, val)] = t0_ + 3.0
        for b in R:
            b.r[key] = val
        for b in W:
            b.w = (key, val)
            b.r = {}
        return inst

    def barrier(self):
        tgt = [(k, self.cnt[k]) for k in self.ENG if self.cnt[k] > 0]
        tgt += [(("dma", i), 16 * self.slot_uses[i]) for i in range(self.nslots) if self.slot_uses[i] > 0]
        for eng in self.ENG:
            assert not self.pending[eng]
            for k, v in tgt:
                if k == eng:
                    continue
                if v > self.seen[eng].get(k, 0):
                    self.e[eng].wait_ge(self.semobj[k], v)
                    self.seen[eng][k] = v

    def finish(self):
        self.barrier()


def make_consts():
    r = np.arange(128)
    same = (r[:, None] // 64) == (r[None, :] // 64)
    c = {}
    c["ident"] = np.eye(128, dtype=np.float32)
    c["ones"] = np.ones((128, 128), dtype=np.float32)
    c["mu_incl"] = (same & (r[None, :] >= r[:, None])).astype(np.float32)
    c["mu_strict"] = (same & (r[None, :] > r[:, None])).astype(np.float32)
    c["ml_strict"] = (same & (r[:, None] > r[None, :])).astype(np.float32)
    c["bd"] = same.astype(np.float32)
    c["blk0"] = np.repeat((r < 64).astype(np.float32)[:, None], 128, axis=1)
    c["blk1"] = np.repeat((r >= 64).astype(np.float32)[:, None], 128, axis=1)
    c["zero"] = np.zeros((128, 128), dtype=np.float32)
    idrep = np.tile(np.eye(16, dtype=np.float32).reshape(1, 256), (128, 1))
    c["idrep0"] = idrep[:, 0:128]
    c["idrep1"] = idrep[:, 128:256]
    names = ["ident", "ones", "mu_incl", "mu_strict", "ml_strict", "bd", "blk0", "blk1", "zero", "idrep0", "idrep1"]
    arr = np.stack([c[n] for n in names], axis=1)
    return names, np.ascontiguousarray(arr.astype(np.float32))


CONST_NAMES, CONST_ARR = make_consts()
NCONST = len(CONST_NAMES)
IDX = {n: i for i, n in enumerate(CONST_NAMES)}


class _Stop(Exception):
    pass


def build(TP=2048, stub_mixer=False, layers=DEPTH, dbg=False, stop=None, mix_sel="all", PRIO_A=0.0):
    nc = bass.Bass("TRN2", target_bir_lowering=False)
    NPOS = NMETA + TP
    T = 32 + TP
    NPT = max(TP // 512, 1)
    TILES = [(0, 32)] + [(32 + 512 * i, 512) for i in range(NPT)]

    def din(name, shape, dt=F32):
        return nc.dram_tensor(name, list(shape), dt, kind="ExternalInput").ap()

    def dout(name, shape, dt=F32):
        return nc.dram_tensor(name, list(shape), dt, kind="ExternalOutput").ap()

    x_prompt = din("x_prompt", [TP, D])
    x_sample = din("x_sample", [NS, D])
    meta = din("meta_tokens", [NMETA, D])
    state_gdn = din("state_gdn", [DEPTH, NS, 4, 128, 128])
    state_gla = din("state_gla", [DEPTH, NS, 4, 64, 128])
    state_conv = din("state_conv", [DEPTH, NS, 3, 1536])
    w_in = din("w_in", [DEPTH, D, D_IN])
    conv_w = din("conv_w", [DEPTH, 4, 1536])
    a_log = din("a_log", [DEPTH, 4])
    dt_bias = din("dt_bias", [DEPTH, 4])
    gdn_norm_w = din("gdn_norm_w", [DEPTH, 128])
    gla_gate_w2 = din("gla_gate_w2", [DEPTH, 16, 256])
    gla_gate_b = din("gla_gate_b", [DEPTH, 256])
    gla_norm_w = din("gla_norm_w", [DEPTH, 128])
    w_branch_a = din("w_branch_a", [DEPTH, 512, D])
    w_branch_b = din("w_branch_b", [DEPTH, 512, D])
    w_out = din("w_out", [DEPTH, D, D])
    ln1_g = din("ln1_g", [DEPTH, D])
    ln1_b = din("ln1_b", [DEPTH, D])
    ln2_g = din("ln2_g", [DEPTH, D])
    ln2_b = din("ln2_b", [DEPTH, D])
    w_ffn_in = din("w_ffn_in", [DEPTH, D, 2 * DFF])
    w_ffn_out = din("w_ffn_out", [DEPTH, DFF, D])
    consts_d = din("consts", [128, NCONST, 128])

    y_prompt = dout("y_prompt", [TP, D])
    y_sample = dout("y_sample", [NS, D])
    o_gdn_p = dout("new_gdn_prompt", [DEPTH, 4, 128, 128])
    o_gla_p = dout("new_gla_prompt", [DEPTH, 4, 64, 128])
    o_conv_p = dout("new_conv_prompt", [DEPTH, 3, 1536])
    o_gdn_s = dout("new_gdn_sample", [DEPTH, NS, 4, 128, 128])
    o_gla_s = dout("new_gla_sample", [DEPTH, NS, 4, 64, 128])
    o_conv_s = dout("new_conv_sample", [DEPTH, NS, 3, 1536])
    dbg_out = dout("dbg", [128, KC, T]) if dbg else None

    xres = nc.dram_tensor("xres_scratch", [128, KC, T], F32, kind="Internal").ap()

    es = ExitStack()
    with es:
      K = Sched(nc, es)

      def chk(name):
          if stop == name:
              raise _Stop()

      try:

        _uid = [0]

        def sbt(stack, name, shape, dt=F32):
            _uid[0] += 1
            return stack.enter_context(nc.sbuf_tensor("%s_%d" % (name, _uid[0]), list(shape), dt))

        cst = sbt(es, "cst", [128, NCONST, 128], F32)
        cstb = sbt(es, "cstb", [128, NCONST, 128], BF16)
        b_cst = Buf("cst")
        C = {n: cst[:, i, :] for i, n in enumerate(CONST_NAMES)}
        CB = {n: cstb[:, i, :] for i, n in enumerate(CONST_NAMES)}
        x_bf = sbt(es, "x_bf", [128, KC, T], BF16)
        b_xbf = Buf("x_bf")
        lnp = sbt(es, "lnp", [128, DEPTH, 4, KC], F32)
        b_lnp = Buf("lnp")
        psp = [es.enter_context(nc.psum_tensor("psp%d" % i, [128, 1024], F32)) for i in range(4)]
        psum = [psp[i // 2][:, (i % 2) * 512:(i % 2) * 512 + 512] for i in range(8)]
        ps67 = psp[3][:, :]
        b_ps = [Buf("ps%d" % i, excl=True) for i in range(8)]
        b_xres = Buf("xres")

        epst = sbt(es, "epst", [128, 2], F32)
        b_eps = Buf("eps")
        K.op("dve", lambda: nc.vector.memset(epst[:, 0:1], 1e-6), W=[b_eps])
        K.op("dve", lambda: nc.vector.memset(epst[:, 1:2], 1.0), W=[b_eps])
        eps6 = epst[:, 0:1]
        one1 = epst[:, 1:2]
        nh_ = (len(TILES) + 1) // 2
        HW = max(sum(n for _, n in TILES[:nh_]), sum(n for _, n in TILES[nh_:]))
        BIGN = max(FC * HW, 8 * T)
        big = sbt(es, "big", [128, BIGN], BF16)
        K.dma("sp", cst[:], consts_d, W=[b_cst])
        K.op("act", lambda: nc.scalar.copy(out=cstb[:], in_=cst[:]), R=[b_cst], W=[b_cst])
        ones_r = sbt(es, "ones_r", [128, 128], F32)
        b_onesr = Buf("ones_r")
        K.op("act", lambda: nc.scalar.copy(out=ones_r[:].bitcast(F32R), in_=C["ones"]), R=[b_cst], W=[b_onesr])
        for l in range(DEPTH):
            for wi, src in enumerate((ln1_g, ln1_b, ln2_g, ln2_b)):
                K.dma("sp", lnp[:, l, wi, :], src[l].rearrange("(k p) -> p k", p=128), W=[b_lnp],
                      allow_slow_non_contiguous=True)

        chk("init")

        def run_threads(gens):
            active = list(gens)
            while active:
                for g in list(active):
                    try:
                        next(g)
                    except StopIteration:
                        active.remove(g)

        def run_pipeline(items, mkA, mkB, nA=2, nbuf=3):
            n_it = len(items)
            nextA, nextB = 0, 0
            activeA = {}
            doneA = set()
            curB = None
            while nextB < n_it:
                while nextA < n_it and len(activeA) < nA and (nextA - nextB) < nbuf:
                    activeA[nextA] = mkA(items[nextA], nextA)
                    nextA += 1
                if curB is None and nextB in doneA:
                    curB = mkB(items[nextB], nextB)
                for i, gen in list(activeA.items()):
                    try:
                        next(gen)
                    except StopIteration:
                        del activeA[i]
                        doneA.add(i)
                if curB is not None:
                    try:
                        next(curB)
                    except StopIteration:
                        curB = None
                        nextB += 1

        def mm_group(ps_ap, pairs, R, W, inc=True):
            n = len(pairs)
            for i, (lt, rh) in enumerate(pairs):
                last = (i == n - 1)
                K.op("pe", lambda lt=lt, rh=rh, i=i, last=last: nc.tensor.matmul(
                    ps_ap, lhsT=lt, rhs=rh, start=(i == 0), stop=last),
                    R=R, W=W, inc=(inc and last))

        with ExitStack() as p0:
            NR = 3
            xin = [sbt(p0, "xin%d" % i, [128, D]) for i in range(NR)]
            b_xin = [Buf() for _ in range(NR)]
            xst = [sbt(p0, "xst%d" % i, [128, KC, 128]) for i in range(NR)]
            b_xst = [Buf() for _ in range(NR)]
            rows = [("ms", 0, 32)] + [("p", 128 * i, 128) for i in range(TP // 128)]

            def p0_tile(ri):
                kind, r0, n = rows[ri]
                s = ri % NR
                if kind == "ms":
                    K.dma("sp", xin[s][0:16, :], x_sample, W=[b_xin[s]])
                    K.dma("sp", xin[s][16:32, :], meta, W=[b_xin[s]])
                    c0 = 0
                else:
                    K.dma("sp", xin[s][0:128, :], x_prompt[r0:r0 + 128, :], W=[b_xin[s]])
                    c0 = 32 + r0
                for g in range(2):
                    pb = (2 * ri + g) % 8
                    for kk in range(4):
                        k = 4 * g + kk
                        K.op("pe", lambda k=k, kk=kk, pb=pb: nc.tensor.transpose(
                            psum[pb][:, kk * 128:kk * 128 + n], xin[s][0:n, k * 128:(k + 1) * 128],
                            C["ident"][0:n, 0:n]), R=[b_xin[s], b_cst], W=[b_ps[pb]], inc=(kk == 3), c=0.12)
                    src = psum[pb][:, :].rearrange("p (a b) -> p a b", a=4)[:, :, 0:n]
                    K.op("act", lambda src=src, g=g: nc.scalar.copy(
                        out=x_bf[:, 4 * g:4 * g + 4, c0:c0 + n], in_=src), R=[b_ps[pb]], W=[b_xbf])
                    K.op("dve", lambda src=src, g=g: nc.vector.tensor_copy(
                        out=xst[s][:, 4 * g:4 * g + 4, 0:n], in_=src), R=[b_ps[pb]], W=[b_xst[s]])
                K.dma("sp", xres[:, :, c0:c0 + n], xst[s][:, :, 0:n], R=[b_xst[s]])

            K.run_sched([(lambda ri=ri: p0_tile(ri), ([ri - NR] if ri >= NR else [])) for ri in range(len(rows))])
        K.barrier()
        chk("p0")

        def layer_norm(stack, v, b_v, l, which, write_bf=True):
            g_i, b_i = 2 * which, 2 * which + 1
            with ExitStack() as st:
                vb_ = [sbt(st, "ln_vb%d" % i, [128, KC, 512], BF16) for i in range(2)]
                sq_ = [sbt(st, "ln_sq%d" % i, [128, KC, 512], BF16) for i in range(2)]
                mt_ = [sbt(st, "ln_m%d" % i, [128, 512]) for i in range(2)]
                m2_ = [sbt(st, "ln_m2%d" % i, [128, 512]) for i in range(2)]
                rs_ = [sbt(st, "ln_rs%d" % i, [128, 512]) for i in range(2)]
                nm_ = [sbt(st, "ln_nm%d" % i, [128, 512]) for i in range(2)]
                bb = [[Buf() for _ in range(6)] for _ in range(2)]
                b_vts = []
                for _ in TILES:
                    t_ = Buf()
                    t_.w = b_v.w
                    t_.r = dict(b_v.r)
                    b_vts.append(t_)

                def ln_tile(ti):
                    c0, n = TILES[ti]
                    u_ = ti % 2
                    vb, sq, mt, m2, rs, nm = vb_[u_], sq_[u_], mt_[u_], m2_[u_], rs_[u_], nm_[u_]
                    b_vb, b_sq, b_mt, b_m2, b_rs, b_nm = bb[u_]
                    b_v = b_vts[ti]
                    cols = slice(c0, c0 + n)
                    pm_, pq_ = (2 * ti) % 8, (2 * ti + 1) % 8
                    K.op("act", lambda cols=cols, n=n: nc.scalar.copy(out=vb[:, :, 0:n], in_=v[:, :, cols]),
                         R=[b_v], W=[b_vb])
                    K.op("act", lambda cols=cols, n=n: nc.scalar.activation(
                        out=sq[:, :, 0:n], in_=v[:, :, cols], func=AF.Square), R=[b_v], W=[b_sq])
                    mm_group(psum[pm_][:, 0:n], [(CB["ones"], vb[:, k, 0:n]) for k in range(KC)],
                             R=[b_vb, b_cst], W=[b_ps[pm_]])
                    mm_group(psum[pq_][:, 0:n], [(CB["ones"], sq[:, k, 0:n]) for k in range(KC)],
                             R=[b_sq, b_cst], W=[b_ps[pq_]])
                    K.op("dve", lambda n=n, pm_=pm_: nc.vector.tensor_scalar(
                        out=mt[:, 0:n], in0=psum[pm_][:, 0:n], scalar1=1.0 / D, scalar2=None, op0=ALU.mult),
                        R=[b_ps[pm_]], W=[b_mt])
                    K.op("dve", lambda n=n: nc.vector.tensor_tensor(
                        out=m2[:, 0:n], in0=mt[:, 0:n], in1=mt[:, 0:n], op=ALU.mult), R=[b_mt], W=[b_m2])
                    K.op("dve", lambda n=n, pq_=pq_: nc.vector.scalar_tensor_tensor(
                        out=m2[:, 0:n], in0=psum[pq_][:, 0:n], scalar=1.0 / D, in1=m2[:, 0:n],
                        op0=ALU.mult, op1=ALU.subtract), R=[b_ps[pq_], b_m2], W=[b_m2])
                    K.op("dve", lambda n=n: nc.vector.tensor_scalar(
                        out=m2[:, 0:n], in0=m2[:, 0:n], scalar1=1e-5, scalar2=None, op0=ALU.add),
                        R=[b_m2], W=[b_m2])
                    K.op("act", lambda n=n: nc.scalar.activation(
                        out=rs[:, 0:n], in_=m2[:, 0:n], func=AF.Ln), R=[b_m2], W=[b_rs])
                    K.op("act", lambda n=n: nc.scalar.activation(
                        out=rs[:, 0:n], in_=rs[:, 0:n], func=AF.Exp, scale=-0.5), R=[b_rs], W=[b_rs])
                    K.op("dve", lambda n=n: nc.vector.scalar_tensor_tensor(
                        out=nm[:, 0:n], in0=mt[:, 0:n], scalar=-1.0, in1=rs[:, 0:n],
                        op0=ALU.mult, op1=ALU.mult), R=[b_mt, b_rs], W=[b_nm])
                    K.op("dve", lambda n=n, cols=cols: nc.vector.tensor_tensor(
                        out=v[:, :, cols], in0=v[:, :, cols],
                        in1=rs[:, 0:n].unsqueeze(1).broadcast_to([128, KC, n]), op=ALU.mult),
                        R=[b_rs], W=[b_v])
                    K.op("dve", lambda n=n, cols=cols: nc.vector.tensor_tensor(
                        out=v[:, :, cols], in0=v[:, :, cols],
                        in1=nm[:, 0:n].unsqueeze(1).broadcast_to([128, KC, n]), op=ALU.add),
                        R=[b_nm], W=[b_v])
                    for k in range(KC):
                        K.op("act", lambda k=k, n=n, cols=cols: nc.scalar.activation(
                            out=v[:, k, cols], in_=v[:, k, cols], func=AF.Identity,
                            scale=lnp[:, l, g_i, k:k + 1], bias=lnp[:, l, b_i, k:k + 1]),
                            R=[b_lnp], W=[b_v])
                    if write_bf:
                        K.op("pool", lambda cols=cols: nc.gpsimd.tensor_copy(out=x_bf[:, :, cols], in_=v[:, :, cols]),
                             R=[b_v], W=[b_xbf])

                K.run_sched([(lambda ti=ti: ln_tile(ti), ([ti - 2] if ti >= 2 else [])) for ti in range(len(TILES))])

        def subtiles(c0, n):
            if c0 == 0:
                return [("sample", 0, 16), ("meta", 16, 16)]
            return [("prompt", c0 + 128 * i, 128) for i in range(n // 128)]

        def gated_norm(st, oB, b_oB, gz, b_gz, nw, b_nw, og, b_og, hbase, c0, n, tagbufs):
            sq, b_sq, rsd, b_rsd, tn, b_tn = tagbufs
            K.op("act", lambda: nc.scalar.activation(out=sq[:, :, 0:n], in_=oB[:, :, 0:n], func=AF.Square),
                 R=[b_oB], W=[b_sq])
            for h in range(4):
                q = h % 2
                mm_group(psum[q][:, 0:n], [(CB["ones"], sq[:, h, 0:n])], R=[b_sq, b_cst], W=[b_ps[q]])
                K.op("act", lambda q=q: nc.scalar.activation(out=rsd[:, 0:n], in_=psum[q][:, 0:n], func=AF.Ln,
                                                             scale=1.0 / 128, bias=eps6[:, 0:1]),
                     R=[b_ps[q], b_eps], W=[b_rsd])
                K.op("act", lambda: nc.scalar.activation(out=rsd[:, 0:n], in_=rsd[:, 0:n], func=AF.Exp, scale=-0.5),
                     R=[b_rsd], W=[b_rsd])
                K.op("dve", lambda h=h: nc.vector.tensor_tensor(out=tn[:, 0:n], in0=oB[:, h, 0:n], in1=rsd[:, 0:n],
                                                                op=ALU.mult), R=[b_oB, b_rsd], W=[b_tn])
                K.op("dve", lambda h=h: nc.vector.scalar_tensor_tensor(
                    out=og[:, hbase + h, c0:c0 + n], in0=tn[:, 0:n], scalar=nw[:, 0:1], in1=gz[:, h, 0:n],
                    op0=ALU.mult, op1=ALU.mult), R=[b_tn, b_nw, b_gz], W=[b_og])

        def gla_phase(l, og, b_og, win_v):
            with ExitStack() as st:
                wB = sbt(st, "wB", [128, KC, 1552], BF16)
                b_wBb = [Buf() for _ in range(13)]
                for blk_ in range(12):
                    K.dma("pool", wB[:, :, blk_ * 128:(blk_ + 1) * 128],
                          win_v[:, :, 2056 + blk_ * 128:2056 + (blk_ + 1) * 128], W=[b_wBb[blk_]])
                K.dma("pool", wB[:, :, 1536:1552], win_v[:, :, 2056 + 1536:2056 + 1552], W=[b_wBb[12]])
                w2e = sbt(st, "w2e", [17, 256])
                b_w2e = Buf()
                K.dma("sp", w2e[0:16, :], gla_gate_w2[l], W=[b_w2e])
                K.dma("sp", w2e[16:17, :], gla_gate_b[l].rearrange("(o n) -> o n", o=1), W=[b_w2e])
                nw = sbt(st, "gla_nw", [128, 1])
                b_nw = Buf()
                K.dma("sp", nw[:, :], gla_norm_w[l].rearrange("(p o) -> p o", o=1), W=[b_nw],
                      allow_slow_non_contiguous=True)
                qk = sbt(st, "g_qk", [128, 4, 512])
                b_qk = Buf()
                sr = sbt(st, "g_sr", [128, 4, 512], BF16)
                b_sr = Buf()
                glr = sbt(st, "g_glr", [17, 512])
                b_glr = Buf()
                oB = sbt(st, "g_oB", [128, 4, 512])
                b_oB = Buf()
                sq = sbt(st, "g_sq", [128, 4, 512], BF16)
                rsd = sbt(st, "g_rsd", [128, 512])
                tn = sbt(st, "g_tn", [128, 512])
                nbufs = (sq, Buf(), rsd, Buf(), tn, Buf())
                Lt = [sbt(st, "g_L%d" % i, [128, 256]) for i in range(3)]
                E12 = [sbt(st, "g_E%d" % i, [128, 2, 2, 128]) for i in range(3)]
                atot = [sbt(st, "g_at%d" % i, [128, 2, 2]) for i in range(3)]
                qdk = [sbt(st, "g_qdk%d" % i, [128, 2, 2, 128], BF16) for i in range(3)]
                qdm = [sbt(st, "g_qdm%d" % i, [128, 4, 128], BF16) for i in range(3)]
                b_qdm = [Buf(), Buf(), Buf()]
                qm = sbt(st, "g_qm", [128, 4, 18])
                b_qm = Buf()
                E3 = [sbt(st, "g_E3%d" % i, [128, 256]) for i in range(3)]
                kdec = [sbt(st, "g_kd%d" % i, [128, 256], BF16) for i in range(3)]
                vtb = [sbt(st, "g_vt%d" % i, [128, 512], BF16) for i in range(3)]
                att = [sbt(st, "g_att%d" % i, [128, 4, 128], BF16) for i in range(3)]
                b_L, b_E12, b_at, b_qdk, b_E3, b_kd, b_vt, b_att = ([Buf(), Buf(), Buf()] for _ in range(8))
                S = sbt(st, "g_S", [128, 2, 128])
                Sb = sbt(st, "g_Sb", [128, 2, 128], BF16)
                b_S, b_Sb = Buf(), Buf()
                K.op("dve", lambda: nc.vector.memset(S[:], 0.0), W=[b_S])
                K.op("dve", lambda: nc.vector.memset(Sb[:], 0.0), W=[b_Sb])
                K.op("dve", lambda: nc.vector.memset(glr[:], 1.0), W=[b_glr])
                Ss = sbt(st, "g_Ss", [128, NS, 2, 128])
                b_Ss = Buf()
                sgl_v = state_gla[l].rearrange("s (pr hf) k v -> (hf k) s pr v", hf=2)
                for s4 in range(4):
                    K.dma("sp", Ss[:, 4 * s4:4 * s4 + 4], sgl_v[:, 4 * s4:4 * s4 + 4], W=[b_Ss])
                vbd = sbt(st, "g_vbd", [16, NS, 512], BF16)
                b_vbd = Buf()
                sub_ctr = [0]
                def glaA(kind, sc0, nt, u, bx, by, c0):
                    lc = sc0 - c0
                    if kind == "sample":
                        m_incl, m_strict_l = C["ident"], C["zero"]
                        mb_incl = CB["ident"]
                    else:
                        m_incl, m_strict_l = C["mu_incl"], C["ml_strict"]
                        mb_incl = CB["mu_incl"]
                    yield
                    mm_group(psum[bx][0:nt, 0:256], [(x_bf[:, k, sc0:sc0 + nt], wB[:, k, 256:512]) for k in range(KC)],
                             R=b_wBb[2:4] + [b_xbf], W=[b_ps[bx]])
                    yield
                    mm_group(psum[by][0:nt, 0:512], [(x_bf[:, k, sc0:sc0 + nt], wB[:, k, 512:1024]) for k in range(KC)],
                             R=b_wBb[4:8] + [b_xbf], W=[b_ps[by]])
                    yield
                    mm_group(psum[bx][0:nt, 256:512], [(glr[0:17, lc:lc + nt], w2e[0:17, :])],
                             R=[b_glr, b_w2e], W=[b_ps[bx]])
                    yield
                    K.op("act", lambda u=u: nc.scalar.activation(out=Lt[u][0:nt, :], in_=psum[bx][0:nt, 256:512],
                                                                 func=AF.Exp, scale=-1.0), R=[b_ps[bx]], W=[b_L[u]])
                    yield
                    K.op("act", lambda u=u: nc.scalar.activation(out=Lt[u][0:nt, :], in_=Lt[u][0:nt, :], func=AF.Ln,
                                                                 bias=one1[0:nt, 0:1]), R=[b_L[u], b_eps], W=[b_L[u]])
                    yield
                    K.op("act", lambda u=u: nc.scalar.copy(out=vtb[u][0:nt, :], in_=psum[by][0:nt, :]),
                         R=[b_ps[by]], W=[b_vt[u]])
                    yield
                    for pr_ in range(2):
                        mm_group(psum[by][:, pr_ * 128:pr_ * 128 + nt],
                                 [(Lt[u][0:nt, pr_ * 128:(pr_ + 1) * 128], m_incl[0:nt, 0:nt])],
                                 R=[b_L[u], b_cst], W=[b_ps[by]])
                    yield
                    mm_group(psum[by][0:nt, 256:512], [(m_strict_l[0:nt, 0:nt], Lt[u][0:nt, :])],
                             R=[b_L[u], b_cst], W=[b_ps[by]])
                    csT = psum[by][:, 0:256].rearrange("p (a b) -> p a b", a=2)[:, :, 0:nt]
                    yield
                    K.op("act", lambda u=u, csT=csT: nc.scalar.activation(out=E12[u][:, 0, :, 0:nt], in_=csT,
                                                                          func=AF.Exp, scale=-1.0 / 16),
                         R=[b_ps[by]], W=[b_E12[u]])
                    yield
                    K.op("act", lambda u=u, csT=csT: nc.scalar.activation(out=E12[u][:, 1, :, 0:nt], in_=csT,
                                                                          func=AF.Exp, scale=1.0 / 16),
                         R=[b_ps[by]], W=[b_E12[u]])
                    nch = 2 if nt == 128 else 1
                    cl = 64 if nt == 128 else nt
                    yield
                    for ch in range(nch):
                        lastc = ch * 64 + cl - 1
                        K.op("act", lambda u=u, ch=ch, lastc=lastc: nc.scalar.activation(
                            out=atot[u][:, :, ch:ch + 1],
                            in_=psum[by][:, 0:256].rearrange("p (a b) -> p a b", a=2)[:, :, lastc:lastc + 1],
                            func=AF.Exp, scale=-1.0 / 16), R=[b_ps[by]], W=[b_at[u]])
                    yield
                    K.op("act", lambda u=u: nc.scalar.activation(out=E3[u][0:nt, :], in_=psum[by][0:nt, 256:512],
                                                                 func=AF.Exp, scale=-1.0 / 16),
                         R=[b_ps[by]], W=[b_E3[u]])
                    yield
                    K.op("dve", lambda u=u, lc=lc: nc.vector.scalar_tensor_tensor(
                        out=qdk[u][:, 0, :, 0:nt], in0=qk[:, 0:2, lc:lc + nt], scalar=0.125, in1=E12[u][:, 0, :, 0:nt],
                        op0=ALU.mult, op1=ALU.mult), R=[b_qk, b_E12[u]], W=[b_qdk[u]])
                    yield
                    K.op("dve", lambda u=u, lc=lc: nc.vector.tensor_tensor(
                        out=qdk[u][:, 1, :, 0:nt], in0=qk[:, 2:4, lc:lc + nt], in1=E12[u][:, 1, :, 0:nt], op=ALU.mult),
                        R=[b_qk, b_E12[u]], W=[b_qdk[u]])
                    yield
                    K.op("dve", lambda u=u: nc.vector.tensor_tensor(
                        out=kdec[u][0:nt, :], in0=psum[bx][0:nt, 0:256], in1=E3[u][0:nt, :], op=ALU.mult),
                        R=[b_ps[bx], b_E3[u]], W=[b_kd[u]])
                    yield
                    if kind == "sample":
                        for h in range(4):
                            K.op("dve", lambda h=h: nc.vector.tensor_scalar(
                                out=qm[:, h, :], in0=qk[:, h // 2, 0:18], scalar1=C["blk%d" % (h % 2)][:, 0:1],
                                scalar2=None, op0=ALU.mult), R=[b_qk, b_cst], W=[b_qm])
                        gla_sample(l, u, qm, b_qm, E12, b_E12, kdec, b_kd, vtb, b_vt, Ss, b_Ss, vbd, b_vbd, oB, b_oB)
                        return
                    yield
                    for h in range(4):
                        K.op("dve", lambda h=h, u=u: nc.vector.tensor_scalar(
                            out=qdm[u][:, h, 0:nt], in0=qdk[u][:, 0, h // 2, 0:nt], scalar1=C["blk%d" % (h % 2)][:, 0:1],
                            scalar2=None, op0=ALU.mult), R=[b_qdk[u], b_cst], W=[b_qdm[u]])
                    yield
                    for h in range(4):
                        pr_, hf = h // 2, h % 2
                        rows = slice(hf * 64, hf * 64 + 64)
                        mm_group(psum[bx][0:nt, h * 128:h * 128 + nt],
                                 [(qdk[u][:, 1, pr_, 0:nt], qdm[u][:, h, 0:nt])],
                                 R=[b_qdk[u], b_qdm[u]], W=[b_ps[bx]], inc=(h == 3))
                    yield
                    K.op("dve", lambda u=u: nc.vector.tensor_tensor(
                        out=att[u][0:nt, :, 0:nt],
                        in0=psum[bx][0:nt, :].rearrange("p (a b) -> p a b", a=4)[:, :, 0:nt],
                        in1=m_incl[0:nt, 0:nt].unsqueeze(1).broadcast_to([nt, 4, nt]), op=ALU.mult),
                        R=[b_ps[bx], b_cst], W=[b_att[u]])

                def glaB(kind, sc0, nt, u, c0):
                    lc = sc0 - c0
                    nch = 2 if nt == 128 else 1
                    cl = 64 if nt == 128 else nt
                    for ch in range(nch):
                        trow = slice(ch * 64, ch * 64 + cl)
                        yield
                        for h in range(4):
                            pr_, hf = h // 2, h % 2
                            rows = slice(hf * 64, hf * 64 + 64)
                            K.op("pe", lambda h=h, pr_=pr_, trow=trow, u=u: nc.tensor.matmul(
                                psum[6][:, h * 64:h * 64 + cl], lhsT=Sb[:, pr_, :], rhs=qdm[u][:, h, trow],
                                start=True, stop=False), R=[b_Sb, b_qdm[u]], W=[b_ps[6]], inc=False)
                            K.op("pe", lambda h=h, trow=trow, u=u: nc.tensor.matmul(
                                psum[6][:, h * 64:h * 64 + cl], lhsT=vtb[u][0:nt, h * 128:(h + 1) * 128],
                                rhs=att[u][0:nt, h, trow], start=False, stop=True),
                                R=[b_vt[u], b_att[u]], W=[b_ps[6]], inc=(h == 3))
                        yield
                        for h in range(4):
                            pr_, hf = h // 2, h % 2
                            K.op("pe", lambda h=h, pr_=pr_, hf=hf, trow=trow, u=u: nc.tensor.matmul(
                                psum[7][hf * 64:hf * 64 + 64, pr_ * 128:(pr_ + 1) * 128],
                                lhsT=kdec[u][trow, h * 64:(h + 1) * 64], rhs=vtb[u][trow, h * 128:(h + 1) * 128],
                                start=True, stop=True), R=[b_kd[u], b_vt[u]], W=[b_ps[7]], inc=(h == 3))
                        yield
                        K.op("act", lambda lc=lc, trow=trow, ch=ch: nc.scalar.copy(
                            out=oB[:, :, lc + ch * 64:lc + ch * 64 + cl],
                            in_=psum[6][:, 0:256].rearrange("p (a b) -> p a b", a=4)[:, :, 0:cl]),
                            R=[b_ps[6]], W=[b_oB])
                        yield
                        for pr_ in range(2):
                            K.op("dve", lambda pr_=pr_, ch=ch, u=u: nc.vector.scalar_tensor_tensor(
                                out=S[:, pr_, :], in0=S[:, pr_, :], scalar=atot[u][:, pr_, ch:ch + 1],
                                in1=psum[7][:, pr_ * 128:(pr_ + 1) * 128], op0=ALU.mult, op1=ALU.add),
                                R=[b_at[u], b_ps[7]], W=[b_S])
                        yield
                        K.op("act", lambda: nc.scalar.copy(out=Sb[:], in_=S[:]), R=[b_S], W=[b_Sb])

                    yield

                for (c0, n) in TILES:
                    for bi in range(9):
                        q = bi % 2
                        if bi < 8:
                            woff = bi * 128 if bi < 4 else 1024 + (bi - 4) * 128
                            mw = 128
                        else:
                            woff, mw = 1536, 16
                        mm_group(psum[q][0:mw, 0:n], [(wB[:, k, woff:woff + mw], x_bf[:, k, c0:c0 + n]) for k in range(KC)],
                                 R=[b_wBb[woff // 128], b_xbf], W=[b_ps[q]])
                        if bi < 4:
                            K.op("dve", lambda bi=bi, q=q: nc.vector.tensor_copy(out=qk[:, bi, 0:n], in_=psum[q][:, 0:n]),
                                 R=[b_ps[q]], W=[b_qk])
                        elif bi < 8:
                            K.op("act", lambda bi=bi, q=q: nc.scalar.activation(
                                out=sr[:, bi - 4, 0:n], in_=psum[q][:, 0:n], func=AF.Silu), R=[b_ps[q]], W=[b_sr])
                        else:
                            K.op("dve", lambda q=q: nc.vector.tensor_copy(out=glr[0:16, 0:n], in_=psum[q][0:16, 0:n]),
                                 R=[b_ps[q]], W=[b_glr])
                    subs = subtiles(c0, n)
                    ths = []
                    for i_, (kind, sc0, nt) in enumerate(subs):
                        gi = sub_ctr[0] + i_
                        afterA = ([2 * (i_ - 2)] if i_ >= 2 else []) + ([2 * (i_ - 3) + 1] if i_ >= 3 else [])
                        afterB = [2 * i_] + ([2 * (i_ - 1) + 1] if i_ >= 1 else [])
                        ths.append((lambda kind=kind, sc0=sc0, nt=nt, gi=gi, c0=c0: glaA(kind, sc0, nt, gi % 3, 2 + 2 * (gi % 2), 3 + 2 * (gi % 2), c0), afterA))
                        if kind == "sample":
                            ths.append((lambda: None, afterB))
                        else:
                            ths.append((lambda kind=kind, sc0=sc0, nt=nt, gi=gi, c0=c0: glaB(kind, sc0, nt, gi % 3, c0), afterB))
                    K.run_sched(ths)
                    sub_ctr[0] += len(subs)
                    gated_norm(st, oB, b_oB, sr, b_sr, nw, b_nw, og, b_og, 4, c0, n, nbufs)
                K.dma("sp", o_gla_p[l].rearrange("(pr hf) k v -> (hf k) pr v", hf=2), S[:], R=[b_S])

        def gla_sample(l, u, qm, b_qm, E12, b_E12, kdec, b_kd, vtb, b_vt, Ss, b_Ss, vbd, b_vbd, oB, b_oB):
            K.op("dve", lambda: nc.vector.tensor_tensor(
                out=vbd[:, :, :], in0=vtb[u][0:16, :].unsqueeze(1).broadcast_to([16, NS, 512]),
                in1=C["ident"][0:16, 0:16].unsqueeze(2).broadcast_to([16, NS, 512]), op=ALU.mult),
                R=[b_vt[u], b_cst], W=[b_vbd])
            for s4 in range(4):
                for si in range(4):
                    s = 4 * s4 + si
                    for h in range(4):
                        pr_, hf = h // 2, h % 2
                        q = 6 + si // 2
                        cb = (si % 2) * 256 + pr_ * 128
                        K.op("pe", lambda s=s, h=h, hf=hf, q=q, cb=cb: nc.tensor.matmul(
                            psum[q][hf * 64:hf * 64 + 64, cb:cb + 128],
                            lhsT=kdec[u][0:16, h * 64:(h + 1) * 64], rhs=vbd[0:16, s, h * 128:(h + 1) * 128],
                            start=True, stop=True), R=[b_kd[u], b_vbd], W=[b_ps[q]], inc=(h == 3))
                for si in range(4):
                    s = 4 * s4 + si
                    q = 6 + si // 2
                    for pr_ in range(2):
                        cb = (si % 2) * 256 + pr_ * 128
                        K.op("dve", lambda s=s, pr_=pr_, q=q, cb=cb: nc.vector.scalar_tensor_tensor(
                            out=Ss[:, s, pr_, :], in0=Ss[:, s, pr_, :], scalar=E12[u][:, 0, pr_, s:s + 1],
                            in1=psum[q][:, cb:cb + 128], op0=ALU.mult, op1=ALU.add),
                            R=[b_E12[u], b_ps[q]], W=[b_Ss])
            for s in range(NS):
                for h in range(4):
                    pr_, hf = h // 2, h % 2
                    K.op("pe", lambda s=s, h=h, pr_=pr_: nc.tensor.matmul(
                        psum[5][:, h * 32 + s:h * 32 + s + 2], lhsT=Ss[:, s, pr_, :], rhs=qm[:, h, s:s + 2],
                        start=True, stop=True), R=[b_Ss, b_qm], W=[b_ps[5]], inc=(s == NS - 1 and h == 3))
            K.op("act", lambda: nc.scalar.activation(
                out=oB[:, :, 0:16], in_=psum[5][:, 0:128].rearrange("p (a b) -> p a b", a=4)[:, :, 0:16],
                func=AF.Copy, scale=0.125), R=[b_ps[5]], W=[b_oB])
            sgl_o = o_gla_s[l].rearrange("s (pr hf) k v -> (hf k) s pr v", hf=2)
            for s4 in range(4):
                K.dma("sp", sgl_o[:, 4 * s4:4 * s4 + 4], Ss[:, 4 * s4:4 * s4 + 4], R=[b_Ss])

        def gdn_phase(l, og, b_og, win_v):
            keep = {}
            with ExitStack() as st0:
                s_wT = sbt(st0, "d_swT", [128, 4, 16], BF16)
                s_u0 = sbt(st0, "d_su0", [16, 4, 128])
                s_kd = sbt(st0, "d_skd", [16, 4, 128], BF16)
                s_qn = sbt(st0, "d_sqn", [128, 4, 16], BF16)
                s_G = sbt(st0, "d_sG", [128, NS, 4])
                s_gz = sbt(st0, "d_sgz", [128, 4, 16], BF16)
                b_skeep = Buf()
                nw = sbt(st0, "gdn_nw", [128, 1])
                b_nw = Buf()
                K.dma("sp", nw[:, :], gdn_norm_w[l].rearrange("(p o) -> p o", o=1), W=[b_nw],
                      allow_slow_non_contiguous=True)
                with ExitStack() as st:
                    wA = sbt(st, "wA", [128, KC, 2056], BF16)
                    b_wAb = [Buf() for _ in range(17)]
                    for blk_ in range(16):
                        K.dma("pool", wA[:, :, blk_ * 128:(blk_ + 1) * 128], win_v[:, :, blk_ * 128:(blk_ + 1) * 128],
                              W=[b_wAb[blk_]])
                    K.dma("pool", wA[:, :, 2048:2056], win_v[:, :, 2048:2056], W=[b_wAb[16]])
                    cw = sbt(st, "d_cw", [128, 12, 4])
                    b_cw = Buf()
                    for i_ in range(4):
                        K.dma("sp", cw[:, :, i_], conv_w[l][i_].rearrange("(b p) -> p b", p=128), W=[b_cw],
                              allow_slow_non_contiguous=True)
                    abt = sbt(st, "d_abt", [128, 2, 4])
                    b_abt = Buf()
                    K.dma("sp", abt[:, 0, :], a_log[l].partition_broadcast(128), W=[b_abt])
                    K.dma("sp", abt[:, 1, :], dt_bias[l].partition_broadcast(128), W=[b_abt])
                    K.op("act", lambda: nc.scalar.activation(out=abt[:, 0, :], in_=abt[:, 0, :], func=AF.Exp),
                         R=[b_abt], W=[b_abt])
                    K.op("dve", lambda: nc.vector.tensor_scalar(out=abt[:, 0, :], in0=abt[:, 0, :], scalar1=-1.0,
                                                                scalar2=None, op0=ALU.mult), R=[b_abt], W=[b_abt])
                    Xs = sbt(st, "d_Xs", [128, 4, 2, 128])
                    tok3 = [sbt(st, "d_tok3%d" % i, [128, 3, 4, 128], BF16) for i in range(2)]
                    cbT = tok3[0][:, :, :, :].rearrange("p a h c -> p (a h c)").bitcast(F32)[:, 0:576].rearrange("p (b s) -> p b s", b=12)
                    b_cbT = Buf()
                    with ExitStack() as stc:
                        cb48 = sbt(stc, "d_cb48", [48, 1536])
                        b_cb48 = Buf()
                        K.dma("sp", cb48[:], state_conv[l].rearrange("s i c -> (s i) c"), W=[b_cb48])
                        for half in range(2):
                            q = half
                            for b6 in range(6):
                                blk = 6 * half + b6
                                K.op("pe", lambda blk=blk, b6=b6, q=q: nc.tensor.transpose(
                                    psum[q][:, b6 * 48:(b6 + 1) * 48], cb48[0:48, blk * 128:(blk + 1) * 128],
                                    C["ident"][0:48, 0:48]), R=[b_cb48, b_cst], W=[b_ps[q]], inc=(b6 == 5))
                            K.op("act", lambda half=half, q=q: nc.scalar.copy(
                                out=cbT[:, 6 * half:6 * half + 6, :],
                                in_=psum[q][:, 0:288].rearrange("p (a b) -> p a b", a=6)), R=[b_ps[q]], W=[b_cbT])
                        K.barrier()
                    K.dma("sp", o_conv_s[l][:, 0:2, :], state_conv[l][:, 1:3, :])
                    halo = sbt(st, "d_halo", [128, 12, 4])
                    b_halo = Buf()
                    K.op("dve", lambda: nc.vector.memset(halo[:], 0.0), W=[b_halo])
                    Pe = [sbt(st, "d_Pe%d" % i, [128, 3 + 512]) for i in range(2)]
                    b_Pe = [Buf(), Buf()]
                    acc = [sbt(st, "d_acc%d" % i, [128, 512]) for i in range(2)]
                    b_acc = [Buf(), Buf()]
                    if 4 * T >= 8192:
                        cq8 = big[:, 4 * T:4 * T + 8192].bitcast(F32).rearrange("p (a b) -> p a b", a=8)
                    else:
                        cq8 = sbt(st, "d_cq8", [128, 8, 512])
                    b_cq8 = [Buf() for _ in range(8)]
                    b_sqs = [Buf(), Buf()]
                    vn = sbt(st, "d_vn", [128, 4, 512], BF16)
                    b_vn = Buf()
                    qkn = sbt(st, "d_qkn", [128, 8, 512], BF16)
                    b_qkn = Buf()
                    gz = sbt(st, "d_gz", [128, 4, 512], BF16)
                    b_gz = Buf()
                    oA = big[:, 8 * T:8 * T + 4096].bitcast(F32).rearrange("p (a b) -> p a b", a=4)
                    b_oA = Buf()
                    sq = big[:, 8 * T + 4096:8 * T + 6144].rearrange("p (a b) -> p a b", a=4)
                    rsd = sbt(st, "d_rsd", [128, 512])
                    rsd_b = sbt(st, "d_rsdb", [128, 512])
                    tn = acc[0]
                    b_sq, b_rsd, b_tn, b_rsdb = Buf(), Buf(), b_acc[0], Buf()
                    nbufs = (sq, b_sq, rsd, b_rsd, tn, b_tn)
                    pnew = big[0:16, 8 * T:8 * T + 3072].bitcast(F32)
                    b_pnew = b_oA
                    tb = [sbt(st, "d_tb%d" % i, [128, 6, 4]) for i in range(2)]
                    b_tb = [Buf(), Buf()]
                    r2 = sbt(st, "d_r2", [128, 8])
                    b_r2 = Buf()
                    Gbc = [sbt(st, "d_Gbc%d" % i, [128, 2, 4]) for i in range(2)]
                    b_Gbc = [Buf(), Buf()]
                    dg = sbt(st, "d_dg", [128, 8, 128])
                    b_dg = Buf()
                    gm = sbt(st, "d_gm", [128, 4, 128])
                    b_gm = Buf()
                    ET = sbt(st, "d_ET", [128, 4, 128])
                    ETi = sbt(st, "d_ETi", [128, 4, 128])
                    ETs = ET
                    b_ET, b_ETi = Buf(), Buf()
                    b_ETs = b_ET
                    QA = [sbt(st, "d_QA%d" % i, [128, 4, 128]) for i in range(2)]
                    XA = [sbt(st, "d_XA%d" % i, [128, 4, 2, 128]) for i in range(2)]
                    QTA = [XA[i][:, :, 0, :] for i in range(2)]
                    b_QA = [Buf(), Buf()]
                    b_QTA = [Buf(), Buf()]
                    Qs = sbt(st, "d_Qs", [128, 4, 128])
                    QTs = Xs[:, :, 0, :]
                    b_Qs, b_QTs, b_Xst = Buf(), Buf(), Buf()
                    TT = [XA[i][:, :, 1, :] for i in range(2)]
                    b_TT = [Buf(), Buf()]
                    TTb = sbt(st, "d_TTb", [128, 4, 128], BF16)
                    b_TTb = Buf()
                    bkq = [sbt(st, "d_bkq%d" % i, [128, 2, 4, 128], BF16) for i in range(2)]
                    b_bkq = [Buf(), Buf()]
                    qkT = [sbt(st, "d_qkT%d" % i, [128, 4, 128], BF16) for i in range(2)]
                    b_qkT = [Buf(), Buf()]
                    b_tok3 = [Buf(), Buf()]
                    u0 = [sbt(st, "d_u0%d" % i, [128, 4, 128]) for i in range(2)]
                    wT = [sbt(st, "d_wT%d" % i, [128, 4, 128], BF16) for i in range(2)]
                    uu = sbt(st, "d_u", [128, 4, 128], BF16)
                    b_u0, b_wT, b_u = [Buf(), Buf()], [Buf(), Buf()], Buf()
                    S = sbt(st, "d_S", [128, 4, 128])
                    Sb = sbt(st, "d_Sb", [128, 4, 128], BF16)
                    b_S, b_Sb = Buf(), Buf()
                    K.op("dve", lambda: nc.vector.memset(S[:], 0.0), W=[b_S])
                    K.op("dve", lambda: nc.vector.memset(Sb[:], 0.0), W=[b_Sb])
                    K.op("dve", lambda: nc.vector.memset(uu[:], 0.0), W=[b_u])
                    sub_ctr = [0]
                    scan_done = [0]
                    def stageA(kind, sc0, nt, a, c0):
                        lc = sc0 - c0
                        smp = (kind == "sample")
                        m_incl = C["ident"] if smp else C["mu_incl"]
                        m_bd = C["ident"] if smp else C["bd"]
                        m_ustrict = C["zero"] if smp else C["mu_strict"]
                        m_lstrict = C["zero"] if smp else C["ml_strict"]
                        T_ = lambda i: tb[a][0:nt, i, :]
                        pv = lambda b: psum[b][:, :].rearrange("p (a c) -> p a c", a=4)[:, :, 0:nt]
                        pt = lambda b: psum[b][0:nt, :].rearrange("p (a c) -> p a c", a=4)[:, :, 0:nt]
                        p4 = lambda b: psum[b][0:nt, :].rearrange("p (a c) -> p a c", a=4)
                        bc = lambda i: tb[a][0:nt, i, :].unsqueeze(2).broadcast_to([nt, 4, 128])
                        mm_group(psum[2][0:nt, 0:8], [(x_bf[:, k, sc0:sc0 + nt], wA[:, k, 2048:2056]) for k in range(KC)],
                                 R=[b_wAb[16], b_xbf], W=[b_ps[2]])
                        yield
                        K.op("act", lambda: nc.scalar.activation(out=T_(0), in_=psum[2][0:nt, 0:4], func=AF.Exp, scale=-1.0),
                             R=[b_ps[2]], W=[b_tb[a]])
                        yield
                        K.op("dve", lambda: nc.vector.tensor_tensor(out=T_(2), in0=psum[2][0:nt, 4:8], in1=abt[0:nt, 1, :],
                                                                    op=ALU.add), R=[b_ps[2], b_abt], W=[b_tb[a]])
                        yield
                        K.op("act", lambda: nc.scalar.activation(out=T_(0), in_=T_(0), func=AF.Ln, bias=one1[0:nt, 0:1]),
                             R=[b_tb[a], b_eps], W=[b_tb[a]])
                        yield
                        K.op("act", lambda: nc.scalar.activation(out=T_(0), in_=T_(0), func=AF.Exp, scale=-1.0),
                             R=[b_tb[a]], W=[b_tb[a]])
                        yield
                        K.op("act", lambda: nc.scalar.activation(out=T_(2), in_=T_(2), func=AF.Exp), R=[b_tb[a]], W=[b_tb[a]])
                        yield
                        K.op("act", lambda: nc.scalar.activation(out=T_(2), in_=T_(2), func=AF.Ln, bias=one1[0:nt, 0:1]),
                             R=[b_tb[a], b_eps], W=[b_tb[a]])
                        yield
                        K.op("dve", lambda: nc.vector.tensor_tensor(out=T_(2), in0=T_(2), in1=abt[0:nt, 0, :], op=ALU.mult),
                             R=[b_tb[a], b_abt], W=[b_tb[a]])
                        yield
                        mm_group(psum[2][0:nt, 8:12], [(m_incl[0:nt, 0:nt], T_(2))], R=[b_tb[a], b_cst], W=[b_ps[2]])
                        yield
                        mm_group(psum[2][0:nt, 12:16], [(m_bd[0:nt, 0:nt], T_(2))], R=[b_tb[a], b_cst], W=[b_ps[2]])
                        yield
                        K.op("act", lambda: nc.scalar.activation(out=T_(1), in_=psum[2][0:nt, 8:12], func=AF.Exp),
                             R=[b_ps[2]], W=[b_tb[a]])
                        yield
                        K.op("dve", lambda: nc.vector.tensor_copy(out=T_(5), in_=psum[2][0:nt, 8:12]), R=[b_ps[2]], W=[b_tb[a]])
                        yield
                        K.op("dve", lambda: nc.vector.tensor_tensor(out=T_(3), in0=psum[2][0:nt, 12:16], in1=T_(5),
                                                                    op=ALU.subtract), R=[b_ps[2], b_tb[a]], W=[b_tb[a]])
                        yield
                        K.op("act", lambda: nc.scalar.activation(out=T_(3), in_=T_(3), func=AF.Exp), R=[b_tb[a]], W=[b_tb[a]])
                        yield
                        K.op("dve", lambda: nc.vector.tensor_tensor(out=T_(4), in0=T_(0), in1=T_(1), op=ALU.mult),
                             R=[b_tb[a]], W=[b_tb[a]])
                        yield
                        if smp:
                            K.op("dve", lambda: nc.vector.tensor_tensor(
                                out=dg[0:16, 0:4, 0:16].bitcast(F32R).rearrange("p h s -> p s h"),
                                in0=T_(2).unsqueeze(1).broadcast_to([16, 16, 4]),
                                in1=C["ident"][0:16, 0:16].unsqueeze(2).broadcast_to([16, 16, 4]), op=ALU.mult),
                                R=[b_tb[a], b_cst], W=[b_dg])
                            for h in range(4):
                                K.op("pe", lambda h=h: nc.tensor.matmul(psum[3][:, h * 16:(h + 1) * 16],
                                                                        lhsT=C["ones"][0:16, :], rhs=dg[0:16, h, 0:16],
                                                                        start=True, stop=True),
                                     R=[b_dg, b_cst], W=[b_ps[3]], inc=(h == 3))
                            K.op("act", lambda: nc.scalar.activation(
                                out=s_G[:, :, :].rearrange("p s h -> p h s"),
                                in_=psum[3][:, 0:64].rearrange("p (h s) -> p h s", h=4), func=AF.Exp),
                                R=[b_ps[3]], W=[b_skeep])
                        else:
                            K.op("dve", lambda: nc.vector.tensor_tensor(
                                out=r2[0:nt, :].rearrange("p (c h) -> p c h", c=2),
                                in0=T_(2).unsqueeze(1).broadcast_to([nt, 2, 4]),
                                in1=cst[0:nt, IDX["blk0"]:IDX["blk0"] + 2, 0:1].broadcast_to([nt, 2, 4]), op=ALU.mult),
                                R=[b_tb[a], b_cst], W=[b_r2])
                            mm_group(psum[2][:, 16:24], [(C["ones"][0:nt, :], r2[0:nt, :])], R=[b_r2, b_cst], W=[b_ps[2]])
                            K.op("act", lambda: nc.scalar.activation(out=Gbc[a][:, :, :].rearrange("p c h -> p (c h)"),
                                                                     in_=psum[2][:, 16:24], func=AF.Exp),
                                 R=[b_ps[2]], W=[b_Gbc[a]])
                        yield
                        K.op("dve", lambda: nc.vector.tensor_tensor(
                            out=dg[0:nt, :, 0:nt].bitcast(F32R), in0=C["ident"][0:nt, 0:nt].unsqueeze(1).broadcast_to([nt, 8, nt]),
                            in1=tb[a][0:nt, 0:2, :].rearrange("p a h -> p (a h)").unsqueeze(2).broadcast_to([nt, 8, nt]),
                            op=ALU.mult), R=[b_tb[a], b_cst, b_skeep], W=[b_dg])
                        yield
                        for wh in range(2):
                            K.op("pe", lambda wh=wh: nc.tensor.matmul(
                                psum[3 + wh][:, :].rearrange("p (a c) -> p a c", a=4)[:, :, 0:nt],
                                lhsT=ones_r[0:nt, :].bitcast(F32R), rhs=dg[0:nt, 4 * wh:4 * wh + 4, 0:nt].bitcast(F32R),
                                start=True, stop=True), R=[b_dg, b_onesr], W=[b_ps[3 + wh]], c=0.25)
                        yield
                        K.op("dve", lambda: nc.vector.tensor_tensor(out=bkq[a][:, 0, :, 0:nt], in0=qkn[:, 4:8, lc:lc + nt],
                                                                    in1=pv(3), op=ALU.mult),
                             R=[b_qkn, b_ps[3]], W=[b_bkq[a]])
                        yield
                        K.op("dve", lambda: nc.vector.tensor_tensor(out=bkq[a][:, 1, :, 0:nt], in0=qkn[:, 0:4, lc:lc + nt],
                                                                    in1=pv(4), op=ALU.mult),
                             R=[b_qkn, b_ps[4]], W=[b_bkq[a]])
                        yield
                        K.op("dve", lambda: nc.vector.tensor_tensor(
                            out=gm[0:nt, :, 0:nt], in0=m_incl[0:nt, 0:nt].unsqueeze(1).broadcast_to([nt, 4, nt]),
                            in1=T_(2).unsqueeze(2).broadcast_to([nt, 4, nt]), op=ALU.mult), R=[b_tb[a], b_cst], W=[b_gm])
                        yield
                        for h in range(4):
                            K.op("pe", lambda h=h: nc.tensor.matmul(psum[2][0:nt, h * 128:h * 128 + nt],
                                                                    lhsT=m_lstrict[0:nt, 0:nt], rhs=gm[0:nt, h, 0:nt],
                                                                    start=True, stop=True),
                                 R=[b_gm, b_cst], W=[b_ps[2]], inc=(h == 3))
                        yield
                        K.op("act", lambda: nc.scalar.activation(out=ET[0:nt, :, 0:nt], in_=pt(2), func=AF.Exp),
                             R=[b_ps[2]], W=[b_ET])
                        yield
                        K.op("dve", lambda: nc.vector.tensor_tensor(
                            out=ETi[0:nt, :, 0:nt], in0=ET[0:nt, :, 0:nt],
                            in1=m_incl[0:nt, 0:nt].unsqueeze(1).broadcast_to([nt, 4, nt]), op=ALU.mult),
                            R=[b_ET, b_cst], W=[b_ETi])
                        yield
                        K.op("dve", lambda: nc.vector.tensor_tensor(
                            out=ETs[0:nt, :, 0:nt], in0=ET[0:nt, :, 0:nt],
                            in1=m_ustrict[0:nt, 0:nt].unsqueeze(1).broadcast_to([nt, 4, nt]), op=ALU.mult),
                            R=[b_ET, b_cst, b_ETi], W=[b_ETs])
                        yield
                        for h in range(4):
                            K.op("pe", lambda h=h: nc.tensor.matmul(psum[3][0:nt, h * 128:h * 128 + nt],
                                                                    lhsT=qkn[:, 4 + h, lc:lc + nt], rhs=bkq[a][:, 0, h, 0:nt],
                                                                    start=True, stop=True),
                                 R=[b_qkn, b_bkq[a]], W=[b_ps[3]], inc=(h == 3))
                        yield
                        for h in range(4):
                            K.op("pe", lambda h=h: nc.tensor.matmul(psum[4][0:nt, h * 128:h * 128 + nt],
                                                                    lhsT=qkn[:, 4 + h, lc:lc + nt], rhs=qkn[:, h, lc:lc + nt],
                                                                    start=True, stop=True),
                                 R=[b_qkn], W=[b_ps[4]], inc=(h == 3))
                        AT, A_ = QTA[a], QA[a]
                        yield
                        K.op("dve", lambda: nc.vector.tensor_tensor(out=AT[0:nt, :, 0:nt].bitcast(F32R), in0=pt(3), in1=ETs[0:nt, :, 0:nt],
                                                                    op=ALU.mult), R=[b_ps[3], b_ETs], W=[b_QTA[a]])
                        yield
                        K.op("dve", lambda: nc.vector.tensor_tensor(out=qkT[a][0:nt, :, 0:nt], in0=pt(4), in1=ETi[0:nt, :, 0:nt],
                                                                    op=ALU.mult), R=[b_ps[4], b_ETi], W=[b_qkT[a]])
                        yield
                        K.op("dve", lambda: nc.vector.scalar_tensor_tensor(
                            out=TT[a][0:nt, :, 0:nt].bitcast(F32R), in0=AT[0:nt, :, 0:nt], scalar=-1.0,
                            in1=C["ident"][0:nt, 0:nt].unsqueeze(1).broadcast_to([nt, 4, nt]),
                            op0=ALU.mult, op1=ALU.add), R=[b_QTA[a], b_cst], W=[b_TT[a]])
                        yield
                        if not smp:
                            for h in range(4):
                                K.op("pe", lambda h=h: nc.tensor.transpose(psum[2][0:nt, h * 128:h * 128 + nt],
                                                                           AT[0:nt, h, 0:nt], C["ident"][0:nt, 0:nt]),
                                     R=[b_QTA[a], b_cst], W=[b_ps[2]], inc=(h == 3))
                            yield
                            K.op("act", lambda: nc.scalar.copy(out=A_[0:nt, :, 0:nt].bitcast(F32R), in_=pt(2)), R=[b_ps[2]], W=[b_QA[a]])

                    def stageB1(kind, sc0, nt, a, c0):
                        lc = sc0 - c0
                        smp = (kind == "sample")
                        m_incl = C["ident"] if smp else C["mu_incl"]
                        m_bd = C["ident"] if smp else C["bd"]
                        m_ustrict = C["zero"] if smp else C["mu_strict"]
                        m_lstrict = C["zero"] if smp else C["ml_strict"]
                        T_ = lambda i: tb[a][0:nt, i, :]
                        pv = lambda b: psum[b][:, :].rearrange("p (a c) -> p a c", a=4)[:, :, 0:nt]
                        pt = lambda b: psum[b][0:nt, :].rearrange("p (a c) -> p a c", a=4)[:, :, 0:nt]
                        p4 = lambda b: psum[b][0:nt, :].rearrange("p (a c) -> p a c", a=4)
                        bc = lambda i: tb[a][0:nt, i, :].unsqueeze(2).broadcast_to([nt, 4, 128])
                        yield
                        for h in range(4):
                            K.op("pe", lambda h=h: nc.tensor.matmul(psum[5][0:nt, h * 128:(h + 1) * 128],
                                                                    lhsT=qkn[:, 4 + h, lc:lc + nt], rhs=CB["ident"],
                                                                    start=True, stop=True),
                                 R=[b_qkn, b_cst], W=[b_ps[5]], inc=(h == 3))
                        yield
                        for h in range(4):
                            K.op("pe", lambda h=h: nc.tensor.matmul(psum[6][0:nt, h * 128:(h + 1) * 128],
                                                                    lhsT=vn[:, h, lc:lc + nt], rhs=CB["ident"],
                                                                    start=True, stop=True),
                                 R=[b_vn, b_cst], W=[b_ps[6]], inc=(h == 3))
                        yield
                        K.op("dve", lambda: nc.vector.tensor_tensor(out=tok3[a][0:nt, 0], in0=p4(5), in1=bc(4), op=ALU.mult),
                             R=[b_ps[5], b_tb[a]], W=[b_tok3[a]])
                        yield
                        K.op("dve", lambda: nc.vector.tensor_tensor(out=tok3[a][0:nt, 1], in0=p4(5), in1=bc(3), op=ALU.mult),
                             R=[b_ps[5], b_tb[a]], W=[b_tok3[a]])
                        yield
                        K.op("dve", lambda: nc.vector.tensor_tensor(out=tok3[a][0:nt, 2], in0=p4(6), in1=bc(0), op=ALU.mult),
                             R=[b_ps[6], b_tb[a]], W=[b_tok3[a]])
                        fin_X, b_fin = XA[a], b_TT[a]
                        if not smp:
                            n_inc = 5 if nt == 128 else 3
                            cq_, b_cq_ = QA[a], b_QA[a]
                            nq_, b_nq_ = Qs, b_Qs
                            cX, b_cXq, b_cXt = XA[a], b_QTA[a], b_TT[a]
                            nX, b_nXq, b_nXt = Xs, b_QTs, b_Xst
                            f32r = lambda ap: ap.bitcast(F32R)
                            for lv in range(0, n_inc + 1):
                                need_q = lv < n_inc
                                need_qt = lv < n_inc - 1
                                if lv >= 1:
                                    for h in range(4):
                                        K.op("pe", lambda h=h: nc.tensor.matmul(
                                            ps67[0:nt, h * 256:(h + 1) * 256].rearrange("p (j c) -> p j c", j=2)[:, :, 0:nt],
                                            lhsT=f32r(cq_[0:nt, h, 0:nt]), rhs=f32r(cX[0:nt, h, :, 0:nt]), start=True, stop=True),
                                            R=[b_cq_, b_cXq, b_cXt], W=[b_ps[6], b_ps[7]], inc=(h == 3), c=0.13)
                                else:
                                    for h in range(4):
                                        K.op("pe", lambda h=h: nc.tensor.matmul(
                                            ps67[0:nt, h * 256:h * 256 + nt], lhsT=cq_[0:nt, h, 0:nt],
                                            rhs=cX[0:nt, h, 0, 0:nt], start=True, stop=True),
                                            R=[b_cq_, b_cXq], W=[b_ps[6], b_ps[7]], inc=(h == 3), c=0.22)
                                if need_q:
                                    for h in range(4):
                                        K.op("pe", lambda h=h: nc.tensor.matmul(
                                            psum[5][0:nt, h * 128:h * 128 + nt], lhsT=cX[0:nt, h, 0, 0:nt],
                                            rhs=cq_[0:nt, h, 0:nt], start=True, stop=True),
                                            R=[b_cq_, b_cXq], W=[b_ps[5]], inc=(h == 3), c=0.22)
                                yield
                                p67 = ps67[0:nt, :].rearrange("p (h j c) -> p h j c", h=4, j=2)
                                if need_q:
                                    K.op("act", lambda: nc.scalar.copy(out=nq_[0:nt, :, 0:nt].bitcast(F32R), in_=pt(5)),
                                         R=[b_ps[5]], W=[b_nq_])
                                if lv >= 1:
                                    K.op("dve", lambda: nc.vector.tensor_tensor(out=nX[0:nt, :, 1, 0:nt].bitcast(F32R), in0=cX[0:nt, :, 1, 0:nt],
                                                                                in1=p67[:, :, 1, 0:nt], op=ALU.add),
                                         R=[b_ps[6], b_ps[7], b_cXt], W=[b_nXt])
                                else:
                                    K.op("dve", lambda: nc.vector.tensor_copy(out=nX[0:nt, :, 1, 0:nt].bitcast(F32R), in_=cX[0:nt, :, 1, 0:nt]),
                                         R=[b_cXt], W=[b_nXt])
                                if need_qt:
                                    K.op("act", lambda: nc.scalar.copy(out=nX[0:nt, :, 0, 0:nt].bitcast(F32R), in_=p67[:, :, 0, 0:nt]),
                                         R=[b_ps[6], b_ps[7]], W=[b_nXq])
                                yield
                                cq_, b_cq_, nq_, b_nq_ = nq_, b_nq_, cq_, b_cq_
                                cX, b_cXq, b_cXt, nX, b_nXq, b_nXt = nX, b_nXq, b_nXt, cX, b_cXq, b_cXt
                            fin_X, b_fin = cX, b_cXt
                        yield
                        K.op("act", lambda: nc.scalar.copy(out=TTb[0:nt, :, 0:nt], in_=fin_X[0:nt, :, 1, 0:nt]), R=[b_fin], W=[b_TTb])
                        yield
                        yield
                        for h in range(4):
                            K.op("pe", lambda h=h: nc.tensor.matmul(psum[7][0:nt, h * 128:(h + 1) * 128],
                                                                    lhsT=TTb[0:nt, h, 0:nt], rhs=tok3[a][0:nt, 2, h, :],
                                                                    start=True, stop=True),
                                 R=[b_TTb, b_tok3[a]], W=[b_ps[7]], inc=(h == 3))
                        yield
                        for h in range(4):
                            K.op("pe", lambda h=h: nc.tensor.matmul(psum[5][:, h * 128:h * 128 + nt],
                                                                    lhsT=tok3[a][0:nt, 0, h, :], rhs=TTb[0:nt, h, 0:nt],
                                                                    start=True, stop=True),
                                 R=[b_TTb, b_tok3[a]], W=[b_ps[5]], inc=(h == 3))
                        yield
                        if smp:
                            K.op("act", lambda: nc.scalar.copy(out=s_u0[:], in_=p4(7)), R=[b_ps[7]], W=[b_skeep])
                            K.op("dve", lambda: nc.vector.tensor_copy(out=s_wT[:], in_=pv(5)), R=[b_ps[5]], W=[b_skeep])
                            K.op("dve", lambda: nc.vector.tensor_copy(out=s_kd[:], in_=tok3[a][0:16, 1]), R=[b_tok3[a]], W=[b_skeep])
                            K.op("dve", lambda: nc.vector.tensor_copy(out=s_qn[:], in_=qkn[:, 0:4, 0:16]), R=[b_qkn], W=[b_skeep])
                            K.op("dve", lambda: nc.vector.tensor_copy(out=s_gz[:], in_=gz[:, :, 0:16]), R=[b_gz], W=[b_skeep])
                            return
                        yield
                        K.op("act", lambda: nc.scalar.copy(out=u0[a][0:nt], in_=p4(7)), R=[b_ps[7]], W=[b_u0[a]])
                        yield
                        K.op("dve", lambda: nc.vector.tensor_copy(out=wT[a][:, :, 0:nt], in_=pv(5)), R=[b_ps[5]], W=[b_wT[a]])

                    def stageB2(kind, sc0, nt, a, c0):
                        lc = sc0 - c0
                        smp = (kind == "sample")
                        m_incl = C["ident"] if smp else C["mu_incl"]
                        m_bd = C["ident"] if smp else C["bd"]
                        m_ustrict = C["zero"] if smp else C["mu_strict"]
                        m_lstrict = C["zero"] if smp else C["ml_strict"]
                        T_ = lambda i: tb[a][0:nt, i, :]
                        pv = lambda b: psum[b][:, :].rearrange("p (a c) -> p a c", a=4)[:, :, 0:nt]
                        pt = lambda b: psum[b][0:nt, :].rearrange("p (a c) -> p a c", a=4)[:, :, 0:nt]
                        p4 = lambda b: psum[b][0:nt, :].rearrange("p (a c) -> p a c", a=4)
                        bc = lambda i: tb[a][0:nt, i, :].unsqueeze(2).broadcast_to([nt, 4, 128])
                        if smp:
                            return
                        nch = 2 if nt == 128 else 1
                        cl = 64 if nt == 128 else nt
                        yield
                        for ch in range(nch):
                            tr = slice(ch * 64, ch * 64 + cl)
                            yield
                            for h in range(4):
                                K.op("pe", lambda h=h, tr=tr: nc.tensor.matmul(psum[0][tr, h * 128:(h + 1) * 128],
                                                                              lhsT=wT[a][:, h, tr], rhs=Sb[:, h, :],
                                                                              start=True, stop=True),
                                     R=[b_wT[a], b_Sb], W=[b_ps[0]], inc=(h == 3))
                            yield
                            K.op("dve", lambda tr=tr: nc.vector.tensor_tensor(
                                out=uu[tr], in0=u0[a][tr], in1=psum[0][tr, :].rearrange("p (a c) -> p a c", a=4),
                                op=ALU.subtract), R=[b_u0[a], b_ps[0]], W=[b_u])
                            yield
                            for h in range(4):
                                K.op("pe", lambda h=h, tr=tr: nc.tensor.matmul(psum[1][:, h * 64:h * 64 + cl],
                                                                              lhsT=Sb[:, h, :], rhs=bkq[a][:, 1, h, tr],
                                                                              start=True, stop=False),
                                     R=[b_Sb, b_bkq[a]], W=[b_ps[1]], inc=False)
                                K.op("pe", lambda h=h, tr=tr: nc.tensor.matmul(psum[1][:, h * 64:h * 64 + cl],
                                                                              lhsT=uu[0:nt, h, :], rhs=qkT[a][0:nt, h, tr],
                                                                              start=False, stop=True),
                                     R=[b_u, b_qkT[a]], W=[b_ps[1]], inc=(h == 3))
                            yield
                            for h in range(4):
                                K.op("pe", lambda h=h, tr=tr: nc.tensor.matmul(psum[0][:, h * 128:(h + 1) * 128],
                                                                              lhsT=tok3[a][tr, 1, h, :], rhs=uu[tr, h, :],
                                                                              start=True, stop=True),
                                     R=[b_tok3[a], b_u], W=[b_ps[0]], inc=(h == 3))
                            yield
                            K.op("act", lambda ch=ch: nc.scalar.copy(
                                out=oA[:, :, lc + ch * 64:lc + ch * 64 + cl],
                                in_=psum[1][:, 0:256].rearrange("p (a b) -> p a b", a=4)[:, :, 0:cl]),
                                R=[b_ps[1]], W=[b_oA])
                            yield
                            for h in range(4):
                                K.op("dve", lambda h=h, ch=ch: nc.vector.scalar_tensor_tensor(
                                    out=S[:, h, :], in0=S[:, h, :], scalar=Gbc[a][:, ch, h:h + 1],
                                    in1=psum[0][:, h * 128:(h + 1) * 128], op0=ALU.mult, op1=ALU.add),
                                    R=[b_Gbc[a], b_ps[0]], W=[b_S])
                            yield
                            K.op("act", lambda: nc.scalar.copy(out=Sb[:], in_=S[:]), R=[b_S], W=[b_Sb])
                        scan_done[0] += 1
                        yield

                    blk_it = 0
                    for (c0, n) in TILES:
                        is_ms = (c0 == 0)
                        def blkfn(blk, c0=c0, n=n, is_ms=is_ms):
                            q = blk % 2
                            mm_group(psum[q][:, 0:n], [(wA[:, k, blk * 128:(blk + 1) * 128], x_bf[:, k, c0:c0 + n])
                                                      for k in range(KC)], R=[b_wAb[blk], b_xbf], W=[b_ps[q]])
                            if blk >= 12:
                                K.op("act", lambda blk=blk, q=q: nc.scalar.activation(
                                    out=gz[:, blk - 12, 0:n], in_=psum[q][:, 0:n], func=AF.Silu), R=[b_ps[q]], W=[b_gz])
                                return
                            pe_ = blk % 2
                            ncv = 16 if is_ms else n
                            K.op("pool", lambda pe_=pe_, blk=blk: nc.gpsimd.tensor_copy(out=Pe[pe_][:, 0:3], in_=halo[:, blk, 0:3]),
                                 R=[b_halo], W=[b_Pe[pe_]])
                            src0 = 16 if is_ms else 0
                            K.op("act", lambda pe_=pe_, q=q, src0=src0, ncv=ncv: nc.scalar.copy(
                                out=Pe[pe_][:, 3:3 + ncv], in_=psum[q][:, src0:src0 + ncv]), R=[b_ps[q]], W=[b_Pe[pe_]])
                            K.op("pool", lambda pe_=pe_, blk=blk, ncv=ncv: nc.gpsimd.tensor_copy(
                                out=halo[:, blk, 0:3], in_=Pe[pe_][:, ncv:ncv + 3]), R=[b_Pe[pe_]], W=[b_halo])
                            a_ = acc[pe_]
                            o0 = 16 if is_ms else 0
                            K.op("act", lambda pe_=pe_, blk=blk, ncv=ncv, o0=o0: nc.scalar.activation(
                                out=acc[pe_][:, o0:o0 + ncv], in_=Pe[pe_][:, 3:3 + ncv], func=AF.Identity, scale=cw[:, blk, 3:4]),
                                R=[b_Pe[pe_], b_cw], W=[b_acc[pe_]])
                            for i in (2, 1, 0):
                                K.op("dve", lambda pe_=pe_, blk=blk, ncv=ncv, o0=o0, i=i: nc.vector.scalar_tensor_tensor(
                                    out=acc[pe_][:, o0:o0 + ncv], in0=Pe[pe_][:, i:i + ncv], scalar=cw[:, blk, i:i + 1],
                                    in1=acc[pe_][:, o0:o0 + ncv], op0=ALU.mult, op1=ALU.add),
                                    R=[b_Pe[pe_], b_cw, b_acc[pe_]], W=[b_acc[pe_]])
                            if is_ms:
                                cbv = cbT[:, blk, :].rearrange("p (s i) -> p s i", i=3)
                                K.op("dve", lambda pe_=pe_, blk=blk, q=q: nc.vector.tensor_scalar(
                                    out=acc[pe_][:, 0:16], in0=psum[q][:, 0:16], scalar1=cw[:, blk, 3:4], scalar2=None,
                                    op0=ALU.mult), R=[b_ps[q], b_cw], W=[b_acc[pe_]])
                                for i in range(3):
                                    K.op("dve", lambda pe_=pe_, blk=blk, i=i, cbv=cbv: nc.vector.scalar_tensor_tensor(
                                        out=acc[pe_][:, 0:16], in0=cbv[:, :, i], scalar=cw[:, blk, i:i + 1],
                                        in1=acc[pe_][:, 0:16], op0=ALU.mult, op1=ALU.add),
                                        R=[b_cbT, b_cw, b_acc[pe_]], W=[b_acc[pe_]])
                            if blk < 8:
                                K.op("act", lambda pe_=pe_, blk=blk: nc.scalar.activation(
                                    out=cq8[:, blk, 0:n], in_=acc[pe_][:, 0:n], func=AF.Silu), R=[b_acc[pe_]], W=[b_cq8[blk]])
                            else:
                                K.op("act", lambda pe_=pe_, blk=blk: nc.scalar.activation(
                                    out=vn[:, blk - 8, 0:n], in_=acc[pe_][:, 0:n], func=AF.Silu), R=[b_acc[pe_]], W=[b_vn])
                        K.run_sched([(lambda blk=blk: blkfn(blk), ([blk - 2] if blk >= 2 else [])) for blk in range(16)])

                        def normfn(blk, n=n):
                            ci = blk % 2
                            q = blk % 2
                            rr = rsd if ci == 0 else rsd_b
                            b_rr = b_rsd if ci == 0 else b_rsdb
                            K.op("act", lambda: nc.scalar.activation(out=sq[:, ci, 0:n], in_=cq8[:, blk, 0:n],
                                                                     func=AF.Square), R=[b_cq8[blk]], W=[b_sqs[ci]])
                            mm_group(psum[q][:, 0:n], [(CB["ones"], sq[:, ci, 0:n])], R=[b_sqs[ci], b_cst], W=[b_ps[q]])
                            K.op("act", lambda: nc.scalar.activation(out=rr[:, 0:n], in_=psum[q][:, 0:n], func=AF.Ln,
                                                                     bias=eps6[:, 0:1]), R=[b_ps[q], b_eps], W=[b_rr])
                            K.op("act", lambda: nc.scalar.activation(out=rr[:, 0:n], in_=rr[:, 0:n], func=AF.Exp,
                                                                     scale=-0.5), R=[b_rr], W=[b_rr])
                            scl = (128.0 ** -0.5) if blk < 4 else 1.0
                            K.op("dve", lambda: nc.vector.scalar_tensor_tensor(
                                out=qkn[:, blk, 0:n], in0=cq8[:, blk, 0:n], scalar=scl, in1=rr[:, 0:n],
                                op0=ALU.mult, op1=ALU.mult), R=[b_cq8[blk], b_rr], W=[b_qkn])
                        K.run_sched([(lambda blk=blk: normfn(blk), ([blk - 2] if blk >= 2 else [])) for blk in range(8)])
                        if is_ms:
                            for g3 in range(3):
                                q = g3 % 2
                                mm_group(psum[q][0:16, :], [(x_bf[:, k, 0:16], wA[:, k, g3 * 512:(g3 + 1) * 512])
                                                            for k in range(KC)], R=b_wAb[4 * g3:4 * g3 + 4] + [b_xbf], W=[b_ps[q]])
                                K.op("act", lambda g3=g3, q=q: nc.scalar.copy(out=pnew[:, g3 * 512:(g3 + 1) * 512],
                                                                             in_=psum[q][0:16, :]), R=[b_ps[q]], W=[b_pnew])
                            K.dma("sp", o_conv_s[l][:, 2, :], pnew[:], R=[b_pnew])
                        subs = subtiles(c0, n)
                        ths = []
                        for i_, (kind, sc0, nt) in enumerate(subs):
                            a_ = (sub_ctr[0] + i_) % 2
                            iA, iB1, iB2 = 3 * i_, 3 * i_ + 1, 3 * i_ + 2
                            afterA = ([iA - 3] if i_ >= 1 else []) + ([iB2 - 6] if i_ >= 2 else [])
                            afterB1 = [iA] + ([iB1 - 3] if i_ >= 1 else []) + ([iB2 - 6] if i_ >= 2 else [])
                            afterB2 = [iB1] + ([iB2 - 3] if i_ >= 1 else [])
                            ths.append((lambda kind=kind, sc0=sc0, nt=nt, a_=a_, c0=c0: stageA(kind, sc0, nt, a_, c0), afterA, PRIO_A))
                            ths.append((lambda kind=kind, sc0=sc0, nt=nt, a_=a_, c0=c0: stageB1(kind, sc0, nt, a_, c0), afterB1))
                            ths.append((lambda kind=kind, sc0=sc0, nt=nt, a_=a_, c0=c0: stageB2(kind, sc0, nt, a_, c0), afterB2))
                        K.run_sched(ths)
                        sub_ctr[0] += len(subs)
                        if is_ms:
                            K.op("dve", lambda: nc.vector.memset(oA[:, :, 0:16], 0.0), W=[b_oA])
                        gated_norm(st, oA, b_oA, gz, b_gz, nw, b_nw, og, b_og, 0, c0, n, nbufs)
                    K.dma("sp", o_gdn_p[l].rearrange("h k v -> k h v"), S[:], R=[b_S])
                    for g3 in range(3):
                        for b4 in range(4):
                            blk = 4 * g3 + b4
                            K.op("pe", lambda blk=blk, b4=b4, g3=g3: nc.tensor.transpose(
                                psum[g3][0:4, b4 * 128:(b4 + 1) * 128], halo[:, blk, :], C["ident"]),
                                R=[b_halo, b_cst], W=[b_ps[g3]], inc=(b4 == 3))
                        K.op("act", lambda g3=g3: nc.scalar.copy(out=pnew[0:4, g3 * 512:(g3 + 1) * 512], in_=psum[g3][0:4, :]),
                             R=[b_ps[g3]], W=[b_pnew])
                    K.dma("sp", o_conv_p[l], pnew[0:3, :], R=[b_pnew])
                K.barrier()
                with ExitStack() as s2:
                    wTm = sbt(s2, "d_wTm", [128, 4, NS, 16], BF16)
                    b_wTm = Buf()
                    K.op("dve", lambda: nc.vector.tensor_tensor(
                        out=wTm[:], in0=s_wT[:, :, :].unsqueeze(2).broadcast_to([128, 4, NS, 16]),
                        in1=cst[:, IDX["idrep0"]:IDX["idrep0"] + 2, :].rearrange("p a (s t) -> p (a s) t", t=16)
                            .unsqueeze(1).broadcast_to([128, 4, NS, 16]), op=ALU.mult),
                        R=[b_skeep, b_cst], W=[b_wTm])
                    Sg = [sbt(s2, "d_Sg%d" % i, [128, 4, 4, 128]) for i in range(2)]
                    Sgb = [sbt(s2, "d_Sgb%d" % i, [128, 4, 4, 128], BF16) for i in range(2)]
                    b_Sg = [Buf(), Buf()]
                    b_Sgb = [Buf(), Buf()]
                    us = sbt(s2, "d_us", [16, 4, 128], BF16)
                    ubd = sbt(s2, "d_ubd", [16, 4, 512], BF16)
                    b_us, b_ubd = Buf(), Buf()
                    oS = sbt(s2, "d_oS", [128, 4, 16])
                    b_oS = Buf()
                    sq2 = sbt(s2, "d_sq2", [128, 4, 512], BF16)
                    rsd2 = sbt(s2, "d_rsd2", [128, 512])
                    tn2 = sbt(s2, "d_tn2", [128, 512])
                    sgd_v = state_gdn[l].rearrange("s h k v -> k s h v")
                    sgd_o = o_gdn_s[l].rearrange("s h k v -> k s h v")
                    for sg_ in range(4):
                        u = sg_ % 2
                        K.dma("sp", Sg[u][:], sgd_v[:, 4 * sg_:4 * sg_ + 4], W=[b_Sg[u]])
                        K.op("act", lambda u=u: nc.scalar.copy(out=Sgb[u][:], in_=Sg[u][:]), R=[b_Sg[u]], W=[b_Sgb[u]])
                        for h in range(4):
                            for si in range(4):
                                s = 4 * sg_ + si
                                K.op("pe", lambda h=h, si=si, s=s, u=u: nc.tensor.matmul(
                                    psum[0][0:16, h * 128:(h + 1) * 128], lhsT=wTm[:, h, s, :], rhs=Sgb[u][:, si, h, :],
                                    start=(si == 0), stop=(si == 3)), R=[b_wTm, b_Sgb[u]], W=[b_ps[0]],
                                    inc=(h == 3 and si == 3))
                        K.op("dve", lambda: nc.vector.tensor_tensor(
                            out=us[:], in0=s_u0[:], in1=psum[0][0:16, :].rearrange("p (a c) -> p a c", a=4), op=ALU.subtract),
                            R=[b_skeep, b_ps[0]], W=[b_us])
                        K.op("dve", lambda sg_=sg_: nc.vector.tensor_tensor(
                            out=ubd[:], in0=us[:, :, :].rearrange("p h v -> p (h v)").unsqueeze(1).broadcast_to([16, 4, 512]),
                            in1=C["ident"][0:16, 4 * sg_:4 * sg_ + 4].unsqueeze(2).broadcast_to([16, 4, 512]), op=ALU.mult),
                            R=[b_us, b_cst], W=[b_ubd])
                        for si in range(4):
                            for h in range(4):
                                K.op("pe", lambda h=h, si=si: nc.tensor.matmul(
                                    psum[1 + si][:, h * 128:(h + 1) * 128], lhsT=s_kd[0:16, h, :],
                                    rhs=ubd[0:16, si, h * 128:(h + 1) * 128], start=True, stop=True),
                                    R=[b_skeep, b_ubd], W=[b_ps[1 + si]], inc=(h == 3))
                        for si in range(4):
                            s = 4 * sg_ + si
                            for h in range(4):
                                K.op("dve", lambda h=h, si=si, s=s, u=u: nc.vector.scalar_tensor_tensor(
                                    out=Sg[u][:, si, h, :], in0=Sg[u][:, si, h, :], scalar=s_G[:, s, h:h + 1],
                                    in1=psum[1 + si][:, h * 128:(h + 1) * 128], op0=ALU.mult, op1=ALU.add),
                                    R=[b_skeep, b_ps[1 + si]], W=[b_Sg[u]])
                        K.op("act", lambda u=u: nc.scalar.copy(out=Sgb[u][:], in_=Sg[u][:]), R=[b_Sg[u]], W=[b_Sgb[u]])
                        for si in range(4):
                            s = 4 * sg_ + si
                            for h in range(4):
                                K.op("pe", lambda h=h, si=si, s=s, u=u: nc.tensor.matmul(
                                    psum[5][:, h * 16 + s:h * 16 + s + 1], lhsT=Sgb[u][:, si, h, :], rhs=s_qn[:, h, s:s + 1],
                                    start=True, stop=True), R=[b_Sgb[u], b_skeep], W=[b_ps[5]],
                                    inc=(h == 3 and si == 3))
                        K.dma("sp", sgd_o[:, 4 * sg_:4 * sg_ + 4], Sg[u][:], R=[b_Sg[u]])
                    K.op("act", lambda: nc.scalar.copy(out=oS[:], in_=psum[5][:, 0:64].rearrange("p (a b) -> p a b", a=4)),
                         R=[b_ps[5]], W=[b_oS])
                    gated_norm(s2, oS, b_oS, s_gz, b_skeep, nw, b_nw, og, b_og, 0, 0, 16,
                               (sq2, Buf(), rsd2, Buf(), tn2, Buf()))

        for l in range(layers):
            last_layer = (l == layers - 1)
            win_v = w_in[l].rearrange("(k p) n -> p k n", p=128)

            with ExitStack() as pm:
              og = big[:, 0:8 * T].rearrange("p (k t) -> p k t", k=8)
              b_og = Buf("og")
              if stub_mixer:
                  K.op("dve", lambda: nc.vector.tensor_copy(out=og, in_=x_bf[:]), R=[b_xbf], W=[b_og])
              else:
                  if mix_sel in ("all", "gdn"):
                      gdn_phase(l, og, b_og, win_v)
                      K.barrier()
                  if mix_sel in ("all", "gla"):
                      gla_phase(l, og, b_og, win_v)
              K.barrier()
              chk("mixer")
              v = sbt(pm, "v", [128, KC, T])
              b_v = Buf("v")
              with ExitStack() as pmg:
                mg = sbt(pmg, "mg", [128, KC, T], BF16)
                b_mg = Buf("mg")
                with ExitStack() as pb1:
                    NW = 2
                    wab = [sbt(pb1, "wab%d" % i, [128, 8, 128], BF16) for i in range(NW)]
                    wgg = [sbt(pb1, "wgg%d" % i, [128, 16, 128], BF16) for i in range(NW)]
                    b_wab = [Buf() for _ in range(NW)]
                    b_wgg = [Buf() for _ in range(NW)]
                    sg = [sbt(pb1, "sg%d" % i, [128, 2, 512]) for i in range(2)]
                    b_sg = [Buf(), Buf()]
                    wa_v = w_branch_a[l].rearrange("(k p) n -> p k n", p=128)
                    wb_v = w_branch_b[l].rearrange("(k p) n -> p k n", p=128)
                    it = 0
                    for jo in range(KC):
                        s = jo % NW
                        cs = slice(jo * 128, (jo + 1) * 128)
                        K.dma("pool", wab[s][:, 0:4, :], wa_v[:, :, cs], W=[b_wab[s]])
                        K.dma("pool", wab[s][:, 4:8, :], wb_v[:, :, cs], W=[b_wab[s]])
                        K.dma("pool", wgg[s][:, 0:8, :], win_v[:, :, 3608 + jo * 128:3608 + (jo + 1) * 128], W=[b_wgg[s]])
                        K.dma("pool", wgg[s][:, 8:16, :], win_v[:, :, 4632 + jo * 128:4632 + (jo + 1) * 128], W=[b_wgg[s]])
                        for (c0, n) in TILES:
                            q = 4 * (it % 2)
                            t2 = it % 2
                            it += 1
                            cols = slice(c0, c0 + n)
                            mm_group(psum[q + 0][:, 0:n], [(wab[s][:, k, :], og[:, k, cols]) for k in range(4)],
                                     R=[b_wab[s], b_og], W=[b_ps[q + 0]])
                            mm_group(psum[q + 1][:, 0:n], [(wab[s][:, 4 + k, :], og[:, 4 + k, cols]) for k in range(4)],
                                     R=[b_wab[s], b_og], W=[b_ps[q + 1]])
                            mm_group(psum[q + 2][:, 0:n], [(wgg[s][:, k, :], x_bf[:, k, cols]) for k in range(8)],
                                     R=[b_wgg[s], b_xbf], W=[b_ps[q + 2]])
                            mm_group(psum[q + 3][:, 0:n], [(wgg[s][:, 8 + k, :], x_bf[:, k, cols]) for k in range(8)],
                                     R=[b_wgg[s], b_xbf], W=[b_ps[q + 3]])
                            K.op("act", lambda q=q, t2=t2, n=n: nc.scalar.activation(
                                out=sg[t2][:, 0, 0:n], in_=psum[q + 2][:, 0:n], func=AF.Sigmoid),
                                R=[b_ps[q + 2]], W=[b_sg[t2]])
                            K.op("act", lambda q=q, t2=t2, n=n: nc.scalar.activation(
                                out=sg[t2][:, 1, 0:n], in_=psum[q + 3][:, 0:n], func=AF.Sigmoid),
                                R=[b_ps[q + 3]], W=[b_sg[t2]])
                            K.op("dve", lambda q=q, t2=t2, n=n: nc.vector.tensor_tensor(
                                out=sg[t2][:, 0, 0:n], in0=sg[t2][:, 0, 0:n], in1=psum[q + 0][:, 0:n], op=ALU.mult),
                                R=[b_sg[t2], b_ps[q + 0]], W=[b_sg[t2]])
                            K.op("dve", lambda q=q, t2=t2, n=n: nc.vector.tensor_tensor(
                                out=sg[t2][:, 1, 0:n], in0=sg[t2][:, 1, 0:n], in1=psum[q + 1][:, 0:n], op=ALU.mult),
                                R=[b_sg[t2], b_ps[q + 1]], W=[b_sg[t2]])
                            K.op("dve", lambda t2=t2, n=n, jo=jo, cols=cols: nc.vector.tensor_tensor(
                                out=mg[:, jo, cols], in0=sg[t2][:, 0, 0:n], in1=sg[t2][:, 1, 0:n], op=ALU.add),
                                R=[b_sg[t2]], W=[b_mg])
                K.barrier()
                chk("b1")
                K.dma("sp", v[:], xres, R=[b_xres], W=[b_v])
                with ExitStack() as pb2:
                    wo = [sbt(pb2, "wo%d" % i, [128, 8, 128], BF16) for i in range(2)]
                    b_wo = [Buf(), Buf()]
                    wo_v = w_out[l].rearrange("(k p) n -> p k n", p=128)
                    it = 0
                    for jo in range(KC):
                        s = jo % 2
                        K.dma("pool", wo[s][:], wo_v[:, :, jo * 128:(jo + 1) * 128], W=[b_wo[s]])
                        for (c0, n) in TILES:
                            q = it % 8
                            it += 1
                            cols = slice(c0, c0 + n)
                            mm_group(psum[q][:, 0:n], [(wo[s][:, k, :], mg[:, k, cols]) for k in range(8)],
                                     R=[b_wo[s], b_mg], W=[b_ps[q]])
                            K.op("dve", lambda q=q, n=n, jo=jo, cols=cols: nc.vector.scalar_tensor_tensor(
                                out=v[:, jo, cols], in0=v[:, jo, cols], scalar=ALPHA, in1=psum[q][:, 0:n],
                                op0=ALU.mult, op1=ALU.add), R=[b_ps[q]], W=[b_v])
                K.barrier()
                chk("b2")
              if True:
                layer_norm(pm, v, b_v, l, 0)
                K.barrier()
                chk("ln1")

                nh = (len(TILES) + 1) // 2
                halves = [TILES[:nh], TILES[nh:]]
                fin_v = w_ffn_in[l].rearrange("(k p) n -> p k n", p=128)
                fout_v = w_ffn_out[l].rearrange("(c p) n -> p c n", p=128)
                with ExitStack() as pf:
                    hid = big[:, 0:FC * HW].rearrange("p (c t) -> p c t", c=FC)
                    b_hid = Buf("hid")
                    wfi = [sbt(pf, "wfi%d" % i, [128, 2, KC, 256], BF16) for i in range(2)]
                    b_wfi = [Buf(), Buf()]
                    wfo = [sbt(pf, "wfo%d" % i, [128, FC, 128], BF16) for i in range(2)]
                    b_wfo = [Buf(), Buf()]
                    sil = [sbt(pf, "sil%d" % i, [128, 512]) for i in range(2)]
                    b_sil = [Buf(), Buf()]
                    it = 0
                    wi_it = 0
                    wo_it = 0
                    for half in halves:
                        if not half:
                            continue
                        h0 = half[0][0]
                        for g in range(FC // 2):
                            s = wi_it % 2
                            wi_it += 1
                            K.dma("pool", wfi[s][:, 0, :, :], fin_v[:, :, g * 256:(g + 1) * 256], W=[b_wfi[s]])
                            K.dma("pool", wfi[s][:, 1, :, :], fin_v[:, :, DFF + g * 256:DFF + (g + 1) * 256], W=[b_wfi[s]])
                            for jj in range(2):
                                j = 2 * g + jj
                                for (c0, n) in half:
                                    q = 2 * (it % 4)
                                    t2 = it % 2
                                    it += 1
                                    cols = slice(c0, c0 + n)
                                    hc = slice(c0 - h0, c0 - h0 + n)
                                    mm_group(psum[q][:, 0:n],
                                             [(wfi[s][:, 0, k, jj * 128:(jj + 1) * 128], x_bf[:, k, cols]) for k in range(8)],
                                             R=[b_wfi[s], b_xbf], W=[b_ps[q]])
                                    mm_group(psum[q + 1][:, 0:n],
                                             [(wfi[s][:, 1, k, jj * 128:(jj + 1) * 128], x_bf[:, k, cols]) for k in range(8)],
                                             R=[b_wfi[s], b_xbf], W=[b_ps[q + 1]])
                                    K.op("act", lambda q=q, t2=t2, n=n: nc.scalar.activation(
                                        out=sil[t2][:, 0:n], in_=psum[q][:, 0:n], func=AF.Silu),
                                        R=[b_ps[q]], W=[b_sil[t2]])
                                    K.op("dve", lambda q=q, t2=t2, n=n, j=j, hc=hc: nc.vector.tensor_tensor(
                                        out=hid[:, j, hc], in0=sil[t2][:, 0:n], in1=psum[q + 1][:, 0:n], op=ALU.mult),
                                        R=[b_sil[t2], b_ps[q + 1]], W=[b_hid])
                        for jo in range(KC):
                            s = wo_it % 2
                            wo_it += 1
                            K.dma("pool", wfo[s][:], fout_v[:, :, jo * 128:(jo + 1) * 128], W=[b_wfo[s]])
                            for (c0, n) in half:
                                q = it % 8
                                it += 1
                                cols = slice(c0, c0 + n)
                                hc = slice(c0 - h0, c0 - h0 + n)
                                mm_group(psum[q][:, 0:n], [(wfo[s][:, c, :], hid[:, c, hc]) for c in range(FC)],
                                         R=[b_wfo[s], b_hid], W=[b_ps[q]])
                                K.op("dve", lambda q=q, n=n, jo=jo, cols=cols: nc.vector.scalar_tensor_tensor(
                                    out=v[:, jo, cols], in0=v[:, jo, cols], scalar=ALPHA, in1=psum[q][:, 0:n],
                                    op0=ALU.mult, op1=ALU.add), R=[b_ps[q]], W=[b_v])
                K.barrier()
                chk("ffn")
                layer_norm(pm, v, b_v, l, 1, write_bf=not last_layer)
                K.barrier()
                chk("ln2")
                if not last_layer:
                    K.dma("sp", xres, v[:], R=[b_v], W=[b_xres])
                else:
                    with ExitStack() as po:
                        NR = 3
                        yst = [sbt(po, "yst%d" % i, [128, D]) for i in range(NR)]
                        b_yst = [Buf() for _ in range(NR)]
                        rows = [("ms", 0, 32)] + [("p", 128 * i, 128) for i in range(TP // 128)]

                        def out_tile(ri):
                            kind, r0, n = rows[ri]
                            s = ri % NR
                            c0 = 0 if kind == "ms" else 32 + r0
                            for g in range(2):
                                pb = (2 * ri + g) % 8
                                for kk in range(4):
                                    k = 4 * g + kk
                                    K.op("pe", lambda k=k, kk=kk, pb=pb: nc.tensor.transpose(
                                        psum[pb][0:n, kk * 128:(kk + 1) * 128], v[:, k, c0:c0 + n], C["ident"]),
                                        R=[b_v, b_cst], W=[b_ps[pb]], inc=(kk == 3), c=0.12)
                                if g == 0:
                                    K.op("act", lambda pb=pb: nc.scalar.copy(
                                        out=yst[s][0:n, 0:512], in_=psum[pb][0:n, :]), R=[b_ps[pb]], W=[b_yst[s]])
                                else:
                                    K.op("dve", lambda pb=pb: nc.vector.tensor_copy(
                                        out=yst[s][0:n, 512:1024], in_=psum[pb][0:n, :]), R=[b_ps[pb]], W=[b_yst[s]])
                            if kind == "ms":
                                K.dma("sp", y_sample, yst[s][0:16, :], R=[b_yst[s]])
                            else:
                                K.dma("sp", y_prompt[r0:r0 + 128, :], yst[s][0:128, :], R=[b_yst[s]])

                        K.run_sched([(lambda ri=ri: out_tile(ri), ([ri - NR] if ri >= NR else [])) for ri in range(len(rows))])
              K.barrier()
      except _Stop:
        pass
      K.finish()
    return nc


_NC_CACHE = {}


def kernel(x_prompt, x_sample, state_gdn, state_gla, state_conv, meta_tokens, w_in, conv_w, a_log, dt_bias,
           gdn_norm_w, gla_gate_w2, gla_gate_b, gla_norm_w, w_branch_a, w_branch_b, w_out,
           ln1_g, ln1_b, ln2_g, ln2_b, w_ffn_in, w_ffn_out, _build_kwargs=None):
    f = lambda a: np.ascontiguousarray(np.asarray(a), dtype=np.float32)
    x_prompt = f(x_prompt)
    TP = x_prompt.shape[1]
    bk = dict(_build_kwargs or {})
    key = (TP, tuple(sorted(bk.items())))
    if key not in _NC_CACHE:
        _NC_CACHE[key] = build(TP=TP, **bk)
    nc = _NC_CACHE[key]
    shared = dict(meta_tokens=f(meta_tokens), w_in=f(w_in), conv_w=f(conv_w), a_log=f(a_log), dt_bias=f(dt_bias),
                  gdn_norm_w=f(gdn_norm_w), gla_gate_w2=f(gla_gate_w2), gla_gate_b=f(gla_gate_b),
                  gla_norm_w=f(gla_norm_w), w_branch_a=f(w_branch_a), w_branch_b=f(w_branch_b), w_out=f(w_out),
                  ln1_g=f(ln1_g), ln1_b=f(ln1_b), ln2_g=f(ln2_g), ln2_b=f(ln2_b),
                  w_ffn_in=f(w_ffn_in), w_ffn_out=f(w_ffn_out), consts=CONST_ARR)
    x_sample = f(x_sample)
    state_gdn = f(state_gdn)
    state_gla = f(state_gla)
    state_conv = f(state_conv)
    in_maps = []
    for c in range(NCORES):
        sl = slice(NS * c, NS * (c + 1))
        m = dict(shared)
        m["x_prompt"] = x_prompt[c]
        m["x_sample"] = np.ascontiguousarray(x_sample[sl, 0, :])
        m["state_gdn"] = np.ascontiguousarray(state_gdn[:, sl])
        m["state_gla"] = np.ascontiguousarray(state_gla[:, sl])
        m["state_conv"] = np.ascontiguousarray(state_conv[:, sl])
        in_maps.append(m)
    res = run_bass_kernel_spmd(nc, in_maps, core_ids=list(range(NCORES)))
    R = res.results
    y_prompt = np.stack([R[c]["y_prompt"] for c in range(NCORES)], axis=0)
    y_sample = np.concatenate([R[c]["y_sample"] for c in range(NCORES)], axis=0)[:, None, :]
    gdn_p = np.stack([R[c]["new_gdn_prompt"] for c in range(NCORES)], axis=1)
    gla_p = np.stack([R[c]["new_gla_prompt"] for c in range(NCORES)], axis=1)
    conv_p = np.stack([R[c]["new_conv_prompt"] for c in range(NCORES)], axis=1)
    gdn_s = np.concatenate([R[c]["new_gdn_sample"] for c in range(NCORES)], axis=1)
    gla_s = np.concatenate([R[c]["new_gla_sample"] for c in range(NCORES)], axis=1)
    conv_s = np.concatenate([R[c]["new_conv_sample"] for c in range(NCORES)], axis=1)
    outs = (y_prompt, y_sample, gdn_p, gla_p, conv_p, gdn_s, gla_s, conv_s)
    return tuple(np.ascontiguousarray(o, dtype=np.float32) for o in outs)
```

```python
import threading
import numpy as np
from contextlib import ExitStack
import concourse.bass as bass
import concourse.mybir as mybir
from concourse.bass_utils import run_bass_kernel_spmd

F32 = mybir.dt.float32
BF16 = mybir.dt.bfloat16
F32R = mybir.dt.float32r
AF = mybir.ActivationFunctionType
ALU = mybir.AluOpType

D = 1024
KC = 8
DEPTH = 2
NS = 16
NMETA = 16
D_IN = 5656
DFF = 2816
FC = DFF // 128
ALPHA = (2.0 * DEPTH) ** 0.25
NCORES = 8


class Buf:
    __slots__ = ("w", "r", "name", "excl")

    def __init__(self, name="", excl=False):
        self.w = None
        self.r = {}
        self.name = name
        self.excl = excl


_tls = threading.local()


class _Worker:
    def __init__(self, fn):
        self.fn = fn
        self.req = None
        self.done = False
        self.exc = None
        self.ev_req = threading.Event()
        self.ev_go = threading.Event()
        self.th = threading.Thread(target=self._run, daemon=True)

    def _run(self):
        _tls.worker = self
        try:
            self.ev_go.wait()
            self.ev_go.clear()
            r = self.fn()
            if r is not None and hasattr(r, "__next__"):
                for _ in r:
                    pass
        except BaseException as e:
            self.exc = e
        finally:
            self.done = True
            self.req = None
            self.ev_req.set()

    def post(self, req):
        self.req = req
        self.ev_req.set()
        self.ev_go.wait()
        self.ev_go.clear()


class Sched:
    ENG = ("pe", "act", "dve", "pool", "sp")
    COST = {"pe": 0.25, "act": 0.6, "dve": 0.7, "pool": 0.5, "sp": 0.1}

    def __init__(self, nc, es, n_dma_slots=24):
        self.nc = nc
        self.e = {"pe": nc.tensor, "act": nc.scalar, "dve": nc.vector, "pool": nc.gpsimd, "sp": nc.sync}
        self.sem = {}
        self.cnt = {}
        for k in self.ENG:
            self.sem[k] = es.enter_context(nc.semaphore("s_" + k))
            self.cnt[k] = 0
        self.seen = {k: {} for k in self.ENG}
        self.pending = {k: False for k in self.ENG}
        self.nslots = n_dma_slots
        self.slot_sem = [es.enter_context(nc.semaphore("s_dma%d" % i)) for i in range(n_dma_slots)]
        self.slot_uses = [0] * n_dma_slots
        self.slot_next = 0
        self.semobj = dict(self.sem)
        for i in range(n_dma_slots):
            self.semobj[("dma", i)] = self.slot_sem[i]
        self.tfree = {k: 0.0 for k in self.ENG}
        self.tdone = {}

    def _est(self, eng, R, W):
        t = self.tfree[eng]
        def dep(key):
            d = self.tdone.get(key)
            if d is None:
                d = self.tfree.get(key[0], 0.0) if not isinstance(key[0], tuple) else 0.0
            return d + (0.05 if key[0] == eng else 0.3)
        for b in R:
            if b.w is not None:
                t = max(t, dep(b.w))
            if b.excl:
                for k, v in b.r.items():
                    t = max(t, dep((k, v)))
        for b in W:
            if b.w is not None:
                t = max(t, dep(b.w))
            for k, v in b.r.items():
                t = max(t, dep((k, v)))
        return t

    def run_sched(self, threads):
        n = len(threads)
        workers = [None] * n
        started, finished = set(), set()

        def advance(w):
            w.ev_req.clear()
            w.ev_go.set()
            w.ev_req.wait()
            if w.exc is not None:
                raise w.exc

        while len(finished) < n:
            for i, th_ in enumerate(threads):
                fn, after = th_[0], th_[1]
                if i not in started and all(j in finished for j in after):
                    started.add(i)
                    workers[i] = _Worker(fn)
                    workers[i].th.start()
                    advance(workers[i])
                    if workers[i].done:
                        finished.add(i)
            cands = [i for i in started if i not in finished]
            if not cands:
                if len(finished) < n and len(started) == len(finished):
                    rem = [i for i in range(n) if i not in started]
                    assert any(all(j in finished for j in threads[i][1]) for i in rem), "scheduler deadlock"
                continue
            best = min(cands, key=lambda i: (self._est(*workers[i].req) - (threads[i][2] if len(threads[i]) > 2 else 0.0), i))
            advance(workers[best])
            if workers[best].done:
                finished.add(best)

    def _collect(self, eng, R, W, extra=()):
        need = {}
        def add(k, v):
            if v > need.get(k, 0):
                need[k] = v
        for b in R:
            if b.w is not None:
                add(*b.w)
        for b in W:
            if b.w is not None:
                add(*b.w)
            for k, v in b.r.items():
                add(k, v)
        for k, v in extra:
            add(k, v)
        out = []
        for k, v in need.items():
            if k == "pe" and eng == "pe":
                continue
            if k == eng and v > self.cnt[eng]:
                continue
            if v > self.seen[eng].get(k, 0):
                out.append((k, v))
        return out

    def op(self, eng, fn, R=(), W=(), inc=True, extra=(), c=None):
        w_ = getattr(_tls, "worker", None)
        if w_ is not None:
            w_.post((eng, tuple(R), tuple(W)))
        t0_ = self._est(eng, R, W)
        t1_ = t0_ + (c if c is not None else self.COST[eng])
        self.tfree[eng] = t1_
        if any(b.excl for b in R):
            W = list(W) + [b for b in R if b.excl]
            R = [b for b in R if not b.excl]
        waits = self._collect(eng, R, W, extra)
        e = self.e[eng]
        if eng == "pe":
            for k, v in waits:
                e.wait_ge(self.semobj[k], v)
                self.seen[eng][k] = v
            waits = []
        for k, v in waits[:-1]:
            e.wait_ge(self.semobj[k], v)
            self.seen[eng][k] = v
        inst = fn()
        if waits:
            k, v = waits[-1]
            inst.wait_op(self.semobj[k], v, "sem-ge")
            self.seen[eng][k] = v
        if inc:
            inst.then_inc(self.sem[eng], 1)
            self.cnt[eng] += 1
            cc = self.cnt[eng]
            self.pending[eng] = False
            self.tdone[(eng, cc)] = t1_
        else:
            cc = self.cnt[eng] + 1
            self.pending[eng] = True
        for b in R:
            if b.r.get(eng, 0) < cc:
                b.r[eng] = cc
        for b in W:
            b.w = (eng, cc)
            b.r = {}
        return inst

    def dma(self, eng, out, in_, R=(), W=(), **kw):
        w_ = getattr(_tls, "worker", None)
        if w_ is not None:
            w_.post((eng, tuple(R), tuple(W)))
        t0_ = self._est(eng, R, W)
        self.tfree[eng] = t0_ + 0.1
        s = self.slot_next
        self.slot_next = (self.slot_next + 1) % self.nslots
        key = ("dma", s)
        prev = 16 * self.slot_uses[s]
        extra = [(key, prev)] if prev > 0 else []
        waits = self._collect(eng, R, W, extra)
        e = self.e[eng]
        for k, v in waits:
            e.wait_ge(self.semobj[k], v)
            self.seen[eng][k] = v
        inst = e.dma_start(out=out, in_=in_, **kw)
        self.slot_uses[s] += 1
        val = 16 * self.slot_uses[s]
        inst.then_inc(self.slot_sem[s], 16)
        self.tdone[(key, val)] = t0_ + 3.0
        for b in R:
            b.r[key] = val
        for b in W:
            b.w = (key, val)
            b.r = {}
        return inst

    def barrier(self):
        tgt = [(k, self.cnt[k]) for k in self.ENG if self.cnt[k] > 0]
        tgt += [(("dma", i), 16 * self.slot_uses[i]) for i in range(self.nslots) if self.slot_uses[i] > 0]
        for eng in self.ENG:
            assert not self.pending[eng]
            for k, v in tgt:
                if k == eng:
                    continue
                if v > self.seen[eng].get(k, 0):
                    self.e[eng].wait_ge(self.semobj[k], v)
                    self.seen[eng][k] = v

    def finish(self):
        self.barrier()


def make_consts():
    r = np.arange(128)
    same = (r[:, None] // 64) == (r[None, :] // 64)
    c = {}
    c["ident"] = np.eye(128, dtype=np.float32)
    c["ones"] = np.ones((128, 128), dtype=np.float32)
    c["mu_incl"] = (same & (r[None, :] >= r[:, None])).astype(np.float32)
    c["mu_strict"] = (same & (r[None, :] > r[:, None])).astype(np.float32)
    c["ml_strict"] = (same & (r[:, None] > r[None, :])).astype(np.float32)
    c["bd"] = same.astype(np.float32)
    c["blk0"] = np.repeat((r < 64).astype(np.float32)[:, None], 128, axis=1)
    c["blk1"] = np.repeat((r >= 64).astype(np.float32)[:, None], 128, axis=1)
    c["zero"] = np.zeros((128, 128), dtype=np.float32)
    idrep = np.tile(np.eye(16, dtype=np.float32).reshape(1, 256), (128, 1))
    c["idrep0"] = idrep[:, 0:128]
    c["idrep1"] = idrep[:, 128:256]
    names = ["ident", "ones", "mu_incl", "mu_strict", "ml_strict", "bd", "blk0", "blk1", "zero", "idrep0", "idrep1"]
    arr = np.stack([c[n] for n in names], axis=1)
    return names, np.ascontiguousarray(arr.astype(np.float32))


CONST_NAMES, CONST_ARR = make_consts()
NCONST = len(CONST_NAMES)
IDX = {n: i for i, n in enumerate(CONST_NAMES)}


class _Stop(Exception):
    pass


def build(TP=2048, stub_mixer=False, layers=DEPTH, dbg=False, stop=None, mix_sel="all", PRIO_A=0.0):
    nc = bass.Bass("TRN2", target_bir_lowering=False)
    NPOS = NMETA + TP
    T = 32 + TP
    NPT = max(TP // 512, 1)
    TILES = [(0, 32)] + [(32 + 512 * i, 512) for i in range(NPT)]

    def din(name, shape, dt=F32):
        return nc.dram_tensor(name, list(shape), dt, kind="ExternalInput").ap()

    def dout(name, shape, dt=F32):
        return nc.dram_tensor(name, list(shape), dt, kind="ExternalOutput").ap()

    x_prompt = din("x_prompt", [TP, D])
    x_sample = din("x_sample", [NS, D])
    meta = din("meta_tokens", [NMETA, D])
    state_gdn = din("state_gdn", [DEPTH, NS, 4, 128, 128])
    state_gla = din("state_gla", [DEPTH, NS, 4, 64, 128])
    state_conv = din("state_conv", [DEPTH, NS, 3, 1536])
    w_in = din("w_in", [DEPTH, D, D_IN])
    conv_w = din("conv_w", [DEPTH, 4, 1536])
    a_log = din("a_log", [DEPTH, 4])
    dt_bias = din("dt_bias", [DEPTH, 4])
    gdn_norm_w = din("gdn_norm_w", [DEPTH, 128])
    gla_gate_w2 = din("gla_gate_w2", [DEPTH, 16, 256])
    gla_gate_b = din("gla_gate_b", [DEPTH, 256])
    gla_norm_w = din("gla_norm_w", [DEPTH, 128])
    w_branch_a = din("w_branch_a", [DEPTH, 512, D])
    w_branch_b = din("w_branch_b", [DEPTH, 512, D])
    w_out = din("w_out", [DEPTH, D, D])
    ln1_g = din("ln1_g", [DEPTH, D])
    ln1_b = din("ln1_b", [DEPTH, D])
    ln2_g = din("ln2_g", [DEPTH, D])
    ln2_b = din("ln2_b", [DEPTH, D])
    w_ffn_in = din("w_ffn_in", [DEPTH, D, 2 * DFF])
    w_ffn_out = din("w_ffn_out", [DEPTH, DFF, D])
    consts_d = din("consts", [128, NCONST, 128])

    y_prompt = dout("y_prompt", [TP, D])
    y_sample = dout("y_sample", [NS, D])
    o_gdn_p = dout("new_gdn_prompt", [DEPTH, 4, 128, 128])
    o_gla_p = dout("new_gla_prompt", [DEPTH, 4, 64, 128])
    o_conv_p = dout("new_conv_prompt", [DEPTH, 3, 1536])
    o_gdn_s = dout("new_gdn_sample", [DEPTH, NS, 4, 128, 128])
    o_gla_s = dout("new_gla_sample", [DEPTH, NS, 4, 64, 128])
    o_conv_s = dout("new_conv_sample", [DEPTH, NS, 3, 1536])
    dbg_out = dout("dbg", [128, KC, T]) if dbg else None

    xres = nc.dram_tensor("xres_scratch", [128, KC, T], F32, kind="Internal").ap()

    es = ExitStack()
    with es:
      K = Sched(nc, es)

      def chk(name):
          if stop == name:
              raise _Stop()

      try:

        _uid = [0]

        def sbt(stack, name, shape, dt=F32):
            _uid[0] += 1
            return stack.enter_context(nc.sbuf_tensor("%s_%d" % (name, _uid[0]), list(shape), dt))

        cst = sbt(es, "cst", [128, NCONST, 128], F32)
        cstb = sbt(es, "cstb", [128, NCONST, 128], BF16)
        b_cst = Buf("cst")
        C = {n: cst[:, i, :] for i, n in enumerate(CONST_NAMES)}
        CB = {n: cstb[:, i, :] for i, n in enumerate(CONST_NAMES)}
        x_bf = sbt(es, "x_bf", [128, KC, T], BF16)
        b_xbf = Buf("x_bf")
        lnp = sbt(es, "lnp", [128, DEPTH, 4, KC], F32)
        b_lnp = Buf("lnp")
        psp = [es.enter_context(nc.psum_tensor("psp%d" % i, [128, 1024], F32)) for i in range(4)]
        psum = [psp[i // 2][:, (i % 2) * 512:(i % 2) * 512 + 512] for i in range(8)]
        ps67 = psp[3][:, :]
        b_ps = [Buf("ps%d" % i, excl=True) for i in range(8)]
        b_xres = Buf("xres")

        epst = sbt(es, "epst", [128, 2], F32)
        b_eps = Buf("eps")
        K.op("dve", lambda: nc.vector.memset(epst[:, 0:1], 1e-6), W=[b_eps])
        K.op("dve", lambda: nc.vector.memset(epst[:, 1:2], 1.0), W=[b_eps])
        eps6 = epst[:, 0:1]
        one1 = epst[:, 1:2]
        nh_ = (len(TILES) + 1) // 2
        HW = max(sum(n for _, n in TILES[:nh_]), sum(n for _, n in TILES[nh_:]))
        BIGN = max(FC * HW, 8 * T)
        big = sbt(es, "big", [128, BIGN], BF16)
        K.dma("sp", cst[:], consts_d, W=[b_cst])
        K.op("act", lambda: nc.scalar.copy(out=cstb[:], in_=cst[:]), R=[b_cst], W=[b_cst])
        ones_r = sbt(es, "ones_r", [128, 128], F32)
        b_onesr = Buf("ones_r")
        K.op("act", lambda: nc.scalar.copy(out=ones_r[:].bitcast(F32R), in_=C["ones"]), R=[b_cst], W=[b_onesr])
        for l in range(DEPTH):
            for wi, src in enumerate((ln1_g, ln1_b, ln2_g, ln2_b)):
                K.dma("sp", lnp[:, l, wi, :], src[l].rearrange("(k p) -> p k", p=128), W=[b_lnp],
                      allow_slow_non_contiguous=True)

        chk("init")

        def run_threads(gens):
            active = list(gens)
            while active:
                for g in list(active):
                    try:
                        next(g)
                    except StopIteration:
                        active.remove(g)

        def run_pipeline(items, mkA, mkB, nA=2, nbuf=3):
            n_it = len(items)
            nextA, nextB = 0, 0
            activeA = {}
            doneA = set()
            curB = None
            while nextB < n_it:
                while nextA < n_it and len(activeA) < nA and (nextA - nextB) < nbuf:
                    activeA[nextA] = mkA(items[nextA], nextA)
                    nextA += 1
                if curB is None and nextB in doneA:
                    curB = mkB(items[nextB], nextB)
                for i, gen in list(activeA.items()):
                    try:
                        next(gen)
                    except StopIteration:
                        del activeA[i]
                        doneA.add(i)
                if curB is not None:
                    try:
                        next(curB)
                    except StopIteration:
                        curB = None
                        nextB += 1

        def mm_group(ps_ap, pairs, R, W, inc=True):
            n = len(pairs)
            for i, (lt, rh) in enumerate(pairs):
                last = (i == n - 1)
                K.op("pe", lambda lt=lt, rh=rh, i=i, last=last: nc.tensor.matmul(
                    ps_ap, lhsT=lt, rhs=rh, start=(i == 0), stop=last),
                    R=R, W=W, inc=(inc and last))

        with ExitStack() as p0:
            NR = 3
            xin = [sbt(p0, "xin%d" % i, [128, D]) for i in range(NR)]
            b_xin = [Buf() for _ in range(NR)]
            xst = [sbt(p0, "xst%d" % i, [128, KC, 128]) for i in range(NR)]
            b_xst = [Buf() for _ in range(NR)]
            rows = [("ms", 0, 32)] + [("p", 128 * i, 128) for i in range(TP // 128)]

            def p0_tile(ri):
                kind, r0, n = rows[ri]
                s = ri % NR
                if kind == "ms":
                    K.dma("sp", xin[s][0:16, :], x_sample, W=[b_xin[s]])
                    K.dma("sp", xin[s][16:32, :], meta, W=[b_xin[s]])
                    c0 = 0
                else:
                    K.dma("sp", xin[s][0:128, :], x_prompt[r0:r0 + 128, :], W=[b_xin[s]])
                    c0 = 32 + r0
                for g in range(2):
                    pb = (2 * ri + g) % 8
                    for kk in range(4):
                        k = 4 * g + kk
                        K.op("pe", lambda k=k, kk=kk, pb=pb: nc.tensor.transpose(
                            psum[pb][:, kk * 128:kk * 128 + n], xin[s][0:n, k * 128:(k + 1) * 128],
                            C["ident"][0:n, 0:n]), R=[b_xin[s], b_cst], W=[b_ps[pb]], inc=(kk == 3), c=0.12)
                    src = psum[pb][:, :].rearrange("p (a b) -> p a b", a=4)[:, :, 0:n]
                    K.op("act", lambda src=src, g=g: nc.scalar.copy(
                        out=x_bf[:, 4 * g:4 * g + 4, c0:c0 + n], in_=src), R=[b_ps[pb]], W=[b_xbf])
                    K.op("dve", lambda src=src, g=g: nc.vector.tensor_copy(
                        out=xst[s][:, 4 * g:4 * g + 4, 0:n], in_=src), R=[b_ps[pb]], W=[b_xst[s]])
                K.dma("sp", xres[:, :, c0:c0 + n], xst[s][:, :, 0:n], R=[b_xst[s]])

            K.run_sched([(lambda ri=ri: p0_tile(ri), ([ri - NR] if ri >= NR else [])) for ri in range(len(rows))])
        K.barrier()
        chk("p0")

        def layer_norm(stack, v, b_v, l, which, write_bf=True):
            g_i, b_i = 2 * which, 2 * which + 1
            with ExitStack() as st:
                vb_ = [sbt(st, "ln_vb%d" % i, [128, KC, 512], BF16) for i in range(2)]
                sq_ = [sbt(st, "ln_sq%d" % i, [128, KC, 512], BF16) for i in range(2)]
                mt_ = [sbt(st, "ln_m%d" % i, [128, 512]) for i in range(2)]
                m2_ = [sbt(st, "ln_m2%d" % i, [128, 512]) for i in range(2)]
                rs_ = [sbt(st, "ln_rs%d" % i, [128, 512]) for i in range(2)]
                nm_ = [sbt(st, "ln_nm%d" % i, [128, 512]) for i in range(2)]
                bb = [[Buf() for _ in range(6)] for _ in range(2)]
                b_vts = []
                for _ in TILES:
                    t_ = Buf()
                    t_.w = b_v.w
                    t_.r = dict(b_v.r)
                    b_vts.append(t_)

                def ln_tile(ti):
                    c0, n = TILES[ti]
                    u_ = ti % 2
                    vb, sq, mt, m2, rs, nm = vb_[u_], sq_[u_], mt_[u_], m2_[u_], rs_[u_], nm_[u_]
                    b_vb, b_sq, b_mt, b_m2, b_rs, b_nm = bb[u_]
                    b_v = b_vts[ti]
                    cols = slice(c0, c0 + n)
                    pm_, pq_ = (2 * ti) % 8, (2 * ti + 1) % 8
                    K.op("act", lambda cols=cols, n=n: nc.scalar.copy(out=vb[:, :, 0:n], in_=v[:, :, cols]),
                         R=[b_v], W=[b_vb])
                    K.op("act", lambda cols=cols, n=n: nc.scalar.activation(
                        out=sq[:, :, 0:n], in_=v[:, :, cols], func=AF.Square), R=[b_v], W=[b_sq])
                    mm_group(psum[pm_][:, 0:n], [(CB["ones"], vb[:, k, 0:n]) for k in range(KC)],
                             R=[b_vb, b_cst], W=[b_ps[pm_]])
                    mm_group(psum[pq_][:, 0:n], [(CB["ones"], sq[:, k, 0:n]) for k in range(KC)],
                             R=[b_sq, b_cst], W=[b_ps[pq_]])
                    K.op("dve", lambda n=n, pm_=pm_: nc.vector.tensor_scalar(
                        out=mt[:, 0:n], in0=psum[pm_][:, 0:n], scalar1=1.0 / D, scalar2=None, op0=ALU.mult),
                        R=[b_ps[pm_]], W=[b_mt])
                    K.op("dve", lambda n=n: nc.vector.tensor_tensor(
                        out=m2[:, 0:n], in0=mt[:, 0:n], in1=mt[:, 0:n], op=ALU.mult), R=[b_mt], W=[b_m2])
                    K.op("dve", lambda n=n, pq_=pq_: nc.vector.scalar_tensor_tensor(
                        out=m2[:, 0:n], in0=psum[pq_][:, 0:n], scalar=1.0 / D, in1=m2[:, 0:n],
                        op0=ALU.mult, op1=ALU.subtract), R=[b_ps[pq_], b_m2], W=[b_m2])
                    K.op("dve", lambda n=n: nc.vector.tensor_scalar(
                        out=m2[:, 0:n], in0=m2[:, 0:n], scalar1=1e-5, scalar2=None, op0=ALU.add),
                        R=[b_m2], W=[b_m2])
                    K.op("act", lambda n=n: nc.scalar.activation(
                        out=rs[:, 0:n], in_=m2[:, 0:n], func=AF.Ln), R=[b_m2], W=[b_rs])
                    K.op("act", lambda n=n: nc.scalar.activation(
                        out=rs[:, 0:n], in_=rs[:, 0:n], func=AF.Exp, scale=-0.5), R=[b_rs], W=[b_rs])
                    K.op("dve", lambda n=n: nc.vector.scalar_tensor_tensor(
                        out=nm[:, 0:n], in0=mt[:, 0:n], scalar=-1.0, in1=rs[:, 0:n],
                        op0=ALU.mult, op1=ALU.mult), R=[b_mt, b_rs], W=[b_nm])
                    K.op("dve", lambda n=n, cols=cols: nc.vector.tensor_tensor(
                        out=v[:, :, cols], in0=v[:, :, cols],
                        in1=rs[:, 0:n].unsqueeze(1).broadcast_to([128, KC, n]), op=ALU.mult),
                        R=[b_rs], W=[b_v])
                    K.op("dve", lambda n=n, cols=cols: nc.vector.tensor_tensor(
                        out=v[:, :, cols], in0=v[:, :, cols],
                        in1=nm[:, 0:n].unsqueeze(1).broadcast_to([128, KC, n]), op=ALU.add),
                        R=[b_nm], W=[b_v])
                    for k in range(KC):
                        K.op("act", lambda k=k, n=n, cols=cols: nc.scalar.activation(
                            out=v[:, k, cols], in_=v[:, k, cols], func=AF.Identity,
                            scale=lnp[:, l, g_i, k:k + 1], bias=lnp[:, l, b_i, k:k + 1]),
                            R=[b_lnp], W=[b_v])
                    if write_bf:
                        K.op("pool", lambda cols=cols: nc.gpsimd.tensor_copy(out=x_bf[:, :, cols], in_=v[:, :, cols]),
                             R=[b_v], W=[b_xbf])

                K.run_sched([(lambda ti=ti: ln_tile(ti), ([ti - 2] if ti >= 2 else [])) for ti in range(len(TILES))])

        def subtiles(c0, n):
            if c0 == 0:
                return [("sample", 0, 16), ("meta", 16, 16)]
            return [("prompt", c0 + 128 * i, 128) for i in range(n // 128)]

        def gated_norm(st, oB, b_oB, gz, b_gz, nw, b_nw, og, b_og, hbase, c0, n, tagbufs):
            sq, b_sq, rsd, b_rsd, tn, b_tn = tagbufs
            K.op("act", lambda: nc.scalar.activation(out=sq[:, :, 0:n], in_=oB[:, :, 0:n], func=AF.Square),
                 R=[b_oB], W=[b_sq])
            for h in range(4):
                q = h % 2
                mm_group(psum[q][:, 0:n], [(CB["ones"], sq[:, h, 0:n])], R=[b_sq, b_cst], W=[b_ps[q]])
                K.op("act", lambda q=q: nc.scalar.activation(out=rsd[:, 0:n], in_=psum[q][:, 0:n], func=AF.Ln,
                                                             scale=1.0 / 128, bias=eps6[:, 0:1]),
                     R=[b_ps[q], b_eps], W=[b_rsd])
                K.op("act", lambda: nc.scalar.activation(out=rsd[:, 0:n], in_=rsd[:, 0:n], func=AF.Exp, scale=-0.5),
                     R=[b_rsd], W=[b_rsd])
                K.op("dve", lambda h=h: nc.vector.tensor_tensor(out=tn[:, 0:n], in0=oB[:, h, 0:n], in1=rsd[:, 0:n],
                                                                op=ALU.mult), R=[b_oB, b_rsd], W=[b_tn])
                K.op("dve", lambda h=h: nc.vector.scalar_tensor_tensor(
                    out=og[:, hbase + h, c0:c0 + n], in0=tn[:, 0:n], scalar=nw[:, 0:1], in1=gz[:, h, 0:n],
                    op0=ALU.mult, op1=ALU.mult), R=[b_tn, b_nw, b_gz], W=[b_og])

        def gla_phase(l, og, b_og, win_v):
            with ExitStack() as st:
                wB = sbt(st, "wB", [128, KC, 1552], BF16)
                b_wBb = [Buf() for _ in range(13)]
                for blk_ in range(12):
                    K.dma("pool", wB[:, :, blk_ * 128:(blk_ + 1) * 128],
                          win_v[:, :, 2056 + blk_ * 128:2056 + (blk_ + 1) * 128], W=[b_wBb[blk_]])
                K.dma("pool", wB[:, :, 1536:1552], win_v[:, :, 2056 + 1536:2056 + 1552], W=[b_wBb[12]])
                w2e = sbt(st, "w2e", [17, 256])
                b_w2e = Buf()
                K.dma("sp", w2e[0:16, :], gla_gate_w2[l], W=[b_w2e])
                K.dma("sp", w2e[16:17, :], gla_gate_b[l].rearrange("(o n) -> o n", o=1), W=[b_w2e])
                nw = sbt(st, "gla_nw", [128, 1])
                b_nw = Buf()
                K.dma("sp", nw[:, :], gla_norm_w[l].rearrange("(p o) -> p o", o=1), W=[b_nw],
                      allow_slow_non_contiguous=True)
                qk = sbt(st, "g_qk", [128, 4, 512])
                b_qk = Buf()
                sr = sbt(st, "g_sr", [128, 4, 512], BF16)
                b_sr = Buf()
                glr = sbt(st, "g_glr", [17, 512])
                b_glr = Buf()
                oB = sbt(st, "g_oB", [128, 4, 512])
                b_oB = Buf()
                sq = sbt(st, "g_sq", [128, 4, 512], BF16)
                rsd = sbt(st, "g_rsd", [128, 512])
                tn = sbt(st, "g_tn", [128, 512])
                nbufs = (sq, Buf(), rsd, Buf(), tn, Buf())
                Lt = [sbt(st, "g_L%d" % i, [128, 256]) for i in range(3)]
                E12 = [sbt(st, "g_E%d" % i, [128, 2, 2, 128]) for i in range(3)]
                atot = [sbt(st, "g_at%d" % i, [128, 2, 2]) for i in range(3)]
                qdk = [sbt(st, "g_qdk%d" % i, [128, 2, 2, 128], BF16) for i in range(3)]
                qdm = [sbt(st, "g_qdm%d" % i, [128, 4, 128], BF16) for i in range(3)]
                b_qdm = [Buf(), Buf(), Buf()]
                qm = sbt(st, "g_qm", [128, 4, 18])
                b_qm = Buf()
                E3 = [sbt(st, "g_E3%d" % i, [128, 256]) for i in range(3)]
                kdec = [sbt(st, "g_kd%d" % i, [128, 256], BF16) for i in range(3)]
                vtb = [sbt(st, "g_vt%d" % i, [128, 512], BF16) for i in range(3)]
                att = [sbt(st, "g_att%d" % i, [128, 4, 128], BF16) for i in range(3)]
                b_L, b_E12, b_at, b_qdk, b_E3, b_kd, b_vt, b_att = ([Buf(), Buf(), Buf()] for _ in range(8))
                S = sbt(st, "g_S", [128, 2, 128])
                Sb = sbt(st, "g_Sb", [128, 2, 128], BF16)
                b_S, b_Sb = Buf(), Buf()
                K.op("dve", lambda: nc.vector.memset(S[:], 0.0), W=[b_S])
                K.op("dve", lambda: nc.vector.memset(Sb[:], 0.0), W=[b_Sb])
                K.op("dve", lambda: nc.vector.memset(glr[:], 1.0), W=[b_glr])
                Ss = sbt(st, "g_Ss", [128, NS, 2, 128])
                b_Ss = Buf()
                sgl_v = state_gla[l].rearrange("s (pr hf) k v -> (hf k) s pr v", hf=2)
                for s4 in range(4):
                    K.dma("sp", Ss[:, 4 * s4:4 * s4 + 4], sgl_v[:, 4 * s4:4 * s4 + 4], W=[b_Ss])
                vbd = sbt(st, "g_vbd", [16, NS, 512], BF16)
                b_vbd = Buf()
                sub_ctr = [0]
                def glaA(kind, sc0, nt, u, bx, by, c0):
                    lc = sc0 - c0
                    if kind == "sample":
                        m_incl, m_strict_l = C["ident"], C["zero"]
                        mb_incl = CB["ident"]
                    else:
                        m_incl, m_strict_l = C["mu_incl"], C["ml_strict"]
                        mb_incl = CB["mu_incl"]
                    yield
                    mm_group(psum[bx][0:nt, 0:256], [(x_bf[:, k, sc0:sc0 + nt], wB[:, k, 256:512]) for k in range(KC)],
                             R=b_wBb[2:4] + [b_xbf], W=[b_ps[bx]])
                    yield
                    mm_group(psum[by][0:nt, 0:512], [(x_bf[:, k, sc0:sc0 + nt], wB[:, k, 512:1024]) for k in range(KC)],
                             R=b_wBb[4:8] + [b_xbf], W=[b_ps[by]])
                    yield
                    mm_group(psum[bx][0:nt, 256:512], [(glr[0:17, lc:lc + nt], w2e[0:17, :])],
                             R=[b_glr, b_w2e], W=[b_ps[bx]])
                    yield
                    K.op("act", lambda u=u: nc.scalar.activation(out=Lt[u][0:nt, :], in_=psum[bx][0:nt, 256:512],
                                                                 func=AF.Exp, scale=-1.0), R=[b_ps[bx]], W=[b_L[u]])
                    yield
                    K.op("act", lambda u=u: nc.scalar.activation(out=Lt[u][0:nt, :], in_=Lt[u][0:nt, :], func=AF.Ln,
                                                                 bias=one1[0:nt, 0:1]), R=[b_L[u], b_eps], W=[b_L[u]])
                    yield
                    K.op("act", lambda u=u: nc.scalar.copy(out=vtb[u][0:nt, :], in_=psum[by][0:nt, :]),
                         R=[b_ps[by]], W=[b_vt[u]])
                    yield
                    for pr_ in range(2):
                        mm_group(psum[by][:, pr_ * 128:pr_ * 128 + nt],
                                 [(Lt[u][0:nt, pr_ * 128:(pr_ + 1) * 128], m_incl[0:nt, 0:nt])],
                                 R=[b_L[u], b_cst], W=[b_ps[by]])
                    yield
                    mm_group(psum[by][0:nt, 256:512], [(m_strict_l[0:nt, 0:nt], Lt[u][0:nt, :])],
                             R=[b_L[u], b_cst], W=[b_ps[by]])
                    csT = psum[by][:, 0:256].rearrange("p (a b) -> p a b", a=2)[:, :, 0:nt]
                    yield
                    K.op("act", lambda u=u, csT=csT: nc.scalar.activation(out=E12[u][:, 0, :, 0:nt], in_=csT,
                                                                          func=AF.Exp, scale=-1.0 / 16),
                         R=[b_ps[by]], W=[b_E12[u]])
                    yield
                    K.op("act", lambda u=u, csT=csT: nc.scalar.activation(out=E12[u][:, 1, :, 0:nt], in_=csT,
                                                                          func=AF.Exp, scale=1.0 / 16),
                         R=[b_ps[by]], W=[b_E12[u]])
                    nch = 2 if nt == 128 else 1
                    cl = 64 if nt == 128 else nt
                    yield
                    for ch in range(nch):
                        lastc = ch * 64 + cl - 1
                        K.op("act", lambda u=u, ch=ch, lastc=lastc: nc.scalar.activation(
                            out=atot[u][:, :, ch:ch + 1],
                            in_=psum[by][:, 0:256].rearrange("p (a b) -> p a b", a=2)[:, :, lastc:lastc + 1],
                            func=AF.Exp, scale=-1.0 / 16), R=[b_ps[by]], W=[b_at[u]])
                    yield
                    K.op("act", lambda u=u: nc.scalar.activation(out=E3[u][0:nt, :], in_=psum[by][0:nt, 256:512],
                                                                 func=AF.Exp, scale=-1.0 / 16),
                         R=[b_ps[by]], W=[b_E3[u]])
                    yield
                    K.op("dve", lambda u=u, lc=lc: nc.vector.scalar_tensor_tensor(
                        out=qdk[u][:, 0, :, 0:nt], in0=qk[:, 0:2, lc:lc + nt], scalar=0.125, in1=E12[u][:, 0, :, 0:nt],
                        op0=ALU.mult, op1=ALU.mult), R=[b_qk, b_E12[u]], W=[b_qdk[u]])
                    yield
                    K.op("dve", lambda u=u, lc=lc: nc.vector.tensor_tensor(
                        out=qdk[u][:, 1, :, 0:nt], in0=qk[:, 2:4, lc:lc + nt], in1=E12[u][:, 1, :, 0:nt], op=ALU.mult),
                        R=[b_qk, b_E12[u]], W=[b_qdk[u]])
                    yield
                    K.op("dve", lambda u=u: nc.vector.tensor_tensor(
                        out=kdec[u][0:nt, :], in0=psum[bx][0:nt, 0:256], in1=E3[u][0:nt, :], op=ALU.mult),
                        R=[b_ps[bx], b_E3[u]], W=[b_kd[u]])
                    yield
                    if kind == "sample":
                        for h in range(4):
                            K.op("dve", lambda h=h: nc.vector.tensor_scalar(
                                out=qm[:, h, :], in0=qk[:, h // 2, 0:18], scalar1=C["blk%d" % (h % 2)][:, 0:1],
                                scalar2=None, op0=ALU.mult), R=[b_qk, b_cst], W=[b_qm])
                        gla_sample(l, u, qm, b_qm, E12, b_E12, kdec, b_kd, vtb, b_vt, Ss, b_Ss, vbd, b_vbd, oB, b_oB)
                        return
                    yield
                    for h in range(4):
                        K.op("dve", lambda h=h, u=u: nc.vector.tensor_scalar(
                            out=qdm[u][:, h, 0:nt], in0=qdk[u][:, 0, h // 2, 0:nt], scalar1=C["blk%d" % (h % 2)][:, 0:1],
                            scalar2=None, op0=ALU.mult), R=[b_qdk[u], b_cst], W=[b_qdm[u]])
                    yield
                    for h in range(4):
                        pr_, hf = h // 2, h % 2
                        rows = slice(hf * 64, hf * 64 + 64)
                        mm_group(psum[bx][0:nt, h * 128:h * 128 + nt],
                                 [(qdk[u][:, 1, pr_, 0:nt], qdm[u][:, h, 0:nt])],
                                 R=[b_qdk[u], b_qdm[u]], W=[b_ps[bx]], inc=(h == 3))
                    yield
                    K.op("dve", lambda u=u: nc.vector.tensor_tensor(
                        out=att[u][0:nt, :, 0:nt],
                        in0=psum[bx][0:nt, :].rearrange("p (a b) -> p a b", a=4)[:, :, 0:nt],
                        in1=m_incl[0:nt, 0:nt].unsqueeze(1).broadcast_to([nt, 4, nt]), op=ALU.mult),
                        R=[b_ps[bx], b_cst], W=[b_att[u]])

                def glaB(kind, sc0, nt, u, c0):
                    lc = sc0 - c0
                    nch = 2 if nt == 128 else 1
                    cl = 64 if nt == 128 else nt
                    for ch in range(nch):
                        trow = slice(ch * 64, ch * 64 + cl)
                        yield
                        for h in range(4):
                            pr_, hf = h // 2, h % 2
                            rows = slice(hf * 64, hf * 64 + 64)
                            K.op("pe", lambda h=h, pr_=pr_, trow=trow, u=u: nc.tensor.matmul(
                                psum[6][:, h * 64:h * 64 + cl], lhsT=Sb[:, pr_, :], rhs=qdm[u][:, h, trow],
                                start=True, stop=False), R=[b_Sb, b_qdm[u]], W=[b_ps[6]], inc=False)
                            K.op("pe", lambda h=h, trow=trow, u=u: nc.tensor.matmul(
                                psum[6][:, h * 64:h * 64 + cl], lhsT=vtb[u][0:nt, h * 128:(h + 1) * 128],
                                rhs=att[u][0:nt, h, trow], start=False, stop=True),
                                R=[b_vt[u], b_att[u]], W=[b_ps[6]], inc=(h == 3))
                        yield
                        for h in range(4):
                            pr_, hf = h // 2, h % 2
                            K.op("pe", lambda h=h, pr_=pr_, hf=hf, trow=trow, u=u: nc.tensor.matmul(
                                psum[7][hf * 64:hf * 64 + 64, pr_ * 128:(pr_ + 1) * 128],
                                lhsT=kdec[u][trow, h * 64:(h + 1) * 64], rhs=vtb[u][trow, h * 128:(h + 1) * 128],
                                start=True, stop=True), R=[b_kd[u], b_vt[u]], W=[b_ps[7]], inc=(h == 3))
                        yield
                        K.op("act", lambda lc=lc, trow=trow, ch=ch: nc.scalar.copy(
                            out=oB[:, :, lc + ch * 64:lc + ch * 64 + cl],
                            in_=psum[6][:, 0:256].rearrange("p (a b) -> p a b", a=4)[:, :, 0:cl]),
                            R=[b_ps[6]], W=[b_oB])
                        yield
                        for pr_ in range(2):
                            K.op("dve", lambda pr_=pr_, ch=ch, u=u: nc.vector.scalar_tensor_tensor(
                                out=S[:, pr_, :], in0=S[:, pr_, :], scalar=atot[u][:, pr_, ch:ch + 1],
                                in1=psum[7][:, pr_ * 128:(pr_ + 1) * 128], op0=ALU.mult, op1=ALU.add),
                                R=[b_at[u], b_ps[7]], W=[b_S])
                        yield
                        K.op("act", lambda: nc.scalar.copy(out=Sb[:], in_=S[:]), R=[b_S], W=[b_Sb])

                    yield

                for (c0, n) in TILES:
                    for bi in range(9):
                        q = bi % 2
                        if bi < 8:
                            woff = bi * 128 if bi < 4 else 1024 + (bi - 4) * 128
                            mw = 128
                        else:
                            woff, mw = 1536, 16
                        mm_group(psum[q][0:mw, 0:n], [(wB[:, k, woff:woff + mw], x_bf[:, k, c0:c0 + n]) for k in range(KC)],
                                 R=[b_wBb[woff // 128], b_xbf], W=[b_ps[q]])
                        if bi < 4:
                            K.op("dve", lambda bi=bi, q=q: nc.vector.tensor_copy(out=qk[:, bi, 0:n], in_=psum[q][:, 0:n]),
                                 R=[b_ps[q]], W=[b_qk])
                        elif bi < 8:
                            K.op("act", lambda bi=bi, q=q: nc.scalar.activation(
                                out=sr[:, bi - 4, 0:n], in_=psum[q][:, 0:n], func=AF.Silu), R=[b_ps[q]], W=[b_sr])
                        else:
                            K.op("dve", lambda q=q: nc.vector.tensor_copy(out=glr[0:16, 0:n], in_=psum[q][0:16, 0:n]),
                                 R=[b_ps[q]], W=[b_glr])
                    subs = subtiles(c0, n)
                    ths = []
                    for i_, (kind, sc0, nt) in enumerate(subs):
                        gi = sub_ctr[0] + i_
                        afterA = ([2 * (i_ - 2)] if i_ >= 2 else []) + ([2 * (i_ - 3) + 1] if i_ >= 3 else [])
                        afterB = [2 * i_] + ([2 * (i_ - 1) + 1] if i_ >= 1 else [])
                        ths.append((lambda kind=kind, sc0=sc0, nt=nt, gi=gi, c0=c0: glaA(kind, sc0, nt, gi % 3, 2 + 2 * (gi % 2), 3 + 2 * (gi % 2), c0), afterA))
                        if kind == "sample":
                            ths.append((lambda: None, afterB))
                        else:
                            ths.append((lambda kind=kind, sc0=sc0, nt=nt, gi=gi, c0=c0: glaB(kind, sc0, nt, gi % 3, c0), afterB))
                    K.run_sched(ths)
                    sub_ctr[0] += len(subs)
                    gated_norm(st, oB, b_oB, sr, b_sr, nw, b_nw, og, b_og, 4, c0, n, nbufs)
                K.dma("sp", o_gla_p[l].rearrange("(pr hf) k v -> (hf k) pr v", hf=2), S[:], R=[b_S])

        def gla_sample(l, u, qm, b_qm, E12, b_E12, kdec, b_kd, vtb, b_vt, Ss, b_Ss, vbd, b_vbd, oB, b_oB):
            K.op("dve", lambda: nc.vector.tensor_tensor(
                out=vbd[:, :, :], in0=vtb[u][0:16, :].unsqueeze(1).broadcast_to([16, NS, 512]),
                in1=C["ident"][0:16, 0:16].unsqueeze(2).broadcast_to([16, NS, 512]), op=ALU.mult),
                R=[b_vt[u], b_cst], W=[b_vbd])
            for s4 in range(4):
                for si in range(4):
                    s = 4 * s4 + si
                    for h in range(4):
                        pr_, hf = h // 2, h % 2
                        q = 6 + si // 2
                        cb = (si % 2) * 256 + pr_ * 128
                        K.op("pe", lambda s=s, h=h, hf=hf, q=q, cb=cb: nc.tensor.matmul(
                            psum[q][hf * 64:hf * 64 + 64, cb:cb + 128],
                            lhsT=kdec[u][0:16, h * 64:(h + 1) * 64], rhs=vbd[0:16, s, h * 128:(h + 1) * 128],
                            start=True, stop=True), R=[b_kd[u], b_vbd], W=[b_ps[q]], inc=(h == 3))
                for si in range(4):
                    s = 4 * s4 + si
                    q = 6 + si // 2
                    for pr_ in range(2):
                        cb = (si % 2) * 256 + pr_ * 128
                        K.op("dve", lambda s=s, pr_=pr_, q=q, cb=cb: nc.vector.scalar_tensor_tensor(
                            out=Ss[:, s, pr_, :], in0=Ss[:, s, pr_, :], scalar=E12[u][:, 0, pr_, s:s + 1],
                            in1=psum[q][:, cb:cb + 128], op0=ALU.mult, op1=ALU.add),
                            R=[b_E12[u], b_ps[q]], W=[b_Ss])
            for s in range(NS):
                for h in range(4):
                    pr_, hf = h // 2, h % 2
                    K.op("pe", lambda s=s, h=h, pr_=pr_: nc.tensor.matmul(
                        psum[5][:, h * 32 + s:h * 32 + s + 2], lhsT=Ss[:, s, pr_, :], rhs=qm[:, h, s:s + 2],
                        start=True, stop=True), R=[b_Ss, b_qm], W=[b_ps[5]], inc=(s == NS - 1 and h == 3))
            K.op("act", lambda: nc.scalar.activation(
                out=oB[:, :, 0:16], in_=psum[5][:, 0:128].rearrange("p (a b) -> p a b", a=4)[:, :, 0:16],
                func=AF.Copy, scale=0.125), R=[b_ps[5]], W=[b_oB])
            sgl_o = o_gla_s[l].rearrange("s (pr hf) k v -> (hf k) s pr v", hf=2)
            for s4 in range(4):
                K.dma("sp", sgl_o[:, 4 * s4:4 * s4 + 4], Ss[:, 4 * s4:4 * s4 + 4], R=[b_Ss])

        def gdn_phase(l, og, b_og, win_v):
            keep = {}
            with ExitStack() as st0:
                s_wT = sbt(st0, "d_swT", [128, 4, 16], BF16)
                s_u0 = sbt(st0, "d_su0", [16, 4, 128])
                s_kd = sbt(st0, "d_skd", [16, 4, 128], BF16)
                s_qn = sbt(st0, "d_sqn", [128, 4, 16], BF16)
                s_G = sbt(st0, "d_sG", [128, NS, 4])
                s_gz = sbt(st0, "d_sgz", [128, 4, 16], BF16)
                b_skeep = Buf()
                nw = sbt(st0, "gdn_nw", [128, 1])
                b_nw = Buf()
                K.dma("sp", nw[:, :], gdn_norm_w[l].rearrange("(p o) -> p o", o=1), W=[b_nw],
                      allow_slow_non_contiguous=True)
                with ExitStack() as st:
                    wA = sbt(st, "wA", [128, KC, 2056], BF16)
                    b_wAb = [Buf() for _ in range(17)]
                    for blk_ in range(16):
                        K.dma("pool", wA[:, :, blk_ * 128:(blk_ + 1) * 128], win_v[:, :, blk_ * 128:(blk_ + 1) * 128],
                              W=[b_wAb[blk_]])
                    K.dma("pool", wA[:, :, 2048:2056], win_v[:, :, 2048:2056], W=[b_wAb[16]])
                    cw = sbt(st, "d_cw", [128, 12, 4])
                    b_cw = Buf()
                    for i_ in range(4):
                        K.dma("sp", cw[:, :, i_], conv_w[l][i_].rearrange("(b p) -> p b", p=128), W=[b_cw],
                              allow_slow_non_contiguous=True)
                    abt = sbt(st, "d_abt", [128, 2, 4])
                    b_abt = Buf()
                    K.dma("sp", abt[:, 0, :], a_log[l].partition_broadcast(128), W=[b_abt])
                    K.dma("sp", abt[:, 1, :], dt_bias[l].partition_broadcast(128), W=[b_abt])
                    K.op("act", lambda: nc.scalar.activation(out=abt[:, 0, :], in_=abt[:, 0, :], func=AF.Exp),
                         R=[b_abt], W=[b_abt])
                    K.op("dve", lambda: nc.vector.tensor_scalar(out=abt[:, 0, :], in0=abt[:, 0, :], scalar1=-1.0,
                                                                scalar2=None, op0=ALU.mult), R=[b_abt], W=[b_abt])
                    Xs = sbt(st, "d_Xs", [128, 4, 2, 128])
                    tok3 = [sbt(st, "d_tok3%d" % i, [128, 3, 4, 128], BF16) for i in range(2)]
                    cbT = tok3[0][:, :, :, :].rearrange("p a h c -> p (a h c)").bitcast(F32)[:, 0:576].rearrange("p (b s) -> p b s", b=12)
                    b_cbT = Buf()
                    with ExitStack() as stc:
                        cb48 = sbt(stc, "d_cb48", [48, 1536])
                        b_cb48 = Buf()
                        K.dma("sp", cb48[:], state_conv[l].rearrange("s i c -> (s i) c"), W=[b_cb48])
                        for half in range(2):
                            q = half
                            for b6 in range(6):
                                blk = 6 * half + b6
                                K.op("pe", lambda blk=blk, b6=b6, q=q: nc.tensor.transpose(
                                    psum[q][:, b6 * 48:(b6 + 1) * 48], cb48[0:48, blk * 128:(blk + 1) * 128],
                                    C["ident"][0:48, 0:48]), R=[b_cb48, b_cst], W=[b_ps[q]], inc=(b6 == 5))
                            K.op("act", lambda half=half, q=q: nc.scalar.copy(
                                out=cbT[:, 6 * half:6 * half + 6, :],
                                in_=psum[q][:, 0:288].rearrange("p (a b) -> p a b", a=6)), R=[b_ps[q]], W=[b_cbT])
                        K.barrier()
                    K.dma("sp", o_conv_s[l][:, 0:2, :], state_conv[l][:, 1:3, :])
                    halo = sbt(st, "d_halo", [128, 12, 4])
                    b_halo = Buf()
                    K.op("dve", lambda: nc.vector.memset(halo[:], 0.0), W=[b_halo])
                    Pe = [sbt(st, "d_Pe%d" % i, [128, 3 + 512]) for i in range(2)]
                    b_Pe = [Buf(), Buf()]
                    acc = [sbt(st, "d_acc%d" % i, [128, 512]) for i in range(2)]
                    b_acc = [Buf(), Buf()]
                    if 4 * T >= 8192:
                        cq8 = big[:, 4 * T:4 * T + 8192].bitcast(F32).rearrange("p (a b) -> p a b", a=8)
                    else:
                        cq8 = sbt(st, "d_cq8", [128, 8, 512])
                    b_cq8 = [Buf() for _ in range(8)]
                    b_sqs = [Buf(), Buf()]
                    vn = sbt(st, "d_vn", [128, 4, 512], BF16)
                    b_vn = Buf()
                    qkn = sbt(st, "d_qkn", [128, 8, 512], BF16)
                    b_qkn = Buf()
                    gz = sbt(st, "d_gz", [128, 4, 512], BF16)
                    b_gz = Buf()
                    oA = big[:, 8 * T:8 * T + 4096].bitcast(F32).rearrange("p (a b) -> p a b", a=4)
                    b_oA = Buf()
                    sq = big[:, 8 * T + 4096:8 * T + 6144].rearrange("p (a b) -> p a b", a=4)
                    rsd = sbt(st, "d_rsd", [128, 512])
                    rsd_b = sbt(st, "d_rsdb", [128, 512])
                    tn = acc[0]
                    b_sq, b_rsd, b_tn, b_rsdb = Buf(), Buf(), b_acc[0], Buf()
                    nbufs = (sq, b_sq, rsd, b_rsd, tn, b_tn)
                    pnew = big[0:16, 8 * T:8 * T + 3072].bitcast(F32)
                    b_pnew = b_oA
                    tb = [sbt(st, "d_tb%d" % i, [128, 6, 4]) for i in range(2)]
                    b_tb = [Buf(), Buf()]
                    r2 = sbt(st, "d_r2", [128, 8])
                    b_r2 = Buf()
                    Gbc = [sbt(st, "d_Gbc%d" % i, [128, 2, 4]) for i in range(2)]
                    b_Gbc = [Buf(), Buf()]
                    dg = sbt(st, "d_dg", [128, 8, 128])
                    b_dg = Buf()
                    gm = sbt(st, "d_gm", [128, 4, 128])
                    b_gm = Buf()
                    ET = sbt(st, "d_ET", [128, 4, 128])
                    ETi = sbt(st, "d_ETi", [128, 4, 128])
                    ETs = ET
                    b_ET, b_ETi = Buf(), Buf()
                    b_ETs = b_ET
                    QA = [sbt(st, "d_QA%d" % i, [128, 4, 128]) for i in range(2)]
                    XA = [sbt(st, "d_XA%d" % i, [128, 4, 2, 128]) for i in range(2)]
                    QTA = [XA[i][:, :, 0, :] for i in range(2)]
                    b_QA = [Buf(), Buf()]
                    b_QTA = [Buf(), Buf()]
                    Qs = sbt(st, "d_Qs", [128, 4, 128])
                    QTs = Xs[:, :, 0, :]
                    b_Qs, b_QTs, b_Xst = Buf(), Buf(), Buf()
                    TT = [XA[i][:, :, 1, :] for i in range(2)]
                    b_TT = [Buf(), Buf()]
                    TTb = sbt(st, "d_TTb", [128, 4, 128], BF16)
                    b_TTb = Buf()
                    bkq = [sbt(st, "d_bkq%d" % i, [128, 2, 4, 128], BF16) for i in range(2)]
                    b_bkq = [Buf(), Buf()]
                    qkT = [sbt(st, "d_qkT%d" % i, [128, 4, 128], BF16) for i in range(2)]
                    b_qkT = [Buf(), Buf()]
                    b_tok3 = [Buf(), Buf()]
                    u0 = [sbt(st, "d_u0%d" % i, [128, 4, 128]) for i in range(2)]
                    wT = [sbt(st, "d_wT%d" % i, [128, 4, 128], BF16) for i in range(2)]
                    uu = sbt(st, "d_u", [128, 4, 128], BF16)
                    b_u0, b_wT, b_u = [Buf(), Buf()], [Buf(), Buf()], Buf()
                    S = sbt(st, "d_S", [128, 4, 128])
                    Sb = sbt(st, "d_Sb", [128, 4, 128], BF16)
                    b_S, b_Sb = Buf(), Buf()
                    K.op("dve", lambda: nc.vector.memset(S[:], 0.0), W=[b_S])
                    K.op("dve", lambda: nc.vector.memset(Sb[:], 0.0), W=[b_Sb])
                    K.op("dve", lambda: nc.vector.memset(uu[:], 0.0), W=[b_u])
                    sub_ctr = [0]
                    scan_done = [0]
                    tb_all = sbt(st, "d_tball", [128, 4, 6, 4])
                    Gbc_all = sbt(st, "d_gball", [128, 4, 2, 4])
                    r2_all = sbt(st, "d_r2all", [128, 4, 2, 4])
                    b_tball, b_gball, b_r2all = Buf(), Buf(), Buf()

                    def stageT(subs, c0):
                        ns = len(subs)
                        nt = 128
                        m_incl, m_bd = C["mu_incl"], C["bd"]
                        for si, (kind, sc0, _) in enumerate(subs):
                            mm_group(psum[2][0:nt, 8 * si:8 * si + 8], [(x_bf[:, k, sc0:sc0 + nt], wA[:, k, 2048:2056]) for k in range(KC)],
                                     R=[b_wAb[16], b_xbf], W=[b_ps[2]])
                        raw = psum[2][0:nt, 0:8 * ns].rearrange("p (s c) -> p s c", s=ns)
                        TA = lambda i: tb_all[0:nt, 0:ns, i, :]
                        bcs = lambda ap: ap.unsqueeze(1).broadcast_to([nt, ns, 4])
                        K.op("act", lambda: nc.scalar.activation(out=TA(0), in_=raw[:, :, 0:4], func=AF.Exp, scale=-1.0),
                             R=[b_ps[2]], W=[b_tball])
                        K.op("dve", lambda: nc.vector.tensor_tensor(out=TA(2), in0=raw[:, :, 4:8], in1=bcs(abt[0:nt, 1, :]),
                                                                    op=ALU.add), R=[b_ps[2], b_abt], W=[b_tball])
                        K.op("act", lambda: nc.scalar.activation(out=TA(0), in_=TA(0), func=AF.Ln, bias=one1[0:nt, 0:1]),
                             R=[b_tball, b_eps], W=[b_tball])
                        K.op("act", lambda: nc.scalar.activation(out=TA(0), in_=TA(0), func=AF.Exp, scale=-1.0),
                             R=[b_tball], W=[b_tball])
                        K.op("act", lambda: nc.scalar.activation(out=TA(2), in_=TA(2), func=AF.Exp), R=[b_tball], W=[b_tball])
                        K.op("act", lambda: nc.scalar.activation(out=TA(2), in_=TA(2), func=AF.Ln, bias=one1[0:nt, 0:1]),
                             R=[b_tball, b_eps], W=[b_tball])
                        K.op("dve", lambda: nc.vector.tensor_tensor(out=TA(2), in0=TA(2), in1=bcs(abt[0:nt, 0, :]), op=ALU.mult),
                             R=[b_tball, b_abt], W=[b_tball])
                        gam_ps = psum[2][0:nt, 32:32 + 4 * ns].rearrange("p (s c) -> p s c", s=ns)
                        gto_ps = psum[2][0:nt, 48:48 + 4 * ns].rearrange("p (s c) -> p s c", s=ns)
                        mm_group(gam_ps, [(m_incl[0:nt, 0:nt], TA(2))], R=[b_tball, b_cst], W=[b_ps[2]])
                        mm_group(gto_ps, [(m_bd[0:nt, 0:nt], TA(2))], R=[b_tball, b_cst], W=[b_ps[2]])
                        K.op("act", lambda: nc.scalar.activation(out=TA(1), in_=gam_ps, func=AF.Exp), R=[b_ps[2]], W=[b_tball])
                        K.op("dve", lambda: nc.vector.tensor_copy(out=TA(5), in_=gam_ps), R=[b_ps[2]], W=[b_tball])
                        K.op("dve", lambda: nc.vector.tensor_tensor(out=TA(3), in0=gto_ps, in1=TA(5), op=ALU.subtract),
                             R=[b_ps[2], b_tball], W=[b_tball])
                        K.op("act", lambda: nc.scalar.activation(out=TA(3), in_=TA(3), func=AF.Exp), R=[b_tball], W=[b_tball])
                        K.op("dve", lambda: nc.vector.tensor_tensor(out=TA(4), in0=TA(0), in1=TA(1), op=ALU.mult),
                             R=[b_tball], W=[b_tball])
                        K.op("dve", lambda: nc.vector.tensor_tensor(
                            out=r2_all[0:nt, 0:ns], in0=TA(2).unsqueeze(2).broadcast_to([nt, ns, 2, 4]),
                            in1=cst[0:nt, IDX["blk0"]:IDX["blk0"] + 2, 0:1].unsqueeze(1).broadcast_to([nt, ns, 2, 4]), op=ALU.mult),
                            R=[b_tball, b_cst], W=[b_r2all])
                        mm_group(psum[2][:, 64:64 + 8 * ns], [(C["ones"][0:nt, :], r2_all[0:nt, 0:ns].rearrange("p s c h -> p (s c h)"))],
                                 R=[b_r2all, b_cst], W=[b_ps[2]])
                        K.op("act", lambda: nc.scalar.activation(out=Gbc_all[:, 0:ns].rearrange("p s c h -> p (s c h)"),
                                                                 in_=psum[2][:, 64:64 + 8 * ns], func=AF.Exp),
                             R=[b_ps[2]], W=[b_gball])

                    def stageA(kind, sc0, nt, a, c0, si=None):
                        TB, b_TB = (tb[a], b_tb[a]) if si is None else (tb_all[:, si], b_tball)
                        GB, b_GB = (Gbc[a], b_Gbc[a]) if si is None else (Gbc_all[:, si], b_gball)
                        lc = sc0 - c0
                        smp = (kind == "sample")
                        m_incl = C["ident"] if smp else C["mu_incl"]
                        m_bd = C["ident"] if smp else C["bd"]
                        m_ustrict = C["zero"] if smp else C["mu_strict"]
                        m_lstrict = C["zero"] if smp else C["ml_strict"]
                        T_ = lambda i: TB[0:nt, i, :]
                        pv = lambda b: psum[b][:, :].rearrange("p (a c) -> p a c", a=4)[:, :, 0:nt]
                        pt = lambda b: psum[b][0:nt, :].rearrange("p (a c) -> p a c", a=4)[:, :, 0:nt]
                        p4 = lambda b: psum[b][0:nt, :].rearrange("p (a c) -> p a c", a=4)
                        bc = lambda i: TB[0:nt, i, :].unsqueeze(2).broadcast_to([nt, 4, 128])
                        if si is None:
                            mm_group(psum[2][0:nt, 0:8], [(x_bf[:, k, sc0:sc0 + nt], wA[:, k, 2048:2056]) for k in range(KC)],
                                     R=[b_wAb[16], b_xbf], W=[b_ps[2]])
                            yield
                            K.op("act", lambda: nc.scalar.activation(out=T_(0), in_=psum[2][0:nt, 0:4], func=AF.Exp, scale=-1.0),
                                 R=[b_ps[2]], W=[b_TB])
                            yield
                            K.op("dve", lambda: nc.vector.tensor_tensor(out=T_(2), in0=psum[2][0:nt, 4:8], in1=abt[0:nt, 1, :],
                                                                        op=ALU.add), R=[b_ps[2], b_abt], W=[b_TB])
                            yield
                            K.op("act", lambda: nc.scalar.activation(out=T_(0), in_=T_(0), func=AF.Ln, bias=one1[0:nt, 0:1]),
                                 R=[b_TB, b_eps], W=[b_TB])
                            yield
                            K.op("act", lambda: nc.scalar.activation(out=T_(0), in_=T_(0), func=AF.Exp, scale=-1.0),
                                 R=[b_TB], W=[b_TB])
                            yield
                            K.op("act", lambda: nc.scalar.activation(out=T_(2), in_=T_(2), func=AF.Exp), R=[b_TB], W=[b_TB])
                            yield
                            K.op("act", lambda: nc.scalar.activation(out=T_(2), in_=T_(2), func=AF.Ln, bias=one1[0:nt, 0:1]),
                                 R=[b_TB, b_eps], W=[b_TB])
                            yield
                            K.op("dve", lambda: nc.vector.tensor_tensor(out=T_(2), in0=T_(2), in1=abt[0:nt, 0, :], op=ALU.mult),
                                 R=[b_TB, b_abt], W=[b_TB])
                            yield
                            mm_group(psum[2][0:nt, 8:12], [(m_incl[0:nt, 0:nt], T_(2))], R=[b_TB, b_cst], W=[b_ps[2]])
                            yield
                            mm_group(psum[2][0:nt, 12:16], [(m_bd[0:nt, 0:nt], T_(2))], R=[b_TB, b_cst], W=[b_ps[2]])
                            yield
                            K.op("act", lambda: nc.scalar.activation(out=T_(1), in_=psum[2][0:nt, 8:12], func=AF.Exp),
                                 R=[b_ps[2]], W=[b_TB])
                            yield
                            K.op("dve", lambda: nc.vector.tensor_copy(out=T_(5), in_=psum[2][0:nt, 8:12]), R=[b_ps[2]], W=[b_TB])
                            yield
                            K.op("dve", lambda: nc.vector.tensor_tensor(out=T_(3), in0=psum[2][0:nt, 12:16], in1=T_(5),
                                                                        op=ALU.subtract), R=[b_ps[2], b_TB], W=[b_TB])
                            yield
                            K.op("act", lambda: nc.scalar.activation(out=T_(3), in_=T_(3), func=AF.Exp), R=[b_TB], W=[b_TB])
                            yield
                            K.op("dve", lambda: nc.vector.tensor_tensor(out=T_(4), in0=T_(0), in1=T_(1), op=ALU.mult),
                                 R=[b_TB], W=[b_TB])
                            yield
                            if smp:
                                K.op("dve", lambda: nc.vector.tensor_tensor(
                                    out=dg[0:16, 0:4, 0:16].bitcast(F32R).rearrange("p h s -> p s h"),
                                    in0=T_(2).unsqueeze(1).broadcast_to([16, 16, 4]),
                                    in1=C["ident"][0:16, 0:16].unsqueeze(2).broadcast_to([16, 16, 4]), op=ALU.mult),
                                    R=[b_TB, b_cst], W=[b_dg])
                                for h in range(4):
                                    K.op("pe", lambda h=h: nc.tensor.matmul(psum[3][:, h * 16:(h + 1) * 16],
                                                                            lhsT=C["ones"][0:16, :], rhs=dg[0:16, h, 0:16],
                                                                            start=True, stop=True),
                                         R=[b_dg, b_cst], W=[b_ps[3]], inc=(h == 3))
                                K.op("act", lambda: nc.scalar.activation(
                                    out=s_G[:, :, :].rearrange("p s h -> p h s"),
                                    in_=psum[3][:, 0:64].rearrange("p (h s) -> p h s", h=4), func=AF.Exp),
                                    R=[b_ps[3]], W=[b_skeep])
                            else:
                                K.op("dve", lambda: nc.vector.tensor_tensor(
                                    out=r2[0:nt, :].rearrange("p (c h) -> p c h", c=2),
                                    in0=T_(2).unsqueeze(1).broadcast_to([nt, 2, 4]),
                                    in1=cst[0:nt, IDX["blk0"]:IDX["blk0"] + 2, 0:1].broadcast_to([nt, 2, 4]), op=ALU.mult),
                                    R=[b_TB, b_cst], W=[b_r2])
                                mm_group(psum[2][:, 16:24], [(C["ones"][0:nt, :], r2[0:nt, :])], R=[b_r2, b_cst], W=[b_ps[2]])
                                K.op("act", lambda: nc.scalar.activation(out=GB[:, :, :].rearrange("p c h -> p (c h)"),
                                                                         in_=psum[2][:, 16:24], func=AF.Exp),
                                     R=[b_ps[2]], W=[b_GB])
                        yield
                        K.op("dve", lambda: nc.vector.tensor_tensor(
                            out=dg[0:nt, :, 0:nt].bitcast(F32R), in0=C["ident"][0:nt, 0:nt].unsqueeze(1).broadcast_to([nt, 8, nt]),
                            in1=TB[0:nt, 0:2, :].rearrange("p a h -> p (a h)").unsqueeze(2).broadcast_to([nt, 8, nt]),
                            op=ALU.mult), R=[b_TB, b_cst, b_skeep], W=[b_dg])
                        yield
                        for wh in range(2):
                            K.op("pe", lambda wh=wh: nc.tensor.matmul(
                                psum[3 + wh][:, :].rearrange("p (a c) -> p a c", a=4)[:, :, 0:nt],
                                lhsT=ones_r[0:nt, :].bitcast(F32R), rhs=dg[0:nt, 4 * wh:4 * wh + 4, 0:nt].bitcast(F32R),
                                start=True, stop=True), R=[b_dg, b_onesr], W=[b_ps[3 + wh]], c=0.25)
                        yield
                        K.op("dve", lambda: nc.vector.tensor_tensor(out=bkq[a][:, 0, :, 0:nt], in0=qkn[:, 4:8, lc:lc + nt],
                                                                    in1=pv(3), op=ALU.mult),
                             R=[b_qkn, b_ps[3]], W=[b_bkq[a]])
                        yield
                        K.op("dve", lambda: nc.vector.tensor_tensor(out=bkq[a][:, 1, :, 0:nt], in0=qkn[:, 0:4, lc:lc + nt],
                                                                    in1=pv(4), op=ALU.mult),
                             R=[b_qkn, b_ps[4]], W=[b_bkq[a]])
                        yield
                        K.op("dve", lambda: nc.vector.tensor_tensor(
                            out=gm[0:nt, :, 0:nt], in0=m_incl[0:nt, 0:nt].unsqueeze(1).broadcast_to([nt, 4, nt]),
                            in1=T_(2).unsqueeze(2).broadcast_to([nt, 4, nt]), op=ALU.mult), R=[b_TB, b_cst], W=[b_gm])
                        yield
                        for h in range(4):
                            K.op("pe", lambda h=h: nc.tensor.matmul(psum[2][0:nt, h * 128:h * 128 + nt],
                                                                    lhsT=m_lstrict[0:nt, 0:nt], rhs=gm[0:nt, h, 0:nt],
                                                                    start=True, stop=True),
                                 R=[b_gm, b_cst], W=[b_ps[2]], inc=(h == 3))
                        yield
                        K.op("act", lambda: nc.scalar.activation(out=ET[0:nt, :, 0:nt], in_=pt(2), func=AF.Exp),
                             R=[b_ps[2]], W=[b_ET])
                        yield
                        K.op("dve", lambda: nc.vector.tensor_tensor(
                            out=ETi[0:nt, :, 0:nt], in0=ET[0:nt, :, 0:nt],
                            in1=m_incl[0:nt, 0:nt].unsqueeze(1).broadcast_to([nt, 4, nt]), op=ALU.mult),
                            R=[b_ET, b_cst], W=[b_ETi])
                        yield
                        K.op("dve", lambda: nc.vector.tensor_tensor(
                            out=ETs[0:nt, :, 0:nt], in0=ET[0:nt, :, 0:nt],
                            in1=m_ustrict[0:nt, 0:nt].unsqueeze(1).broadcast_to([nt, 4, nt]), op=ALU.mult),
                            R=[b_ET, b_cst, b_ETi], W=[b_ETs])
                        yield
                        for h in range(4):
                            K.op("pe", lambda h=h: nc.tensor.matmul(psum[3][0:nt, h * 128:h * 128 + nt],
                                                                    lhsT=qkn[:, 4 + h, lc:lc + nt], rhs=bkq[a][:, 0, h, 0:nt],
                                                                    start=True, stop=True),
                                 R=[b_qkn, b_bkq[a]], W=[b_ps[3]], inc=(h == 3))
                        yield
                        for h in range(4):
                            K.op("pe", lambda h=h: nc.tensor.matmul(psum[4][0:nt, h * 128:h * 128 + nt],
                                                                    lhsT=qkn[:, 4 + h, lc:lc + nt], rhs=qkn[:, h, lc:lc + nt],
                                                                    start=True, stop=True),
                                 R=[b_qkn], W=[b_ps[4]], inc=(h == 3))
                        AT, A_ = QTA[a], QA[a]
                        yield
                        K.op("dve", lambda: nc.vector.tensor_tensor(out=AT[0:nt, :, 0:nt].bitcast(F32R), in0=pt(3), in1=ETs[0:nt, :, 0:nt],
                                                                    op=ALU.mult), R=[b_ps[3], b_ETs], W=[b_QTA[a]])
                        yield
                        K.op("dve", lambda: nc.vector.tensor_tensor(out=qkT[a][0:nt, :, 0:nt], in0=pt(4), in1=ETi[0:nt, :, 0:nt],
                                                                    op=ALU.mult), R=[b_ps[4], b_ETi], W=[b_qkT[a]])
                        yield
                        K.op("dve", lambda: nc.vector.scalar_tensor_tensor(
                            out=TT[a][0:nt, :, 0:nt].bitcast(F32R), in0=AT[0:nt, :, 0:nt], scalar=-1.0,
                            in1=C["ident"][0:nt, 0:nt].unsqueeze(1).broadcast_to([nt, 4, nt]),
                            op0=ALU.mult, op1=ALU.add), R=[b_QTA[a], b_cst], W=[b_TT[a]])
                        yield
                        if not smp:
                            for h in range(4):
                                K.op("pe", lambda h=h: nc.tensor.transpose(psum[2][0:nt, h * 128:h * 128 + nt],
                                                                           AT[0:nt, h, 0:nt], C["ident"][0:nt, 0:nt]),
                                     R=[b_QTA[a], b_cst], W=[b_ps[2]], inc=(h == 3))
                            yield
                            K.op("act", lambda: nc.scalar.copy(out=A_[0:nt, :, 0:nt].bitcast(F32R), in_=pt(2)), R=[b_ps[2]], W=[b_QA[a]])

                    def stageB1(kind, sc0, nt, a, c0, si=None):
                        TB, b_TB = (tb[a], b_tb[a]) if si is None else (tb_all[:, si], b_tball)
                        GB, b_GB = (Gbc[a], b_Gbc[a]) if si is None else (Gbc_all[:, si], b_gball)
                        lc = sc0 - c0
                        smp = (kind == "sample")
                        m_incl = C["ident"] if smp else C["mu_incl"]
                        m_bd = C["ident"] if smp else C["bd"]
                        m_ustrict = C["zero"] if smp else C["mu_strict"]
                        m_lstrict = C["zero"] if smp else C["ml_strict"]
                        T_ = lambda i: TB[0:nt, i, :]
                        pv = lambda b: psum[b][:, :].rearrange("p (a c) -> p a c", a=4)[:, :, 0:nt]
                        pt = lambda b: psum[b][0:nt, :].rearrange("p (a c) -> p a c", a=4)[:, :, 0:nt]
                        p4 = lambda b: psum[b][0:nt, :].rearrange("p (a c) -> p a c", a=4)
                        bc = lambda i: TB[0:nt, i, :].unsqueeze(2).broadcast_to([nt, 4, 128])
                        yield
                        for h in range(4):
                            K.op("pe", lambda h=h: nc.tensor.matmul(psum[5][0:nt, h * 128:(h + 1) * 128],
                                                                    lhsT=qkn[:, 4 + h, lc:lc + nt], rhs=CB["ident"],
                                                                    start=True, stop=True),
                                 R=[b_qkn, b_cst], W=[b_ps[5]], inc=(h == 3))
                        yield
                        for h in range(4):
                            K.op("pe", lambda h=h: nc.tensor.matmul(psum[6][0:nt, h * 128:(h + 1) * 128],
                                                                    lhsT=vn[:, h, lc:lc + nt], rhs=CB["ident"],
                                                                    start=True, stop=True),
                                 R=[b_vn, b_cst], W=[b_ps[6]], inc=(h == 3))
                        yield
                        K.op("dve", lambda: nc.vector.tensor_tensor(out=tok3[a][0:nt, 0], in0=p4(5), in1=bc(4), op=ALU.mult),
                             R=[b_ps[5], b_TB], W=[b_tok3[a]])
                        yield
                        K.op("dve", lambda: nc.vector.tensor_tensor(out=tok3[a][0:nt, 1], in0=p4(5), in1=bc(3), op=ALU.mult),
                             R=[b_ps[5], b_TB], W=[b_tok3[a]])
                        yield
                        K.op("dve", lambda: nc.vector.tensor_tensor(out=tok3[a][0:nt, 2], in0=p4(6), in1=bc(0), op=ALU.mult),
                             R=[b_ps[6], b_TB], W=[b_tok3[a]])
                        fin_X, b_fin = XA[a], b_TT[a]
                        if not smp:
                            n_inc = 5 if nt == 128 else 3
                            cq_, b_cq_ = QA[a], b_QA[a]
                            nq_, b_nq_ = Qs, b_Qs
                            cX, b_cXq, b_cXt = XA[a], b_QTA[a], b_TT[a]
                            nX, b_nXq, b_nXt = Xs, b_QTs, b_Xst
                            f32r = lambda ap: ap.bitcast(F32R)
                            for lv in range(0, n_inc + 1):
                                need_q = lv < n_inc
                                need_qt = lv < n_inc - 1
                                if lv >= 1:
                                    for h in range(4):
                                        K.op("pe", lambda h=h: nc.tensor.matmul(
                                            ps67[0:nt, h * 256:(h + 1) * 256].rearrange("p (j c) -> p j c", j=2)[:, :, 0:nt],
                                            lhsT=f32r(cq_[0:nt, h, 0:nt]), rhs=f32r(cX[0:nt, h, :, 0:nt]), start=True, stop=True),
                                            R=[b_cq_, b_cXq, b_cXt], W=[b_ps[6], b_ps[7]], inc=(h == 3), c=0.13)
                                else:
                                    for h in range(4):
                                        K.op("pe", lambda h=h: nc.tensor.matmul(
                                            ps67[0:nt, h * 256:h * 256 + nt], lhsT=cq_[0:nt, h, 0:nt],
                                            rhs=cX[0:nt, h, 0, 0:nt], start=True, stop=True),
                                            R=[b_cq_, b_cXq], W=[b_ps[6], b_ps[7]], inc=(h == 3), c=0.22)
                                if need_q:
                                    for h in range(4):
                                        K.op("pe", lambda h=h: nc.tensor.matmul(
                                            psum[5][0:nt, h * 128:h * 128 + nt], lhsT=cX[0:nt, h, 0, 0:nt],
                                            rhs=cq_[0:nt, h, 0:nt], start=True, stop=True),
                                            R=[b_cq_, b_cXq], W=[b_ps[5]], inc=(h == 3), c=0.22)
                                yield
                                p67 = ps67[0:nt, :].rearrange("p (h j c) -> p h j c", h=4, j=2)
                                if need_q:
                                    K.op("act", lambda: nc.scalar.copy(out=nq_[0:nt, :, 0:nt].bitcast(F32R), in_=pt(5)),
                                         R=[b_ps[5]], W=[b_nq_])
                                if lv >= 1:
                                    K.op("dve", lambda: nc.vector.tensor_tensor(out=nX[0:nt, :, 1, 0:nt].bitcast(F32R), in0=cX[0:nt, :, 1, 0:nt],
                                                                                in1=p67[:, :, 1, 0:nt], op=ALU.add),
                                         R=[b_ps[6], b_ps[7], b_cXt], W=[b_nXt])
                                else:
                                    K.op("dve", lambda: nc.vector.tensor_copy(out=nX[0:nt, :, 1, 0:nt].bitcast(F32R), in_=cX[0:nt, :, 1, 0:nt]),
                                         R=[b_cXt], W=[b_nXt])
                                if need_qt:
                                    K.op("act", lambda: nc.scalar.copy(out=nX[0:nt, :, 0, 0:nt].bitcast(F32R), in_=p67[:, :, 0, 0:nt]),
                                         R=[b_ps[6], b_ps[7]], W=[b_nXq])
                                yield
                                cq_, b_cq_, nq_, b_nq_ = nq_, b_nq_, cq_, b_cq_
                                cX, b_cXq, b_cXt, nX, b_nXq, b_nXt = nX, b_nXq, b_nXt, cX, b_cXq, b_cXt
                            fin_X, b_fin = cX, b_cXt
                        yield
                        K.op("act", lambda: nc.scalar.copy(out=TTb[0:nt, :, 0:nt], in_=fin_X[0:nt, :, 1, 0:nt]), R=[b_fin], W=[b_TTb])
                        yield
                        yield
                        for h in range(4):
                            K.op("pe", lambda h=h: nc.tensor.matmul(psum[7][0:nt, h * 128:(h + 1) * 128],
                                                                    lhsT=TTb[0:nt, h, 0:nt], rhs=tok3[a][0:nt, 2, h, :],
                                                                    start=True, stop=True),
                                 R=[b_TTb, b_tok3[a]], W=[b_ps[7]], inc=(h == 3))
                        yield
                        for h in range(4):
                            K.op("pe", lambda h=h: nc.tensor.matmul(psum[5][:, h * 128:h * 128 + nt],
                                                                    lhsT=tok3[a][0:nt, 0, h, :], rhs=TTb[0:nt, h, 0:nt],
                                                                    start=True, stop=True),
                                 R=[b_TTb, b_tok3[a]], W=[b_ps[5]], inc=(h == 3))
                        yield
                        if smp:
                            K.op("act", lambda: nc.scalar.copy(out=s_u0[:], in_=p4(7)), R=[b_ps[7]], W=[b_skeep])
                            K.op("dve", lambda: nc.vector.tensor_copy(out=s_wT[:], in_=pv(5)), R=[b_ps[5]], W=[b_skeep])
                            K.op("dve", lambda: nc.vector.tensor_copy(out=s_kd[:], in_=tok3[a][0:16, 1]), R=[b_tok3[a]], W=[b_skeep])
                            K.op("dve", lambda: nc.vector.tensor_copy(out=s_qn[:], in_=qkn[:, 0:4, 0:16]), R=[b_qkn], W=[b_skeep])
                            K.op("dve", lambda: nc.vector.tensor_copy(out=s_gz[:], in_=gz[:, :, 0:16]), R=[b_gz], W=[b_skeep])
                            return
                        yield
                        K.op("act", lambda: nc.scalar.copy(out=u0[a][0:nt], in_=p4(7)), R=[b_ps[7]], W=[b_u0[a]])
                        yield
                        K.op("dve", lambda: nc.vector.tensor_copy(out=wT[a][:, :, 0:nt], in_=pv(5)), R=[b_ps[5]], W=[b_wT[a]])

                    def stageB2(kind, sc0, nt, a, c0, si=None):
                        TB, b_TB = (tb[a], b_tb[a]) if si is None else (tb_all[:, si], b_tball)
                        GB, b_GB = (Gbc[a], b_Gbc[a]) if si is None else (Gbc_all[:, si], b_gball)
                        lc = sc0 - c0
                        smp = (kind == "sample")
                        m_incl = C["ident"] if smp else C["mu_incl"]
                        m_bd = C["ident"] if smp else C["bd"]
                        m_ustrict = C["zero"] if smp else C["mu_strict"]
                        m_lstrict = C["zero"] if smp else C["ml_strict"]
                        T_ = lambda i: TB[0:nt, i, :]
                        pv = lambda b: psum[b][:, :].rearrange("p (a c) -> p a c", a=4)[:, :, 0:nt]
                        pt = lambda b: psum[b][0:nt, :].rearrange("p (a c) -> p a c", a=4)[:, :, 0:nt]
                        p4 = lambda b: psum[b][0:nt, :].rearrange("p (a c) -> p a c", a=4)
                        bc = lambda i: TB[0:nt, i, :].unsqueeze(2).broadcast_to([nt, 4, 128])
                        if smp:
                            return
                        nch = 2 if nt == 128 else 1
                        cl = 64 if nt == 128 else nt
                        yield
                        for ch in range(nch):
                            tr = slice(ch * 64, ch * 64 + cl)
                            yield
                            for h in range(4):
                                K.op("pe", lambda h=h, tr=tr: nc.tensor.matmul(psum[0][tr, h * 128:(h + 1) * 128],
                                                                              lhsT=wT[a][:, h, tr], rhs=Sb[:, h, :],
                                                                              start=True, stop=True),
                                     R=[b_wT[a], b_Sb], W=[b_ps[0]], inc=(h == 3))
                            yield
                            K.op("dve", lambda tr=tr: nc.vector.tensor_tensor(
                                out=uu[tr], in0=u0[a][tr], in1=psum[0][tr, :].rearrange("p (a c) -> p a c", a=4),
                                op=ALU.subtract), R=[b_u0[a], b_ps[0]], W=[b_u])
                            yield
                            for h in range(4):
                                K.op("pe", lambda h=h, tr=tr: nc.tensor.matmul(psum[1][:, h * 64:h * 64 + cl],
                                                                              lhsT=Sb[:, h, :], rhs=bkq[a][:, 1, h, tr],
                                                                              start=True, stop=False),
                                     R=[b_Sb, b_bkq[a]], W=[b_ps[1]], inc=False)
                                K.op("pe", lambda h=h, tr=tr: nc.tensor.matmul(psum[1][:, h * 64:h * 64 + cl],
                                                                              lhsT=uu[0:nt, h, :], rhs=qkT[a][0:nt, h, tr],
                                                                              start=False, stop=True),
                                     R=[b_u, b_qkT[a]], W=[b_ps[1]], inc=(h == 3))
                            yield
                            for h in range(4):
                                K.op("pe", lambda h=h, tr=tr: nc.tensor.matmul(psum[0][:, h * 128:(h + 1) * 128],
                                                                              lhsT=tok3[a][tr, 1, h, :], rhs=uu[tr, h, :],
                                                                              start=True, stop=True),
                                     R=[b_tok3[a], b_u], W=[b_ps[0]], inc=(h == 3))
                            yield
                            K.op("act", lambda ch=ch: nc.scalar.copy(
                                out=oA[:, :, lc + ch * 64:lc + ch * 64 + cl],
                                in_=psum[1][:, 0:256].rearrange("p (a b) -> p a b", a=4)[:, :, 0:cl]),
                                R=[b_ps[1]], W=[b_oA])
                            yield
                            for h in range(4):
                                K.op("dve", lambda h=h, ch=ch: nc.vector.scalar_tensor_tensor(
                                    out=S[:, h, :], in0=S[:, h, :], scalar=GB[:, ch, h:h + 1],
                                    in1=psum[0][:, h * 128:(h + 1) * 128], op0=ALU.mult, op1=ALU.add),
                                    R=[b_GB, b_ps[0]], W=[b_S])
                            yield
                            K.op("act", lambda: nc.scalar.copy(out=Sb[:], in_=S[:]), R=[b_S], W=[b_Sb])
                        scan_done[0] += 1
                        yield

                    blk_it = 0
                    for (c0, n) in TILES:
                        is_ms = (c0 == 0)
                        def blkfn(blk, c0=c0, n=n, is_ms=is_ms):
                            q = blk % 2
                            mm_group(psum[q][:, 0:n], [(wA[:, k, blk * 128:(blk + 1) * 128], x_bf[:, k, c0:c0 + n])
                                                      for k in range(KC)], R=[b_wAb[blk], b_xbf], W=[b_ps[q]])
                            if blk >= 12:
                                K.op("act", lambda blk=blk, q=q: nc.scalar.activation(
                                    out=gz[:, blk - 12, 0:n], in_=psum[q][:, 0:n], func=AF.Silu), R=[b_ps[q]], W=[b_gz])
                                return
                            pe_ = blk % 2
                            ncv = 16 if is_ms else n
                            K.op("pool", lambda pe_=pe_, blk=blk: nc.gpsimd.tensor_copy(out=Pe[pe_][:, 0:3], in_=halo[:, blk, 0:3]),
                                 R=[b_halo], W=[b_Pe[pe_]])
                            src0 = 16 if is_ms else 0
                            K.op("act", lambda pe_=pe_, q=q, src0=src0, ncv=ncv: nc.scalar.copy(
                                out=Pe[pe_][:, 3:3 + ncv], in_=psum[q][:, src0:src0 + ncv]), R=[b_ps[q]], W=[b_Pe[pe_]])
                            K.op("pool", lambda pe_=pe_, blk=blk, ncv=ncv: nc.gpsimd.tensor_copy(
                                out=halo[:, blk, 0:3], in_=Pe[pe_][:, ncv:ncv + 3]), R=[b_Pe[pe_]], W=[b_halo])
                            a_ = acc[pe_]
                            o0 = 16 if is_ms else 0
                            K.op("act", lambda pe_=pe_, blk=blk, ncv=ncv, o0=o0: nc.scalar.activation(
                                out=acc[pe_][:, o0:o0 + ncv], in_=Pe[pe_][:, 3:3 + ncv], func=AF.Identity, scale=cw[:, blk, 3:4]),
                                R=[b_Pe[pe_], b_cw], W=[b_acc[pe_]])
                            for i in (2, 1, 0):
                                K.op("dve", lambda pe_=pe_, blk=blk, ncv=ncv, o0=o0, i=i: nc.vector.scalar_tensor_tensor(
                                    out=acc[pe_][:, o0:o0 + ncv], in0=Pe[pe_][:, i:i + ncv], scalar=cw[:, blk, i:i + 1],
                                    in1=acc[pe_][:, o0:o0 + ncv], op0=ALU.mult, op1=ALU.add),
                                    R=[b_Pe[pe_], b_cw, b_acc[pe_]], W=[b_acc[pe_]])
                            if is_ms:
                                cbv = cbT[:, blk, :].rearrange("p (s i) -> p s i", i=3)
                                K.op("dve", lambda pe_=pe_, blk=blk, q=q: nc.vector.tensor_scalar(
                                    out=acc[pe_][:, 0:16], in0=psum[q][:, 0:16], scalar1=cw[:, blk, 3:4], scalar2=None,
                                    op0=ALU.mult), R=[b_ps[q], b_cw], W=[b_acc[pe_]])
                                for i in range(3):
                                    K.op("dve", lambda pe_=pe_, blk=blk, i=i, cbv=cbv: nc.vector.scalar_tensor_tensor(
                                        out=acc[pe_][:, 0:16], in0=cbv[:, :, i], scalar=cw[:, blk, i:i + 1],
                                        in1=acc[pe_][:, 0:16], op0=ALU.mult, op1=ALU.add),
                                        R=[b_cbT, b_cw, b_acc[pe_]], W=[b_acc[pe_]])
                            if blk < 8:
                                K.op("act", lambda pe_=pe_, blk=blk: nc.scalar.activation(
                                    out=cq8[:, blk, 0:n], in_=acc[pe_][:, 0:n], func=AF.Silu), R=[b_acc[pe_]], W=[b_cq8[blk]])
                            else:
                                K.op("act", lambda pe_=pe_, blk=blk: nc.scalar.activation(
                                    out=vn[:, blk - 8, 0:n], in_=acc[pe_][:, 0:n], func=AF.Silu), R=[b_acc[pe_]], W=[b_vn])
                        K.run_sched([(lambda blk=blk: blkfn(blk), ([blk - 2] if blk >= 2 else [])) for blk in range(16)])

                        def normfn(blk, n=n):
                            ci = blk % 2
                            q = blk % 2
                            rr = rsd if ci == 0 else rsd_b
                            b_rr = b_rsd if ci == 0 else b_rsdb
                            K.op("act", lambda: nc.scalar.activation(out=sq[:, ci, 0:n], in_=cq8[:, blk, 0:n],
                                                                     func=AF.Square), R=[b_cq8[blk]], W=[b_sqs[ci]])
                            mm_group(psum[q][:, 0:n], [(CB["ones"], sq[:, ci, 0:n])], R=[b_sqs[ci], b_cst], W=[b_ps[q]])
                            K.op("act", lambda: nc.scalar.activation(out=rr[:, 0:n], in_=psum[q][:, 0:n], func=AF.Ln,
                                                                     bias=eps6[:, 0:1]), R=[b_ps[q], b_eps], W=[b_rr])
                            K.op("act", lambda: nc.scalar.activation(out=rr[:, 0:n], in_=rr[:, 0:n], func=AF.Exp,
                                                                     scale=-0.5), R=[b_rr], W=[b_rr])
                            scl = (128.0 ** -0.5) if blk < 4 else 1.0
                            K.op("dve", lambda: nc.vector.scalar_tensor_tensor(
                                out=qkn[:, blk, 0:n], in0=cq8[:, blk, 0:n], scalar=scl, in1=rr[:, 0:n],
                                op0=ALU.mult, op1=ALU.mult), R=[b_cq8[blk], b_rr], W=[b_qkn])
                        K.run_sched([(lambda blk=blk: normfn(blk), ([blk - 2] if blk >= 2 else [])) for blk in range(8)])
                        if is_ms:
                            for g3 in range(3):
                                q = g3 % 2
                                mm_group(psum[q][0:16, :], [(x_bf[:, k, 0:16], wA[:, k, g3 * 512:(g3 + 1) * 512])
                                                            for k in range(KC)], R=b_wAb[4 * g3:4 * g3 + 4] + [b_xbf], W=[b_ps[q]])
                                K.op("act", lambda g3=g3, q=q: nc.scalar.copy(out=pnew[:, g3 * 512:(g3 + 1) * 512],
                                                                             in_=psum[q][0:16, :]), R=[b_ps[q]], W=[b_pnew])
                            K.dma("sp", o_conv_s[l][:, 2, :], pnew[:], R=[b_pnew])
                        subs = subtiles(c0, n)
                        ths = []
                        pre = (not is_ms)
                        off = 1 if pre else 0
                        if pre:
                            ths.append((lambda subs=subs, c0=c0: stageT(subs, c0), []))
                        for i_, (kind, sc0, nt) in enumerate(subs):
                            a_ = (sub_ctr[0] + i_) % 2
                            si_ = i_ if pre else None
                            iA, iB1, iB2 = off + 3 * i_, off + 3 * i_ + 1, off + 3 * i_ + 2
                            afterA = ([iA - 3] if i_ >= 1 else ([0] if pre else [])) + ([iB2 - 6] if i_ >= 2 else [])
                            afterB1 = [iA] + ([iB1 - 3] if i_ >= 1 else []) + ([iB2 - 6] if i_ >= 2 else [])
                            afterB2 = [iB1] + ([iB2 - 3] if i_ >= 1 else [])
                            ths.append((lambda kind=kind, sc0=sc0, nt=nt, a_=a_, c0=c0, si_=si_: stageA(kind, sc0, nt, a_, c0, si_), afterA))
                            ths.append((lambda kind=kind, sc0=sc0, nt=nt, a_=a_, c0=c0, si_=si_: stageB1(kind, sc0, nt, a_, c0, si_), afterB1))
                            ths.append((lambda kind=kind, sc0=sc0, nt=nt, a_=a_, c0=c0, si_=si_: stageB2(kind, sc0, nt, a_, c0, si_), afterB2))
                        K.run_sched(ths)
                        sub_ctr[0] += len(subs)
                        if is_ms:
                            K.op("dve", lambda: nc.vector.memset(oA[:, :, 0:16], 0.0), W=[b_oA])
                        gated_norm(st, oA, b_oA, gz, b_gz, nw, b_nw, og, b_og, 0, c0, n, nbufs)
                    K.dma("sp", o_gdn_p[l].rearrange("h k v -> k h v"), S[:], R=[b_S])
                    for g3 in range(3):
                        for b4 in range(4):
                            blk = 4 * g3 + b4
                            K.op("pe", lambda blk=blk, b4=b4, g3=g3: nc.tensor.transpose(
                                psum[g3][0:4, b4 * 128:(b4 + 1) * 128], halo[:, blk, :], C["ident"]),
                                R=[b_halo, b_cst], W=[b_ps[g3]], inc=(b4 == 3))
                        K.op("act", lambda g3=g3: nc.scalar.copy(out=pnew[0:4, g3 * 512:(g3 + 1) * 512], in_=psum[g3][0:4, :]),
                             R=[b_ps[g3]], W=[b_pnew])
                    K.dma("sp", o_conv_p[l], pnew[0:3, :], R=[b_pnew])
                K.barrier()
                with ExitStack() as s2:
                    wTm = sbt(s2, "d_wTm", [128, 4, NS, 16], BF16)
                    b_wTm = Buf()
                    K.op("dve", lambda: nc.vector.tensor_tensor(
                        out=wTm[:], in0=s_wT[:, :, :].unsqueeze(2).broadcast_to([128, 4, NS, 16]),
                        in1=cst[:, IDX["idrep0"]:IDX["idrep0"] + 2, :].rearrange("p a (s t) -> p (a s) t", t=16)
                            .unsqueeze(1).broadcast_to([128, 4, NS, 16]), op=ALU.mult),
                        R=[b_skeep, b_cst], W=[b_wTm])
                    Sg = [sbt(s2, "d_Sg%d" % i, [128, 4, 4, 128]) for i in range(2)]
                    Sgb = [sbt(s2, "d_Sgb%d" % i, [128, 4, 4, 128], BF16) for i in range(2)]
                    b_Sg = [Buf(), Buf()]
                    b_Sgb = [Buf(), Buf()]
                    us = sbt(s2, "d_us", [16, 4, 128], BF16)
                    ubd = sbt(s2, "d_ubd", [16, 4, 512], BF16)
                    b_us, b_ubd = Buf(), Buf()
                    oS = sbt(s2, "d_oS", [128, 4, 16])
                    b_oS = Buf()
                    sq2 = sbt(s2, "d_sq2", [128, 4, 512], BF16)
                    rsd2 = sbt(s2, "d_rsd2", [128, 512])
                    tn2 = sbt(s2, "d_tn2", [128, 512])
                    sgd_v = state_gdn[l].rearrange("s h k v -> k s h v")
                    sgd_o = o_gdn_s[l].rearrange("s h k v -> k s h v")
                    for sg_ in range(4):
                        u = sg_ % 2
                        K.dma("sp", Sg[u][:], sgd_v[:, 4 * sg_:4 * sg_ + 4], W=[b_Sg[u]])
                        K.op("act", lambda u=u: nc.scalar.copy(out=Sgb[u][:], in_=Sg[u][:]), R=[b_Sg[u]], W=[b_Sgb[u]])
                        for h in range(4):
                            for si in range(4):
                                s = 4 * sg_ + si
                                K.op("pe", lambda h=h, si=si, s=s, u=u: nc.tensor.matmul(
                                    psum[0][0:16, h * 128:(h + 1) * 128], lhsT=wTm[:, h, s, :], rhs=Sgb[u][:, si, h, :],
                                    start=(si == 0), stop=(si == 3)), R=[b_wTm, b_Sgb[u]], W=[b_ps[0]],
                                    inc=(h == 3 and si == 3))
                        K.op("dve", lambda: nc.vector.tensor_tensor(
                            out=us[:], in0=s_u0[:], in1=psum[0][0:16, :].rearrange("p (a c) -> p a c", a=4), op=ALU.subtract),
                            R=[b_skeep, b_ps[0]], W=[b_us])
                        K.op("dve", lambda sg_=sg_: nc.vector.tensor_tensor(
                            out=ubd[:], in0=us[:, :, :].rearrange("p h v -> p (h v)").unsqueeze(1).broadcast_to([16, 4, 512]),
                            in1=C["ident"][0:16, 4 * sg_:4 * sg_ + 4].unsqueeze(2).broadcast_to([16, 4, 512]), op=ALU.mult),
                            R=[b_us, b_cst], W=[b_ubd])
                        for si in range(4):
                            for h in range(4):
                                K.op("pe", lambda h=h, si=si: nc.tensor.matmul(
                                    psum[1 + si][:, h * 128:(h + 1) * 128], lhsT=s_kd[0:16, h, :],
                                    rhs=ubd[0:16, si, h * 128:(h + 1) * 128], start=True, stop=True),
                                    R=[b_skeep, b_ubd], W=[b_ps[1 + si]], inc=(h == 3))
                        for si in range(4):
                            s = 4 * sg_ + si
                            for h in range(4):
                                K.op("dve", lambda h=h, si=si, s=s, u=u: nc.vector.scalar_tensor_tensor(
                                    out=Sg[u][:, si, h, :], in0=Sg[u][:, si, h, :], scalar=s_G[:, s, h:h + 1],
                                    in1=psum[1 + si][:, h * 128:(h + 1) * 128], op0=ALU.mult, op1=ALU.add),
                                    R=[b_skeep, b_ps[1 + si]], W=[b_Sg[u]])
                        K.op("act", lambda u=u: nc.scalar.copy(out=Sgb[u][:], in_=Sg[u][:]), R=[b_Sg[u]], W=[b_Sgb[u]])
                        for si in range(4):
                            s = 4 * sg_ + si
                            for h in range(4):
                                K.op("pe", lambda h=h, si=si, s=s, u=u: nc.tensor.matmul(
                                    psum[5][:, h * 16 + s:h * 16 + s + 1], lhsT=Sgb[u][:, si, h, :], rhs=s_qn[:, h, s:s + 1],
                                    start=True, stop=True), R=[b_Sgb[u], b_skeep], W=[b_ps[5]],
                                    inc=(h == 3 and si == 3))
                        K.dma("sp", sgd_o[:, 4 * sg_:4 * sg_ + 4], Sg[u][:], R=[b_Sg[u]])
                    K.op("act", lambda: nc.scalar.copy(out=oS[:], in_=psum[5][:, 0:64].rearrange("p (a b) -> p a b", a=4)),
                         R=[b_ps[5]], W=[b_oS])
                    gated_norm(s2, oS, b_oS, s_gz, b_skeep, nw, b_nw, og, b_og, 0, 0, 16,
                               (sq2, Buf(), rsd2, Buf(), tn2, Buf()))

        for l in range(layers):
            last_layer = (l == layers - 1)
            win_v = w_in[l].rearrange("(k p) n -> p k n", p=128)

            with ExitStack() as pm:
              og = big[:, 0:8 * T].rearrange("p (k t) -> p k t", k=8)
              b_og = Buf("og")
              if stub_mixer:
                  K.op("dve", lambda: nc.vector.tensor_copy(out=og, in_=x_bf[:]), R=[b_xbf], W=[b_og])
              else:
                  if mix_sel in ("all", "gdn"):
                      gdn_phase(l, og, b_og, win_v)
                      K.barrier()
                  if mix_sel in ("all", "gla"):
                      gla_phase(l, og, b_og, win_v)
              K.barrier()
              chk("mixer")
              v = sbt(pm, "v", [128, KC, T])
              b_v = Buf("v")
              with ExitStack() as pmg:
                mg = sbt(pmg, "mg", [128, KC, T], BF16)
                b_mg = Buf("mg")
                with ExitStack() as pb1:
                    NW = 2
                    wab = [sbt(pb1, "wab%d" % i, [128, 8, 128], BF16) for i in range(NW)]
                    wgg = [sbt(pb1, "wgg%d" % i, [128, 16, 128], BF16) for i in range(NW)]
                    b_wab = [Buf() for _ in range(NW)]
                    b_wgg = [Buf() for _ in range(NW)]
                    sg = [sbt(pb1, "sg%d" % i, [128, 2, 512]) for i in range(2)]
                    b_sg = [Buf(), Buf()]
                    wa_v = w_branch_a[l].rearrange("(k p) n -> p k n", p=128)
                    wb_v = w_branch_b[l].rearrange("(k p) n -> p k n", p=128)
                    it = 0
                    for jo in range(KC):
                        s = jo % NW
                        cs = slice(jo * 128, (jo + 1) * 128)
                        K.dma("pool", wab[s][:, 0:4, :], wa_v[:, :, cs], W=[b_wab[s]])
                        K.dma("pool", wab[s][:, 4:8, :], wb_v[:, :, cs], W=[b_wab[s]])
                        K.dma("pool", wgg[s][:, 0:8, :], win_v[:, :, 3608 + jo * 128:3608 + (jo + 1) * 128], W=[b_wgg[s]])
                        K.dma("pool", wgg[s][:, 8:16, :], win_v[:, :, 4632 + jo * 128:4632 + (jo + 1) * 128], W=[b_wgg[s]])
                        for (c0, n) in TILES:
                            q = 4 * (it % 2)
                            t2 = it % 2
                            it += 1
                            cols = slice(c0, c0 + n)
                            mm_group(psum[q + 0][:, 0:n], [(wab[s][:, k, :], og[:, k, cols]) for k in range(4)],
                                     R=[b_wab[s], b_og], W=[b_ps[q + 0]])
                            mm_group(psum[q + 1][:, 0:n], [(wab[s][:, 4 + k, :], og[:, 4 + k, cols]) for k in range(4)],
                                     R=[b_wab[s], b_og], W=[b_ps[q + 1]])
                            mm_group(psum[q + 2][:, 0:n], [(wgg[s][:, k, :], x_bf[:, k, cols]) for k in range(8)],
                                     R=[b_wgg[s], b_xbf], W=[b_ps[q + 2]])
                            mm_group(psum[q + 3][:, 0:n], [(wgg[s][:, 8 + k, :], x_bf[:, k, cols]) for k in range(8)],
                                     R=[b_wgg[s], b_xbf], W=[b_ps[q + 3]])
                            K.op("act", lambda q=q, t2=t2, n=n: nc.scalar.activation(
                                out=sg[t2][:, 0, 0:n], in_=psum[q + 2][:, 0:n], func=AF.Sigmoid),
                                R=[b_ps[q + 2]], W=[b_sg[t2]])
                            K.op("act", lambda q=q, t2=t2, n=n: nc.scalar.activation(
                                out=sg[t2][:, 1, 0:n], in_=psum[q + 3][:, 0:n], func=AF.Sigmoid),
                                R=[b_ps[q + 3]], W=[b_sg[t2]])
                            K.op("dve", lambda q=q, t2=t2, n=n: nc.vector.tensor_tensor(
                                out=sg[t2][:, 0, 0:n], in0=sg[t2][:, 0, 0:n], in1=psum[q + 0][:, 0:n], op=ALU.mult),
                                R=[b_sg[t2], b_ps[q + 0]], W=[b_sg[t2]])
                            K.op("dve", lambda q=q, t2=t2, n=n: nc.vector.tensor_tensor(
                                out=sg[t2][:, 1, 0:n], in0=sg[t2][:, 1, 0:n], in1=psum[q + 1][:, 0:n], op=ALU.mult),
                                R=[b_sg[t2], b_ps[q + 1]], W=[b_sg[t2]])
                            K.op("dve", lambda t2=t2, n=n, jo=jo, cols=cols: nc.vector.tensor_tensor(
                                out=mg[:, jo, cols], in0=sg[t2][:, 0, 0:n], in1=sg[t2][:, 1, 0:n], op=ALU.add),
                                R=[b_sg[t2]], W=[b_mg])
                K.barrier()
                chk("b1")
                K.dma("sp", v[:], xres, R=[b_xres], W=[b_v])
                with ExitStack() as pb2:
                    wo = [sbt(pb2, "wo%d" % i, [128, 8, 128], BF16) for i in range(2)]
                    b_wo = [Buf(), Buf()]
                    wo_v = w_out[l].rearrange("(k p) n -> p k n", p=128)
                    it = 0
                    for jo in range(KC):
                        s = jo % 2
                        K.dma("pool", wo[s][:], wo_v[:, :, jo * 128:(jo + 1) * 128], W=[b_wo[s]])
                        for (c0, n) in TILES:
                            q = it % 8
                            it += 1
                            cols = slice(c0, c0 + n)
                            mm_group(psum[q][:, 0:n], [(wo[s][:, k, :], mg[:, k, cols]) for k in range(8)],
                                     R=[b_wo[s], b_mg], W=[b_ps[q]])
                            K.op("dve", lambda q=q, n=n, jo=jo, cols=cols: nc.vector.scalar_tensor_tensor(
                                out=v[:, jo, cols], in0=v[:, jo, cols], scalar=ALPHA, in1=psum[q][:, 0:n],
                                op0=ALU.mult, op1=ALU.add), R=[b_ps[q]], W=[b_v])
                K.barrier()
                chk("b2")
              if True:
                layer_norm(pm, v, b_v, l, 0)
                K.barrier()
                chk("ln1")

                nh = (len(TILES) + 1) // 2
                halves = [TILES[:nh], TILES[nh:]]
                fin_v = w_ffn_in[l].rearrange("(k p) n -> p k n", p=128)
                fout_v = w_ffn_out[l].rearrange("(c p) n -> p c n", p=128)
                with ExitStack() as pf:
                    hid = big[:, 0:FC * HW].rearrange("p (c t) -> p c t", c=FC)
                    b_hid = Buf("hid")
                    wfi = [sbt(pf, "wfi%d" % i, [128, 2, KC, 256], BF16) for i in range(2)]
                    b_wfi = [Buf(), Buf()]
                    wfo = [sbt(pf, "wfo%d" % i, [128, FC, 128], BF16) for i in range(2)]
                    b_wfo = [Buf(), Buf()]
                    sil = [sbt(pf, "sil%d" % i, [128, 512]) for i in range(2)]
                    b_sil = [Buf(), Buf()]
                    it = 0
                    wi_it = 0
                    wo_it = 0
                    for half in halves:
                        if not half:
                            continue
                        h0 = half[0][0]
                        for g in range(FC // 2):
                            s = wi_it % 2
                            wi_it += 1
                            K.dma("pool", wfi[s][:, 0, :, :], fin_v[:, :, g * 256:(g + 1) * 256], W=[b_wfi[s]])
                            K.dma("pool", wfi[s][:, 1, :, :], fin_v[:, :, DFF + g * 256:DFF + (g + 1) * 256], W=[b_wfi[s]])
                            for jj in range(2):
                                j = 2 * g + jj
                                for (c0, n) in half:
                                    q = 2 * (it % 4)
                                    t2 = it % 2
                                    it += 1
                                    cols = slice(c0, c0 + n)
                                    hc = slice(c0 - h0, c0 - h0 + n)
                                    mm_group(psum[q][:, 0:n],
                                             [(wfi[s][:, 0, k, jj * 128:(jj + 1) * 128], x_bf[:, k, cols]) for k in range(8)],
                                             R=[b_wfi[s], b_xbf], W=[b_ps[q]])
                                    mm_group(psum[q + 1][:, 0:n],
                                             [(wfi[s][:, 1, k, jj * 128:(jj + 1) * 128], x_bf[:, k, cols]) for k in range(8)],
                                             R=[b_wfi[s], b_xbf], W=[b_ps[q + 1]])
                                    K.op("act", lambda q=q, t2=t2, n=n: nc.scalar.activation(
                                        out=sil[t2][:, 0:n], in_=psum[q][:, 0:n], func=AF.Silu),
                                        R=[b_ps[q]], W=[b_sil[t2]])
                                    K.op("dve", lambda q=q, t2=t2, n=n, j=j, hc=hc: nc.vector.tensor_tensor(
                                        out=hid[:, j, hc], in0=sil[t2][:, 0:n], in1=psum[q + 1][:, 0:n], op=ALU.mult),
                                        R=[b_sil[t2], b_ps[q + 1]], W=[b_hid])
                        for jo in range(KC):
                            s = wo_it % 2
                            wo_it += 1
                            K.dma("pool", wfo[s][:], fout_v[:, :, jo * 128:(jo + 1) * 128], W=[b_wfo[s]])
                            for (c0, n) in half:
                                q = it % 8
                                it += 1
                                cols = slice(c0, c0 + n)
                                hc = slice(c0 - h0, c0 - h0 + n)
                                mm_group(psum[q][:, 0:n], [(wfo[s][:, c, :], hid[:, c, hc]) for c in range(FC)],
                                         R=[b_wfo[s], b_hid], W=[b_ps[q]])
                                K.op("dve", lambda q=q, n=n, jo=jo, cols=cols: nc.vector.scalar_tensor_tensor(
                                    out=v[:, jo, cols], in0=v[:, jo, cols], scalar=ALPHA, in1=psum[q][:, 0:n],
                                    op0=ALU.mult, op1=ALU.add), R=[b_ps[q]], W=[b_v])
                K.barrier()
                chk("ffn")
                layer_norm(pm, v, b_v, l, 1, write_bf=not last_layer)
                K.barrier()
                chk("ln2")
                if not last_layer:
                    K.dma("sp", xres, v[:], R=[b_v], W=[b_xres])
                else:
                    with ExitStack() as po:
                        NR = 3
                        yst = [sbt(po, "yst%d" % i, [128, D]) for i in range(NR)]
                        b_yst = [Buf() for _ in range(NR)]
                        rows = [("ms", 0, 32)] + [("p", 128 * i, 128) for i in range(TP // 128)]

                        def out_tile(ri):
                            kind, r0, n = rows[ri]
                            s = ri % NR
                            c0 = 0 if kind == "ms" else 32 + r0
                            for g in range(2):
                                pb = (2 * ri + g) % 8
                                for kk in range(4):
                                    k = 4 * g + kk
                                    K.op("pe", lambda k=k, kk=kk, pb=pb: nc.tensor.transpose(
                                        psum[pb][0:n, kk * 128:(kk + 1) * 128], v[:, k, c0:c0 + n], C["ident"]),
                                        R=[b_v, b_cst], W=[b_ps[pb]], inc=(kk == 3), c=0.12)
                                if g == 0:
                                    K.op("act", lambda pb=pb: nc.scalar.copy(
                                        out=yst[s][0:n, 0:512], in_=psum[pb][0:n, :]), R=[b_ps[pb]], W=[b_yst[s]])
                                else:
                                    K.op("dve", lambda pb=pb: nc.vector.tensor_copy(
                                        out=yst[s][0:n, 512:1024], in_=psum[pb][0:n, :]), R=[b_ps[pb]], W=[b_yst[s]])
                            if kind == "ms":
                                K.dma("sp", y_sample, yst[s][0:16, :], R=[b_yst[s]])
                            else:
                                K.dma("sp", y_prompt[r0:r0 + 128, :], yst[s][0:128, :], R=[b_yst[s]])

                        K.run_sched([(lambda ri=ri: out_tile(ri), ([ri - NR] if ri >= NR else [])) for ri in range(len(rows))])
              K.barrier()
      except _Stop:
        pass
      K.finish()
    return nc


_NC_CACHE = {}


def kernel(x_prompt, x_sample, state_gdn, state_gla, state_conv, meta_tokens, w_in, conv_w, a_log, dt_bias,
           gdn_norm_w, gla_gate_w2, gla_gate_b, gla_norm_w, w_branch_a, w_branch_b, w_out,
           ln1_g, ln1_b, ln2_g, ln2_b, w_ffn_in, w_ffn_out, _build_kwargs=None):
    f = lambda a: np.ascontiguousarray(np.asarray(a), dtype=np.float32)
    x_prompt = f(x_prompt)
    TP = x_prompt.shape[1]
    bk = dict(_build_kwargs or {})
    key = (TP, tuple(sorted(bk.items())))
    if key not in _NC_CACHE:
        _NC_CACHE[key] = build(TP=TP, **bk)
    nc = _NC_CACHE[key]
    shared = dict(meta_tokens=f(meta_tokens), w_in=f(w_in), conv_w=f(conv_w), a_log=f(a_log), dt_bias=f(dt_bias),
                  gdn_norm_w=f(gdn_norm_w), gla_gate_w2=f(gla_gate_w2), gla_gate_b=f(gla_gate_b),
                  gla_norm_w=f(gla_norm_w), w_branch_a=f(w_branch_a), w_branch_b=f(w_branch_b), w_out=f(w_out),
                  ln1_g=f(ln1_g), ln1_b=f(ln1_b), ln2_g=f(ln2_g), ln2_b=f(ln2_b),
                  w_ffn_in=f(w_ffn_in), w_ffn_out=f(w_ffn_out), consts=CONST_ARR)
    x_sample = f(x_sample)
    state_gdn = f(state_gdn)
    state_gla = f(state_gla)
    state_conv = f(state_conv)
    in_maps = []
    for c in range(NCORES):
        sl = slice(NS * c, NS * (c + 1))
        m = dict(shared)
        m["x_prompt"] = x_prompt[c]
        m["x_sample"] = np.ascontiguousarray(x_sample[sl, 0, :])
        m["state_gdn"] = np.ascontiguousarray(state_gdn[:, sl])
        m["state_gla"] = np.ascontiguousarray(state_gla[:, sl])
        m["state_conv"] = np.ascontiguousarray(state_conv[:, sl])
        in_maps.append(m)
    res = run_bass_kernel_spmd(nc, in_maps, core_ids=list(range(NCORES)))
    R = res.results
    y_prompt = np.stack([R[c]["y_prompt"] for c in range(NCORES)], axis=0)
    y_sample = np.concatenate([R[c]["y_sample"] for c in range(NCORES)], axis=0)[:, None, :]
    gdn_p = np.stack([R[c]["new_gdn_prompt"] for c in range(NCORES)], axis=1)
    gla_p = np.stack([R[c]["new_gla_prompt"] for c in range(NCORES)], axis=1)
    conv_p = np.stack([R[c]["new_conv_prompt"] for c in range(NCORES)], axis=1)
    gdn_s = np.concatenate([R[c]["new_gdn_sample"] for c in range(NCORES)], axis=1)
    gla_s = np.concatenate([R[c]["new_gla_sample"] for c in range(NCORES)], axis=1)
    conv_s = np.concatenate([R[c]["new_conv_sample"] for c in range(NCORES)], axis=1)
    outs = (y_prompt, y_sample, gdn_p, gla_p, conv_p, gdn_s, gla_s, conv_s)
    return tuple(np.ascontiguousarray(o, dtype=np.float32) for o in outs)
```

```python
import threading
import numpy as np
from contextlib import ExitStack
import concourse.bass as bass
import concourse.mybir as mybir
from concourse.bass_utils import run_bass_kernel_spmd

F32 = mybir.dt.float32
BF16 = mybir.dt.bfloat16
F32R = mybir.dt.float32r
AF = mybir.ActivationFunctionType
ALU = mybir.AluOpType

D = 1024
KC = 8
DEPTH = 2
NS = 16
NMETA = 16
D_IN = 5656
DFF = 2816
FC = DFF // 128
ALPHA = (2.0 * DEPTH) ** 0.25
NCORES = 8


class Buf:
    __slots__ = ("w", "r", "name", "excl")

    def __init__(self, name="", excl=False):
        self.w = None
        self.r = {}
        self.name = name
        self.excl = excl


_tls = threading.local()


class _Worker:
    def __init__(self, fn):
        self.fn = fn
        self.req = None
        self.done = False
        self.exc = None
        self.ev_req = threading.Event()
        self.ev_go = threading.Event()
        self.th = threading.Thread(target=self._run, daemon=True)

    def _run(self):
        _tls.worker = self
        try:
            self.ev_go.wait()
            self.ev_go.clear()
            r = self.fn()
            if r is not None and hasattr(r, "__next__"):
                for _ in r:
                    pass
        except BaseException as e:
            self.exc = e
        finally:
            self.done = True
            self.req = None
            self.ev_req.set()

    def post(self, req):
        self.req = req
        self.ev_req.set()
        self.ev_go.wait()
        self.ev_go.clear()


class Sched:
    ENG = ("pe", "act", "dve", "pool", "sp")
    COST = {"pe": 0.25, "act": 0.6, "dve": 0.7, "pool": 0.5, "sp": 0.1}

    def __init__(self, nc, es, n_dma_slots=24):
        self.nc = nc
        self.e = {"pe": nc.tensor, "act": nc.scalar, "dve": nc.vector, "pool": nc.gpsimd, "sp": nc.sync}
        self.sem = {}
        self.cnt = {}
        for k in self.ENG:
            self.sem[k] = es.enter_context(nc.semaphore("s_" + k))
            self.cnt[k] = 0
        self.seen = {k: {} for k in self.ENG}
        self.pending = {k: False for k in self.ENG}
        self.nslots = n_dma_slots
        self.slot_sem = [es.enter_context(nc.semaphore("s_dma%d" % i)) for i in range(n_dma_slots)]
        self.slot_uses = [0] * n_dma_slots
        self.slot_next = 0
        self.semobj = dict(self.sem)
        for i in range(n_dma_slots):
            self.semobj[("dma", i)] = self.slot_sem[i]
        self.tfree = {k: 0.0 for k in self.ENG}
        self.tdone = {}

    def _est(self, eng, R, W):
        t = self.tfree[eng]
        def dep(key):
            d = self.tdone.get(key)
            if d is None:
                d = self.tfree.get(key[0], 0.0) if not isinstance(key[0], tuple) else 0.0
            return d + (0.05 if key[0] == eng else 0.3)
        for b in R:
            if b.w is not None:
                t = max(t, dep(b.w))
            if b.excl:
                for k, v in b.r.items():
                    t = max(t, dep((k, v)))
        for b in W:
            if b.w is not None:
                t = max(t, dep(b.w))
            for k, v in b.r.items():
                t = max(t, dep((k, v)))
        return t

    def run_sched(self, threads):
        n = len(threads)
        workers = [None] * n
        started, finished = set(), set()

        def advance(w):
            w.ev_req.clear()
            w.ev_go.set()
            w.ev_req.wait()
            if w.exc is not None:
                raise w.exc

        while len(finished) < n:
            for i, th_ in enumerate(threads):
                fn, after = th_[0], th_[1]
                if i not in started and all(j in finished for j in after):
                    started.add(i)
                    workers[i] = _Worker(fn)
                    workers[i].th.start()
                    advance(workers[i])
                    if workers[i].done:
                        finished.add(i)
            cands = [i for i in started if i not in finished]
            if not cands:
                if len(finished) < n and len(started) == len(finished):
                    rem = [i for i in range(n) if i not in started]
                    assert any(all(j in finished for j in threads[i][1]) for i in rem), "scheduler deadlock"
                continue
            best = min(cands, key=lambda i: (self._est(*workers[i].req) - (threads[i][2] if len(threads[i]) > 2 else 0.0), i))
            advance(workers[best])
            if workers[best].done:
                finished.add(best)

    def _collect(self, eng, R, W, extra=()):
        need = {}
        def add(k, v):
            if v > need.get(k, 0):
                need[k] = v
        for b in R:
            if b.w is not None:
                add(*b.w)
        for b in W:
            if b.w is not None:
                add(*b.w)
            for k, v in b.r.items():
                add(k, v)
        for k, v in extra:
            add(k, v)
        out = []
        for k, v in need.items():
            if k == "pe" and eng == "pe":
                continue
            if k == eng and v > self.cnt[eng]:
                continue
            if v > self.seen[eng].get(k, 0):
                out.append((k, v))
        return out

    def op(self, eng, fn, R=(), W=(), inc=True, extra=(), c=None):
        w_ = getattr(_tls, "worker", None)
        if w_ is not None:
            w_.post((eng, tuple(R), tuple(W)))
        t0_ = self._est(eng, R, W)
        t1_ = t0_ + (c if c is not None else self.COST[eng])
        self.tfree[eng] = t1_
        if any(b.excl for b in R):
            W = list(W) + [b for b in R if b.excl]
            R = [b for b in R if not b.excl]
        waits = self._collect(eng, R, W, extra)
        e = self.e[eng]
        if eng == "pe":
            for k, v in waits:
                e.wait_ge(self.semobj[k], v)
                self.seen[eng][k] = v
            waits = []
        for k, v in waits[:-1]:
            e.wait_ge(self.semobj[k], v)
            self.seen[eng][k] = v
        inst = fn()
        if waits:
            k, v = waits[-1]
            inst.wait_op(self.semobj[k], v, "sem-ge")
            self.seen[eng][k] = v
        if inc:
            inst.then_inc(self.sem[eng], 1)
            self.cnt[eng] += 1
            cc = self.cnt[eng]
            self.pending[eng] = False
            self.tdone[(eng, cc)] = t1_
        else:
            cc = self.cnt[eng] + 1
            self.pending[eng] = True
        for b in R:
            if b.r.get(eng, 0) < cc:
                b.r[eng] = cc
        for b in W:
            b.w = (eng, cc)
            b.r = {}
        return inst

    def dma(self, eng, out, in_, R=(), W=(), **kw):
        w_ = getattr(_tls, "worker", None)
        if w_ is not None:
            w_.post((eng, tuple(R), tuple(W)))
        t0_ = self._est(eng, R, W)
        self.tfree[eng] = t0_ + 0.1
        s = self.slot_next
        self.slot_next = (self.slot_next + 1) % self.nslots
        key = ("dma", s)
        prev = 16 * self.slot_uses[s]
        extra = [(key, prev)] if prev > 0 else []
        waits = self._collect(eng, R, W, extra)
        e = self.e[eng]
        for k, v in waits:
            e.wait_ge(self.semobj[k], v)
            self.seen[eng][k] = v
        inst = e.dma_start(out=out, in_=in_, **kw)
        self.slot_uses[s] += 1
        val = 16 * self.slot_uses[s]
        inst.then_inc(self.slot_sem[s], 16)
        self.tdone[(key, val)] = t0_ + 3.0
        for b in R:
            b.r[key] = val
        for b in W:
            b.w = (key, val)
            b.r = {}
        return inst

    def barrier(self):
        tgt = [(k, self.cnt[k]) for k in self.ENG if self.cnt[k] > 0]
        tgt += [(("dma", i), 16 * self.slot_uses[i]) for i in range(self.nslots) if self.slot_uses[i] > 0]
        for eng in self.ENG:
            assert not self.pending[eng]
            for k, v in tgt:
                if k == eng:
                    continue
                if v > self.seen[eng].get(k, 0):
                    self.e[eng].wait_ge(self.semobj[k], v)
                    self.seen[eng][k] = v

    def finish(self):
        self.barrier()


def make_consts():
    r = np.arange(128)
    same = (r[:, None] // 64) == (r[None, :] // 64)
    c = {}
    c["ident"] = np.eye(128, dtype=np.float32)
    c["ones"] = np.ones((128, 128), dtype=np.float32)
    c["mu_incl"] = (same & (r[None, :] >= r[:, None])).astype(np.float32)
    c["mu_strict"] = (same & (r[None, :] > r[:, None])).astype(np.float32)
    c["ml_strict"] = (same & (r[:, None] > r[None, :])).astype(np.float32)
    c["bd"] = same.astype(np.float32)
    c["blk0"] = np.repeat((r < 64).astype(np.float32)[:, None], 128, axis=1)
    c["blk1"] = np.repeat((r >= 64).astype(np.float32)[:, None], 128, axis=1)
    c["zero"] = np.zeros((128, 128), dtype=np.float32)
    idrep = np.tile(np.eye(16, dtype=np.float32).reshape(1, 256), (128, 1))
    c["idrep0"] = idrep[:, 0:128]
    c["idrep1"] = idrep[:, 128:256]
    names = ["ident", "ones", "mu_incl", "mu_strict", "ml_strict", "bd", "blk0", "blk1", "zero", "idrep0", "idrep1"]
    arr = np.stack([c[n] for n in names], axis=1)
    return names, np.ascontiguousarray(arr.astype(np.float32))


CONST_NAMES, CONST_ARR = make_consts()
NCONST = len(CONST_NAMES)
IDX = {n: i for i, n in enumerate(CONST_NAMES)}


class _Stop(Exception):
    pass


def build(TP=2048, stub_mixer=False, layers=DEPTH, dbg=False, stop=None, mix_sel="all", PRIO_A=0.0):
    nc = bass.Bass("TRN2", target_bir_lowering=False)
    NPOS = NMETA + TP
    T = 32 + TP
    NPT = max(TP // 512, 1)
    TILES = [(0, 32)] + [(32 + 512 * i, 512) for i in range(NPT)]

    def din(name, shape, dt=F32):
        return nc.dram_tensor(name, list(shape), dt, kind="ExternalInput").ap()

    def dout(name, shape, dt=F32):
        return nc.dram_tensor(name, list(shape), dt, kind="ExternalOutput").ap()

    x_prompt = din("x_prompt", [TP, D])
    x_sample = din("x_sample", [NS, D])
    meta = din("meta_tokens", [NMETA, D])
    state_gdn = din("state_gdn", [DEPTH, NS, 4, 128, 128])
    state_gla = din("state_gla", [DEPTH, NS, 4, 64, 128])
    state_conv = din("state_conv", [DEPTH, NS, 3, 1536])
    w_in = din("w_in", [DEPTH, D, D_IN])
    conv_w = din("conv_w", [DEPTH, 4, 1536])
    a_log = din("a_log", [DEPTH, 4])
    dt_bias = din("dt_bias", [DEPTH, 4])
    gdn_norm_w = din("gdn_norm_w", [DEPTH, 128])
    gla_gate_w2 = din("gla_gate_w2", [DEPTH, 16, 256])
    gla_gate_b = din("gla_gate_b", [DEPTH, 256])
    gla_norm_w = din("gla_norm_w", [DEPTH, 128])
    w_branch_a = din("w_branch_a", [DEPTH, 512, D])
    w_branch_b = din("w_branch_b", [DEPTH, 512, D])
    w_out = din("w_out", [DEPTH, D, D])
    ln1_g = din("ln1_g", [DEPTH, D])
    ln1_b = din("ln1_b", [DEPTH, D])
    ln2_g = din("ln2_g", [DEPTH, D])
    ln2_b = din("ln2_b", [DEPTH, D])
    w_ffn_in = din("w_ffn_in", [DEPTH, D, 2 * DFF])
    w_ffn_out = din("w_ffn_out", [DEPTH, DFF, D])
    consts_d = din("consts", [128, NCONST, 128])

    y_prompt = dout("y_prompt", [TP, D])
    y_sample = dout("y_sample", [NS, D])
    o_gdn_p = dout("new_gdn_prompt", [DEPTH, 4, 128, 128])
    o_gla_p = dout("new_gla_prompt", [DEPTH, 4, 64, 128])
    o_conv_p = dout("new_conv_prompt", [DEPTH, 3, 1536])
    o_gdn_s = dout("new_gdn_sample", [DEPTH, NS, 4, 128, 128])
    o_gla_s = dout("new_gla_sample", [DEPTH, NS, 4, 64, 128])
    o_conv_s = dout("new_conv_sample", [DEPTH, NS, 3, 1536])
    dbg_out = dout("dbg", [128, KC, T]) if dbg else None

    xres = nc.dram_tensor("xres_scratch", [128, KC, T], F32, kind="Internal").ap()

    es = ExitStack()
    with es:
      K = Sched(nc, es)

      def chk(name):
          if stop == name:
              raise _Stop()

      try:

        _uid = [0]

        def sbt(stack, name, shape, dt=F32):
            _uid[0] += 1
            return stack.enter_context(nc.sbuf_tensor("%s_%d" % (name, _uid[0]), list(shape), dt))

        cst = sbt(es, "cst", [128, NCONST, 128], F32)
        cstb = sbt(es, "cstb", [128, NCONST, 128], BF16)
        b_cst = Buf("cst")
        C = {n: cst[:, i, :] for i, n in enumerate(CONST_NAMES)}
        CB = {n: cstb[:, i, :] for i, n in enumerate(CONST_NAMES)}
        x_bf = sbt(es, "x_bf", [128, KC, T], BF16)
        b_xbf = Buf("x_bf")
        lnp = sbt(es, "lnp", [128, DEPTH, 4, KC], F32)
        b_lnp = Buf("lnp")
        psp = [es.enter_context(nc.psum_tensor("psp%d" % i, [128, 1024], F32)) for i in range(4)]
        psum = [psp[i // 2][:, (i % 2) * 512:(i % 2) * 512 + 512] for i in range(8)]
        ps67 = psp[3][:, :]
        b_ps = [Buf("ps%d" % i, excl=True) for i in range(8)]
        b_xres = Buf("xres")

        epst = sbt(es, "epst", [128, 2], F32)
        b_eps = Buf("eps")
        K.op("dve", lambda: nc.vector.memset(epst[:, 0:1], 1e-6), W=[b_eps])
        K.op("dve", lambda: nc.vector.memset(epst[:, 1:2], 1.0), W=[b_eps])
        eps6 = epst[:, 0:1]
        one1 = epst[:, 1:2]
        nh_ = (len(TILES) + 1) // 2
        HW = max(sum(n for _, n in TILES[:nh_]), sum(n for _, n in TILES[nh_:]))
        BIGN = max(FC * HW, 8 * T)
        big = sbt(es, "big", [128, BIGN], BF16)
        K.dma("sp", cst[:], consts_d, W=[b_cst])
        K.op("act", lambda: nc.scalar.copy(out=cstb[:], in_=cst[:]), R=[b_cst], W=[b_cst])
        ones_r = sbt(es, "ones_r", [128, 128], F32)
        b_onesr = Buf("ones_r")
        K.op("act", lambda: nc.scalar.copy(out=ones_r[:].bitcast(F32R), in_=C["ones"]), R=[b_cst], W=[b_onesr])
        for l in range(DEPTH):
            for wi, src in enumerate((ln1_g, ln1_b, ln2_g, ln2_b)):
                K.dma("sp", lnp[:, l, wi, :], src[l].rearrange("(k p) -> p k", p=128), W=[b_lnp],
                      allow_slow_non_contiguous=True)

        chk("init")

        def run_threads(gens):
            active = list(gens)
            while active:
                for g in list(active):
                    try:
                        next(g)
                    except StopIteration:
                        active.remove(g)

        def run_pipeline(items, mkA, mkB, nA=2, nbuf=3):
            n_it = len(items)
            nextA, nextB = 0, 0
            activeA = {}
            doneA = set()
            curB = None
            while nextB < n_it:
                while nextA < n_it and len(activeA) < nA and (nextA - nextB) < nbuf:
                    activeA[nextA] = mkA(items[nextA], nextA)
                    nextA += 1
                if curB is None and nextB in doneA:
                    curB = mkB(items[nextB], nextB)
                for i, gen in list(activeA.items()):
                    try:
                        next(gen)
                    except StopIteration:
                        del activeA[i]
                        doneA.add(i)
                if curB is not None:
                    try:
                        next(curB)
                    except StopIteration:
                        curB = None
                        nextB += 1

        def mm_group(ps_ap, pairs, R, W, inc=True):
            n = len(pairs)
            for i, (lt, rh) in enumerate(pairs):
                last = (i == n - 1)
                K.op("pe", lambda lt=lt, rh=rh, i=i, last=last: nc.tensor.matmul(
                    ps_ap, lhsT=lt, rhs=rh, start=(i == 0), stop=last),
                    R=R, W=W, inc=(inc and last))

        with ExitStack() as p0:
            NR = 3
            xin = [sbt(p0, "xin%d" % i, [128, D]) for i in range(NR)]
            b_xin = [Buf() for _ in range(NR)]
            xst = [sbt(p0, "xst%d" % i, [128, KC, 128]) for i in range(NR)]
            b_xst = [Buf() for _ in range(NR)]
            rows = [("ms", 0, 32)] + [("p", 128 * i, 128) for i in range(TP // 128)]

            def p0_tile(ri):
                kind, r0, n = rows[ri]
                s = ri % NR
                if kind == "ms":
                    K.dma("sp", xin[s][0:16, :], x_sample, W=[b_xin[s]])
                    K.dma("sp", xin[s][16:32, :], meta, W=[b_xin[s]])
                    c0 = 0
                else:
                    K.dma("sp", xin[s][0:128, :], x_prompt[r0:r0 + 128, :], W=[b_xin[s]])
                    c0 = 32 + r0
                for g in range(2):
                    pb = (2 * ri + g) % 8
                    for kk in range(4):
                        k = 4 * g + kk
                        K.op("pe", lambda k=k, kk=kk, pb=pb: nc.tensor.transpose(
                            psum[pb][:, kk * 128:kk * 128 + n], xin[s][0:n, k * 128:(k + 1) * 128],
                            C["ident"][0:n, 0:n]), R=[b_xin[s], b_cst], W=[b_ps[pb]], inc=(kk == 3), c=0.12)
                    src = psum[pb][:, :].rearrange("p (a b) -> p a b", a=4)[:, :, 0:n]
                    K.op("act", lambda src=src, g=g: nc.scalar.copy(
                        out=x_bf[:, 4 * g:4 * g + 4, c0:c0 + n], in_=src), R=[b_ps[pb]], W=[b_xbf])
                    K.op("dve", lambda src=src, g=g: nc.vector.tensor_copy(
                        out=xst[s][:, 4 * g:4 * g + 4, 0:n], in_=src), R=[b_ps[pb]], W=[b_xst[s]])
                K.dma("sp", xres[:, :, c0:c0 + n], xst[s][:, :, 0:n], R=[b_xst[s]])

            K.run_sched([(lambda ri=ri: p0_tile(ri), ([ri - NR] if ri >= NR else [])) for ri in range(len(rows))])
        K.barrier()
        chk("p0")

        def layer_norm(stack, v, b_v, l, which, write_bf=True):
            g_i, b_i = 2 * which, 2 * which + 1
            with ExitStack() as st:
                vb_ = [sbt(st, "ln_vb%d" % i, [128, KC, 512], BF16) for i in range(2)]
                sq_ = [sbt(st, "ln_sq%d" % i, [128, KC, 512], BF16) for i in range(2)]
                mt_ = [sbt(st, "ln_m%d" % i, [128, 512]) for i in range(2)]
                m2_ = [sbt(st, "ln_m2%d" % i, [128, 512]) for i in range(2)]
                rs_ = [sbt(st, "ln_rs%d" % i, [128, 512]) for i in range(2)]
                nm_ = [sbt(st, "ln_nm%d" % i, [128, 512]) for i in range(2)]
                bb = [[Buf() for _ in range(6)] for _ in range(2)]
                b_vts = []
                for _ in TILES:
                    t_ = Buf()
                    t_.w = b_v.w
                    t_.r = dict(b_v.r)
                    b_vts.append(t_)

                def ln_tile(ti):
                    c0, n = TILES[ti]
                    u_ = ti % 2
                    vb, sq, mt, m2, rs, nm = vb_[u_], sq_[u_], mt_[u_], m2_[u_], rs_[u_], nm_[u_]
                    b_vb, b_sq, b_mt, b_m2, b_rs, b_nm = bb[u_]
                    b_v = b_vts[ti]
                    cols = slice(c0, c0 + n)
                    pm_, pq_ = (2 * ti) % 8, (2 * ti + 1) % 8
                    K.op("act", lambda cols=cols, n=n: nc.scalar.copy(out=vb[:, :, 0:n], in_=v[:, :, cols]),
                         R=[b_v], W=[b_vb])
                    K.op("act", lambda cols=cols, n=n: nc.scalar.activation(
                        out=sq[:, :, 0:n], in_=v[:, :, cols], func=AF.Square), R=[b_v], W=[b_sq])
                    mm_group(psum[pm_][:, 0:n], [(CB["ones"], vb[:, k, 0:n]) for k in range(KC)],
                             R=[b_vb, b_cst], W=[b_ps[pm_]])
                    mm_group(psum[pq_][:, 0:n], [(CB["ones"], sq[:, k, 0:n]) for k in range(KC)],
                             R=[b_sq, b_cst], W=[b_ps[pq_]])
                    K.op("dve", lambda n=n, pm_=pm_: nc.vector.tensor_scalar(
                        out=mt[:, 0:n], in0=psum[pm_][:, 0:n], scalar1=1.0 / D, scalar2=None, op0=ALU.mult),
                        R=[b_ps[pm_]], W=[b_mt])
                    K.op("dve", lambda n=n: nc.vector.tensor_tensor(
                        out=m2[:, 0:n], in0=mt[:, 0:n], in1=mt[:, 0:n], op=ALU.mult), R=[b_mt], W=[b_m2])
                    K.op("dve", lambda n=n, pq_=pq_: nc.vector.scalar_tensor_tensor(
                        out=m2[:, 0:n], in0=psum[pq_][:, 0:n], scalar=1.0 / D, in1=m2[:, 0:n],
                        op0=ALU.mult, op1=ALU.subtract), R=[b_ps[pq_], b_m2], W=[b_m2])
                    K.op("dve", lambda n=n: nc.vector.tensor_scalar(
                        out=m2[:, 0:n], in0=m2[:, 0:n], scalar1=1e-5, scalar2=None, op0=ALU.add),
                        R=[b_m2], W=[b_m2])
                    K.op("act", lambda n=n: nc.scalar.activation(
                        out=rs[:, 0:n], in_=m2[:, 0:n], func=AF.Ln), R=[b_m2], W=[b_rs])
                    K.op("act", lambda n=n: nc.scalar.activation(
                        out=rs[:, 0:n], in_=rs[:, 0:n], func=AF.Exp, scale=-0.5), R=[b_rs], W=[b_rs])
                    K.op("dve", lambda n=n: nc.vector.scalar_tensor_tensor(
                        out=nm[:, 0:n], in0=mt[:, 0:n], scalar=-1.0, in1=rs[:, 0:n],
                        op0=ALU.mult, op1=ALU.mult), R=[b_mt, b_rs], W=[b_nm])
                    K.op("dve", lambda n=n, cols=cols: nc.vector.tensor_tensor(
                        out=v[:, :, cols], in0=v[:, :, cols],
                        in1=rs[:, 0:n].unsqueeze(1).broadcast_to([128, KC, n]), op=ALU.mult),
                        R=[b_rs], W=[b_v])
                    K.op("dve", lambda n=n, cols=cols: nc.vector.tensor_tensor(
                        out=v[:, :, cols], in0=v[:, :, cols],
                        in1=nm[:, 0:n].unsqueeze(1).broadcast_to([128, KC, n]), op=ALU.add),
                        R=[b_nm], W=[b_v])
                    for k in range(KC):
                        K.op("act", lambda k=k, n=n, cols=cols: nc.scalar.activation(
                            out=v[:, k, cols], in_=v[:, k, cols], func=AF.Identity,
                            scale=lnp[:, l, g_i, k:k + 1], bias=lnp[:, l, b_i, k:k + 1]),
                            R=[b_lnp], W=[b_v])
                    if write_bf:
                        K.op("pool", lambda cols=cols: nc.gpsimd.tensor_copy(out=x_bf[:, :, cols], in_=v[:, :, cols]),
                             R=[b_v], W=[b_xbf])

                K.run_sched([(lambda ti=ti: ln_tile(ti), ([ti - 2] if ti >= 2 else [])) for ti in range(len(TILES))])

        def subtiles(c0, n):
            if c0 == 0:
                return [("sample", 0, 16), ("meta", 16, 16)]
            return [("prompt", c0 + 128 * i, 128) for i in range(n // 128)]

        def gated_norm(st, oB, b_oB, gz, b_gz, nw, b_nw, og, b_og, hbase, c0, n, tagbufs):
            sq, b_sq, rsd, b_rsd, tn, b_tn = tagbufs
            K.op("act", lambda: nc.scalar.activation(out=sq[:, :, 0:n], in_=oB[:, :, 0:n], func=AF.Square),
                 R=[b_oB], W=[b_sq])
            for h in range(4):
                q = h % 2
                mm_group(psum[q][:, 0:n], [(CB["ones"], sq[:, h, 0:n])], R=[b_sq, b_cst], W=[b_ps[q]])
                K.op("act", lambda q=q: nc.scalar.activation(out=rsd[:, 0:n], in_=psum[q][:, 0:n], func=AF.Ln,
                                                             scale=1.0 / 128, bias=eps6[:, 0:1]),
                     R=[b_ps[q], b_eps], W=[b_rsd])
                K.op("act", lambda: nc.scalar.activation(out=rsd[:, 0:n], in_=rsd[:, 0:n], func=AF.Exp, scale=-0.5),
                     R=[b_rsd], W=[b_rsd])
                K.op("dve", lambda h=h: nc.vector.tensor_tensor(out=tn[:, 0:n], in0=oB[:, h, 0:n], in1=rsd[:, 0:n],
                                                                op=ALU.mult), R=[b_oB, b_rsd], W=[b_tn])
                K.op("dve", lambda h=h: nc.vector.scalar_tensor_tensor(
                    out=og[:, hbase + h, c0:c0 + n], in0=tn[:, 0:n], scalar=nw[:, 0:1], in1=gz[:, h, 0:n],
                    op0=ALU.mult, op1=ALU.mult), R=[b_tn, b_nw, b_gz], W=[b_og])

        def gla_phase(l, og, b_og, win_v):
            with ExitStack() as st:
                wB = sbt(st, "wB", [128, KC, 1552], BF16)
                b_wBb = [Buf() for _ in range(13)]
                for blk_ in range(12):
                    K.dma("pool", wB[:, :, blk_ * 128:(blk_ + 1) * 128],
                          win_v[:, :, 2056 + blk_ * 128:2056 + (blk_ + 1) * 128], W=[b_wBb[blk_]])
                K.dma("pool", wB[:, :, 1536:1552], win_v[:, :, 2056 + 1536:2056 + 1552], W=[b_wBb[12]])
                w2e = sbt(st, "w2e", [17, 256])
                b_w2e = Buf()
                K.dma("sp", w2e[0:16, :], gla_gate_w2[l], W=[b_w2e])
                K.dma("sp", w2e[16:17, :], gla_gate_b[l].rearrange("(o n) -> o n", o=1), W=[b_w2e])
                nw = sbt(st, "gla_nw", [128, 1])
                b_nw = Buf()
                K.dma("sp", nw[:, :], gla_norm_w[l].rearrange("(p o) -> p o", o=1), W=[b_nw],
                      allow_slow_non_contiguous=True)
                qk = sbt(st, "g_qk", [128, 4, 512])
                b_qk = Buf()
                sr = sbt(st, "g_sr", [128, 4, 512], BF16)
                b_sr = Buf()
                glr = sbt(st, "g_glr", [17, 512])
                b_glr = Buf()
                oB = sbt(st, "g_oB", [128, 4, 512])
                b_oB = Buf()
                sq = sbt(st, "g_sq", [128, 4, 512], BF16)
                rsd = sbt(st, "g_rsd", [128, 512])
                tn = sbt(st, "g_tn", [128, 512])
                nbufs = (sq, Buf(), rsd, Buf(), tn, Buf())
                Lt = [sbt(st, "g_L%d" % i, [128, 256]) for i in range(3)]
                E12 = [sbt(st, "g_E%d" % i, [128, 2, 2, 128]) for i in range(3)]
                atot = [sbt(st, "g_at%d" % i, [128, 2, 2]) for i in range(3)]
                qdk = [sbt(st, "g_qdk%d" % i, [128, 2, 2, 128], BF16) for i in range(3)]
                qdm = [sbt(st, "g_qdm%d" % i, [128, 4, 128], BF16) for i in range(3)]
                b_qdm = [Buf(), Buf(), Buf()]
                qm = sbt(st, "g_qm", [128, 4, 18])
                b_qm = Buf()
                E3 = [sbt(st, "g_E3%d" % i, [128, 256]) for i in range(3)]
                kdec = [sbt(st, "g_kd%d" % i, [128, 256], BF16) for i in range(3)]
                vtb = [sbt(st, "g_vt%d" % i, [128, 512], BF16) for i in range(3)]
                att = [sbt(st, "g_att%d" % i, [128, 4, 128], BF16) for i in range(3)]
                b_L, b_E12, b_at, b_qdk, b_E3, b_kd, b_vt, b_att = ([Buf(), Buf(), Buf()] for _ in range(8))
                S = sbt(st, "g_S", [128, 2, 128])
                Sb = sbt(st, "g_Sb", [128, 2, 128], BF16)
                b_S, b_Sb = Buf(), Buf()
                K.op("dve", lambda: nc.vector.memset(S[:], 0.0), W=[b_S])
                K.op("dve", lambda: nc.vector.memset(Sb[:], 0.0), W=[b_Sb])
                K.op("dve", lambda: nc.vector.memset(glr[:], 1.0), W=[b_glr])
                Ss = sbt(st, "g_Ss", [128, NS, 2, 128])
                b_Ss = Buf()
                sgl_v = state_gla[l].rearrange("s (pr hf) k v -> (hf k) s pr v", hf=2)
                for s4 in range(4):
                    K.dma("sp", Ss[:, 4 * s4:4 * s4 + 4], sgl_v[:, 4 * s4:4 * s4 + 4], W=[b_Ss])
                vbd = sbt(st, "g_vbd", [16, NS, 512], BF16)
                b_vbd = Buf()
                sub_ctr = [0]
                def glaA(kind, sc0, nt, u, bx, by, c0):
                    lc = sc0 - c0
                    if kind == "sample":
                        m_incl, m_strict_l = C["ident"], C["zero"]
                        mb_incl = CB["ident"]
                    else:
                        m_incl, m_strict_l = C["mu_incl"], C["ml_strict"]
                        mb_incl = CB["mu_incl"]
                    yield
                    mm_group(psum[bx][0:nt, 0:256], [(x_bf[:, k, sc0:sc0 + nt], wB[:, k, 256:512]) for k in range(KC)],
                             R=b_wBb[2:4] + [b_xbf], W=[b_ps[bx]])
                    yield
                    mm_group(psum[by][0:nt, 0:512], [(x_bf[:, k, sc0:sc0 + nt], wB[:, k, 512:1024]) for k in range(KC)],
                             R=b_wBb[4:8] + [b_xbf], W=[b_ps[by]])
                    yield
                    mm_group(psum[bx][0:nt, 256:512], [(glr[0:17, lc:lc + nt], w2e[0:17, :])],
                             R=[b_glr, b_w2e], W=[b_ps[bx]])
                    yield
                    K.op("act", lambda u=u: nc.scalar.activation(out=Lt[u][0:nt, :], in_=psum[bx][0:nt, 256:512],
                                                                 func=AF.Exp, scale=-1.0), R=[b_ps[bx]], W=[b_L[u]])
                    yield
                    K.op("act", lambda u=u: nc.scalar.activation(out=Lt[u][0:nt, :], in_=Lt[u][0:nt, :], func=AF.Ln,
                                                                 bias=one1[0:nt, 0:1]), R=[b_L[u], b_eps], W=[b_L[u]])
                    yield
                    K.op("act", lambda u=u: nc.scalar.copy(out=vtb[u][0:nt, :], in_=psum[by][0:nt, :]),
                         R=[b_ps[by]], W=[b_vt[u]])
                    yield
                    for pr_ in range(2):
                        mm_group(psum[by][:, pr_ * 128:pr_ * 128 + nt],
                                 [(Lt[u][0:nt, pr_ * 128:(pr_ + 1) * 128], m_incl[0:nt, 0:nt])],
                                 R=[b_L[u], b_cst], W=[b_ps[by]])
                    yield
                    mm_group(psum[by][0:nt, 256:512], [(m_strict_l[0:nt, 0:nt], Lt[u][0:nt, :])],
                             R=[b_L[u], b_cst], W=[b_ps[by]])
                    csT = psum[by][:, 0:256].rearrange("p (a b) -> p a b", a=2)[:, :, 0:nt]
                    yield
                    K.op("act", lambda u=u, csT=csT: nc.scalar.activation(out=E12[u][:, 0, :, 0:nt], in_=csT,
                                                                          func=AF.Exp, scale=-1.0 / 16),
                         R=[b_ps[by]], W=[b_E12[u]])
                    yield
                    K.op("act", lambda u=u, csT=csT: nc.scalar.activation(out=E12[u][:, 1, :, 0:nt], in_=csT,
                                                                          func=AF.Exp, scale=1.0 / 16),
                         R=[b_ps[by]], W=[b_E12[u]])
                    nch = 2 if nt == 128 else 1
                    cl = 64 if nt == 128 else nt
                    yield
                    for ch in range(nch):
                        lastc = ch * 64 + cl - 1
                        K.op("act", lambda u=u, ch=ch, lastc=lastc: nc.scalar.activation(
                            out=atot[u][:, :, ch:ch + 1],
                            in_=psum[by][:, 0:256].rearrange("p (a b) -> p a b", a=2)[:, :, lastc:lastc + 1],
                            func=AF.Exp, scale=-1.0 / 16), R=[b_ps[by]], W=[b_at[u]])
                    yield
                    K.op("act", lambda u=u: nc.scalar.activation(out=E3[u][0:nt, :], in_=psum[by][0:nt, 256:512],
                                                                 func=AF.Exp, scale=-1.0 / 16),
                         R=[b_ps[by]], W=[b_E3[u]])
                    yield
                    K.op("dve", lambda u=u, lc=lc: nc.vector.scalar_tensor_tensor(
                        out=qdk[u][:, 0, :, 0:nt], in0=qk[:, 0:2, lc:lc + nt], scalar=0.125, in1=E12[u][:, 0, :, 0:nt],
                        op0=ALU.mult, op1=ALU.mult), R=[b_qk, b_E12[u]], W=[b_qdk[u]])
                    yield
                    K.op("dve", lambda u=u, lc=lc: nc.vector.tensor_tensor(
                        out=qdk[u][:, 1, :, 0:nt], in0=qk[:, 2:4, lc:lc + nt], in1=E12[u][:, 1, :, 0:nt], op=ALU.mult),
                        R=[b_qk, b_E12[u]], W=[b_qdk[u]])
                    yield
                    K.op("dve", lambda u=u: nc.vector.tensor_tensor(
                        out=kdec[u][0:nt, :], in0=psum[bx][0:nt, 0:256], in1=E3[u][0:nt, :], op=ALU.mult),
                        R=[b_ps[bx], b_E3[u]], W=[b_kd[u]])
                    yield
                    if kind == "sample":
                        for h in range(4):
                            K.op("dve", lambda h=h: nc.vector.tensor_scalar(
                                out=qm[:, h, :], in0=qk[:, h // 2, 0:18], scalar1=C["blk%d" % (h % 2)][:, 0:1],
                                scalar2=None, op0=ALU.mult), R=[b_qk, b_cst], W=[b_qm])
                        gla_sample(l, u, qm, b_qm, E12, b_E12, kdec, b_kd, vtb, b_vt, Ss, b_Ss, vbd, b_vbd, oB, b_oB)
                        return
                    yield
                    for h in range(4):
                        K.op("dve", lambda h=h, u=u: nc.vector.tensor_scalar(
                            out=qdm[u][:, h, 0:nt], in0=qdk[u][:, 0, h // 2, 0:nt], scalar1=C["blk%d" % (h % 2)][:, 0:1],
                            scalar2=None, op0=ALU.mult), R=[b_qdk[u], b_cst], W=[b_qdm[u]])
                    yield
                    for h in range(4):
                        pr_, hf = h // 2, h % 2
                        rows = slice(hf * 64, hf * 64 + 64)
                        mm_group(psum[bx][0:nt, h * 128:h * 128 + nt],
                                 [(qdk[u][:, 1, pr_, 0:nt], qdm[u][:, h, 0:nt])],
                                 R=[b_qdk[u], b_qdm[u]], W=[b_ps[bx]], inc=(h == 3))
                    yield
                    K.op("dve", lambda u=u: nc.vector.tensor_tensor(
                        out=att[u][0:nt, :, 0:nt],
                        in0=psum[bx][0:nt, :].rearrange("p (a b) -> p a b", a=4)[:, :, 0:nt],
                        in1=m_incl[0:nt, 0:nt].unsqueeze(1).broadcast_to([nt, 4, nt]), op=ALU.mult),
                        R=[b_ps[bx], b_cst], W=[b_att[u]])

                def glaB(kind, sc0, nt, u, c0):
                    lc = sc0 - c0
                    nch = 2 if nt == 128 else 1
                    cl = 64 if nt == 128 else nt
                    for ch in range(nch):
                        trow = slice(ch * 64, ch * 64 + cl)
                        yield
                        for h in range(4):
                            pr_, hf = h // 2, h % 2
                            rows = slice(hf * 64, hf * 64 + 64)
                            K.op("pe", lambda h=h, pr_=pr_, trow=trow, u=u: nc.tensor.matmul(
                                psum[6][:, h * 64:h * 64 + cl], lhsT=Sb[:, pr_, :], rhs=qdm[u][:, h, trow],
                                start=True, stop=False), R=[b_Sb, b_qdm[u]], W=[b_ps[6]], inc=False)
                            K.op("pe", lambda h=h, trow=trow, u=u: nc.tensor.matmul(
                                psum[6][:, h * 64:h * 64 + cl], lhsT=vtb[u][0:nt, h * 128:(h + 1) * 128],
                                rhs=att[u][0:nt, h, trow], start=False, stop=True),
                                R=[b_vt[u], b_att[u]], W=[b_ps[6]], inc=(h == 3))
                        yield
                        for h in range(4):
                            pr_, hf = h // 2, h % 2
                            K.op("pe", lambda h=h, pr_=pr_, hf=hf, trow=trow, u=u: nc.tensor.matmul(
                                psum[7][hf * 64:hf * 64 + 64, pr_ * 128:(pr_ + 1) * 128],
                                lhsT=kdec[u][trow, h * 64:(h + 1) * 64], rhs=vtb[u][trow, h * 128:(h + 1) * 128],
                                start=True, stop=True), R=[b_kd[u], b_vt[u]], W=[b_ps[7]], inc=(h == 3))
                        yield
                        K.op("act", lambda lc=lc, trow=trow, ch=ch: nc.scalar.copy(
                            out=oB[:, :, lc + ch * 64:lc + ch * 64 + cl],
                            in_=psum[6][:, 0:256].rearrange("p (a b) -> p a b", a=4)[:, :, 0:cl]),
                            R=[b_ps[6]], W=[b_oB])
                        yield
                        for pr_ in range(2):
                            K.op("dve", lambda pr_=pr_, ch=ch, u=u: nc.vector.scalar_tensor_tensor(
                                out=S[:, pr_, :], in0=S[:, pr_, :], scalar=atot[u][:, pr_, ch:ch + 1],
                                in1=psum[7][:, pr_ * 128:(pr_ + 1) * 128], op0=ALU.mult, op1=ALU.add),
                                R=[b_at[u], b_ps[7]], W=[b_S])
                        yield
                        K.op("act", lambda: nc.scalar.copy(out=Sb[:], in_=S[:]), R=[b_S], W=[b_Sb])

                    yield

                for (c0, n) in TILES:
                    for bi in range(9):
                        q = bi % 2
                        if bi < 8:
                            woff = bi * 128 if bi < 4 else 1024 + (bi - 4) * 128
                            mw = 128
                        else:
                            woff, mw = 1536, 16
                        mm_group(psum[q][0:mw, 0:n], [(wB[:, k, woff:woff + mw], x_bf[:, k, c0:c0 + n]) for k in range(KC)],
                                 R=[b_wBb[woff // 128], b_xbf], W=[b_ps[q]])
                        if bi < 4:
                            K.op("dve", lambda bi=bi, q=q: nc.vector.tensor_copy(out=qk[:, bi, 0:n], in_=psum[q][:, 0:n]),
                                 R=[b_ps[q]], W=[b_qk])
                        elif bi < 8:
                            K.op("act", lambda bi=bi, q=q: nc.scalar.activation(
                                out=sr[:, bi - 4, 0:n], in_=psum[q][:, 0:n], func=AF.Silu), R=[b_ps[q]], W=[b_sr])
                        else:
                            K.op("dve", lambda q=q: nc.vector.tensor_copy(out=glr[0:16, 0:n], in_=psum[q][0:16, 0:n]),
                                 R=[b_ps[q]], W=[b_glr])
                    subs = subtiles(c0, n)
                    ths = []
                    for i_, (kind, sc0, nt) in enumerate(subs):
                        gi = sub_ctr[0] + i_
                        afterA = ([2 * (i_ - 2)] if i_ >= 2 else []) + ([2 * (i_ - 3) + 1] if i_ >= 3 else [])
                        afterB = [2 * i_] + ([2 * (i_ - 1) + 1] if i_ >= 1 else [])
                        ths.append((lambda kind=kind, sc0=sc0, nt=nt, gi=gi, c0=c0: glaA(kind, sc0, nt, gi % 3, 2 + 2 * (gi % 2), 3 + 2 * (gi % 2), c0), afterA))
                        if kind == "sample":
                            ths.append((lambda: None, afterB))
                        else:
                            ths.append((lambda kind=kind, sc0=sc0, nt=nt, gi=gi, c0=c0: glaB(kind, sc0, nt, gi % 3, c0), afterB))
                    K.run_sched(ths)
                    sub_ctr[0] += len(subs)
                    gated_norm(st, oB, b_oB, sr, b_sr, nw, b_nw, og, b_og, 4, c0, n, nbufs)
                K.dma("sp", o_gla_p[l].rearrange("(pr hf) k v -> (hf k) pr v", hf=2), S[:], R=[b_S])

        def gla_sample(l, u, qm, b_qm, E12, b_E12, kdec, b_kd, vtb, b_vt, Ss, b_Ss, vbd, b_vbd, oB, b_oB):
            K.op("dve", lambda: nc.vector.tensor_tensor(
                out=vbd[:, :, :], in0=vtb[u][0:16, :].unsqueeze(1).broadcast_to([16, NS, 512]),
                in1=C["ident"][0:16, 0:16].unsqueeze(2).broadcast_to([16, NS, 512]), op=ALU.mult),
                R=[b_vt[u], b_cst], W=[b_vbd])
            for s4 in range(4):
                for si in range(4):
                    s = 4 * s4 + si
                    for h in range(4):
                        pr_, hf = h // 2, h % 2
                        q = 6 + si // 2
                        cb = (si % 2) * 256 + pr_ * 128
                        K.op("pe", lambda s=s, h=h, hf=hf, q=q, cb=cb: nc.tensor.matmul(
                            psum[q][hf * 64:hf * 64 + 64, cb:cb + 128],
                            lhsT=kdec[u][0:16, h * 64:(h + 1) * 64], rhs=vbd[0:16, s, h * 128:(h + 1) * 128],
                            start=True, stop=True), R=[b_kd[u], b_vbd], W=[b_ps[q]], inc=(h == 3))
                for si in range(4):
                    s = 4 * s4 + si
                    q = 6 + si // 2
                    for pr_ in range(2):
                        cb = (si % 2) * 256 + pr_ * 128
                        K.op("dve", lambda s=s, pr_=pr_, q=q, cb=cb: nc.vector.scalar_tensor_tensor(
                            out=Ss[:, s, pr_, :], in0=Ss[:, s, pr_, :], scalar=E12[u][:, 0, pr_, s:s + 1],
                            in1=psum[q][:, cb:cb + 128], op0=ALU.mult, op1=ALU.add),
                            R=[b_E12[u], b_ps[q]], W=[b_Ss])
            for s in range(NS):
                for h in range(4):
                    pr_, hf = h // 2, h % 2
                    K.op("pe", lambda s=s, h=h, pr_=pr_: nc.tensor.matmul(
                        psum[5][:, h * 32 + s:h * 32 + s + 2], lhsT=Ss[:, s, pr_, :], rhs=qm[:, h, s:s + 2],
                        start=True, stop=True), R=[b_Ss, b_qm], W=[b_ps[5]], inc=(s == NS - 1 and h == 3))
            K.op("act", lambda: nc.scalar.activation(
                out=oB[:, :, 0:16], in_=psum[5][:, 0:128].rearrange("p (a b) -> p a b", a=4)[:, :, 0:16],
                func=AF.Copy, scale=0.125), R=[b_ps[5]], W=[b_oB])
            sgl_o = o_gla_s[l].rearrange("s (pr hf) k v -> (hf k) s pr v", hf=2)
            for s4 in range(4):
                K.dma("sp", sgl_o[:, 4 * s4:4 * s4 + 4], Ss[:, 4 * s4:4 * s4 + 4], R=[b_Ss])

        def gdn_phase(l, og, b_og, win_v):
            keep = {}
            with ExitStack() as st0:
                s_wT = sbt(st0, "d_swT", [128, 4, 16], BF16)
                s_u0 = sbt(st0, "d_su0", [16, 4, 128])
                s_kd = sbt(st0, "d_skd", [16, 4, 128], BF16)
                s_qn = sbt(st0, "d_sqn", [128, 4, 16], BF16)
                s_G = sbt(st0, "d_sG", [128, NS, 4])
                s_gz = sbt(st0, "d_sgz", [128, 4, 16], BF16)
                b_skeep = Buf()
                nw = sbt(st0, "gdn_nw", [128, 1])
                b_nw = Buf()
                K.dma("sp", nw[:, :], gdn_norm_w[l].rearrange("(p o) -> p o", o=1), W=[b_nw],
                      allow_slow_non_contiguous=True)
                with ExitStack() as st:
                    wA = sbt(st, "wA", [128, KC, 2056], BF16)
                    b_wAb = [Buf() for _ in range(17)]
                    for blk_ in range(16):
                        K.dma("pool", wA[:, :, blk_ * 128:(blk_ + 1) * 128], win_v[:, :, blk_ * 128:(blk_ + 1) * 128],
                              W=[b_wAb[blk_]])
                    K.dma("pool", wA[:, :, 2048:2056], win_v[:, :, 2048:2056], W=[b_wAb[16]])
                    cw = sbt(st, "d_cw", [128, 12, 4])
                    b_cw = Buf()
                    for i_ in range(4):
                        K.dma("sp", cw[:, :, i_], conv_w[l][i_].rearrange("(b p) -> p b", p=128), W=[b_cw],
                              allow_slow_non_contiguous=True)
                    abt = sbt(st, "d_abt", [128, 2, 4])
                    b_abt = Buf()
                    K.dma("sp", abt[:, 0, :], a_log[l].partition_broadcast(128), W=[b_abt])
                    K.dma("sp", abt[:, 1, :], dt_bias[l].partition_broadcast(128), W=[b_abt])
                    K.op("act", lambda: nc.scalar.activation(out=abt[:, 0, :], in_=abt[:, 0, :], func=AF.Exp),
                         R=[b_abt], W=[b_abt])
                    K.op("dve", lambda: nc.vector.tensor_scalar(out=abt[:, 0, :], in0=abt[:, 0, :], scalar1=-1.0,
                                                                scalar2=None, op0=ALU.mult), R=[b_abt], W=[b_abt])
                    Xs = sbt(st, "d_Xs", [128, 4, 2, 128])
                    tok3 = [sbt(st, "d_tok3%d" % i, [128, 3, 4, 128], BF16) for i in range(2)]
                    cbT = tok3[0][:, :, :, :].rearrange("p a h c -> p (a h c)").bitcast(F32)[:, 0:576].rearrange("p (b s) -> p b s", b=12)
                    b_cbT = Buf()
                    with ExitStack() as stc:
                        cb48 = sbt(stc, "d_cb48", [48, 1536])
                        b_cb48 = Buf()
                        K.dma("sp", cb48[:], state_conv[l].rearrange("s i c -> (s i) c"), W=[b_cb48])
                        for half in range(2):
                            q = half
                            for b6 in range(6):
                                blk = 6 * half + b6
                                K.op("pe", lambda blk=blk, b6=b6, q=q: nc.tensor.transpose(
                                    psum[q][:, b6 * 48:(b6 + 1) * 48], cb48[0:48, blk * 128:(blk + 1) * 128],
                                    C["ident"][0:48, 0:48]), R=[b_cb48, b_cst], W=[b_ps[q]], inc=(b6 == 5))
                            K.op("act", lambda half=half, q=q: nc.scalar.copy(
                                out=cbT[:, 6 * half:6 * half + 6, :],
                                in_=psum[q][:, 0:288].rearrange("p (a b) -> p a b", a=6)), R=[b_ps[q]], W=[b_cbT])
                        K.barrier()
                    K.dma("sp", o_conv_s[l][:, 0:2, :], state_conv[l][:, 1:3, :])
                    halo = sbt(st, "d_halo", [128, 12, 4])
                    b_halo = Buf()
                    K.op("dve", lambda: nc.vector.memset(halo[:], 0.0), W=[b_halo])
                    Pe = [sbt(st, "d_Pe%d" % i, [128, 3 + 512]) for i in range(2)]
                    b_Pe = [Buf(), Buf()]
                    acc = [sbt(st, "d_acc%d" % i, [128, 512]) for i in range(2)]
                    b_acc = [Buf(), Buf()]
                    if 4 * T >= 8192:
                        cq8 = big[:, 4 * T:4 * T + 8192].bitcast(F32).rearrange("p (a b) -> p a b", a=8)
                    else:
                        cq8 = sbt(st, "d_cq8", [128, 8, 512])
                    b_cq8 = [Buf() for _ in range(8)]
                    b_sqs = [Buf(), Buf()]
                    vn = sbt(st, "d_vn", [128, 4, 512], BF16)
                    b_vn = Buf()
                    qkn = sbt(st, "d_qkn", [128, 8, 512], BF16)
                    b_qkn = Buf()
                    gz = sbt(st, "d_gz", [128, 4, 512], BF16)
                    b_gz = Buf()
                    oA = big[:, 8 * T:8 * T + 4096].bitcast(F32).rearrange("p (a b) -> p a b", a=4)
                    b_oA = Buf()
                    sq = big[:, 8 * T + 4096:8 * T + 6144].rearrange("p (a b) -> p a b", a=4)
                    rsd = sbt(st, "d_rsd", [128, 512])
                    rsd_b = sbt(st, "d_rsdb", [128, 512])
                    tn = acc[0]
                    b_sq, b_rsd, b_tn, b_rsdb = Buf(), Buf(), b_acc[0], Buf()
                    nbufs = (sq, b_sq, rsd, b_rsd, tn, b_tn)
                    pnew = big[0:16, 8 * T:8 * T + 3072].bitcast(F32)
                    b_pnew = b_oA
                    tb = [sbt(st, "d_tb%d" % i, [128, 6, 4]) for i in range(2)]
                    b_tb = [Buf(), Buf()]
                    r2 = sbt(st, "d_r2", [128, 8])
                    b_r2 = Buf()
                    Gbc = [sbt(st, "d_Gbc%d" % i, [128, 2, 4]) for i in range(2)]
                    b_Gbc = [Buf(), Buf()]
                    dg = sbt(st, "d_dg", [128, 8, 128])
                    b_dg = Buf()
                    gm = sbt(st, "d_gm", [128, 4, 128])
                    b_gm = Buf()
                    ET = sbt(st, "d_ET", [128, 4, 128])
                    ETi = sbt(st, "d_ETi", [128, 4, 128])
                    ETs = ET
                    b_ET, b_ETi = Buf(), Buf()
                    b_ETs = b_ET
                    QA = [sbt(st, "d_QA%d" % i, [128, 4, 128]) for i in range(2)]
                    XA = [sbt(st, "d_XA%d" % i, [128, 4, 2, 128]) for i in range(2)]
                    QTA = [XA[i][:, :, 0, :] for i in range(2)]
                    b_QA = [Buf(), Buf()]
                    b_QTA = [Buf(), Buf()]
                    Qs = sbt(st, "d_Qs", [128, 4, 128])
                    QTs = Xs[:, :, 0, :]
                    b_Qs, b_QTs, b_Xst = Buf(), Buf(), Buf()
                    TT = [XA[i][:, :, 1, :] for i in range(2)]
                    b_TT = [Buf(), Buf()]
                    TTb = sbt(st, "d_TTb", [128, 4, 128], BF16)
                    b_TTb = Buf()
                    bkq = [sbt(st, "d_bkq%d" % i, [128, 2, 4, 128], BF16) for i in range(3)]
                    b_bkq = [Buf(), Buf(), Buf()]
                    qkT = [sbt(st, "d_qkT%d" % i, [128, 4, 128], BF16) for i in range(3)]
                    b_qkT = [Buf(), Buf(), Buf()]
                    b_tok3 = [Buf(), Buf()]
                    u0 = [sbt(st, "d_u0%d" % i, [128, 4, 128]) for i in range(2)]
                    wT = [sbt(st, "d_wT%d" % i, [128, 4, 128], BF16) for i in range(2)]
                    uu = sbt(st, "d_u", [128, 4, 128], BF16)
                    b_u0, b_wT, b_u = [Buf(), Buf()], [Buf(), Buf()], Buf()
                    S = sbt(st, "d_S", [128, 4, 128])
                    Sb = sbt(st, "d_Sb", [128, 4, 128], BF16)
                    b_S, b_Sb = Buf(), Buf()
                    K.op("dve", lambda: nc.vector.memset(S[:], 0.0), W=[b_S])
                    K.op("dve", lambda: nc.vector.memset(Sb[:], 0.0), W=[b_Sb])
                    K.op("dve", lambda: nc.vector.memset(uu[:], 0.0), W=[b_u])
                    sub_ctr = [0]
                    scan_done = [0]
                    tb_all = sbt(st, "d_tball", [128, 4, 6, 4])
                    Gbc_all = sbt(st, "d_gball", [128, 4, 2, 4])
                    r2_all = sbt(st, "d_r2all", [128, 4, 2, 4])
                    b_tball, b_gball, b_r2all = Buf(), Buf(), Buf()

                    def stageT(subs, c0):
                        ns = len(subs)
                        nt = 128
                        m_incl, m_bd = C["mu_incl"], C["bd"]
                        for si, (kind, sc0, _) in enumerate(subs):
                            mm_group(psum[2][0:nt, 8 * si:8 * si + 8], [(x_bf[:, k, sc0:sc0 + nt], wA[:, k, 2048:2056]) for k in range(KC)],
                                     R=[b_wAb[16], b_xbf], W=[b_ps[2]])
                        raw = psum[2][0:nt, 0:8 * ns].rearrange("p (s c) -> p s c", s=ns)
                        TA = lambda i: tb_all[0:nt, 0:ns, i, :]
                        bcs = lambda ap: ap.unsqueeze(1).broadcast_to([nt, ns, 4])
                        K.op("act", lambda: nc.scalar.activation(out=TA(0), in_=raw[:, :, 0:4], func=AF.Exp, scale=-1.0),
                             R=[b_ps[2]], W=[b_tball])
                        K.op("dve", lambda: nc.vector.tensor_tensor(out=TA(2), in0=raw[:, :, 4:8], in1=bcs(abt[0:nt, 1, :]),
                                                                    op=ALU.add), R=[b_ps[2], b_abt], W=[b_tball])
                        K.op("act", lambda: nc.scalar.activation(out=TA(0), in_=TA(0), func=AF.Ln, bias=one1[0:nt, 0:1]),
                             R=[b_tball, b_eps], W=[b_tball])
                        K.op("act", lambda: nc.scalar.activation(out=TA(0), in_=TA(0), func=AF.Exp, scale=-1.0),
                             R=[b_tball], W=[b_tball])
                        K.op("act", lambda: nc.scalar.activation(out=TA(2), in_=TA(2), func=AF.Exp), R=[b_tball], W=[b_tball])
                        K.op("act", lambda: nc.scalar.activation(out=TA(2), in_=TA(2), func=AF.Ln, bias=one1[0:nt, 0:1]),
                             R=[b_tball, b_eps], W=[b_tball])
                        K.op("dve", lambda: nc.vector.tensor_tensor(out=TA(2), in0=TA(2), in1=bcs(abt[0:nt, 0, :]), op=ALU.mult),
                             R=[b_tball, b_abt], W=[b_tball])
                        gam_ps = psum[2][0:nt, 32:32 + 4 * ns].rearrange("p (s c) -> p s c", s=ns)
                        gto_ps = psum[2][0:nt, 48:48 + 4 * ns].rearrange("p (s c) -> p s c", s=ns)
                        mm_group(gam_ps, [(m_incl[0:nt, 0:nt], TA(2))], R=[b_tball, b_cst], W=[b_ps[2]])
                        mm_group(gto_ps, [(m_bd[0:nt, 0:nt], TA(2))], R=[b_tball, b_cst], W=[b_ps[2]])
                        K.op("act", lambda: nc.scalar.activation(out=TA(1), in_=gam_ps, func=AF.Exp), R=[b_ps[2]], W=[b_tball])
                        K.op("dve", lambda: nc.vector.tensor_copy(out=TA(5), in_=gam_ps), R=[b_ps[2]], W=[b_tball])
                        K.op("dve", lambda: nc.vector.tensor_tensor(out=TA(3), in0=gto_ps, in1=TA(5), op=ALU.subtract),
                             R=[b_ps[2], b_tball], W=[b_tball])
                        K.op("act", lambda: nc.scalar.activation(out=TA(3), in_=TA(3), func=AF.Exp), R=[b_tball], W=[b_tball])
                        K.op("dve", lambda: nc.vector.tensor_tensor(out=TA(4), in0=TA(0), in1=TA(1), op=ALU.mult),
                             R=[b_tball], W=[b_tball])
                        K.op("dve", lambda: nc.vector.tensor_tensor(
                            out=r2_all[0:nt, 0:ns], in0=TA(2).unsqueeze(2).broadcast_to([nt, ns, 2, 4]),
                            in1=cst[0:nt, IDX["blk0"]:IDX["blk0"] + 2, 0:1].unsqueeze(1).broadcast_to([nt, ns, 2, 4]), op=ALU.mult),
                            R=[b_tball, b_cst], W=[b_r2all])
                        mm_group(psum[2][:, 64:64 + 8 * ns], [(C["ones"][0:nt, :], r2_all[0:nt, 0:ns].rearrange("p s c h -> p (s c h)"))],
                                 R=[b_r2all, b_cst], W=[b_ps[2]])
                        K.op("act", lambda: nc.scalar.activation(out=Gbc_all[:, 0:ns].rearrange("p s c h -> p (s c h)"),
                                                                 in_=psum[2][:, 64:64 + 8 * ns], func=AF.Exp),
                             R=[b_ps[2]], W=[b_gball])

                    def stageA(kind, sc0, nt, a, c0, si=None, a3=0):
                        TB, b_TB = (tb[a], b_tb[a]) if si is None else (tb_all[:, si], b_tball)
                        GB, b_GB = (Gbc[a], b_Gbc[a]) if si is None else (Gbc_all[:, si], b_gball)
                        lc = sc0 - c0
                        smp = (kind == "sample")
                        m_incl = C["ident"] if smp else C["mu_incl"]
                        m_bd = C["ident"] if smp else C["bd"]
                        m_ustrict = C["zero"] if smp else C["mu_strict"]
                        m_lstrict = C["zero"] if smp else C["ml_strict"]
                        T_ = lambda i: TB[0:nt, i, :]
                        pv = lambda b: psum[b][:, :].rearrange("p (a c) -> p a c", a=4)[:, :, 0:nt]
                        pt = lambda b: psum[b][0:nt, :].rearrange("p (a c) -> p a c", a=4)[:, :, 0:nt]
                        p4 = lambda b: psum[b][0:nt, :].rearrange("p (a c) -> p a c", a=4)
                        bc = lambda i: TB[0:nt, i, :].unsqueeze(2).broadcast_to([nt, 4, 128])
                        if si is None:
                            mm_group(psum[2][0:nt, 0:8], [(x_bf[:, k, sc0:sc0 + nt], wA[:, k, 2048:2056]) for k in range(KC)],
                                     R=[b_wAb[16], b_xbf], W=[b_ps[2]])
                            yield
                            K.op("act", lambda: nc.scalar.activation(out=T_(0), in_=psum[2][0:nt, 0:4], func=AF.Exp, scale=-1.0),
                                 R=[b_ps[2]], W=[b_TB])
                            yield
                            K.op("dve", lambda: nc.vector.tensor_tensor(out=T_(2), in0=psum[2][0:nt, 4:8], in1=abt[0:nt, 1, :],
                                                                        op=ALU.add), R=[b_ps[2], b_abt], W=[b_TB])
                            yield
                            K.op("act", lambda: nc.scalar.activation(out=T_(0), in_=T_(0), func=AF.Ln, bias=one1[0:nt, 0:1]),
                                 R=[b_TB, b_eps], W=[b_TB])
                            yield
                            K.op("act", lambda: nc.scalar.activation(out=T_(0), in_=T_(0), func=AF.Exp, scale=-1.0),
                                 R=[b_TB], W=[b_TB])
                            yield
                            K.op("act", lambda: nc.scalar.activation(out=T_(2), in_=T_(2), func=AF.Exp), R=[b_TB], W=[b_TB])
                            yield
                            K.op("act", lambda: nc.scalar.activation(out=T_(2), in_=T_(2), func=AF.Ln, bias=one1[0:nt, 0:1]),
                                 R=[b_TB, b_eps], W=[b_TB])
                            yield
                            K.op("dve", lambda: nc.vector.tensor_tensor(out=T_(2), in0=T_(2), in1=abt[0:nt, 0, :], op=ALU.mult),
                                 R=[b_TB, b_abt], W=[b_TB])
                            yield
                            mm_group(psum[2][0:nt, 8:12], [(m_incl[0:nt, 0:nt], T_(2))], R=[b_TB, b_cst], W=[b_ps[2]])
                            yield
                            mm_group(psum[2][0:nt, 12:16], [(m_bd[0:nt, 0:nt], T_(2))], R=[b_TB, b_cst], W=[b_ps[2]])
                            yield
                            K.op("act", lambda: nc.scalar.activation(out=T_(1), in_=psum[2][0:nt, 8:12], func=AF.Exp),
                                 R=[b_ps[2]], W=[b_TB])
                            yield
                            K.op("dve", lambda: nc.vector.tensor_copy(out=T_(5), in_=psum[2][0:nt, 8:12]), R=[b_ps[2]], W=[b_TB])
                            yield
                            K.op("dve", lambda: nc.vector.tensor_tensor(out=T_(3), in0=psum[2][0:nt, 12:16], in1=T_(5),
                                                                        op=ALU.subtract), R=[b_ps[2], b_TB], W=[b_TB])
                            yield
                            K.op("act", lambda: nc.scalar.activation(out=T_(3), in_=T_(3), func=AF.Exp), R=[b_TB], W=[b_TB])
                            yield
                            K.op("dve", lambda: nc.vector.tensor_tensor(out=T_(4), in0=T_(0), in1=T_(1), op=ALU.mult),
                                 R=[b_TB], W=[b_TB])
                            yield
                            if smp:
                                K.op("dve", lambda: nc.vector.tensor_tensor(
                                    out=dg[0:16, 0:4, 0:16].bitcast(F32R).rearrange("p h s -> p s h"),
                                    in0=T_(2).unsqueeze(1).broadcast_to([16, 16, 4]),
                                    in1=C["ident"][0:16, 0:16].unsqueeze(2).broadcast_to([16, 16, 4]), op=ALU.mult),
                                    R=[b_TB, b_cst], W=[b_dg])
                                for h in range(4):
                                    K.op("pe", lambda h=h: nc.tensor.matmul(psum[3][:, h * 16:(h + 1) * 16],
                                                                            lhsT=C["ones"][0:16, :], rhs=dg[0:16, h, 0:16],
                                                                            start=True, stop=True),
                                         R=[b_dg, b_cst], W=[b_ps[3]], inc=(h == 3))
                                K.op("act", lambda: nc.scalar.activation(
                                    out=s_G[:, :, :].rearrange("p s h -> p h s"),
                                    in_=psum[3][:, 0:64].rearrange("p (h s) -> p h s", h=4), func=AF.Exp),
                                    R=[b_ps[3]], W=[b_skeep])
                            else:
                                K.op("dve", lambda: nc.vector.tensor_tensor(
                                    out=r2[0:nt, :].rearrange("p (c h) -> p c h", c=2),
                                    in0=T_(2).unsqueeze(1).broadcast_to([nt, 2, 4]),
                                    in1=cst[0:nt, IDX["blk0"]:IDX["blk0"] + 2, 0:1].broadcast_to([nt, 2, 4]), op=ALU.mult),
                                    R=[b_TB, b_cst], W=[b_r2])
                                mm_group(psum[2][:, 16:24], [(C["ones"][0:nt, :], r2[0:nt, :])], R=[b_r2, b_cst], W=[b_ps[2]])
                                K.op("act", lambda: nc.scalar.activation(out=GB[:, :, :].rearrange("p c h -> p (c h)"),
                                                                         in_=psum[2][:, 16:24], func=AF.Exp),
                                     R=[b_ps[2]], W=[b_GB])
                        yield
                        K.op("dve", lambda: nc.vector.tensor_tensor(
                            out=dg[0:nt, :, 0:nt].bitcast(F32R), in0=C["ident"][0:nt, 0:nt].unsqueeze(1).broadcast_to([nt, 8, nt]),
                            in1=TB[0:nt, 0:2, :].rearrange("p a h -> p (a h)").unsqueeze(2).broadcast_to([nt, 8, nt]),
                            op=ALU.mult), R=[b_TB, b_cst, b_skeep], W=[b_dg])
                        yield
                        for wh in range(2):
                            K.op("pe", lambda wh=wh: nc.tensor.matmul(
                                psum[3 + wh][:, :].rearrange("p (a c) -> p a c", a=4)[:, :, 0:nt],
                                lhsT=ones_r[0:nt, :].bitcast(F32R), rhs=dg[0:nt, 4 * wh:4 * wh + 4, 0:nt].bitcast(F32R),
                                start=True, stop=True), R=[b_dg, b_onesr], W=[b_ps[3 + wh]], c=0.25)
                        yield
                        K.op("dve", lambda: nc.vector.tensor_tensor(out=bkq[a3][:, 0, :, 0:nt], in0=qkn[:, 4:8, lc:lc + nt],
                                                                    in1=pv(3), op=ALU.mult),
                             R=[b_qkn, b_ps[3]], W=[b_bkq[a3]])
                        yield
                        K.op("dve", lambda: nc.vector.tensor_tensor(out=bkq[a3][:, 1, :, 0:nt], in0=qkn[:, 0:4, lc:lc + nt],
                                                                    in1=pv(4), op=ALU.mult),
                             R=[b_qkn, b_ps[4]], W=[b_bkq[a3]])
                        yield
                        K.op("dve", lambda: nc.vector.tensor_tensor(
                            out=gm[0:nt, :, 0:nt], in0=m_incl[0:nt, 0:nt].unsqueeze(1).broadcast_to([nt, 4, nt]),
                            in1=T_(2).unsqueeze(2).broadcast_to([nt, 4, nt]), op=ALU.mult), R=[b_TB, b_cst], W=[b_gm])
                        yield
                        for h in range(4):
                            K.op("pe", lambda h=h: nc.tensor.matmul(psum[2][0:nt, h * 128:h * 128 + nt],
                                                                    lhsT=m_lstrict[0:nt, 0:nt], rhs=gm[0:nt, h, 0:nt],
                                                                    start=True, stop=True),
                                 R=[b_gm, b_cst], W=[b_ps[2]], inc=(h == 3))
                        yield
                        K.op("act", lambda: nc.scalar.activation(out=ET[0:nt, :, 0:nt], in_=pt(2), func=AF.Exp),
                             R=[b_ps[2]], W=[b_ET])
                        yield
                        K.op("dve", lambda: nc.vector.tensor_tensor(
                            out=ETi[0:nt, :, 0:nt], in0=ET[0:nt, :, 0:nt],
                            in1=m_incl[0:nt, 0:nt].unsqueeze(1).broadcast_to([nt, 4, nt]), op=ALU.mult),
                            R=[b_ET, b_cst], W=[b_ETi])
                        yield
                        K.op("dve", lambda: nc.vector.tensor_tensor(
                            out=ETs[0:nt, :, 0:nt], in0=ET[0:nt, :, 0:nt],
                            in1=m_ustrict[0:nt, 0:nt].unsqueeze(1).broadcast_to([nt, 4, nt]), op=ALU.mult),
                            R=[b_ET, b_cst, b_ETi], W=[b_ETs])
                        yield
                        for h in range(4):
                            K.op("pe", lambda h=h: nc.tensor.matmul(psum[3][0:nt, h * 128:h * 128 + nt],
                                                                    lhsT=qkn[:, 4 + h, lc:lc + nt], rhs=bkq[a3][:, 0, h, 0:nt],
                                                                    start=True, stop=True),
                                 R=[b_qkn, b_bkq[a3]], W=[b_ps[3]], inc=(h == 3))
                        yield
                        for h in range(4):
                            K.op("pe", lambda h=h: nc.tensor.matmul(psum[4][0:nt, h * 128:h * 128 + nt],
                                                                    lhsT=qkn[:, 4 + h, lc:lc + nt], rhs=qkn[:, h, lc:lc + nt],
                                                                    start=True, stop=True),
                                 R=[b_qkn], W=[b_ps[4]], inc=(h == 3))
                        AT, A_ = QTA[a], QA[a]
                        yield
                        K.op("dve", lambda: nc.vector.tensor_tensor(out=AT[0:nt, :, 0:nt].bitcast(F32R), in0=pt(3), in1=ETs[0:nt, :, 0:nt],
                                                                    op=ALU.mult), R=[b_ps[3], b_ETs], W=[b_QTA[a]])
                        yield
                        K.op("dve", lambda: nc.vector.tensor_tensor(out=qkT[a3][0:nt, :, 0:nt], in0=pt(4), in1=ETi[0:nt, :, 0:nt],
                                                                    op=ALU.mult), R=[b_ps[4], b_ETi], W=[b_qkT[a3]])
                        yield
                        K.op("dve", lambda: nc.vector.scalar_tensor_tensor(
                            out=TT[a][0:nt, :, 0:nt].bitcast(F32R), in0=AT[0:nt, :, 0:nt], scalar=-1.0,
                            in1=C["ident"][0:nt, 0:nt].unsqueeze(1).broadcast_to([nt, 4, nt]),
                            op0=ALU.mult, op1=ALU.add), R=[b_QTA[a], b_cst], W=[b_TT[a]])
                        yield
                        if not smp:
                            for h in range(4):
                                K.op("pe", lambda h=h: nc.tensor.transpose(psum[2][0:nt, h * 128:h * 128 + nt],
                                                                           AT[0:nt, h, 0:nt], C["ident"][0:nt, 0:nt]),
                                     R=[b_QTA[a], b_cst], W=[b_ps[2]], inc=(h == 3))
                            yield
                            K.op("act", lambda: nc.scalar.copy(out=A_[0:nt, :, 0:nt].bitcast(F32R), in_=pt(2)), R=[b_ps[2]], W=[b_QA[a]])

                    def stageB1(kind, sc0, nt, a, c0, si=None, a3=0):
                        TB, b_TB = (tb[a], b_tb[a]) if si is None else (tb_all[:, si], b_tball)
                        GB, b_GB = (Gbc[a], b_Gbc[a]) if si is None else (Gbc_all[:, si], b_gball)
                        lc = sc0 - c0
                        smp = (kind == "sample")
                        m_incl = C["ident"] if smp else C["mu_incl"]
                        m_bd = C["ident"] if smp else C["bd"]
                        m_ustrict = C["zero"] if smp else C["mu_strict"]
                        m_lstrict = C["zero"] if smp else C["ml_strict"]
                        T_ = lambda i: TB[0:nt, i, :]
                        pv = lambda b: psum[b][:, :].rearrange("p (a c) -> p a c", a=4)[:, :, 0:nt]
                        pt = lambda b: psum[b][0:nt, :].rearrange("p (a c) -> p a c", a=4)[:, :, 0:nt]
                        p4 = lambda b: psum[b][0:nt, :].rearrange("p (a c) -> p a c", a=4)
                        bc = lambda i: TB[0:nt, i, :].unsqueeze(2).broadcast_to([nt, 4, 128])
                        yield
                        for h in range(4):
                            K.op("pe", lambda h=h: nc.tensor.matmul(psum[5][0:nt, h * 128:(h + 1) * 128],
                                                                    lhsT=qkn[:, 4 + h, lc:lc + nt], rhs=CB["ident"],
                                                                    start=True, stop=True),
                                 R=[b_qkn, b_cst], W=[b_ps[5]], inc=(h == 3))
                        yield
                        for h in range(4):
                            K.op("pe", lambda h=h: nc.tensor.matmul(psum[6][0:nt, h * 128:(h + 1) * 128],
                                                                    lhsT=vn[:, h, lc:lc + nt], rhs=CB["ident"],
                                                                    start=True, stop=True),
                                 R=[b_vn, b_cst], W=[b_ps[6]], inc=(h == 3))
                        yield
                        K.op("dve", lambda: nc.vector.tensor_tensor(out=tok3[a][0:nt, 0], in0=p4(5), in1=bc(4), op=ALU.mult),
                             R=[b_ps[5], b_TB], W=[b_tok3[a]])
                        yield
                        K.op("dve", lambda: nc.vector.tensor_tensor(out=tok3[a][0:nt, 1], in0=p4(5), in1=bc(3), op=ALU.mult),
                             R=[b_ps[5], b_TB], W=[b_tok3[a]])
                        yield
                        K.op("dve", lambda: nc.vector.tensor_tensor(out=tok3[a][0:nt, 2], in0=p4(6), in1=bc(0), op=ALU.mult),
                             R=[b_ps[6], b_TB], W=[b_tok3[a]])
                        fin_X, b_fin = XA[a], b_TT[a]
                        if not smp:
                            n_inc = 5 if nt == 128 else 3
                            cq_, b_cq_ = QA[a], b_QA[a]
                            nq_, b_nq_ = Qs, b_Qs
                            cX, b_cXq, b_cXt = XA[a], b_QTA[a], b_TT[a]
                            nX, b_nXq, b_nXt = Xs, b_QTs, b_Xst
                            f32r = lambda ap: ap.bitcast(F32R)
                            for lv in range(0, n_inc + 1):
                                need_q = lv < n_inc
                                need_qt = lv < n_inc - 1
                                if lv >= 1:
                                    for h in range(4):
                                        K.op("pe", lambda h=h: nc.tensor.matmul(
                                            ps67[0:nt, h * 256:(h + 1) * 256].rearrange("p (j c) -> p j c", j=2)[:, :, 0:nt],
                                            lhsT=f32r(cq_[0:nt, h, 0:nt]), rhs=f32r(cX[0:nt, h, :, 0:nt]), start=True, stop=True),
                                            R=[b_cq_, b_cXq, b_cXt], W=[b_ps[6], b_ps[7]], inc=(h == 3), c=0.13)
                                else:
                                    for h in range(4):
                                        K.op("pe", lambda h=h: nc.tensor.matmul(
                                            ps67[0:nt, h * 256:h * 256 + nt], lhsT=cq_[0:nt, h, 0:nt],
                                            rhs=cX[0:nt, h, 0, 0:nt], start=True, stop=True),
                                            R=[b_cq_, b_cXq], W=[b_ps[6], b_ps[7]], inc=(h == 3), c=0.22)
                                if need_q:
                                    for h in range(4):
                                        K.op("pe", lambda h=h: nc.tensor.matmul(
                                            psum[5][0:nt, h * 128:h * 128 + nt], lhsT=cX[0:nt, h, 0, 0:nt],
                                            rhs=cq_[0:nt, h, 0:nt], start=True, stop=True),
                                            R=[b_cq_, b_cXq], W=[b_ps[5]], inc=(h == 3), c=0.22)
                                yield
                                p67 = ps67[0:nt, :].rearrange("p (h j c) -> p h j c", h=4, j=2)
                                if need_q:
                                    K.op("act", lambda: nc.scalar.copy(out=nq_[0:nt, :, 0:nt].bitcast(F32R), in_=pt(5)),
                                         R=[b_ps[5]], W=[b_nq_])
                                if lv >= 1:
                                    K.op("dve", lambda: nc.vector.tensor_tensor(out=nX[0:nt, :, 1, 0:nt].bitcast(F32R), in0=cX[0:nt, :, 1, 0:nt],
                                                                                in1=p67[:, :, 1, 0:nt], op=ALU.add),
                                         R=[b_ps[6], b_ps[7], b_cXt], W=[b_nXt])
                                else:
                                    K.op("dve", lambda: nc.vector.tensor_copy(out=nX[0:nt, :, 1, 0:nt].bitcast(F32R), in_=cX[0:nt, :, 1, 0:nt]),
                                         R=[b_cXt], W=[b_nXt])
                                if need_qt:
                                    K.op("act", lambda: nc.scalar.copy(out=nX[0:nt, :, 0, 0:nt].bitcast(F32R), in_=p67[:, :, 0, 0:nt]),
                                         R=[b_ps[6], b_ps[7]], W=[b_nXq])
                                yield
                                cq_, b_cq_, nq_, b_nq_ = nq_, b_nq_, cq_, b_cq_
                                cX, b_cXq, b_cXt, nX, b_nXq, b_nXt = nX, b_nXq, b_nXt, cX, b_cXq, b_cXt
                            fin_X, b_fin = cX, b_cXt
                        yield
                        K.op("act", lambda: nc.scalar.copy(out=TTb[0:nt, :, 0:nt], in_=fin_X[0:nt, :, 1, 0:nt]), R=[b_fin], W=[b_TTb])
                        yield
                        yield
                        for h in range(4):
                            K.op("pe", lambda h=h: nc.tensor.matmul(psum[7][0:nt, h * 128:(h + 1) * 128],
                                                                    lhsT=TTb[0:nt, h, 0:nt], rhs=tok3[a][0:nt, 2, h, :],
                                                                    start=True, stop=True),
                                 R=[b_TTb, b_tok3[a]], W=[b_ps[7]], inc=(h == 3))
                        yield
                        for h in range(4):
                            K.op("pe", lambda h=h: nc.tensor.matmul(psum[5][:, h * 128:h * 128 + nt],
                                                                    lhsT=tok3[a][0:nt, 0, h, :], rhs=TTb[0:nt, h, 0:nt],
                                                                    start=True, stop=True),
                                 R=[b_TTb, b_tok3[a]], W=[b_ps[5]], inc=(h == 3))
                        yield
                        if smp:
                            K.op("act", lambda: nc.scalar.copy(out=s_u0[:], in_=p4(7)), R=[b_ps[7]], W=[b_skeep])
                            K.op("dve", lambda: nc.vector.tensor_copy(out=s_wT[:], in_=pv(5)), R=[b_ps[5]], W=[b_skeep])
                            K.op("dve", lambda: nc.vector.tensor_copy(out=s_kd[:], in_=tok3[a][0:16, 1]), R=[b_tok3[a]], W=[b_skeep])
                            K.op("dve", lambda: nc.vector.tensor_copy(out=s_qn[:], in_=qkn[:, 0:4, 0:16]), R=[b_qkn], W=[b_skeep])
                            K.op("dve", lambda: nc.vector.tensor_copy(out=s_gz[:], in_=gz[:, :, 0:16]), R=[b_gz], W=[b_skeep])
                            return
                        yield
                        K.op("act", lambda: nc.scalar.copy(out=u0[a][0:nt], in_=p4(7)), R=[b_ps[7]], W=[b_u0[a]])
                        yield
                        K.op("dve", lambda: nc.vector.tensor_copy(out=wT[a][:, :, 0:nt], in_=pv(5)), R=[b_ps[5]], W=[b_wT[a]])

                    def stageB2(kind, sc0, nt, a, c0, si=None, a3=0):
                        TB, b_TB = (tb[a], b_tb[a]) if si is None else (tb_all[:, si], b_tball)
                        GB, b_GB = (Gbc[a], b_Gbc[a]) if si is None else (Gbc_all[:, si], b_gball)
                        lc = sc0 - c0
                        smp = (kind == "sample")
                        m_incl = C["ident"] if smp else C["mu_incl"]
                        m_bd = C["ident"] if smp else C["bd"]
                        m_ustrict = C["zero"] if smp else C["mu_strict"]
                        m_lstrict = C["zero"] if smp else C["ml_strict"]
                        T_ = lambda i: TB[0:nt, i, :]
                        pv = lambda b: psum[b][:, :].rearrange("p (a c) -> p a c", a=4)[:, :, 0:nt]
                        pt = lambda b: psum[b][0:nt, :].rearrange("p (a c) -> p a c", a=4)[:, :, 0:nt]
                        p4 = lambda b: psum[b][0:nt, :].rearrange("p (a c) -> p a c", a=4)
                        bc = lambda i: TB[0:nt, i, :].unsqueeze(2).broadcast_to([nt, 4, 128])
                        if smp:
                            return
                        nch = 2 if nt == 128 else 1
                        cl = 64 if nt == 128 else nt
                        yield
                        for ch in range(nch):
                            tr = slice(ch * 64, ch * 64 + cl)
                            yield
                            for h in range(4):
                                K.op("pe", lambda h=h, tr=tr: nc.tensor.matmul(psum[0][tr, h * 128:(h + 1) * 128],
                                                                              lhsT=wT[a][:, h, tr], rhs=Sb[:, h, :],
                                                                              start=True, stop=True),
                                     R=[b_wT[a], b_Sb], W=[b_ps[0]], inc=(h == 3))
                            yield
                            K.op("dve", lambda tr=tr: nc.vector.tensor_tensor(
                                out=uu[tr], in0=u0[a][tr], in1=psum[0][tr, :].rearrange("p (a c) -> p a c", a=4),
                                op=ALU.subtract), R=[b_u0[a], b_ps[0]], W=[b_u])
                            yield
                            for h in range(4):
                                K.op("pe", lambda h=h, tr=tr: nc.tensor.matmul(psum[1][:, h * 64:h * 64 + cl],
                                                                              lhsT=Sb[:, h, :], rhs=bkq[a3][:, 1, h, tr],
                                                                              start=True, stop=False),
                                     R=[b_Sb, b_bkq[a3]], W=[b_ps[1]], inc=False)
                                K.op("pe", lambda h=h, tr=tr: nc.tensor.matmul(psum[1][:, h * 64:h * 64 + cl],
                                                                              lhsT=uu[0:nt, h, :], rhs=qkT[a3][0:nt, h, tr],
                                                                              start=False, stop=True),
                                     R=[b_u, b_qkT[a3]], W=[b_ps[1]], inc=(h == 3))
                            yield
                            for h in range(4):
                                K.op("pe", lambda h=h, tr=tr: nc.tensor.matmul(psum[0][:, h * 128:(h + 1) * 128],
                                                                              lhsT=tok3[a][tr, 1, h, :], rhs=uu[tr, h, :],
                                                                              start=True, stop=True),
                                     R=[b_tok3[a], b_u], W=[b_ps[0]], inc=(h == 3))
                            yield
                            K.op("act", lambda ch=ch: nc.scalar.copy(
                                out=oA[:, :, lc + ch * 64:lc + ch * 64 + cl],
                                in_=psum[1][:, 0:256].rearrange("p (a b) -> p a b", a=4)[:, :, 0:cl]),
                                R=[b_ps[1]], W=[b_oA])
                            yield
                            for h in range(4):
                                K.op("dve", lambda h=h, ch=ch: nc.vector.scalar_tensor_tensor(
                                    out=S[:, h, :], in0=S[:, h, :], scalar=GB[:, ch, h:h + 1],
                                    in1=psum[0][:, h * 128:(h + 1) * 128], op0=ALU.mult, op1=ALU.add),
                                    R=[b_GB, b_ps[0]], W=[b_S])
                            yield
                            K.op("act", lambda: nc.scalar.copy(out=Sb[:], in_=S[:]), R=[b_S], W=[b_Sb])
                        scan_done[0] += 1
                        yield

                    blk_it = 0
                    for (c0, n) in TILES:
                        is_ms = (c0 == 0)
                        def blkfn(blk, c0=c0, n=n, is_ms=is_ms):
                            q = blk % 2
                            mm_group(psum[q][:, 0:n], [(wA[:, k, blk * 128:(blk + 1) * 128], x_bf[:, k, c0:c0 + n])
                                                      for k in range(KC)], R=[b_wAb[blk], b_xbf], W=[b_ps[q]])
                            if blk >= 12:
                                K.op("act", lambda blk=blk, q=q: nc.scalar.activation(
                                    out=gz[:, blk - 12, 0:n], in_=psum[q][:, 0:n], func=AF.Silu), R=[b_ps[q]], W=[b_gz])
                                return
                            pe_ = blk % 2
                            ncv = 16 if is_ms else n
                            K.op("pool", lambda pe_=pe_, blk=blk: nc.gpsimd.tensor_copy(out=Pe[pe_][:, 0:3], in_=halo[:, blk, 0:3]),
                                 R=[b_halo], W=[b_Pe[pe_]])
                            src0 = 16 if is_ms else 0
                            K.op("act", lambda pe_=pe_, q=q, src0=src0, ncv=ncv: nc.scalar.copy(
                                out=Pe[pe_][:, 3:3 + ncv], in_=psum[q][:, src0:src0 + ncv]), R=[b_ps[q]], W=[b_Pe[pe_]])
                            K.op("pool", lambda pe_=pe_, blk=blk, ncv=ncv: nc.gpsimd.tensor_copy(
                                out=halo[:, blk, 0:3], in_=Pe[pe_][:, ncv:ncv + 3]), R=[b_Pe[pe_]], W=[b_halo])
                            a_ = acc[pe_]
                            o0 = 16 if is_ms else 0
                            K.op("act", lambda pe_=pe_, blk=blk, ncv=ncv, o0=o0: nc.scalar.activation(
                                out=acc[pe_][:, o0:o0 + ncv], in_=Pe[pe_][:, 3:3 + ncv], func=AF.Identity, scale=cw[:, blk, 3:4]),
                                R=[b_Pe[pe_], b_cw], W=[b_acc[pe_]])
                            for i in (2, 1, 0):
                                K.op("dve", lambda pe_=pe_, blk=blk, ncv=ncv, o0=o0, i=i: nc.vector.scalar_tensor_tensor(
                                    out=acc[pe_][:, o0:o0 + ncv], in0=Pe[pe_][:, i:i + ncv], scalar=cw[:, blk, i:i + 1],
                                    in1=acc[pe_][:, o0:o0 + ncv], op0=ALU.mult, op1=ALU.add),
                                    R=[b_Pe[pe_], b_cw, b_acc[pe_]], W=[b_acc[pe_]])
                            if is_ms:
                                cbv = cbT[:, blk, :].rearrange("p (s i) -> p s i", i=3)
                                K.op("dve", lambda pe_=pe_, blk=blk, q=q: nc.vector.tensor_scalar(
                                    out=acc[pe_][:, 0:16], in0=psum[q][:, 0:16], scalar1=cw[:, blk, 3:4], scalar2=None,
                                    op0=ALU.mult), R=[b_ps[q], b_cw], W=[b_acc[pe_]])
                                for i in range(3):
                                    K.op("dve", lambda pe_=pe_, blk=blk, i=i, cbv=cbv: nc.vector.scalar_tensor_tensor(
                                        out=acc[pe_][:, 0:16], in0=cbv[:, :, i], scalar=cw[:, blk, i:i + 1],
                                        in1=acc[pe_][:, 0:16], op0=ALU.mult, op1=ALU.add),
                                        R=[b_cbT, b_cw, b_acc[pe_]], W=[b_acc[pe_]])
                            if blk < 8:
                                K.op("act", lambda pe_=pe_, blk=blk: nc.scalar.activation(
                                    out=cq8[:, blk, 0:n], in_=acc[pe_][:, 0:n], func=AF.Silu), R=[b_acc[pe_]], W=[b_cq8[blk]])
                            else:
                                K.op("act", lambda pe_=pe_, blk=blk: nc.scalar.activation(
                                    out=vn[:, blk - 8, 0:n], in_=acc[pe_][:, 0:n], func=AF.Silu), R=[b_acc[pe_]], W=[b_vn])
                        K.run_sched([(lambda blk=blk: blkfn(blk), ([blk - 2] if blk >= 2 else [])) for blk in range(16)])

                        def normfn(blk, n=n):
                            ci = blk % 2
                            q = blk % 2
                            rr = rsd if ci == 0 else rsd_b
                            b_rr = b_rsd if ci == 0 else b_rsdb
                            K.op("act", lambda: nc.scalar.activation(out=sq[:, ci, 0:n], in_=cq8[:, blk, 0:n],
                                                                     func=AF.Square), R=[b_cq8[blk]], W=[b_sqs[ci]])
                            mm_group(psum[q][:, 0:n], [(CB["ones"], sq[:, ci, 0:n])], R=[b_sqs[ci], b_cst], W=[b_ps[q]])
                            K.op("act", lambda: nc.scalar.activation(out=rr[:, 0:n], in_=psum[q][:, 0:n], func=AF.Ln,
                                                                     bias=eps6[:, 0:1]), R=[b_ps[q], b_eps], W=[b_rr])
                            K.op("act", lambda: nc.scalar.activation(out=rr[:, 0:n], in_=rr[:, 0:n], func=AF.Exp,
                                                                     scale=-0.5), R=[b_rr], W=[b_rr])
                            scl = (128.0 ** -0.5) if blk < 4 else 1.0
                            K.op("dve", lambda: nc.vector.scalar_tensor_tensor(
                                out=qkn[:, blk, 0:n], in0=cq8[:, blk, 0:n], scalar=scl, in1=rr[:, 0:n],
                                op0=ALU.mult, op1=ALU.mult), R=[b_cq8[blk], b_rr], W=[b_qkn])
                        K.run_sched([(lambda blk=blk: normfn(blk), ([blk - 2] if blk >= 2 else [])) for blk in range(8)])
                        if is_ms:
                            for g3 in range(3):
                                q = g3 % 2
                                mm_group(psum[q][0:16, :], [(x_bf[:, k, 0:16], wA[:, k, g3 * 512:(g3 + 1) * 512])
                                                            for k in range(KC)], R=b_wAb[4 * g3:4 * g3 + 4] + [b_xbf], W=[b_ps[q]])
                                K.op("act", lambda g3=g3, q=q: nc.scalar.copy(out=pnew[:, g3 * 512:(g3 + 1) * 512],
                                                                             in_=psum[q][0:16, :]), R=[b_ps[q]], W=[b_pnew])
                            K.dma("sp", o_conv_s[l][:, 2, :], pnew[:], R=[b_pnew])
                        subs = subtiles(c0, n)
                        ths = []
                        pre = (not is_ms)
                        off = 1 if pre else 0
                        if pre:
                            ths.append((lambda subs=subs, c0=c0: stageT(subs, c0), []))
                        for i_, (kind, sc0, nt) in enumerate(subs):
                            a_ = (sub_ctr[0] + i_) % 2
                            a3_ = (sub_ctr[0] + i_) % 3
                            si_ = i_ if pre else None
                            iA, iB1, iB2 = off + 3 * i_, off + 3 * i_ + 1, off + 3 * i_ + 2
                            afterA = ([iA - 3] if i_ >= 1 else ([0] if pre else [])) + ([iB1 - 6] if i_ >= 2 else []) + ([iB2 - 9] if i_ >= 3 else [])
                            afterB1 = [iA] + ([iB1 - 3] if i_ >= 1 else []) + ([iB2 - 6] if i_ >= 2 else [])
                            afterB2 = [iB1] + ([iB2 - 3] if i_ >= 1 else [])
                            ths.append((lambda kind=kind, sc0=sc0, nt=nt, a_=a_, c0=c0, si_=si_, a3_=a3_: stageA(kind, sc0, nt, a_, c0, si_, a3_), afterA))
                            ths.append((lambda kind=kind, sc0=sc0, nt=nt, a_=a_, c0=c0, si_=si_, a3_=a3_: stageB1(kind, sc0, nt, a_, c0, si_, a3_), afterB1))
                            ths.append((lambda kind=kind, sc0=sc0, nt=nt, a_=a_, c0=c0, si_=si_, a3_=a3_: stageB2(kind, sc0, nt, a_, c0, si_, a3_), afterB2))
                        K.run_sched(ths)
                        sub_ctr[0] += len(subs)
                        if is_ms:
                            K.op("dve", lambda: nc.vector.memset(oA[:, :, 0:16], 0.0), W=[b_oA])
                        gated_norm(st, oA, b_oA, gz, b_gz, nw, b_nw, og, b_og, 0, c0, n, nbufs)
                    K.dma("sp", o_gdn_p[l].rearrange("h k v -> k h v"), S[:], R=[b_S])
                    for g3 in range(3):
                        for b4 in range(4):
                            blk = 4 * g3 + b4
                            K.op("pe", lambda blk=blk, b4=b4, g3=g3: nc.tensor.transpose(
                                psum[g3][0:4, b4 * 128:(b4 + 1) * 128], halo[:, blk, :], C["ident"]),
                                R=[b_halo, b_cst], W=[b_ps[g3]], inc=(b4 == 3))
                        K.op("act", lambda g3=g3: nc.scalar.copy(out=pnew[0:4, g3 * 512:(g3 + 1) * 512], in_=psum[g3][0:4, :]),
                             R=[b_ps[g3]], W=[b_pnew])
                    K.dma("sp", o_conv_p[l], pnew[0:3, :], R=[b_pnew])
                K.barrier()
                with ExitStack() as s2:
                    wTm = sbt(s2, "d_wTm", [128, 4, NS, 16], BF16)
                    b_wTm = Buf()
                    K.op("dve", lambda: nc.vector.tensor_tensor(
                        out=wTm[:], in0=s_wT[:, :, :].unsqueeze(2).broadcast_to([128, 4, NS, 16]),
                        in1=cst[:, IDX["idrep0"]:IDX["idrep0"] + 2, :].rearrange("p a (s t) -> p (a s) t", t=16)
                            .unsqueeze(1).broadcast_to([128, 4, NS, 16]), op=ALU.mult),
                        R=[b_skeep, b_cst], W=[b_wTm])
                    Sg = [sbt(s2, "d_Sg%d" % i, [128, 4, 4, 128]) for i in range(2)]
                    Sgb = [sbt(s2, "d_Sgb%d" % i, [128, 4, 4, 128], BF16) for i in range(2)]
                    b_Sg = [Buf(), Buf()]
                    b_Sgb = [Buf(), Buf()]
                    us = sbt(s2, "d_us", [16, 4, 128], BF16)
                    ubd = sbt(s2, "d_ubd", [16, 4, 512], BF16)
                    b_us, b_ubd = Buf(), Buf()
                    oS = sbt(s2, "d_oS", [128, 4, 16])
                    b_oS = Buf()
                    sq2 = sbt(s2, "d_sq2", [128, 4, 512], BF16)
                    rsd2 = sbt(s2, "d_rsd2", [128, 512])
                    tn2 = sbt(s2, "d_tn2", [128, 512])
                    sgd_v = state_gdn[l].rearrange("s h k v -> k s h v")
                    sgd_o = o_gdn_s[l].rearrange("s h k v -> k s h v")
                    for sg_ in range(4):
                        u = sg_ % 2
                        K.dma("sp", Sg[u][:], sgd_v[:, 4 * sg_:4 * sg_ + 4], W=[b_Sg[u]])
                        K.op("act", lambda u=u: nc.scalar.copy(out=Sgb[u][:], in_=Sg[u][:]), R=[b_Sg[u]], W=[b_Sgb[u]])
                        for h in range(4):
                            for si in range(4):
                                s = 4 * sg_ + si
                                K.op("pe", lambda h=h, si=si, s=s, u=u: nc.tensor.matmul(
                                    psum[0][0:16, h * 128:(h + 1) * 128], lhsT=wTm[:, h, s, :], rhs=Sgb[u][:, si, h, :],
                                    start=(si == 0), stop=(si == 3)), R=[b_wTm, b_Sgb[u]], W=[b_ps[0]],
                                    inc=(h == 3 and si == 3))
                        K.op("dve", lambda: nc.vector.tensor_tensor(
                            out=us[:], in0=s_u0[:], in1=psum[0][0:16, :].rearrange("p (a c) -> p a c", a=4), op=ALU.subtract),
                            R=[b_skeep, b_ps[0]], W=[b_us])
                        K.op("dve", lambda sg_=sg_: nc.vector.tensor_tensor(
                            out=ubd[:], in0=us[:, :, :].rearrange("p h v -> p (h v)").unsqueeze(1).broadcast_to([16, 4, 512]),
                            in1=C["ident"][0:16, 4 * sg_:4 * sg_ + 4].unsqueeze(2).broadcast_to([16, 4, 512]), op=ALU.mult),
                            R=[b_us, b_cst], W=[b_ubd])
                        for si in range(4):
                            for h in range(4):
                                K.op("pe", lambda h=h, si=si: nc.tensor.matmul(
                                    psum[1 + si][:, h * 128:(h + 1) * 128], lhsT=s_kd[0:16, h, :],
                                    rhs=ubd[0:16, si, h * 128:(h + 1) * 128], start=True, stop=True),
                                    R=[b_skeep, b_ubd], W=[b_ps[1 + si]], inc=(h == 3))
                        for si in range(4):
                            s = 4 * sg_ + si
                            for h in range(4):
                                K.op("dve", lambda h=h, si=si, s=s, u=u: nc.vector.scalar_tensor_tensor(
                                    out=Sg[u][:, si, h, :], in0=Sg[u][:, si, h, :], scalar=s_G[:, s, h:h + 1],
                                    in1=psum[1 + si][:, h * 128:(h + 1) * 128], op0=ALU.mult, op1=ALU.add),
                                    R=[b_skeep, b_ps[1 + si]], W=[b_Sg[u]])
                        K.op("act", lambda u=u: nc.scalar.copy(out=Sgb[u][:], in_=Sg[u][:]), R=[b_Sg[u]], W=[b_Sgb[u]])
                        for si in range(4):
                            s = 4 * sg_ + si
                            for h in range(4):
                                K.op("pe", lambda h=h, si=si, s=s, u=u: nc.tensor.matmul(
                                    psum[5][:, h * 16 + s:h * 16 + s + 1], lhsT=Sgb[u][:, si, h, :], rhs=s_qn[:, h, s:s + 1],
                                    start=True, stop=True), R=[b_Sgb[u], b_skeep], W=[b_ps[5]],
                                    inc=(h == 3 and si == 3))
                        K.dma("sp", sgd_o[:, 4 * sg_:4 * sg_ + 4], Sg[u][:], R=[b_Sg[u]])
                    K.op("act", lambda: nc.scalar.copy(out=oS[:], in_=psum[5][:, 0:64].rearrange("p (a b) -> p a b", a=4)),
                         R=[b_ps[5]], W=[b_oS])
                    gated_norm(s2, oS, b_oS, s_gz, b_skeep, nw, b_nw, og, b_og, 0, 0, 16,
                               (sq2, Buf(), rsd2, Buf(), tn2, Buf()))

        for l in range(layers):
            last_layer = (l == layers - 1)
            win_v = w_in[l].rearrange("(k p) n -> p k n", p=128)

            with ExitStack() as pm:
              og = big[:, 0:8 * T].rearrange("p (k t) -> p k t", k=8)
              b_og = Buf("og")
              if stub_mixer:
                  K.op("dve", lambda: nc.vector.tensor_copy(out=og, in_=x_bf[:]), R=[b_xbf], W=[b_og])
              else:
                  if mix_sel in ("all", "gdn"):
                      gdn_phase(l, og, b_og, win_v)
                      K.barrier()
                  if mix_sel in ("all", "gla"):
                      gla_phase(l, og, b_og, win_v)
              K.barrier()
              chk("mixer")
              v = sbt(pm, "v", [128, KC, T])
              b_v = Buf("v")
              with ExitStack() as pmg:
                mg = sbt(pmg, "mg", [128, KC, T], BF16)
                b_mg = Buf("mg")
                with ExitStack() as pb1:
                    NW = 2
                    wab = [sbt(pb1, "wab%d" % i, [128, 8, 128], BF16) for i in range(NW)]
                    wgg = [sbt(pb1, "wgg%d" % i, [128, 16, 128], BF16) for i in range(NW)]
                    b_wab = [Buf() for _ in range(NW)]
                    b_wgg = [Buf() for _ in range(NW)]
                    sg = [sbt(pb1, "sg%d" % i, [128, 2, 512]) for i in range(2)]
                    b_sg = [Buf(), Buf()]
                    wa_v = w_branch_a[l].rearrange("(k p) n -> p k n", p=128)
                    wb_v = w_branch_b[l].rearrange("(k p) n -> p k n", p=128)
                    it = 0
                    for jo in range(KC):
                        s = jo % NW
                        cs = slice(jo * 128, (jo + 1) * 128)
                        K.dma("pool", wab[s][:, 0:4, :], wa_v[:, :, cs], W=[b_wab[s]])
                        K.dma("pool", wab[s][:, 4:8, :], wb_v[:, :, cs], W=[b_wab[s]])
                        K.dma("pool", wgg[s][:, 0:8, :], win_v[:, :, 3608 + jo * 128:3608 + (jo + 1) * 128], W=[b_wgg[s]])
                        K.dma("pool", wgg[s][:, 8:16, :], win_v[:, :, 4632 + jo * 128:4632 + (jo + 1) * 128], W=[b_wgg[s]])
                        for (c0, n) in TILES:
                            q = 4 * (it % 2)
                            t2 = it % 2
                            it += 1
                            cols = slice(c0, c0 + n)
                            mm_group(psum[q + 0][:, 0:n], [(wab[s][:, k, :], og[:, k, cols]) for k in range(4)],
                                     R=[b_wab[s], b_og], W=[b_ps[q + 0]])
                            mm_group(psum[q + 1][:, 0:n], [(wab[s][:, 4 + k, :], og[:, 4 + k, cols]) for k in range(4)],
                                     R=[b_wab[s], b_og], W=[b_ps[q + 1]])
                            mm_group(psum[q + 2][:, 0:n], [(wgg[s][:, k, :], x_bf[:, k, cols]) for k in range(8)],
                                     R=[b_wgg[s], b_xbf], W=[b_ps[q + 2]])
                            mm_group(psum[q + 3][:, 0:n], [(wgg[s][:, 8 + k, :], x_bf[:, k, cols]) for k in range(8)],
                                     R=[b_wgg[s], b_xbf], W=[b_ps[q + 3]])
                            K.op("act", lambda q=q, t2=t2, n=n: nc.scalar.activation(
                                out=sg[t2][:, 0, 0:n], in_=psum[q + 2][:, 0:n], func=AF.Sigmoid),
                                R=[b_ps[q + 2]], W=[b_sg[t2]])
                            K.op("act", lambda q=q, t2=t2, n=n: nc.scalar.activation(
                                out=sg[t2][:, 1, 0:n], in_=psum[q + 3][:, 0:n], func=AF.Sigmoid),
                                R=[b_ps[q + 3]], W=[b_sg[t2]])
                            K.op("dve", lambda q=q, t2=t2, n=n: nc.vector.tensor_tensor(
                                out=sg[t2][:, 0, 0:n], in0=sg[t2][:, 0, 0:n], in1=psum[q + 0][:, 0:n], op=ALU.mult),
                                R=[b_sg[t2], b_ps[q + 0]], W=[b_sg[t2]])
                            K.op("dve", lambda q=q, t2=t2, n=n: nc.vector.tensor_tensor(
                                out=sg[t2][:, 1, 0:n], in0=sg[t2][:, 1, 0:n], in1=psum[q + 1][:, 0:n], op=ALU.mult),
                                R=[b_sg[t2], b_ps[q + 1]], W=[b_sg[t2]])
                            K.op("dve", lambda t2=t2, n=n, jo=jo, cols=cols: nc.vector.tensor_tensor(
                                out=mg[:, jo, cols], in0=sg[t2][:, 0, 0:n], in1=sg[t2][:, 1, 0:n], op=ALU.add),
                                R=[b_sg[t2]], W=[b_mg])
                K.barrier()
                chk("b1")
                K.dma("sp", v[:], xres, R=[b_xres], W=[b_v])
                with ExitStack() as pb2:
                    wo = [sbt(pb2, "wo%d" % i, [128, 8, 128], BF16) for i in range(2)]
                    b_wo = [Buf(), Buf()]
                    wo_v = w_out[l].rearrange("(k p) n -> p k n", p=128)
                    it = 0
                    for jo in range(KC):
                        s = jo % 2
                        K.dma("pool", wo[s][:], wo_v[:, :, jo * 128:(jo + 1) * 128], W=[b_wo[s]])
                        for (c0, n) in TILES:
                            q = it % 8
                            it += 1
                            cols = slice(c0, c0 + n)
                            mm_group(psum[q][:, 0:n], [(wo[s][:, k, :], mg[:, k, cols]) for k in range(8)],
                                     R=[b_wo[s], b_mg], W=[b_ps[q]])
                            K.op("dve", lambda q=q, n=n, jo=jo, cols=cols: nc.vector.scalar_tensor_tensor(
                                out=v[:, jo, cols], in0=v[:, jo, cols], scalar=ALPHA, in1=psum[q][:, 0:n],
                                op0=ALU.mult, op1=ALU.add), R=[b_ps[q]], W=[b_v])
                K.barrier()
                chk("b2")
              if True:
                layer_norm(pm, v, b_v, l, 0)
                K.barrier()
                chk("ln1")

                nh = (len(TILES) + 1) // 2
                halves = [TILES[:nh], TILES[nh:]]
                fin_v = w_ffn_in[l].rearrange("(k p) n -> p k n", p=128)
                fout_v = w_ffn_out[l].rearrange("(c p) n -> p c n", p=128)
                with ExitStack() as pf:
                    hid = big[:, 0:FC * HW].rearrange("p (c t) -> p c t", c=FC)
                    b_hid = Buf("hid")
                    wfi = [sbt(pf, "wfi%d" % i, [128, 2, KC, 256], BF16) for i in range(2)]
                    b_wfi = [Buf(), Buf()]
                    wfo = [sbt(pf, "wfo%d" % i, [128, FC, 128], BF16) for i in range(2)]
                    b_wfo = [Buf(), Buf()]
                    sil = [sbt(pf, "sil%d" % i, [128, 512]) for i in range(2)]
                    b_sil = [Buf(), Buf()]
                    it = 0
                    wi_it = 0
                    wo_it = 0
                    for half in halves:
                        if not half:
                            continue
                        h0 = half[0][0]
                        for g in range(FC // 2):
                            s = wi_it % 2
                            wi_it += 1
                            K.dma("pool", wfi[s][:, 0, :, :], fin_v[:, :, g * 256:(g + 1) * 256], W=[b_wfi[s]])
                            K.dma("pool", wfi[s][:, 1, :, :], fin_v[:, :, DFF + g * 256:DFF + (g + 1) * 256], W=[b_wfi[s]])
                            for jj in range(2):
                                j = 2 * g + jj
                                for (c0, n) in half:
                                    q = 2 * (it % 4)
                                    t2 = it % 2
                                    it += 1
                                    cols = slice(c0, c0 + n)
                                    hc = slice(c0 - h0, c0 - h0 + n)
                                    mm_group(psum[q][:, 0:n],
                                             [(wfi[s][:, 0, k, jj * 128:(jj + 1) * 128], x_bf[:, k, cols]) for k in range(8)],
                                             R=[b_wfi[s], b_xbf], W=[b_ps[q]])
                                    mm_group(psum[q + 1][:, 0:n],
                                             [(wfi[s][:, 1, k, jj * 128:(jj + 1) * 128], x_bf[:, k, cols]) for k in range(8)],
                                             R=[b_wfi[s], b_xbf], W=[b_ps[q + 1]])
                                    K.op("act", lambda q=q, t2=t2, n=n: nc.scalar.activation(
                                        out=sil[t2][:, 0:n], in_=psum[q][:, 0:n], func=AF.Silu),
                                        R=[b_ps[q]], W=[b_sil[t2]])
                                    K.op("dve", lambda q=q, t2=t2, n=n, j=j, hc=hc: nc.vector.tensor_tensor(
                                        out=hid[:, j, hc], in0=sil[t2][:, 0:n], in1=psum[q + 1][:, 0:n], op=ALU.mult),
                                        R=[b_sil[t2], b_ps[q + 1]], W=[b_hid])
                        for jo in range(KC):
                            s = wo_it % 2
                            wo_it += 1
                            K.dma("pool", wfo[s][:], fout_v[:, :, jo * 128:(jo + 1) * 128], W=[b_wfo[s]])
                            for (c0, n) in half:
                                q = it % 8
                                it += 1
                                cols = slice(c0, c0 + n)
                                hc = slice(c0 - h0, c0 - h0 + n)
                                mm_group(psum[q][:, 0:n], [(wfo[s][:, c, :], hid[:, c, hc]) for c in range(FC)],
                                         R=[b_wfo[s], b_hid], W=[b_ps[q]])
                                K.op("dve", lambda q=q, n=n, jo=jo, cols=cols: nc.vector.scalar_tensor_tensor(
                                    out=v[:, jo, cols], in0=v[:, jo, cols], scalar=ALPHA, in1=psum[q][:, 0:n],
                                    op0=ALU.mult, op1=ALU.add), R=[b_ps[q]], W=[b_v])
                K.barrier()
                chk("ffn")
                layer_norm(pm, v, b_v, l, 1, write_bf=not last_layer)
                K.barrier()
                chk("ln2")
                if not last_layer:
                    K.dma("sp", xres, v[:], R=[b_v], W=[b_xres])
                else:
                    with ExitStack() as po:
                        NR = 3
                        yst = [sbt(po, "yst%d" % i, [128, D]) for i in range(NR)]
                        b_yst = [Buf() for _ in range(NR)]
                        rows = [("ms", 0, 32)] + [("p", 128 * i, 128) for i in range(TP // 128)]

                        def out_tile(ri):
                            kind, r0, n = rows[ri]
                            s = ri % NR
                            c0 = 0 if kind == "ms" else 32 + r0
                            for g in range(2):
                                pb = (2 * ri + g) % 8
                                for kk in range(4):
                                    k = 4 * g + kk
                                    K.op("pe", lambda k=k, kk=kk, pb=pb: nc.tensor.transpose(
                                        psum[pb][0:n, kk * 128:(kk + 1) * 128], v[:, k, c0:c0 + n], C["ident"]),
                                        R=[b_v, b_cst], W=[b_ps[pb]], inc=(kk == 3), c=0.12)
                                if g == 0:
                                    K.op("act", lambda pb=pb: nc.scalar.copy(
                                        out=yst[s][0:n, 0:512], in_=psum[pb][0:n, :]), R=[b_ps[pb]], W=[b_yst[s]])
                                else:
                                    K.op("dve", lambda pb=pb: nc.vector.tensor_copy(
                                        out=yst[s][0:n, 512:1024], in_=psum[pb][0:n, :]), R=[b_ps[pb]], W=[b_yst[s]])
                            if kind == "ms":
                                K.dma("sp", y_sample, yst[s][0:16, :], R=[b_yst[s]])
                            else:
                                K.dma("sp", y_prompt[r0:r0 + 128, :], yst[s][0:128, :], R=[b_yst[s]])

                        K.run_sched([(lambda ri=ri: out_tile(ri), ([ri - NR] if ri >= NR else [])) for ri in range(len(rows))])
              K.barrier()
      except _Stop:
        pass
      K.finish()
    return nc


_NC_CACHE = {}


def kernel(x_prompt, x_sample, state_gdn, state_gla, state_conv, meta_tokens, w_in, conv_w, a_log, dt_bias,
           gdn_norm_w, gla_gate_w2, gla_gate_b, gla_norm_w, w_branch_a, w_branch_b, w_out,
           ln1_g, ln1_b, ln2_g, ln2_b, w_ffn_in, w_ffn_out, _build_kwargs=None):
    f = lambda a: np.ascontiguousarray(np.asarray(a), dtype=np.float32)
    x_prompt = f(x_prompt)
    TP = x_prompt.shape[1]
    bk = dict(_build_kwargs or {})
    key = (TP, tuple(sorted(bk.items())))
    if key not in _NC_CACHE:
        _NC_CACHE[key] = build(TP=TP, **bk)
    nc = _NC_CACHE[key]
    shared = dict(meta_tokens=f(meta_tokens), w_in=f(w_in), conv_w=f(conv_w), a_log=f(a_log), dt_bias=f(dt_bias),
                  gdn_norm_w=f(gdn_norm_w), gla_gate_w2=f(gla_gate_w2), gla_gate_b=f(gla_gate_b),
                  gla_norm_w=f(gla_norm_w), w_branch_a=f(w_branch_a), w_branch_b=f(w_branch_b), w_out=f(w_out),
                  ln1_g=f(ln1_g), ln1_b=f(ln1_b), ln2_g=f(ln2_g), ln2_b=f(ln2_b),
                  w_ffn_in=f(w_ffn_in), w_ffn_out=f(w_ffn_out), consts=CONST_ARR)
    x_sample = f(x_sample)
    state_gdn = f(state_gdn)
    state_gla = f(state_gla)
    state_conv = f(state_conv)
    in_maps = []
    for c in range(NCORES):
        sl = slice(NS * c, NS * (c + 1))
        m = dict(shared)
        m["x_prompt"] = x_prompt[c]
        m["x_sample"] = np.ascontiguousarray(x_sample[sl, 0, :])
        m["state_gdn"] = np.ascontiguousarray(state_gdn[:, sl])
        m["state_gla"] = np.ascontiguousarray(state_gla[:, sl])
        m["state_conv"] = np.ascontiguousarray(state_conv[:, sl])
        in_maps.append(m)
    res = run_bass_kernel_spmd(nc, in_maps, core_ids=list(range(NCORES)))
    R = res.results
    y_prompt = np.stack([R[c]["y_prompt"] for c in range(NCORES)], axis=0)
    y_sample = np.concatenate([R[c]["y_sample"] for c in range(NCORES)], axis=0)[:, None, :]
    gdn_p = np.stack([R[c]["new_gdn_prompt"] for c in range(NCORES)], axis=1)
    gla_p = np.stack([R[c]["new_gla_prompt"] for c in range(NCORES)], axis=1)
    conv_p = np.stack([R[c]["new_conv_prompt"] for c in range(NCORES)], axis=1)
    gdn_s = np.concatenate([R[c]["new_gdn_sample"] for c in range(NCORES)], axis=1)
    gla_s = np.concatenate([R[c]["new_gla_sample"] for c in range(NCORES)], axis=1)
    conv_s = np.concatenate([R[c]["new_conv_sample"] for c in range(NCORES)], axis=1)
    outs = (y_prompt, y_sample, gdn_p, gla_p, conv_p, gdn_s, gla_s, conv_s)
    return tuple(np.ascontiguousarray(o, dtype=np.float32) for o in outs)
```
